# Optimizing a Trainium2 kernel written in Bass

```python
import math
import jax
import jax.numpy as jnp
from jax import lax
import numpy as np

D_MODEL = 1024
BATCH = 2
SEQ = 16384
DEPTH = 2

GRID_W = 64
CTX_LEN = 256
HEAD_DIM = 64
NA_HEADS = 8
NA_WIN_ROWS = 8
NA_WIN_COLS = 16
MLA_HEADS = 8
MLA_Q_RANK = 256
MLA_KV_RANK = 128
MLA_NOPE = 64
MLA_ROPE = 32
MLA_V = 64
DIFF_HEADS = 4
DIFF_D = 64
DIFF_V = 2 * DIFF_D
A_W = NA_HEADS * HEAD_DIM
B_W = MLA_HEADS * MLA_V
C_QK = DIFF_HEADS * 2 * DIFF_D
C_W = DIFF_HEADS * DIFF_V
IN_SPLITS = (A_W, A_W, A_W, MLA_Q_RANK, MLA_KV_RANK, MLA_ROPE, C_QK, C_QK, C_W)
IN_DIM = sum(IN_SPLITS)
N_BRANCH = 3
N_EXPERTS = 16
EC_CAPACITY_FACTOR = 2
EXPERT_FF = 2816
QUERY_BLOCK = 128
ROPE_BASE = 10000.0
DEEPNORM_ALPHA = (2 * DEPTH) ** 0.25
DEEPNORM_BETA = (8 * DEPTH) ** -0.25
EPS = 1e-6

kernel_name = 'hybrid_natten_mla_diff_ecmoe_dit'


def layer_norm(x, g=None, b=None):
    xf = x.astype(jnp.float32)
    mu = jnp.mean(xf, axis=-1, keepdims=True)
    var = jnp.mean(jnp.square(xf - mu), axis=-1, keepdims=True)
    y = (xf - mu) * lax.rsqrt(var + EPS)
    if g is not None:
        y = y * g.astype(jnp.float32) + b.astype(jnp.float32)
    return y.astype(x.dtype)


def rms_norm(x, g):
    xf = x.astype(jnp.float32)
    y = xf * lax.rsqrt(jnp.mean(jnp.square(xf), axis=-1, keepdims=True) + EPS)
    return (y * g.astype(jnp.float32)).astype(x.dtype)


def modulate(x, shift, scale):
    return layer_norm(x) * (1 + scale) + shift


def to_heads(t, n_heads):
    return t.reshape(t.shape[0], t.shape[1], n_heads, -1)


def split_in(z):
    offs = np.cumsum(IN_SPLITS)[:-1].tolist()
    return jnp.split(z, offs, axis=-1)


def axial_angles(n_tokens, rot_dim):
    t = jnp.arange(n_tokens, dtype=jnp.int32)
    row = (t // GRID_W).astype(jnp.float32)
    col = (t % GRID_W).astype(jnp.float32)
    m = rot_dim // 2
    inv = ROPE_BASE ** (-jnp.arange(0, m, 2, dtype=jnp.float32) / m)
    return row[:, None] * inv, col[:, None] * inv


def rope_1d(x, ang):
    cos = jnp.cos(ang)[None, :, None, :].astype(x.dtype)
    sin = jnp.sin(ang)[None, :, None, :].astype(x.dtype)
    x1, x2 = jnp.split(x, 2, axis=-1)
    return jnp.concatenate([x1 * cos - x2 * sin, x1 * sin + x2 * cos], axis=-1)


def rope_2d(x, ang_row, ang_col):
    m = x.shape[-1] // 2
    return jnp.concatenate([rope_1d(x[..., :m], ang_row), rope_1d(x[..., m:], ang_col)], axis=-1)


def softmax_attend(q, k, v, scale):
    s = jnp.einsum('bqhd,bkhd->bhqk', q, k) * scale
    p = jax.nn.softmax(s.astype(jnp.float32), axis=-1).astype(v.dtype)
    return jnp.einsum('bhqk,bkhd->bqhd', p, v)


def diff_attend(q1, q2, k1, k2, v, lam, scale):
    p1 = jax.nn.softmax((jnp.einsum('bqhd,bkhd->bhqk', q1, k1) * scale).astype(jnp.float32), axis=-1)
    p2 = jax.nn.softmax((jnp.einsum('bqhd,bkhd->bhqk', q2, k2) * scale).astype(jnp.float32), axis=-1)
    a = (p1 - lam * p2).astype(v.dtype)
    return jnp.einsum('bhqk,bkhd->bqhd', a, v)


def sweep_query_blocks(fn, *qs):
    b, n = qs[0].shape[:2]
    nb = n // QUERY_BLOCK
    blocks = tuple(jnp.moveaxis(q.reshape(b, nb, QUERY_BLOCK, *q.shape[2:]), 1, 0) for q in qs)
    out = lax.map(lambda args: fn(*args), blocks)
    return jnp.moveaxis(out, 0, 1).reshape(b, n, *out.shape[3:])


def neighborhood_attend(q, k, v, k_ctx, v_ctx, rpb):
    b, n, h, d = q.shape
    rows = n // GRID_W
    kr = min(NA_WIN_ROWS, rows)
    kc = NA_WIN_COLS
    scale = d ** -0.5
    kg = k.reshape(b, rows, GRID_W, h, d)
    vg = v.reshape(b, rows, GRID_W, h, d)
    r_idx = jnp.arange(rows, dtype=jnp.int32)
    r_start = jnp.clip(r_idx - kr // 2, 0, rows - kr)
    c_idx = jnp.arange(GRID_W, dtype=jnp.int32)
    c_start = jnp.clip(c_idx - kc // 2, 0, GRID_W - kc)
    key_cols = c_start[:, None] + jnp.arange(kc, dtype=jnp.int32)[None, :]
    col_off = key_cols - c_idx[:, None] + (NA_WIN_COLS - 1)
    q_rows = jnp.moveaxis(q.reshape(b, rows, GRID_W, h, d), 1, 0)

    def one_row(args):
        qr, r, rs = args
        kw = lax.dynamic_slice_in_dim(kg, rs, kr, axis=1)[:, :, key_cols]
        vw = lax.dynamic_slice_in_dim(vg, rs, kr, axis=1)[:, :, key_cols]
        row_off = rs + jnp.arange(kr, dtype=jnp.int32) - r + (NA_WIN_ROWS - 1)
        bias = rpb[:, row_off][:, :, col_off]
        s_win = jnp.einsum('bchd,brckhd->bhcrk', qr, kw) * scale + jnp.transpose(bias, (0, 2, 1, 3))[None].astype(qr.dtype)
        s_ctx = jnp.einsum('bchd,blhd->bhcl', qr, k_ctx) * scale
        s = jnp.concatenate([s_win.reshape(b, h, GRID_W, kr * kc), s_ctx], axis=-1)
        p = jax.nn.softmax(s.astype(jnp.float32), axis=-1).astype(v.dtype)
        p_win = p[..., :kr * kc].reshape(b, h, GRID_W, kr, kc)
        p_ctx = p[..., kr * kc:]
        return (jnp.einsum('bhcrk,brckhd->bchd', p_win, vw)
                + jnp.einsum('bhcl,blhd->bchd', p_ctx, v_ctx))

    out = lax.map(one_row, (q_rows, r_idx, r_start))
    return jnp.moveaxis(out, 0, 1).reshape(b, n, h, d)


def mla_heads(cq, ckv, k_rope, g_q, g_kv, w_uq, w_ukv):
    q = to_heads(rms_norm(cq, g_q) @ w_uq, MLA_HEADS)
    kv = to_heads(rms_norm(ckv, g_kv) @ w_ukv, MLA_HEADS)
    return (q[..., :MLA_NOPE], q[..., MLA_NOPE:], kv[..., :MLA_NOPE], k_rope[:, :, None, :],
            kv[..., MLA_NOPE:])


def mla_qk(q_nope, q_rope, k_nope, k_rope):
    q = jnp.concatenate([q_nope, q_rope], axis=-1)
    k = jnp.concatenate([k_nope, jnp.broadcast_to(k_rope, k_nope.shape[:3] + (MLA_ROPE,))], axis=-1)
    return q, k


def diff_qk(t):
    t = t.reshape(t.shape[0], t.shape[1], DIFF_HEADS, 2, DIFF_D)
    return t[..., 0, :], t[..., 1, :]


def gated_merge(h, ya, yb, yc, w_br_a, w_br_b, w_br_c, w_gate, b_gate, w_out):
    ga, gb, gc = jnp.split(jax.nn.sigmoid(h @ w_gate + b_gate), N_BRANCH, axis=-1)
    flat = lambda y: y.reshape(y.shape[0], y.shape[1], -1)
    m = ga * (flat(ya) @ w_br_a) + gb * (flat(yb) @ w_br_b) + gc * (flat(yc) @ w_br_c)
    return m @ w_out


def token_mixer(h, hc, lam_init, w_in, rpb, g_q, g_kv, w_uq, w_ukv, lq1, lk1, lq2, lk2, g_sub,
                w_br_a, w_br_b, w_br_c, w_gate, b_gate, w_out, with_ctx_out):
    n = h.shape[1]
    z = split_in(h @ w_in)
    zc = split_in(hc @ w_in)
    ang_b = axial_angles(n, MLA_ROPE)
    ang_c = axial_angles(n, DIFF_D)

    qa, ka, va = (to_heads(t, NA_HEADS) for t in z[0:3])
    qa_c, ka_c, va_c = (to_heads(t, NA_HEADS) for t in zc[0:3])
    ya = neighborhood_attend(qa, ka, va, ka_c, va_c, rpb)

    qn, qr, kn, krp, vb = mla_heads(z[3], z[4], z[5], g_q, g_kv, w_uq, w_ukv)
    qb, kb = mla_qk(qn, rope_2d(qr, *ang_b), kn, rope_2d(krp, *ang_b))
    qn_c, qr_c, kn_c, krp_c, vb_c = mla_heads(zc[3], zc[4], zc[5], g_q, g_kv, w_uq, w_ukv)
    qb_c, kb_c = mla_qk(qn_c, qr_c, kn_c, krp_c)
    kb_all = jnp.concatenate([kb_c, kb], axis=1)
    vb_all = jnp.concatenate([vb_c, vb], axis=1)
    scale_b = (MLA_NOPE + MLA_ROPE) ** -0.5
    yb = sweep_query_blocks(lambda q: softmax_attend(q, kb_all, vb_all, scale_b), qb)

    lam = (jnp.exp(jnp.sum(lq1.astype(jnp.float32) * lk1.astype(jnp.float32)))
           - jnp.exp(jnp.sum(lq2.astype(jnp.float32) * lk2.astype(jnp.float32))) + lam_init)
    q1, q2 = diff_qk(z[6])
    k1, k2 = diff_qk(z[7])
    q1, q2, k1, k2 = (rope_2d(t, *ang_c) for t in (q1, q2, k1, k2))
    vc = to_heads(z[8], DIFF_HEADS)
    q1_c, q2_c = diff_qk(zc[6])
    k1_c, k2_c = diff_qk(zc[7])
    vc_c = to_heads(zc[8], DIFF_HEADS)
    k1_all = jnp.concatenate([k1_c, k1], axis=1)
    k2_all = jnp.concatenate([k2_c, k2], axis=1)
    vc_all = jnp.concatenate([vc_c, vc], axis=1)
    scale_c = DIFF_D ** -0.5
    yc = sweep_query_blocks(lambda a, bq: diff_attend(a, bq, k1_all, k2_all, vc_all, lam, scale_c), q1, q2)
    yc = rms_norm(yc, g_sub) * (1 - lam_init)

    y = gated_merge(h, ya, yb, yc, w_br_a, w_br_b, w_br_c, w_gate, b_gate, w_out)
    if not with_ctx_out:
        return y, None
    ya_c = softmax_attend(qa_c, ka_c, va_c, HEAD_DIM ** -0.5)
    yb_c = softmax_attend(qb_c, kb_c, vb_c, scale_b)
    yc_c = rms_norm(diff_attend(q1_c, q2_c, k1_c, k2_c, vc_c, lam, scale_c), g_sub) * (1 - lam_init)
    y_c = gated_merge(hc, ya_c, yb_c, yc_c, w_br_a, w_br_b, w_br_c, w_gate, b_gate, w_out)
    return y, y_c


def expert_choice_ffn(h, w_router, w_g, w_u, w_d):
    b, t, _ = h.shape
    cap = max(1, EC_CAPACITY_FACTOR * t // N_EXPERTS)
    aff = jax.nn.softmax((h @ w_router).astype(jnp.float32), axis=-1)
    gate_w, idx = lax.top_k(jnp.swapaxes(aff, 1, 2), cap)
    bidx = jnp.arange(b, dtype=jnp.int32)[:, None, None]
    x_sel = jnp.moveaxis(h[bidx, idx], 1, 0)

    def expert(args):
        xe, wg, wu, wd = args
        return (jax.nn.silu(xe @ wg) * (xe @ wu)) @ wd

    y = jnp.moveaxis(lax.map(expert, (x_sel, w_g, w_u, w_d)), 0, 1)
    y = y * gate_w[..., None].astype(y.dtype)
    return jnp.zeros_like(h).at[bidx, idx].add(y)


def setup_inputs(seed: int = 0) -> dict:
    key = jax.random.key(seed)
    ks = jax.random.split(key, 31)
    L, D, E, F = DEPTH, D_MODEL, N_EXPERTS, EXPERT_FF

    def nrm(i, shape, s):
        return jax.random.normal(ks[i], shape, jnp.float32) * s

    return {
        'x': nrm(0, (BATCH, SEQ, D), 1.0),
        'c': nrm(1, (BATCH, D), 1.0),
        'ctx': nrm(2, (BATCH, CTX_LEN, D), 1.0),
        'c_ctx': nrm(3, (D,), 1.0),
        'w_ada': nrm(4, (L, D, 6 * D), 0.5 * D ** -0.5),
        'b_ada': nrm(5, (L, 6 * D), 0.02),
        'w_in': nrm(6, (L, D, IN_DIM), D ** -0.5),
        'na_rpb': nrm(7, (L, NA_HEADS, 2 * NA_WIN_ROWS - 1, 2 * NA_WIN_COLS - 1), 0.1),
        'mla_g_q': 1.0 + nrm(8, (L, MLA_Q_RANK), 0.02),
        'mla_g_kv': 1.0 + nrm(9, (L, MLA_KV_RANK), 0.02),
        'mla_w_uq': nrm(10, (L, MLA_Q_RANK, MLA_HEADS * (MLA_NOPE + MLA_ROPE)), MLA_Q_RANK ** -0.5),
        'mla_w_ukv': nrm(11, (L, MLA_KV_RANK, MLA_HEADS * (MLA_NOPE + MLA_V)), MLA_KV_RANK ** -0.5),
        'diff_lq1': nrm(12, (L, DIFF_D), 0.1),
        'diff_lk1': nrm(13, (L, DIFF_D), 0.1),
        'diff_lq2': nrm(14, (L, DIFF_D), 0.1),
        'diff_lk2': nrm(15, (L, DIFF_D), 0.1),
        'diff_g_sub': 1.0 + nrm(16, (L, DIFF_V), 0.02),
        'w_br_a': nrm(17, (L, A_W, D), DEEPNORM_BETA * A_W ** -0.5),
        'w_br_b': nrm(18, (L, B_W, D), DEEPNORM_BETA * B_W ** -0.5),
        'w_br_c': nrm(19, (L, C_W, D), DEEPNORM_BETA * C_W ** -0.5),
        'w_gate': nrm(20, (L, D, N_BRANCH * D), D ** -0.5),
        'b_gate': nrm(21, (L, N_BRANCH * D), 0.02),
        'w_out': nrm(22, (L, D, D), DEEPNORM_BETA * D ** -0.5),
        'ln1_g': 1.0 + nrm(23, (L, D), 0.02),
        'ln1_b': nrm(24, (L, D), 0.02),
        'w_router': nrm(25, (L, D, E), D ** -0.5),
        'w_exp_gate': nrm(26, (L, E, D, F), D ** -0.5),
        'w_exp_up': nrm(27, (L, E, D, F), D ** -0.5),
        'w_exp_down': nrm(28, (L, E, F, D), DEEPNORM_BETA * F ** -0.5),
        'ln2_g': 1.0 + nrm(29, (L, D), 0.02),
        'ln2_b': nrm(30, (L, D), 0.02),
    }


def reference(x, c, ctx, c_ctx, w_ada, b_ada, w_in, na_rpb, mla_g_q, mla_g_kv, mla_w_uq, mla_w_ukv,
              diff_lq1, diff_lk1, diff_lq2, diff_lk2, diff_g_sub, w_br_a, w_br_b, w_br_c,
              w_gate, b_gate, w_out, ln1_g, ln1_b, w_router, w_exp_gate, w_exp_up, w_exp_down,
              ln2_g, ln2_b):
    for l in range(DEPTH):
        last = l == DEPTH - 1
        lam_init = 0.8 - 0.6 * math.exp(-0.3 * l)
        mod = jax.nn.silu(c) @ w_ada[l] + b_ada[l]
        mod_c = jax.nn.silu(c_ctx) @ w_ada[l] + b_ada[l]
        sh1, sc1, g1, sh2, sc2, g2 = jnp.split(mod[:, None, :], 6, axis=-1)
        csh1, csc1, cg1, csh2, csc2, cg2 = jnp.split(mod_c, 6, axis=-1)

        h = modulate(x, sh1, sc1)
        hc = modulate(ctx, csh1, csc1)
        y, y_c = token_mixer(h, hc, lam_init, w_in[l], na_rpb[l], mla_g_q[l], mla_g_kv[l],
                             mla_w_uq[l], mla_w_ukv[l], diff_lq1[l], diff_lk1[l], diff_lq2[l],
                             diff_lk2[l], diff_g_sub[l], w_br_a[l], w_br_b[l], w_br_c[l],
                             w_gate[l], b_gate[l], w_out[l], not last)
        x = layer_norm(DEEPNORM_ALPHA * x + g1 * y, ln1_g[l], ln1_b[l])

        h2 = modulate(x, sh2, sc2)
        f = expert_choice_ffn(h2, w_router[l], w_exp_gate[l], w_exp_up[l], w_exp_down[l])
        x = layer_norm(DEEPNORM_ALPHA * x + g2 * f, ln2_g[l], ln2_b[l])

        if not last:
            ctx = layer_norm(DEEPNORM_ALPHA * ctx + cg1 * y_c, ln1_g[l], ln1_b[l])
            hc2 = modulate(ctx, csh2, csc2)
            f_c = expert_choice_ffn(hc2, w_router[l], w_exp_gate[l], w_exp_up[l], w_exp_down[l])
            ctx = layer_norm(DEEPNORM_ALPHA * ctx + cg2 * f_c, ln2_g[l], ln2_b[l])
    return x
```

```python
import numpy as np
import concourse.bass as bass
import concourse.mybir as mybir
from concourse.bass_utils import run_bass_kernel_spmd

F32 = mybir.dt.float32
BF16 = mybir.dt.bfloat16
I32 = mybir.dt.int32
U32 = mybir.dt.uint32
ALU = mybir.AluOpType
AF = mybir.ActivationFunctionType
AX = mybir.AxisListType

COMPUTE = ("pe", "act", "dve", "pool")
SAME_ENGINE_SYNC = True


class T:
    def __init__(self, prog, handle, name):
        self.p = prog
        self.h = handle
        self.name = name
        self.lastw = None
        self.reads = []
        self.dsem = None
        self.dcnt = 0

    def __getitem__(self, idx):
        return self.h[idx]


class Prog:
    def __init__(self, name):
        self.name = name
        self.nc = bass.Bass("TRN2", target_bir_lowering=False)
        self.ops = {e: [] for e in ("pe", "act", "dve", "pool", "sp")}
        self.cnt = {e: 0 for e in COMPUTE}
        self.seen = {e: {} for e in self.ops}
        self.ctxs = []
        self.tiles = []
        self.sems = {}
        self.outs = []
        self.nsem = 0

    def _enter(self, cm):
        v = cm.__enter__()
        self.ctxs.append(cm)
        return v

    def sem(self, name):
        self.nsem += 1
        return self._enter(self.nc.semaphore(name))

    def sbuf(self, name, shape, dt):
        t = T(self, self._enter(self.nc.sbuf_tensor(name, list(shape), dt)), name)
        self.tiles.append(t)
        return t

    def psum(self, name, shape, dt=F32):
        t = T(self, self._enter(self.nc.psum_tensor(name, list(shape), dt)), name)
        self.tiles.append(t)
        return t

    def dram(self, name, shape, dt, kind="Internal"):
        h = self.nc.dram_tensor(name, list(shape), dt, kind=kind)
        t = T(self, h.ap(), name)
        self.tiles.append(t)
        if kind == "ExternalOutput":
            self.outs.append(t)
        return t

    def start(self):
        for e in COMPUTE:
            self.sems[e] = self.sem("s_" + e)

    def _need(self, eng, hz, waits):
        if hz is None:
            return
        kind, key, val = hz
        if kind == "eng" and key == eng and (eng == "pe" or not SAME_ENGINE_SYNC):
            return
        k = (kind, key if kind == "eng" else id(key))
        if self.seen[eng].get(k, 0) >= val:
            return
        self.seen[eng][k] = val
        semh = self.sems[key] if kind == "eng" else key.dsem
        waits.append((semh, val))

    def op(self, eng, fn, reads=(), writes=(), dma_dst=None, ordered=False):
        waits = []
        for t in reads:
            self._need(eng, t.lastw, waits)
        for t in writes:
            if dma_dst is not None and t is dma_dst and not ordered and t.lastw is not None \
                    and t.lastw[0] == "dma" and not t.reads:
                pass
            else:
                self._need(eng, t.lastw, waits)
            for r in t.reads:
                self._need(eng, r, waits)
        if dma_dst is not None:
            if dma_dst.dsem is None:
                dma_dst.dsem = self.sem("d_" + dma_dst.name)
            dma_dst.dcnt += 16
            hz = ("dma", dma_dst, dma_dst.dcnt)
            inc = (dma_dst.dsem, 16)
        else:
            self.cnt[eng] += 1
            hz = ("eng", eng, self.cnt[eng])
            inc = (self.sems[eng], 1)
        for t in reads:
            t.reads.append(hz)
        for t in writes:
            t.lastw = hz
            t.reads = []
        self.ops[eng].append((waits, fn, inc))

    def dma(self, out_t, out_ap, in_t, in_ap, q="sp", ordered=False, **kw):
        self.op(q, lambda e: e.dma_start(out=out_ap, in_=in_ap, **kw),
                reads=[in_t], writes=[out_t], dma_dst=out_t, ordered=ordered)

    def mm(self, out_t, out_ap, a_t, lhsT, b_t, rhs, start=True, stop=True, **kw):
        self.op("pe", lambda e: e.matmul(out_ap, lhsT, rhs, start=start, stop=stop, **kw),
                reads=[a_t, b_t], writes=[out_t])

    def tr(self, out_t, out_ap, in_t, in_ap, ident_t, ident_ap):
        self.op("pe", lambda e: e.transpose(out_ap, in_ap, ident_ap), reads=[in_t, ident_t], writes=[out_t])


    def act(self, out_t, out_ap, in_t, in_ap, func, extra=(), **kw):
        self.op("act", lambda e: e.activation(out_ap, in_ap, func, **kw), reads=[in_t, *extra], writes=[out_t])

    def ts(self, eng, out_t, out_ap, in_t, in_ap, s1, s2, op0, op1=None, extra=()):
        if op1 is None:
            self.op(eng, lambda e: e.tensor_scalar(out_ap, in_ap, s1, None, op0), reads=[in_t, *extra], writes=[out_t])
        else:
            self.op(eng, lambda e: e.tensor_scalar(out_ap, in_ap, s1, s2, op0, op1), reads=[in_t, *extra], writes=[out_t])

    def tt(self, eng, out_t, out_ap, a_t, a_ap, b_t, b_ap, op):
        self.op(eng, lambda e: e.tensor_tensor(out_ap, a_ap, b_ap, op), reads=[a_t, b_t], writes=[out_t])

    def stt(self, eng, out_t, out_ap, a_t, a_ap, scalar, b_t, b_ap, op0, op1, extra=()):
        self.op(eng, lambda e: e.scalar_tensor_tensor(out_ap, a_ap, scalar, b_ap, op0, op1),
                reads=[a_t, b_t, *extra], writes=[out_t])

    def cp(self, eng, out_t, out_ap, in_t, in_ap):
        if eng == "act":
            self.op("act", lambda e: e.copy(out_ap, in_ap), reads=[in_t], writes=[out_t])
        else:
            self.op(eng, lambda e: e.tensor_copy(out_ap, in_ap), reads=[in_t], writes=[out_t])

    def memset(self, eng, t, ap, val):
        self.op(eng, lambda e: e.memset(ap, val), reads=[], writes=[t])

    def banks(self, n=7):
        self.pb = [self.psum("pb%d" % i, [128, 512], F32) for i in range(n)]
        self.pbi = 0

    def bank(self):
        b = self.pb[self.pbi % len(self.pb)]
        self.pbi += 1
        return b

    def rsqrt(self, out_t, out_ap, in_t, in_ap, scale, eps_t, eps_ap):
        self.op("act", lambda e: e.activation(out_ap, in_ap, AF.Sqrt, bias=eps_ap, scale=scale),
                reads=[in_t, eps_t], writes=[out_t])
        self.op("dve", lambda e: e.reciprocal(out_ap, out_ap), reads=[out_t], writes=[out_t])

    def finish(self):
        waits = []
        for t in self.tiles:
            if t.lastw is not None and t.lastw[0] == "dma":
                self._need("sp", t.lastw, waits)
        self.ops["sp"].append((waits, None, None))

    def build(self):
        nc = self.nc
        engs = {"pe": "tensor", "act": "scalar", "dve": "vector", "pool": "gpsimd", "sp": "sync"}
        with nc.Block() as block:
            for e, attr in engs.items():
                ops = self.ops[e]

                def body(engobj, ops=ops):
                    for waits, fn, inc in ops:
                        for semh, val in waits:
                            engobj.wait_ge(semh, val)
                        if fn is not None:
                            ins = fn(engobj)
                            ins.then_inc(inc[0], inc[1])
                getattr(block, attr)(body)
        for cm in reversed(self.ctxs):
            cm.__exit__(None, None, None)
        self.ctxs = []
        return nc

    def n_ops(self):
        return {e: len(v) for e, v in self.ops.items()}


def run(prog_nc, in_maps, trace=False):
    import sys, time
    t0 = time.time()
    res = run_bass_kernel_spmd(prog_nc, in_maps, core_ids=list(range(len(in_maps))), trace=trace)
    print("[kernel] launch done in %.1fs" % (time.time() - t0), file=sys.stderr, flush=True)
    return res


def build_L0():
    P = Prog("L0")
    nc = P.nc
    w = P.dram("w", [1024, 1536], F32, kind="ExternalInput")
    bT = P.dram("bT", [128, 12], F32, kind="ExternalInput")
    cT = P.dram("cT", [128, 8, 3], F32, kind="ExternalInput")
    out = P.dram("modT", [128, 12, 3], F32, kind="ExternalOutput")
    P.start()
    wt = P.sbuf("wt", [128, 8, 1536], F32)
    bt = P.sbuf("bt", [128, 12], F32)
    ct = P.sbuf("ct", [128, 8, 3], F32)
    st = P.sbuf("st", [128, 8, 3], F32)
    ot = P.sbuf("ot", [128, 12, 3], F32)
    ps = P.psum("ps", [128, 512], F32)
    P.dma(ct, ct[:], cT, cT[:])
    P.dma(bt, bt[:], bT, bT[:])
    for k in range(8):
        P.dma(wt, wt[:, k, :], w, w[k * 128:(k + 1) * 128, :])
    P.op("act", lambda e: e.activation(st[:], ct[:], AF.Silu), reads=[ct], writes=[st])
    for j in range(12):
        for k in range(8):
            P.mm(ps, ps[:, j * 3:(j + 1) * 3], wt, wt[:, k, j * 128:(j + 1) * 128], st, st[:, k, :],
                 start=(k == 0), stop=(k == 7))
    for j in range(12):
        P.op("dve", lambda e, j=j: e.tensor_scalar(ot[:, j, :], ps[:, j * 3:(j + 1) * 3], bt[:, j:j + 1], None, ALU.add),
             reads=[ps, bt], writes=[ot])
    P.dma(out, out[:], ot, ot[:])
    P.finish()
    return P.build()


def run_L0(inp):
    cvec = np.stack([inp["c"][0], inp["c"][1], inp["c_ctx"]], 0)
    cT = np.ascontiguousarray(cvec.T.reshape(8, 128, 3).transpose(1, 0, 2))
    in_maps = []
    for core in range(8):
        l, q = core // 4, core % 4
        cols = slice(q * 1536, (q + 1) * 1536)
        in_maps.append({
            "w": np.ascontiguousarray(inp["w_ada"][l][:, cols]),
            "bT": np.ascontiguousarray(inp["b_ada"][l][cols].reshape(12, 128).T),
            "cT": cT,
        })
    res = run(build_L0(), in_maps)
    mods = []
    for l in range(2):
        parts = []
        for q in range(4):
            m = np.asarray(res.results[l * 4 + q]["modT"])
            parts.append(m.transpose(1, 0, 2).reshape(1536, 3))
        mods.append(np.concatenate(parts, 0))
    return mods


NT = 16640
LN_EPS = 1e-6
NCH = 33


def chunk_rng(c):
    return (0, 256) if c == 0 else (256 + (c - 1) * 512, 512)


def build_A(lam_init, dbg=False):
    P = Prog("A")
    EI = "ExternalInput"
    xin = P.dram("xin", [NT, 1024], F32, EI)
    msT = P.dram("msT", [128, 8, 4], F32, EI)
    wfeat = P.dram("wfeat", [1024, 1216], F32, EI)
    wtok = P.dram("wtok", [1024, 256], F32, EI)
    wuq = P.dram("wuq", [256, 2, 192], F32, EI)
    wukvk = P.dram("wukvk", [128, 2, 64], F32, EI)
    wukvv = P.dram("wukvv", [128, 128], F32, EI)
    gq = P.dram("gq", [128, 2], F32, EI)
    gkv = P.dram("gkv", [128, 1], F32, EI)
    cs32 = P.dram("cs32", [32, 2, NT], F32, EI)
    cs128 = P.dram("cs128", [128, 2, NT], F32, EI)
    nabias = P.dram("nabias", [2, 3, 768, 256], F32, EI)
    lamv = P.dram("lamv", [128, 4, 64], F32, EI)
    gsub = P.dram("gsub", [128, 1], F32, EI)
    identd = P.dram("ident", [128, 128], F32, EI)
    EO = "ExternalOutput"
    yaT = P.dram("yaT", [128, NT], BF16, EO)
    ybT = P.dram("ybT", [128, NT], BF16, EO)
    ycT = P.dram("ycT", [128, NT], BF16, EO)
    sk = EO if dbg else "Internal"
    qaT = P.dram("qaT", [128, NT], BF16, sk)
    kaT = P.dram("kaT", [128, NT], BF16, sk)
    va = P.dram("va", [NT, 128], BF16, sk)
    qbT = P.dram("qbT", [2, 96, NT], BF16, sk)
    kbT = P.dram("kbT", [2, 96, NT], BF16, sk)
    vb = P.dram("vb", [NT, 128], BF16, sk)
    dqT = P.dram("dqT", [128, NT], BF16, sk)
    dkT = P.dram("dkT", [128, NT], BF16, sk)
    dvv = P.dram("dvv", [NT, 128], BF16, sk)
    P.start()
    P.banks(7)
    tp = P.psum("tp", [128, 1024], BF16)

    Wf = P.sbuf("Wf", [128, 8, 1216], BF16)
    Wt = P.sbuf("Wt", [128, 8, 256], BF16)
    for k in range(8):
        P.dma(Wf, Wf[:, k, :], wfeat, wfeat[k * 128:(k + 1) * 128, :], q="pool")
        P.dma(Wt, Wt[:, k, :], wtok, wtok[k * 128:(k + 1) * 128, :], q="pool")
    ident = P.sbuf("identb", [128, 128], BF16)
    P.dma(ident, ident[:], identd, identd[:], q="pool")
    ones = P.sbuf("ones", [128, 128], BF16)
    P.memset("dve", ones, ones[:], 1.0)
    onesf = P.sbuf("onesf", [128, 64], F32)
    epst = P.sbuf("epst", [128, 1], F32)
    P.memset("dve", epst, epst[:], LN_EPS)
    P.memset("dve", onesf, onesf[:], 1.0)
    ms = P.sbuf("ms", [128, 8, 4], F32)
    P.dma(ms, ms[:], msT, msT[:])
    scp = P.sbuf("scp", [128, 8, 2], F32)
    for m in range(2):
        P.ts("dve", scp, scp[:, :, m], ms, ms[:, :, 2 * m + 1], 1.0, None, ALU.add)
    gqt = P.sbuf("gqt", [128, 2], F32)
    P.dma(gqt, gqt[:], gq, gq[:])
    gkvt = P.sbuf("gkvt", [128, 1], F32)
    P.dma(gkvt, gkvt[:], gkv, gkv[:])
    wuq_r = P.sbuf("wuq_r", [128, 2, 2, 192], F32)
    for k in range(2):
        P.dma(wuq_r, wuq_r[:, k], wuq, wuq[k * 128:(k + 1) * 128])
    Wuq = P.sbuf("Wuq", [128, 2, 2, 192], BF16)
    for k in range(2):
        P.ts("dve", Wuq, Wuq[:, k], wuq_r, wuq_r[:, k], gqt[:, k:k + 1], None, ALU.mult, extra=[gqt])
    wk_r = P.sbuf("wk_r", [128, 2, 64], F32)
    P.dma(wk_r, wk_r[:], wukvk, wukvk[:])
    Wkk = P.sbuf("Wkk", [128, 2, 64], BF16)
    P.ts("dve", Wkk, Wkk[:], wk_r, wk_r[:], gkvt[:, 0:1], None, ALU.mult, extra=[gkvt])
    wv_r = P.sbuf("wv_r", [128, 128], F32)
    P.dma(wv_r, wv_r[:], wukvv, wukvv[:])
    Wkv = P.sbuf("Wkv", [128, 128], BF16)
    P.ts("dve", Wkv, Wkv[:], wv_r, wv_r[:], gkvt[:, 0:1], None, ALU.mult, extra=[gkvt])
    lv = P.sbuf("lv", [128, 4, 64], F32)
    P.dma(lv, lv[:], lamv, lamv[:])
    lpr = P.sbuf("lpr", [128, 2, 64], F32)
    P.tt("dve", lpr, lpr[:, 0], lv, lv[:, 0], lv, lv[:, 1], ALU.mult)
    P.tt("dve", lpr, lpr[:, 1], lv, lv[:, 2], lv, lv[:, 3], ALU.mult)
    lsum = P.sbuf("lsum", [128, 2], F32)
    P.op("dve", lambda e: e.reduce_sum(lsum[:], lpr[:], AX.X), reads=[lpr], writes=[lsum])
    lexp = P.sbuf("lexp", [128, 2], F32)
    P.act(lexp, lexp[:], lsum, lsum[:], AF.Exp)
    neglam = P.sbuf("neglam", [128, 1], F32)
    P.stt("dve", neglam, neglam[:], lexp, lexp[:, 1:2], -float(lam_init), lexp, lexp[:, 0:1], ALU.add, ALU.subtract)
    gsr = P.sbuf("gsr", [128, 1], F32)
    P.dma(gsr, gsr[:], gsub, gsub[:])
    gss = P.sbuf("gss", [128, 1], F32)
    P.ts("dve", gss, gss[:], gsr, gsr[:], 1.0 - float(lam_init), None, ALU.mult)

    xb = [P.sbuf("xb%d" % i, [128, 1024], F32) for i in range(2)]
    xnb = [P.sbuf("xn%d" % i, [128, 1024], BF16) for i in range(2)]
    st6 = P.sbuf("st6", [128, 2, 6], F32)
    mv = P.sbuf("mv", [128, 2], F32)
    rstd = P.sbuf("rstd", [128, 1], F32)
    htmp = P.sbuf("htmp", [128, 8, 128], F32)
    hTb = [P.sbuf("hT%d" % i, [128, 8, 512], BF16) for i in range(2)]
    t32 = [P.sbuf("t32_%d" % i, [128, 2, 512], F32) for i in range(2)]
    t128 = [P.sbuf("t128_%d" % i, [128, 2, 512], F32) for i in range(2)]
    stg = [P.sbuf("stg%d" % i, [128, 512], BF16) for i in range(3)]
    stgi = [0]

    def nstg():
        s_ = stg[stgi[0] % len(stg)]
        stgi[0] += 1
        return s_
    cqT = P.sbuf("cqT", [128, 2, 512], BF16)
    cqsq = P.sbuf("cqsq", [128, 2, 512], BF16)
    ckvT = P.sbuf("ckvT", [128, 512], BF16)
    ckvsq = P.sbuf("ckvsq", [128, 512], BF16)
    rq = P.sbuf("rq", [128, 512], F32)
    rkv = P.sbuf("rkv", [128, 512], F32)
    rkvt = P.sbuf("rkvt", [128, 4], F32)
    ra = [P.sbuf("ra%d" % i, [128, 512], F32) for i in range(2)]
    rb = [P.sbuf("rb%d" % i, [128, 512], F32) for i in range(2)]
    vst = [P.sbuf("vst%d" % i, [128, 256], BF16) for i in range(2)]
    krT = P.sbuf("krT", [32, 512], BF16)

    def rope(n, pa, pb_, rows, tab, ti, out_t, out_ap, mul_t=None, mul_ap=None):
        a = ra[ti % 2]
        b = rb[ti % 2]
        P.tt("dve", a, a[rows, :n], pa, pa[rows, :n], tab, tab[rows, 0, :n], ALU.mult)
        P.tt("dve", b, b[rows, :n], pb_, pb_[rows, :n], tab, tab[rows, 1, :n], ALU.mult)
        if mul_t is None:
            P.tt("pool", out_t, out_ap, a, a[rows, :n], b, b[rows, :n], ALU.add)
        else:
            P.tt("pool", a, a[rows, :n], a, a[rows, :n], b, b[rows, :n], ALU.add)
            P.tt("pool", out_t, out_ap, a, a[rows, :n], mul_t, mul_ap, ALU.mult)

    FO = {"qa": 0, "ka": 128, "cq0": 256, "cq1": 384, "ckv": 512, "kr": 640, "krp": 672,
          "dq": 704, "dqp": 832, "dk": 960, "dkp": 1088}

    def proj(hT, name, M, n):
        pbk = P.bank()
        off = FO[name]
        for k in range(8):
            P.mm(pbk, pbk[:M, :n], Wf, Wf[:, k, off:off + M], hT, hT[:, k, :n], start=(k == 0), stop=(k == 7))
        return pbk

    for c in range(NCH):
        tok0, n = chunk_rng(c)
        m = 1 if c == 0 else 0
        nt = n // 128
        hT = hTb[c % 2]
        tb32 = t32[c % 2]
        tb128 = t128[c % 2]
        P.dma(tb32, tb32[0:32, :, :n], cs32, cs32[:, :, tok0:tok0 + n])
        P.dma(tb32, tb32[64:96, :, :n], cs32, cs32[:, :, tok0:tok0 + n])
        P.dma(tb128, tb128[:, :, :n], cs128, cs128[:, :, tok0:tok0 + n])
        for i in range(nt):
            xt = xb[i % 2]
            xn = xnb[i % 2]
            P.dma(xt, xt[:], xin, xin[tok0 + i * 128: tok0 + (i + 1) * 128, :])
            for a_ in range(2):
                P.op("dve", lambda e, xt=xt, a_=a_: e.bn_stats(st6[:, a_, :], xt[:, a_ * 512:(a_ + 1) * 512]),
                     reads=[xt], writes=[st6])
            P.op("dve", lambda e: e.bn_aggr(mv[:], st6[:].rearrange("p a b -> p (a b)")), reads=[st6], writes=[mv])
            P.rsqrt(rstd, rstd[:], mv, mv[:, 1:2], 1.0, epst, epst[:, 0:1])
            P.ts("dve", xn, xn[:], xt, xt[:], mv[:, 0:1], rstd[:, 0:1], ALU.subtract, ALU.mult, extra=[mv, rstd])
            for k in range(8):
                P.tr(tp, tp[:, k * 128:(k + 1) * 128], xn, xn[:, k * 128:(k + 1) * 128], ident, ident[:])
            P.tt("dve", htmp, htmp[:], tp, tp[:].rearrange("p (a b) -> p a b", a=8),
                 scp, scp[:, :, m:m + 1].to_broadcast([128, 8, 128]), ALU.mult)
            P.tt("pool", hT, hT[:, :, i * 128:(i + 1) * 128], htmp, htmp[:],
                 ms, ms[:, :, 2 * m:2 * m + 1].to_broadcast([128, 8, 128]), ALU.add)
        for name, dst in (("qa", qaT), ("ka", kaT)):
            pbk = proj(hT, name, 128, n)
            s_ = nstg()
            P.cp("act", s_, s_[:, :n], pbk, pbk[:, :n])
            P.dma(dst, dst[:, tok0:tok0 + n], s_, s_[:, :n])
        for j, name in enumerate(("cq0", "cq1")):
            pbk = proj(hT, name, 128, n)
            P.cp("act", cqT, cqT[:, j, :n], pbk, pbk[:, :n])
            P.act(cqsq, cqsq[:, j, :n], pbk, pbk[:, :n], AF.Square)
        pbk = proj(hT, "ckv", 128, n)
        P.cp("act", ckvT, ckvT[:, :n], pbk, pbk[:, :n])
        P.act(ckvsq, ckvsq[:, :n], pbk, pbk[:, :n], AF.Square)
        pa = proj(hT, "kr", 32, n)
        pb_ = proj(hT, "krp", 32, n)
        rope(n, pa, pb_, slice(0, 32), tb32, 0, krT, krT[:, :n])
        for h in range(2):
            P.dma(kbT, kbT[h, 64:96, tok0:tok0 + n], krT, krT[:, :n])
        for (nm, nmp, dst, ti) in (("dq", "dqp", dqT, 0), ("dk", "dkp", dkT, 1)):
            pa = proj(hT, nm, 128, n)
            pb_ = proj(hT, nmp, 128, n)
            s_ = nstg()
            rope(n, pa, pb_, slice(0, 128), tb128, ti, s_, s_[:, :n])
            P.dma(dst, dst[:, tok0:tok0 + n], s_, s_[:, :n])
        for i in range(nt):
            pbk = P.bank()
            for k in range(8):
                P.mm(pbk, pbk[:, 0:256], hT, hT[:, k, i * 128:(i + 1) * 128], Wt, Wt[:, k, :], start=(k == 0), stop=(k == 7))
            v_ = vst[i % 2]
            P.cp("act", v_, v_[:], pbk, pbk[:, 0:256])
            r0 = tok0 + i * 128
            P.dma(va, va[r0:r0 + 128, :], v_, v_[:, 0:128])
            P.dma(dvv, dvv[r0:r0 + 128, :], v_, v_[:, 128:256])
        pbk = P.bank()
        P.mm(pbk, pbk[:, :n], ones, ones[:], cqsq, cqsq[:, 0, :n], start=True, stop=False)
        P.mm(pbk, pbk[:, :n], ones, ones[:], cqsq, cqsq[:, 1, :n], start=False, stop=True)
        P.rsqrt(rq, rq[:, :n], pbk, pbk[:, :n], 1.0 / 256, epst, epst[:, 0:1])
        pbk = P.bank()
        P.mm(pbk, pbk[:, :n], ones, ones[:], ckvsq, ckvsq[:, :n])
        P.rsqrt(rkv, rkv[:, :n], pbk, pbk[:, :n], 1.0 / 128, epst, epst[:, 0:1])
        pbk = P.bank()
        for i in range(nt):
            P.mm(pbk, pbk[:, i:i + 1], ckvsq, ckvsq[:, i * 128:(i + 1) * 128], ones, ones[:, 0:1])
        P.rsqrt(rkvt, rkvt[:, :nt], pbk, pbk[:, :nt], 1.0 / 128, epst, epst[:, 0:1])
        for h in range(2):
            qa_ = P.bank()
            for k in range(2):
                P.mm(qa_, qa_[:96, :n], Wuq, Wuq[:, k, h, 0:96], cqT, cqT[:, k, :n], start=(k == 0), stop=(k == 1))
            qb_ = P.bank()
            for k in range(2):
                P.mm(qb_, qb_[:96, :n], Wuq, Wuq[:, k, h, 96:192], cqT, cqT[:, k, :n], start=(k == 0), stop=(k == 1))
            s_ = nstg()
            P.tt("dve", s_, s_[0:64, :n], qa_, qa_[0:64, :n], rq, rq[0:64, :n], ALU.mult)
            rope(n, qa_, qb_, slice(64, 96), tb32, h, s_, s_[64:96, :n], rq, rq[64:96, :n])
            P.dma(qbT, qbT[h, :, tok0:tok0 + n], s_, s_[0:96, :n])
            kn = P.bank()
            P.mm(kn, kn[:64, :n], Wkk, Wkk[:, h, :], ckvT, ckvT[:, :n])
            s_ = nstg()
            P.tt("dve", s_, s_[0:64, :n], kn, kn[0:64, :n], rkv, rkv[0:64, :n], ALU.mult)
            P.dma(kbT, kbT[h, 0:64, tok0:tok0 + n], s_, s_[0:64, :n])
        for i in range(nt):
            pbk = P.bank()
            P.mm(pbk, pbk[:, 0:128], ckvT, ckvT[:, i * 128:(i + 1) * 128], Wkv, Wkv[:])
            v_ = vst[i % 2]
            P.ts("dve", v_, v_[:, 0:128], pbk, pbk[:, 0:128], rkvt[:, i:i + 1], None, ALU.mult, extra=[rkvt])
            r0 = tok0 + i * 128
            P.dma(vb, vb[r0:r0 + 128, :], v_, v_[:, 0:128])
    if dbg == "A1":
        P.finish()
        return P.build()

    BK = P.sbuf("BK", [128, NT], BF16)
    BV = P.sbuf("BV", [128, 130, 130], BF16)
    pts = [P.sbuf("pt%d" % i, [128, 512], BF16) for i in range(4)]
    pti = [0]

    def npt():
        t_ = pts[pti[0] % 4]
        pti[0] += 1
        return t_
    qcs = [P.sbuf("qc%d" % i, [128, 512], BF16) for i in range(2)]
    rec = P.sbuf("rec", [128, 512], F32)
    rec2 = P.sbuf("rec2", [128, 512], F32)
    bsb = P.sbuf("bsb", [128, 512], F32)
    ysb = [P.sbuf("ysb%d" % i, [128, 512], BF16) for i in range(2)]
    yf = P.sbuf("yf", [128, 512], F32)
    yf2 = P.sbuf("yf2", [128, 512], F32)
    tmpf = [P.sbuf("tmpf%d" % i, [128, 256], F32) for i in range(2)]
    nb = P.sbuf("nb", [128, 3, 6, 256], F32)
    acc_banks = [P.pb[0], P.pb[1]]
    bb_bank = P.pb[2]
    s_banks = [P.pb[3], P.pb[4], P.pb[5], P.pb[6]]
    cnt = {"acc": 0, "s": 0, "y": 0}

    def sbank():
        b = s_banks[cnt["s"] % 4]
        cnt["s"] += 1
        return b

    def finalize(acc, n, dst, drow, tok0):
        P.op("dve", lambda e: e.reciprocal(rec[64:65, :n], acc[64:65, :n]), reads=[acc], writes=[rec])
        P.mm(bb_bank, bb_bank[:64, :n], onesf, onesf[64:65, 0:64], rec, rec[64:65, :n])
        P.cp("act", bsb, bsb[:64, :n], bb_bank, bb_bank[:64, :n])
        y_ = ysb[cnt["y"] % 2]
        cnt["y"] += 1
        P.tt("dve", y_, y_[:64, :n], acc, acc[:64, :n], bsb, bsb[:64, :n], ALU.mult)
        P.dma(dst, dst[drow:drow + 64, tok0:tok0 + n], y_, y_[:64, :n])

    for q4 in range(4):
        P.dma(BK, BK[:, q4 * 4160:(q4 + 1) * 4160], kaT, kaT[:, q4 * 4160:(q4 + 1) * 4160])
    BV4 = BV[:].rearrange("p t (h d) -> p t h d", h=2)
    for h in range(2):
        P.dma(BV, BV4[:, :, h, 0:64], va, va[:, h * 64:(h + 1) * 64].rearrange("(t p) d -> p t d", p=128))
    P.memset("pool", BV, BV4[:, :, :, 64:65], 1.0)
    for h in range(2):
        P.dma(nb, nb[:], nabias, nabias[h].rearrange("v (j p) q -> p v j q", p=128))
        for blk in range(-1, 64):
            if blk < 0:
                qtok0, tiles = 0, [(0, 0, None), (128, 1, None)]
            else:
                r0 = blk * 4
                var = 0 if blk == 0 else (2 if blk == 63 else 1)
                se = 0 if blk == 0 else (244 if blk == 63 else r0 - 4)
                qtok0 = 256 + r0 * 64
                tiles = [(0, 0, None), (128, 1, None)]
                for j in range(6):
                    tiles.append((256 + se * 64 + j * 128, 2 + se // 2 + j, (var, j)))
            qc = qcs[cnt["acc"] % 2]
            P.dma(qc, qc[:, :256], qaT, qaT[:, qtok0:qtok0 + 256])
            acc = acc_banks[cnt["acc"] % 2]
            cnt["acc"] += 1
            for idx, (kcol, vt, bi) in enumerate(tiles):
                sb = sbank()
                P.mm(sb, sb[:, :256], BK, BK[h * 64:(h + 1) * 64, kcol:kcol + 128], qc, qc[h * 64:(h + 1) * 64, :256])
                pt = npt()
                if bi is None:
                    P.act(pt, pt[:, :256], sb, sb[:, :256], AF.Exp, scale=0.125)
                else:
                    tf = tmpf[idx % 2]
                    P.stt("dve", tf, tf[:], sb, sb[:, :256], 0.125, nb, nb[:, bi[0], bi[1], :], ALU.mult, ALU.add)
                    P.act(pt, pt[:, :256], tf, tf[:], AF.Exp)
                P.mm(acc, acc[:65, :256], BV, BV4[:, vt, h, :], pt, pt[:, :256], start=(idx == 0), stop=(idx == len(tiles) - 1))
            finalize(acc, 256, yaT, h * 64, qtok0)

    BV3 = BV[:]
    sc_b = 96.0 ** -0.5
    for h in range(2):
        for q4 in range(4):
            P.dma(BK, BK[0:96, q4 * 4160:(q4 + 1) * 4160], kbT, kbT[h, :, q4 * 4160:(q4 + 1) * 4160])
        P.dma(BV, BV3[:, :, 0:64], vb, vb[:, h * 64:(h + 1) * 64].rearrange("(t p) d -> p t d", p=128))
        P.memset("pool", BV, BV3[:, :, 64:65], 1.0)
        for c in range(NCH):
            tok0, n = chunk_rng(c)
            kts = [0, 1] if c == 0 else list(range(130))
            qc = qcs[cnt["acc"] % 2]
            P.dma(qc, qc[0:96, :n], qbT, qbT[h, :, tok0:tok0 + n])
            acc = acc_banks[cnt["acc"] % 2]
            cnt["acc"] += 1
            for idx, kt in enumerate(kts):
                sb = sbank()
                P.mm(sb, sb[:, :n], BK, BK[0:96, kt * 128:(kt + 1) * 128], qc, qc[0:96, :n])
                pt = npt()
                P.act(pt, pt[:, :n], sb, sb[:, :n], AF.Exp, scale=sc_b)
                P.mm(acc, acc[:65, :n], BV, BV3[:, kt, 0:65], pt, pt[:, :n], start=(idx == 0), stop=(idx == len(kts) - 1))
            finalize(acc, n, ybT, h * 64, tok0)

    for q4 in range(4):
        P.dma(BK, BK[:, q4 * 4160:(q4 + 1) * 4160], dkT, dkT[:, q4 * 4160:(q4 + 1) * 4160])
    P.dma(BV, BV3[:, :, 0:128], dvv, dvv[:, :].rearrange("(t p) d -> p t d", p=128))
    O1, O2, S1, S2 = P.pb[0], P.pb[1], P.pb[2], P.pb[3]
    s3 = [P.pb[4], P.pb[5], P.pb[6]]
    for c in range(NCH):
        tok0, n = chunk_rng(c)
        kts = [0, 1] if c == 0 else list(range(130))
        qc = qcs[c % 2]
        P.dma(qc, qc[:, :n], dqT, dqT[:, tok0:tok0 + n])
        for idx, kt in enumerate(kts):
            st_, sp_ = (idx == 0), (idx == len(kts) - 1)
            for half, (O, S) in enumerate(((O1, S1), (O2, S2))):
                sb = s3[cnt["s"] % 3]
                cnt["s"] += 1
                rows = slice(half * 64, (half + 1) * 64)
                P.mm(sb, sb[:, :n], BK, BK[rows, kt * 128:(kt + 1) * 128], qc, qc[rows, :n])
                pt = npt()
                P.act(pt, pt[:, :n], sb, sb[:, :n], AF.Exp, scale=0.125)
                P.mm(O, O[:, :n], BV, BV3[:, kt, 0:128], pt, pt[:, :n], start=st_, stop=sp_)
                P.mm(S, S[:, :n], ones, ones[:], pt, pt[:, :n], start=st_, stop=sp_)
        P.op("dve", lambda e, n=n: e.reciprocal(rec[:, :n], S1[:, :n]), reads=[S1], writes=[rec])
        P.op("dve", lambda e, n=n: e.reciprocal(rec2[:, :n], S2[:, :n]), reads=[S2], writes=[rec2])
        P.tt("dve", yf, yf[:, :n], O1, O1[:, :n], rec, rec[:, :n], ALU.mult)
        P.tt("dve", yf2, yf2[:, :n], O2, O2[:, :n], rec2, rec2[:, :n], ALU.mult)
        P.stt("dve", yf, yf[:, :n], yf2, yf2[:, :n], neglam[:, 0:1], yf, yf[:, :n], ALU.mult, ALU.add, extra=[neglam])
        pt = npt()
        P.act(pt, pt[:, :n], yf, yf[:, :n], AF.Square)
        sb = s3[cnt["s"] % 3]
        cnt["s"] += 1
        P.mm(sb, sb[:, :n], ones, ones[:], pt, pt[:, :n])
        P.rsqrt(bsb, bsb[:, :n], sb, sb[:, :n], 1.0 / 128, epst, epst[:, 0:1])
        P.tt("dve", yf, yf[:, :n], yf, yf[:, :n], bsb, bsb[:, :n], ALU.mult)
        y_ = ysb[c % 2]
        P.ts("pool", y_, y_[:, :n], yf, yf[:, :n], gss[:, 0:1], None, ALU.mult, extra=[gss])
        P.dma(ycT, ycT[:, tok0:tok0 + n], y_, y_[:, :n])
    P.finish()
    return P.build()


def _pm(vec):
    return np.ascontiguousarray(np.asarray(vec).reshape(-1, 128).T)


def _partner_perm(D):
    m = D // 2
    half = m // 2
    perm = np.zeros(D, np.int64)
    sign = np.zeros(D, np.float32)
    for d in range(D):
        seg, i = d // m, d % m
        if i < half:
            perm[d] = seg * m + i + half
            sign[d] = -1.0
        else:
            perm[d] = seg * m + i - half
            sign[d] = 1.0
    return perm, sign


def _rope_table(D):
    m = D // 2
    half = m // 2
    inv = (np.float32(10000.0) ** (-np.arange(0, m, 2, dtype=np.float32) / np.float32(m))).astype(np.float32)
    t = np.arange(16384)
    row = (t // 64).astype(np.float32)
    col = (t % 64).astype(np.float32)
    _, sign = _partner_perm(D)
    tab = np.zeros((D, 2, NT), np.float32)
    tab[:, 0, :256] = 1.0
    for d in range(D):
        seg, i = d // m, d % m
        fi = i % half
        pos = row if seg == 0 else col
        ang = (pos * inv[fi]).astype(np.float32)
        tab[d, 0, 256:] = np.cos(ang)
        tab[d, 1, 256:] = sign[d] * np.sin(ang)
    return tab


def _na_bias(rpb2):
    out = np.full((2, 3, 768, 256), -30000.0, np.float32)
    for var, (r0, se) in enumerate(((0, 0), (8, 4), (252, 244))):
        kr = se + np.arange(768) // 64
        kc = np.arange(768) % 64
        r = r0 + np.arange(256) // 64
        c = np.arange(256) % 64
        rs = np.clip(r - 4, 0, 248)
        cs = np.clip(c - 8, 0, 48)
        okr = (kr[:, None] >= rs[None, :]) & (kr[:, None] < rs[None, :] + 8)
        okc = (kc[:, None] >= cs[None, :]) & (kc[:, None] < cs[None, :] + 16)
        ok = okr & okc
        ro = np.clip(kr[:, None] - r[None, :] + 7, 0, 14)
        co = np.clip(kc[:, None] - c[None, :] + 15, 0, 30)
        for h in range(2):
            g = rpb2[h][ro, co]
            out[h, var] = np.where(ok, g, np.float32(-30000.0))
    return out


_CONST = {}


def _consts():
    if not _CONST:
        _CONST["cs32"] = _rope_table(32)
        t64 = _rope_table(64)
        _CONST["cs128"] = np.ascontiguousarray(np.concatenate([t64, t64], 0))
        _CONST["ident"] = np.eye(128, dtype=np.float32)
    return _CONST


def prep_A(inp, l, mod, x_cur, ctx_cur):
    C = _consts()
    w_in = inp["w_in"][l]
    p32, _ = _partner_perm(32)
    p64, _ = _partner_perm(64)
    p128 = np.concatenate([p64, 64 + p64])
    maps = []
    for core in range(8):
        b, g = core // 4, core % 4
        xin = np.ascontiguousarray(np.concatenate([ctx_cur[b], x_cur[b]], 0))
        msT = np.stack([_pm(mod[0:1024, b]), _pm(mod[1024:2048, b]), _pm(mod[0:1024, 2]), _pm(mod[1024:2048, 2])], -1)
        kr = w_in[:, 1920:1952]
        dq = w_in[:, 1952 + 128 * g: 1952 + 128 * (g + 1)]
        dk = w_in[:, 2464 + 128 * g: 2464 + 128 * (g + 1)]
        wfeat = np.concatenate([
            w_in[:, 128 * g:128 * (g + 1)], w_in[:, 512 + 128 * g:512 + 128 * (g + 1)],
            w_in[:, 1536:1664], w_in[:, 1664:1792], w_in[:, 1792:1920],
            kr, kr[:, p32], dq, dq[:, p128], dk, dk[:, p128]], 1)
        wtok = np.concatenate([w_in[:, 1024 + 128 * g:1024 + 128 * (g + 1)], w_in[:, 2976 + 128 * g:2976 + 128 * (g + 1)]], 1)
        wuq = np.zeros((256, 2, 192), np.float32)
        wukvk = np.zeros((128, 2, 64), np.float32)
        wukvv = np.zeros((128, 128), np.float32)
        for hh in range(2):
            H = 2 * g + hh
            uq = inp["mla_w_uq"][l][:, H * 96:(H + 1) * 96]
            rope_c = uq[:, 64:96]
            wuq[:, hh, 0:96] = uq
            wuq[:, hh, 160:192] = rope_c[:, p32]
            ukv = inp["mla_w_ukv"][l][:, H * 128:(H + 1) * 128]
            wukvk[:, hh, :] = ukv[:, 0:64]
            wukvv[:, hh * 64:(hh + 1) * 64] = ukv[:, 64:128]
        lamv = np.stack([inp["diff_lq1"][l], inp["diff_lk1"][l], inp["diff_lq2"][l], inp["diff_lk2"][l]], 0)
        maps.append({
            "xin": xin, "msT": np.ascontiguousarray(msT), "wfeat": np.ascontiguousarray(wfeat),
            "wtok": np.ascontiguousarray(wtok), "wuq": wuq, "wukvk": wukvk, "wukvv": wukvv,
            "gq": _pm(inp["mla_g_q"][l]), "gkv": _pm(inp["mla_g_kv"][l]),
            "cs32": C["cs32"], "cs128": C["cs128"],
            "nabias": _na_bias(inp["na_rpb"][l][2 * g:2 * g + 2]),
            "lamv": np.ascontiguousarray(np.broadcast_to(lamv[None], (128, 4, 64))),
            "gsub": _pm(inp["diff_g_sub"][l]), "ident": C["ident"],
        })
    return maps


NB = 4160
ALPHA = 4.0 ** 0.25


def chunkB(c):
    return (0, 64) if c == 0 else (64 + (c - 1) * 512, 512)


def ln_tile(P, src_t, src_ap, np_, st6, mv, rstd, epst, out_t, out_ap):
    for a_ in range(2):
        P.op("dve", lambda e, a_=a_: e.bn_stats(st6[:np_, a_, :], src_ap[:, a_ * 512:(a_ + 1) * 512]),
             reads=[src_t], writes=[st6])
    P.op("dve", lambda e: e.bn_aggr(mv[:np_, :], st6[:np_].rearrange("p a b -> p (a b)")), reads=[st6], writes=[mv])
    P.rsqrt(rstd, rstd[:np_, :], mv, mv[:np_, 1:2], 1.0, epst, epst[:np_, 0:1])
    P.ts("dve", out_t, out_ap, src_t, src_ap, mv[:np_, 0:1], rstd[:np_, 0:1], ALU.subtract, ALU.mult, extra=[mv, rstd])


def build_B():
    P = Prog("B")
    EI, EO = "ExternalInput", "ExternalOutput"
    xin = P.dram("xin", [NB, 1024], F32, EI)
    yT = P.dram("yT", [1536, NB], BF16, EI)
    msT = P.dram("msT", [128, 8, 4], F32, EI)
    wgate = P.dram("wgate", [1024, 3072], F32, EI)
    bgT = P.dram("bgT", [128, 24], F32, EI)
    wbr = P.dram("wbr", [1536, 1024], F32, EI)
    wout = P.dram("wout", [1024, 1024], F32, EI)
    wrt = P.dram("wrt", [128, 8, 16], F32, EI)
    rows = P.dram("rows", [128, 8, 1024], F32, EI)
    identd = P.dram("ident", [128, 128], F32, EI)
    x1o = P.dram("x1", [NB, 1024], F32, EO)
    h2To = P.dram("h2T", [1024, NB], BF16, EO)
    affo = P.dram("aff", [NB, 16], F32, EO)
    P.start()
    P.banks(7)
    tp = P.psum("tp", [128, 1024], BF16)
    Wg = P.sbuf("Wg", [128, 8, 3072], BF16)
    Wb = P.sbuf("Wb", [128, 12, 1024], BF16)
    Wo = P.sbuf("Wo", [128, 8, 1024], BF16)
    for k in range(8):
        P.dma(Wg, Wg[:, k, :], wgate, wgate[k * 128:(k + 1) * 128, :], q="pool")
        P.dma(Wo, Wo[:, k, :], wout, wout[k * 128:(k + 1) * 128, :], q="pool")
    for k in range(12):
        P.dma(Wb, Wb[:, k, :], wbr, wbr[k * 128:(k + 1) * 128, :], q="pool")
    ident = P.sbuf("identb", [128, 128], BF16)
    P.dma(ident, ident[:], identd, identd[:], q="pool")
    bg = P.sbuf("bg", [128, 24], F32)
    P.dma(bg, bg[:], bgT, bgT[:])
    ms = P.sbuf("ms", [128, 8, 4], F32)
    P.dma(ms, ms[:], msT, msT[:])
    scp = P.sbuf("scp", [128, 8, 2], F32)
    for m in range(2):
        P.ts("dve", scp, scp[:, :, m], ms, ms[:, :, 2 * m + 1], 1.0, None, ALU.add)
    rw = P.sbuf("rw", [128, 8, 1024], F32)
    P.dma(rw, rw[:], rows, rows[:])
    for j in (4, 6):
        P.ts("dve", rw, rw[:, j, :], rw, rw[:, j, :], 1.0, None, ALU.add)
    wr_f = P.sbuf("wr_f", [128, 8, 16], F32)
    P.dma(wr_f, wr_f[:], wrt, wrt[:])
    wr_hi = P.sbuf("wr_hi", [128, 8, 16], BF16)
    wr_lo = P.sbuf("wr_lo", [128, 8, 16], BF16)
    P.cp("dve", wr_hi, wr_hi[:], wr_f, wr_f[:])
    P.tt("dve", wr_lo, wr_lo[:], wr_f, wr_f[:], wr_hi, wr_hi[:], ALU.subtract)
    epst = P.sbuf("epst", [128, 1], F32)
    P.memset("dve", epst, epst[:], LN_EPS)

    xc = P.sbuf("xc", [128, 4, 1024], F32)
    xnb = [P.sbuf("xn%d" % i, [128, 1024], BF16) for i in range(2)]
    st6 = P.sbuf("st6", [128, 2, 6], F32)
    mv = P.sbuf("mv", [128, 2], F32)
    rstd = P.sbuf("rstd", [128, 1], F32)
    htmp = P.sbuf("htmp", [128, 8, 128], F32)
    hT = P.sbuf("hT", [128, 8, 512], BF16)
    yt = P.sbuf("yt", [128, 12, 512], BF16)
    sig = [P.sbuf("sig%d" % i, [128, 512], F32) for i in range(2)]
    mf = P.sbuf("mf", [128, 512], F32)
    mt2 = P.sbuf("mt2", [128, 512], F32)
    mT = P.sbuf("mT", [128, 8, 512], BF16)
    u = P.sbuf("u", [128, 1024], F32)
    ut = P.sbuf("ut", [128, 1024], F32)
    x1 = [P.sbuf("x1_%d" % i, [128, 1024], F32) for i in range(1)]
    h2 = P.sbuf("h2", [128, 1024], F32)
    h2hi = P.sbuf("h2hi", [128, 1024], BF16)
    h2lo = P.sbuf("h2lo", [128, 1024], BF16)
    h2Ts = [P.sbuf("h2Ts%d" % i, [128, 8, 128], BF16) for i in range(1)]
    h2Tl = P.sbuf("h2Tl", [128, 8, 128], BF16)
    lg = P.sbuf("lg", [128, 16], F32)
    mx = P.sbuf("mx", [128, 1], F32)
    sm = P.sbuf("sm", [128, 1], F32)
    affs = [P.sbuf("affs%d" % i, [128, 16], F32) for i in range(2)]

    for c in range(9):
        tok0, n = chunkB(c)
        m = 1 if c == 0 else 0
        tsz = 64 if c == 0 else 128
        nt = n // tsz
        P.dma(yt, yt[:, :, :n], yT, yT[:, tok0:tok0 + n].rearrange("(j p) t -> p j t", p=128))
        for i in range(nt):
            xn = xnb[i % 2]
            P.dma(xc, xc[:tsz, i, :], xin, xin[tok0 + i * tsz: tok0 + (i + 1) * tsz, :])
            ln_tile(P, xc, xc[:tsz, i, :], tsz, st6, mv, rstd, epst, xn, xn[:tsz, :])
            for k in range(8):
                P.tr(tp, tp[:, k * 128:k * 128 + tsz], xn, xn[:tsz, k * 128:(k + 1) * 128], ident, ident[:tsz, :tsz])
            P.tt("dve", htmp, htmp[:, :, :tsz], tp, tp[:].rearrange("p (a b) -> p a b", a=8)[:, :, :tsz],
                 scp, scp[:, :, m:m + 1].to_broadcast([128, 8, tsz]), ALU.mult)
            P.tt("pool", hT, hT[:, :, i * tsz:(i + 1) * tsz], htmp, htmp[:, :, :tsz],
                 ms, ms[:, :, 2 * m:2 * m + 1].to_broadcast([128, 8, tsz]), ALU.add)
        for oc in range(8):
            for br in range(3):
                G = P.bank()
                for k in range(8):
                    P.mm(G, G[:, :n], Wg, Wg[:, k, br * 1024 + oc * 128: br * 1024 + (oc + 1) * 128], hT, hT[:, k, :n],
                         start=(k == 0), stop=(k == 7))
                sg = sig[br % 2]
                P.act(sg, sg[:, :n], G, G[:, :n], AF.Sigmoid, extra=[bg], bias=bg[:, br * 8 + oc: br * 8 + oc + 1])
                Pb = P.bank()
                for k in range(4):
                    P.mm(Pb, Pb[:, :n], Wb, Wb[:, br * 4 + k, oc * 128:(oc + 1) * 128], yt, yt[:, br * 4 + k, :n],
                         start=(k == 0), stop=(k == 3))
                if br == 0:
                    P.tt("dve", mf, mf[:, :n], Pb, Pb[:, :n], sg, sg[:, :n], ALU.mult)
                else:
                    P.tt("dve", mt2, mt2[:, :n], Pb, Pb[:, :n], sg, sg[:, :n], ALU.mult)
                    P.tt("pool", mf, mf[:, :n], mf, mf[:, :n], mt2, mt2[:, :n], ALU.add)
            P.cp("pool", mT, mT[:, oc, :n], mf, mf[:, :n])
        for i in range(nt):
            for half in range(2):
                O = P.bank()
                for k in range(8):
                    P.mm(O, O[:tsz, :], mT, mT[:, k, i * tsz:(i + 1) * tsz], Wo, Wo[:, k, half * 512:(half + 1) * 512],
                         start=(k == 0), stop=(k == 7))
                hs = slice(half * 512, (half + 1) * 512)
                P.tt("dve", ut, ut[:tsz, hs], O, O[:tsz, :], rw, rw[:tsz, m, hs], ALU.mult)
            P.stt("dve", u, u[:tsz, :], xc, xc[:tsz, i, :], ALPHA, ut, ut[:tsz, :], ALU.mult, ALU.add)
            x1t = x1[0]
            ln_tile(P, u, u[:tsz, :], tsz, st6, mv, rstd, epst, x1t, x1t[:tsz, :])
            P.tt("dve", x1t, x1t[:tsz, :], x1t, x1t[:tsz, :], rw, rw[:tsz, 2, :], ALU.mult)
            P.tt("pool", x1t, x1t[:tsz, :], x1t, x1t[:tsz, :], rw, rw[:tsz, 3, :], ALU.add)
            r0 = tok0 + i * tsz
            P.dma(x1o, x1o[r0:r0 + tsz, :], x1t, x1t[:tsz, :])
            ln_tile(P, x1t, x1t[:tsz, :], tsz, st6, mv, rstd, epst, h2, h2[:tsz, :])
            P.tt("dve", h2, h2[:tsz, :], h2, h2[:tsz, :], rw, rw[:tsz, 4 + 2 * m, :], ALU.mult)
            P.tt("pool", h2, h2[:tsz, :], h2, h2[:tsz, :], rw, rw[:tsz, 5 + 2 * m, :], ALU.add)
            P.cp("act", h2hi, h2hi[:tsz, :], h2, h2[:tsz, :])
            P.tt("dve", h2lo, h2lo[:tsz, :], h2, h2[:tsz, :], h2hi, h2hi[:tsz, :], ALU.subtract)
            hs_ = h2Ts[0]
            for k in range(8):
                P.tr(tp, tp[:, k * 128:k * 128 + tsz], h2hi, h2hi[:tsz, k * 128:(k + 1) * 128], ident, ident[:tsz, :tsz])
            P.cp("act", hs_, hs_[:, :, :tsz], tp, tp[:].rearrange("p (a b) -> p a b", a=8)[:, :, :tsz])
            P.dma(h2To, h2To[:, r0:r0 + tsz].rearrange("(k p) t -> p k t", p=128), hs_, hs_[:, :, :tsz])
            for k in range(8):
                P.tr(tp, tp[:, k * 128:k * 128 + tsz], h2lo, h2lo[:tsz, k * 128:(k + 1) * 128], ident, ident[:tsz, :tsz])
            P.cp("dve", h2Tl, h2Tl[:, :, :tsz], tp, tp[:].rearrange("p (a b) -> p a b", a=8)[:, :, :tsz])
            L = P.bank()
            for k in range(8):
                P.mm(L, L[:tsz, 0:16], hs_, hs_[:, k, :tsz], wr_hi, wr_hi[:, k, :], start=(k == 0), stop=False)
                P.mm(L, L[:tsz, 0:16], hs_, hs_[:, k, :tsz], wr_lo, wr_lo[:, k, :], start=False, stop=False)
                P.mm(L, L[:tsz, 0:16], h2Tl, h2Tl[:, k, :tsz], wr_hi, wr_hi[:, k, :], start=False, stop=(k == 7))
            P.op("dve", lambda e, L=L, tsz=tsz: e.reduce_max(mx[:tsz, :], L[:tsz, 0:16], AX.X), reads=[L], writes=[mx])
            P.ts("dve", lg, lg[:tsz, :], L, L[:tsz, 0:16], mx[:tsz, 0:1], None, ALU.subtract, extra=[mx])
            P.act(lg, lg[:tsz, :], lg, lg[:tsz, :], AF.Exp)
            P.op("dve", lambda e, tsz=tsz: e.reduce_sum(sm[:tsz, :], lg[:tsz, :], AX.X), reads=[lg], writes=[sm])
            P.op("dve", lambda e, tsz=tsz: e.reciprocal(sm[:tsz, :], sm[:tsz, :]), reads=[sm], writes=[sm])
            af_ = affs[i % 2]
            P.ts("dve", af_, af_[:tsz, :], lg, lg[:tsz, :], sm[:tsz, 0:1], None, ALU.mult, extra=[sm])
            P.dma(affo, affo[r0:r0 + tsz, :], af_, af_[:tsz, :])
    P.finish()
    return P.build()


def prep_B(inp, l, mod, x_cur, ctx_cur, yTs):
    C = _consts()
    wbr = np.ascontiguousarray(np.concatenate([inp["w_br_a"][l], inp["w_br_b"][l], inp["w_br_c"][l]], 0))
    maps = []
    for core in range(8):
        b, s = core // 4, core % 4
        xin = np.concatenate([ctx_cur[b][64 * s:64 * (s + 1)], x_cur[b][4096 * s:4096 * (s + 1)]], 0)
        yT = np.concatenate([yTs[b][:, 64 * s:64 * (s + 1)], yTs[b][:, 256 + 4096 * s:256 + 4096 * (s + 1)]], 1)
        msT = np.stack([_pm(mod[0:1024, b]), _pm(mod[1024:2048, b]), _pm(mod[0:1024, 2]), _pm(mod[1024:2048, 2])], -1)
        rows = np.stack([mod[2048:3072, b], mod[2048:3072, 2], inp["ln1_g"][l], inp["ln1_b"][l],
                         mod[4096:5120, b], mod[3072:4096, b], mod[4096:5120, 2], mod[3072:4096, 2]], 0)
        maps.append({
            "xin": np.ascontiguousarray(xin), "yT": np.ascontiguousarray(yT), "msT": np.ascontiguousarray(msT),
            "wgate": inp["w_gate"][l], "bgT": _pm(inp["b_gate"][l]), "wbr": wbr, "wout": inp["w_out"][l],
            "wrt": np.ascontiguousarray(inp["w_router"][l].reshape(8, 128, 16).transpose(1, 0, 2)),
            "rows": np.ascontiguousarray(np.broadcast_to(rows[None], (128, 8, 1024))), "ident": C["ident"],
        })
    return maps


def chunkC(c):
    return (0, 256) if c == 0 else (256 + (c - 1) * 1024, 1024)


def build_C():
    P = Prog("C")
    EI, EO = "ExternalInput", "ExternalOutput"
    h2T = P.dram("h2T", [1024, NT], BF16, EI)
    affd = P.dram("affL", [128, 4, 130], F32, EI)
    wg = P.dram("wg", [4, 1024, 2816], F32, EI)
    wu = P.dram("wu", [4, 1024, 2816], F32, EI)
    wd = P.dram("wd", [4, 2816, 1024], F32, EI)
    fpo = P.dram("fp", [NT, 1024], F32, EO)
    P.start()
    P.banks(7)
    onesf = P.sbuf("onesf", [128, 128], F32)
    P.memset("dve", onesf, onesf[:], 1.0)
    aff = P.sbuf("aff", [128, 4, 130], F32)
    P.dma(aff, aff[:], affd, affd[:])
    lo = P.sbuf("lo", [128, 8], F32)
    hi = P.sbuf("hi", [128, 8], F32)
    mid = P.sbuf("mid", [128, 8], F32)
    kv = P.sbuf("kv", [128, 8], F32)
    cnt = P.sbuf("cnt", [128, 8], F32)
    ge = P.sbuf("ge", [128, 8], F32)
    d1 = P.sbuf("d1", [128, 8], F32)
    cmp_ = P.sbuf("cmp", [128, 128], F32)
    P.memset("dve", lo, lo[:], 0.0)
    P.memset("dve", hi, hi[:], 1.0)
    P.memset("dve", kv, kv[:, 0:4], 2048.0)
    P.memset("dve", kv, kv[:, 4:8], 32.0)
    for it in range(30):
        P.tt("dve", mid, mid[:], lo, lo[:], hi, hi[:], ALU.add)
        P.ts("dve", mid, mid[:], mid, mid[:], 0.5, None, ALU.mult)
        for e in range(4):
            P.ts("dve", cmp_, cmp_[:, 0:128], aff, aff[:, e, 2:130], mid[:, e:e + 1], None, ALU.is_ge, extra=[mid])
            P.op("dve", lambda e_, e=e: e_.reduce_sum(cnt[:, e:e + 1], cmp_[:, 0:128], AX.X), reads=[cmp_], writes=[cnt])
            P.ts("dve", cmp_, cmp_[:, 0:2], aff, aff[:, e, 0:2], mid[:, 4 + e:5 + e], None, ALU.is_ge, extra=[mid])
            P.op("dve", lambda e_, e=e: e_.reduce_sum(cnt[:, 4 + e:5 + e], cmp_[:, 0:2], AX.X), reads=[cmp_], writes=[cnt])
        tb = P.bank()
        P.mm(tb, tb[:, 0:8], onesf, onesf[:], cnt, cnt[:])
        P.tt("dve", ge, ge[:], tb, tb[:, 0:8], kv, kv[:], ALU.is_ge)
        P.tt("dve", d1, d1[:], mid, mid[:], lo, lo[:], ALU.subtract)
        P.tt("dve", d1, d1[:], d1, d1[:], ge, ge[:], ALU.mult)
        P.tt("dve", lo, lo[:], lo, lo[:], d1, d1[:], ALU.add)
        P.tt("dve", d1, d1[:], hi, hi[:], mid, mid[:], ALU.subtract)
        P.tt("dve", d1, d1[:], d1, d1[:], ge, ge[:], ALU.mult)
        P.tt("dve", hi, hi[:], mid, mid[:], d1, d1[:], ALU.add)
    gm = P.sbuf("gm", [128, 4, 130], F32)
    for e in range(4):
        P.ts("dve", gm, gm[:, e, 2:130], aff, aff[:, e, 2:130], lo[:, e:e + 1], None, ALU.is_ge, extra=[lo])
        P.ts("dve", gm, gm[:, e, 0:2], aff, aff[:, e, 0:2], lo[:, 4 + e:5 + e], None, ALU.is_ge, extra=[lo])
    P.tt("dve", gm, gm[:], gm, gm[:], aff, aff[:], ALU.mult)

    hxs = [P.sbuf("hx%d" % i, [128, 8, 1024], BF16) for i in range(2)]
    facc = P.sbuf("facc", [128, 8, 1024], F32)
    wgs = [P.sbuf("wgu%d" % i, [128, 8, 256], BF16) for i in range(3)]
    wus = [P.sbuf("wuu%d" % i, [128, 8, 256], BF16) for i in range(3)]
    wds = [P.sbuf("wdu%d" % i, [128, 2, 1024], BF16) for i in range(3)]
    hids = [P.sbuf("hid%d" % i, [128, 2, 1024], BF16) for i in range(2)]
    sgs = [P.sbuf("sg%d" % i, [128, 512], F32) for i in range(2)]
    ui = 0
    si = 0
    for c in range(17):
        tok0, n = chunkC(c)
        hx = hxs[c % 2]
        P.dma(hx, hx[:, :, :n], h2T, h2T[:, tok0:tok0 + n].rearrange("(k p) t -> p k t", p=128))
        P.memset("pool", facc, facc[:], 0.0)
        for e in range(4):
            for un in range(11):
                f0 = un * 256
                wgu, wuu, wdu, hid = wgs[ui % 3], wus[ui % 3], wds[ui % 3], hids[ui % 2]
                ui += 1
                P.dma(wgu, wgu[:], wg, wg[e][:, f0:f0 + 256].rearrange("(k p) f -> p k f", p=128), q="pool")
                P.dma(wuu, wuu[:], wu, wu[e][:, f0:f0 + 256].rearrange("(k p) f -> p k f", p=128), q="pool")
                P.dma(wdu, wdu[:], wd, wd[e][f0:f0 + 256, :].rearrange("(c p) o -> p c o", p=128), q="pool")
                for fc in range(2):
                    for th in range(n // 512 if n >= 512 else 1):
                        w_ = min(512, n)
                        ts_ = slice(th * 512, th * 512 + w_)
                        G = P.bank()
                        for k in range(8):
                            P.mm(G, G[:, :w_], wgu, wgu[:, k, fc * 128:(fc + 1) * 128], hx, hx[:, k, ts_], start=(k == 0), stop=(k == 7))
                        U = P.bank()
                        for k in range(8):
                            P.mm(U, U[:, :w_], wuu, wuu[:, k, fc * 128:(fc + 1) * 128], hx, hx[:, k, ts_], start=(k == 0), stop=(k == 7))
                        sg = sgs[si % 2]
                        si += 1
                        P.act(sg, sg[:, :w_], G, G[:, :w_], AF.Silu)
                        P.tt("dve", hid, hid[:, fc, ts_], U, U[:, :w_], sg, sg[:, :w_], ALU.mult)
                for tl in range(n // 128):
                    gt = tok0 // 128 + tl
                    for oh in range(2):
                        O = P.bank()
                        for fc in range(2):
                            P.mm(O, O[:, :], hid, hid[:, fc, tl * 128:(tl + 1) * 128], wdu, wdu[:, fc, oh * 512:(oh + 1) * 512],
                                 start=(fc == 0), stop=(fc == 1))
                        fs = facc[:, tl, oh * 512:(oh + 1) * 512]
                        P.stt("dve", facc, fs, O, O[:, :], gm[:, e, gt:gt + 1], facc, fs, ALU.mult, ALU.add, extra=[gm])
        P.dma(fpo, fpo[tok0:tok0 + n, :].rearrange("(t p) o -> p t o", p=128), facc, facc[:, :n // 128, :])
    P.finish()
    return P.build()


def build_D():
    P = Prog("D")
    EI, EO = "ExternalInput", "ExternalOutput"
    x1d = P.dram("x1", [NB, 1024], F32, EI)
    fps = P.dram("fps", [4, NB, 1024], F32, EI)
    rows = P.dram("rows", [128, 4, 1024], F32, EI)
    x2o = P.dram("x2", [NB, 1024], F32, EO)
    P.start()
    rw = P.sbuf("rw", [128, 4, 1024], F32)
    P.dma(rw, rw[:], rows, rows[:])
    epst = P.sbuf("epst", [128, 1], F32)
    P.memset("dve", epst, epst[:], LN_EPS)
    st6 = P.sbuf("st6", [128, 2, 6], F32)
    mv = P.sbuf("mv", [128, 2], F32)
    rstd = P.sbuf("rstd", [128, 1], F32)
    xts = [P.sbuf("xt%d" % i, [128, 1024], F32) for i in range(2)]
    pts = [P.sbuf("pp%d" % i, [128, 4, 1024], F32) for i in range(2)]
    u = P.sbuf("u", [128, 1024], F32)
    ots = [P.sbuf("ot%d" % i, [128, 1024], F32) for i in range(2)]
    for t in range(33):
        r0, tsz = (0, 64) if t == 0 else (64 + (t - 1) * 128, 128)
        m = 1 if t == 0 else 0
        xt, pp, ot = xts[t % 2], pts[t % 2], ots[t % 2]
        P.dma(xt, xt[:tsz, :], x1d, x1d[r0:r0 + tsz, :])
        for j in range(4):
            P.dma(pp, pp[:tsz, j, :], fps, fps[j, r0:r0 + tsz, :])
        P.tt("dve", pp, pp[:tsz, 0, :], pp, pp[:tsz, 0, :], pp, pp[:tsz, 1, :], ALU.add)
        P.tt("pool", pp, pp[:tsz, 2, :], pp, pp[:tsz, 2, :], pp, pp[:tsz, 3, :], ALU.add)
        P.tt("dve", pp, pp[:tsz, 0, :], pp, pp[:tsz, 0, :], pp, pp[:tsz, 2, :], ALU.add)
        P.tt("dve", pp, pp[:tsz, 0, :], pp, pp[:tsz, 0, :], rw, rw[:tsz, m, :], ALU.mult)
        P.stt("dve", u, u[:tsz, :], xt, xt[:tsz, :], ALPHA, pp, pp[:tsz, 0, :], ALU.mult, ALU.add)
        ln_tile(P, u, u[:tsz, :], tsz, st6, mv, rstd, epst, ot, ot[:tsz, :])
        P.tt("dve", ot, ot[:tsz, :], ot, ot[:tsz, :], rw, rw[:tsz, 2, :], ALU.mult)
        P.tt("pool", ot, ot[:tsz, :], ot, ot[:tsz, :], rw, rw[:tsz, 3, :], ALU.add)
        P.dma(x2o, x2o[r0:r0 + tsz, :], ot, ot[:tsz, :])
    P.finish()
    return P.build()


def _lam_init(l):
    import math
    return 0.8 - 0.6 * math.exp(-0.3 * l)


def kernel(**inp):
    inp = {k: np.asarray(v) for k, v in inp.items()}
    mods = run_L0(inp)
    x_cur = [np.asarray(inp["x"][b], np.float32) for b in range(2)]
    ctx_cur = [np.asarray(inp["ctx"][b], np.float32) for b in range(2)]
    for l in range(2):
        mod = mods[l]
        resA = run(build_A(_lam_init(l)), prep_A(inp, l, mod, x_cur, ctx_cur)).results
        yTs = []
        for b in range(2):
            parts = [np.asarray(resA[b * 4 + g][nm]) for nm in ("yaT", "ybT", "ycT") for g in range(4)]
            yTs.append(np.concatenate(parts, 0))
        del resA
        resB = run(build_B(), prep_B(inp, l, mod, x_cur, ctx_cur, yTs)).results
        x1p = [np.asarray(resB[c]["x1"]) for c in range(8)]
        h2Tf, afff = [], []
        for b in range(2):
            hp = [np.asarray(resB[b * 4 + s]["h2T"]) for s in range(4)]
            ap = [np.asarray(resB[b * 4 + s]["aff"]) for s in range(4)]
            h2Tf.append(np.concatenate([p[:, :64] for p in hp] + [p[:, 64:] for p in hp], 1))
            afff.append(np.concatenate([p[:64] for p in ap] + [p[64:] for p in ap], 0))
        del resB
        mapsC = []
        for core in range(8):
            b, j = core // 4, core % 4
            a4 = afff[b][:, 4 * j:4 * j + 4].reshape(130, 128, 4).transpose(1, 2, 0)
            mapsC.append({
                "h2T": np.ascontiguousarray(h2Tf[b]), "affL": np.ascontiguousarray(a4),
                "wg": np.ascontiguousarray(inp["w_exp_gate"][l][4 * j:4 * j + 4]),
                "wu": np.ascontiguousarray(inp["w_exp_up"][l][4 * j:4 * j + 4]),
                "wd": np.ascontiguousarray(inp["w_exp_down"][l][4 * j:4 * j + 4]),
            })
        resC = run(build_C(), mapsC).results
        fpc = [np.asarray(resC[c]["fp"]) for c in range(8)]
        del resC, mapsC
        rowsD = lambda b: np.ascontiguousarray(np.broadcast_to(
            np.stack([mod[5120:6144, b], mod[5120:6144, 2], inp["ln2_g"][l], inp["ln2_b"][l]], 0)[None], (128, 4, 1024)))
        mapsD = []
        for core in range(8):
            b, s = core // 4, core % 4
            fps = np.stack([np.concatenate([fpc[b * 4 + jj][64 * s:64 * (s + 1)],
                                            fpc[b * 4 + jj][256 + 4096 * s:256 + 4096 * (s + 1)]], 0) for jj in range(4)], 0)
            mapsD.append({"x1": x1p[core], "fps": np.ascontiguousarray(fps), "rows": rowsD(b)})
        resD = run(build_D(), mapsD).results
        for b in range(2):
            xp = [np.asarray(resD[b * 4 + s]["x2"]) for s in range(4)]
            ctx_cur[b] = np.concatenate([p[:64] for p in xp], 0)
            x_cur[b] = np.concatenate([p[64:] for p in xp], 0)
        del resD, mapsD, fpc
    return np.stack(x_cur, 0).astype(np.float32)
```

```python
import numpy as np
import concourse.bass as bass
import concourse.mybir as mybir
from concourse.bass_utils import run_bass_kernel_spmd

F32 = mybir.dt.float32
BF16 = mybir.dt.bfloat16
I32 = mybir.dt.int32
U32 = mybir.dt.uint32
ALU = mybir.AluOpType
AF = mybir.ActivationFunctionType
AX = mybir.AxisListType

COMPUTE = ("pe", "act", "dve", "pool")
SAME_ENGINE_SYNC = True
import os
SIGALL = bool(int(os.environ.get('K_SIGALL', '0')))
SIGENG = set(os.environ.get('K_SIGENG', 'act,dve,pool').split(','))


class T:
    def __init__(self, prog, handle, name):
        self.p = prog
        self.h = handle
        self.name = name
        self.lastw = None
        self.reads = []
        self.dsem = None
        self.dcnt = 0

    def __getitem__(self, idx):
        return self.h[idx]


class Prog:
    def __init__(self, name):
        self.name = name
        self.nc = bass.Bass("TRN2", target_bir_lowering=False)
        self.ops = {e: [] for e in ("pe", "act", "dve", "pool", "sp")}
        self.cnt = {e: 0 for e in COMPUTE}
        self.opidx = {e: {} for e in COMPUTE}
        self.seen = {e: {} for e in self.ops}
        self.ctxs = []
        self.tiles = []
        self.sems = {}
        self.outs = []
        self.nsem = 0

    def _enter(self, cm):
        v = cm.__enter__()
        self.ctxs.append(cm)
        return v

    def sem(self, name):
        self.nsem += 1
        return self._enter(self.nc.semaphore(name))

    def sbuf(self, name, shape, dt):
        t = T(self, self._enter(self.nc.sbuf_tensor(name, list(shape), dt)), name)
        self.tiles.append(t)
        return t

    def psum(self, name, shape, dt=F32):
        t = T(self, self._enter(self.nc.psum_tensor(name, list(shape), dt)), name)
        self.tiles.append(t)
        return t

    def dram(self, name, shape, dt, kind="Internal", addr_space="Local"):
        h = self.nc.dram_tensor(name, list(shape), dt, kind=kind, addr_space=addr_space)
        t = T(self, h.ap(), name)
        self.tiles.append(t)
        if kind == "ExternalOutput":
            self.outs.append(t)
        return t

    def start(self):
        for e in COMPUTE:
            self.sems[e] = self.sem("s_" + e)

    def _need(self, eng, hz, waits):
        if hz is None:
            return
        kind, key, val = hz
        if kind == "eng" and key == eng and (eng == "pe" or not SAME_ENGINE_SYNC):
            return
        k = (kind, key if kind == "eng" else id(key))
        if self.seen[eng].get(k, 0) >= val:
            return
        self.seen[eng][k] = val
        if kind == "eng":
            rec = self.ops[key][self.opidx[key][val]]
            rec[3] = True
            waits.append(("eng", key, rec))
        else:
            waits.append(("dma", key.dsem, val))

    def op(self, eng, fn, reads=(), writes=(), dma_dst=None, ordered=False):
        waits = []
        for t in reads:
            self._need(eng, t.lastw, waits)
        for t in writes:
            if dma_dst is not None and t is dma_dst and not ordered and t.lastw is not None \
                    and t.lastw[0] == "dma" and not t.reads:
                pass
            else:
                self._need(eng, t.lastw, waits)
            for r in t.reads:
                self._need(eng, r, waits)
        if dma_dst is not None:
            if dma_dst.dsem is None:
                dma_dst.dsem = self.sem("d_" + dma_dst.name)
            dma_dst.dcnt += 16
            hz = ("dma", dma_dst, dma_dst.dcnt)
            inc = (dma_dst.dsem, 16)
        else:
            self.cnt[eng] += 1
            hz = ("eng", eng, self.cnt[eng])
            inc = None
            self.opidx[eng][self.cnt[eng]] = len(self.ops[eng])
        for t in reads:
            t.reads.append(hz)
        for t in writes:
            t.lastw = hz
            t.reads = []
        self.ops[eng].append([waits, fn, inc, False, 0])

    def dma(self, out_t, out_ap, in_t, in_ap, q="sp", ordered=False, **kw):
        self.op(q, lambda e: e.dma_start(out=out_ap, in_=in_ap, **kw),
                reads=[in_t], writes=[out_t], dma_dst=out_t, ordered=ordered)

    def mm(self, out_t, out_ap, a_t, lhsT, b_t, rhs, start=True, stop=True, **kw):
        self.op("pe", lambda e: e.matmul(out_ap, lhsT, rhs, start=start, stop=stop, **kw),
                reads=[a_t, b_t], writes=[out_t])

    def tr(self, out_t, out_ap, in_t, in_ap, ident_t, ident_ap):
        self.op("pe", lambda e: e.transpose(out_ap, in_ap, ident_ap), reads=[in_t, ident_t], writes=[out_t])


    def act(self, out_t, out_ap, in_t, in_ap, func, extra=(), **kw):
        self.op("act", lambda e: e.activation(out_ap, in_ap, func, **kw), reads=[in_t, *extra], writes=[out_t])

    def ts(self, eng, out_t, out_ap, in_t, in_ap, s1, s2, op0, op1=None, extra=()):
        if op1 is None:
            self.op(eng, lambda e: e.tensor_scalar(out_ap, in_ap, s1, None, op0), reads=[in_t, *extra], writes=[out_t])
        else:
            self.op(eng, lambda e: e.tensor_scalar(out_ap, in_ap, s1, s2, op0, op1), reads=[in_t, *extra], writes=[out_t])

    def tt(self, eng, out_t, out_ap, a_t, a_ap, b_t, b_ap, op):
        self.op(eng, lambda e: e.tensor_tensor(out_ap, a_ap, b_ap, op), reads=[a_t, b_t], writes=[out_t])

    def stt(self, eng, out_t, out_ap, a_t, a_ap, scalar, b_t, b_ap, op0, op1, extra=()):
        self.op(eng, lambda e: e.scalar_tensor_tensor(out_ap, a_ap, scalar, b_ap, op0, op1),
                reads=[a_t, b_t, *extra], writes=[out_t])

    def cp(self, eng, out_t, out_ap, in_t, in_ap):
        if eng == "act":
            self.op("act", lambda e: e.copy(out_ap, in_ap), reads=[in_t], writes=[out_t])
        else:
            self.op(eng, lambda e: e.tensor_copy(out_ap, in_ap), reads=[in_t], writes=[out_t])

    def memset(self, eng, t, ap, val):
        self.op(eng, lambda e: e.memset(ap, val), reads=[], writes=[t])

    def banks(self, n=7):
        self.pb = [self.psum("pb%d" % i, [128, 512], F32) for i in range(n)]
        self.pbi = 0

    def bank(self):
        b = self.pb[self.pbi % len(self.pb)]
        self.pbi += 1
        return b

    def rsqrt(self, out_t, out_ap, in_t, in_ap, scale, eps_t, eps_ap):
        self.op("act", lambda e: e.activation(out_ap, in_ap, AF.Sqrt, bias=eps_ap, scale=scale),
                reads=[in_t, eps_t], writes=[out_t])
        self.op("dve", lambda e: e.reciprocal(out_ap, out_ap), reads=[out_t], writes=[out_t])

    def finish(self):
        waits = []
        for t in self.tiles:
            if t.lastw is not None and t.lastw[0] == "dma":
                self._need("sp", t.lastw, waits)
        self.ops["sp"].append([waits, None, None, False, 0])

    def build(self):
        nc = self.nc
        engs = {"pe": "tensor", "act": "scalar", "dve": "vector", "pool": "gpsimd", "sp": "sync"}
        for e in COMPUTE:
            c = 0
            for rec in self.ops[e]:
                if (SIGALL or e in SIGENG) and rec[2] is None and rec[1] is not None:
                    rec[3] = True
                if rec[2] is None and rec[1] is not None and rec[3]:
                    c += 1
                    rec[4] = c
        with nc.Block() as block:
            for e, attr in engs.items():
                ops = self.ops[e]

                def body(engobj, ops=ops, e=e):
                    for waits, fn, inc, sig, _c in ops:
                        for w in waits:
                            if w[0] == "eng":
                                engobj.wait_ge(self.sems[w[1]], w[2][4])
                            else:
                                engobj.wait_ge(w[1], w[2])
                        if fn is not None:
                            ins = fn(engobj)
                            if inc is not None:
                                ins.then_inc(inc[0], inc[1])
                            elif sig:
                                ins.then_inc(self.sems[e], 1)
                getattr(block, attr)(body)
        for cm in reversed(self.ctxs):
            cm.__exit__(None, None, None)
        self.ctxs = []
        return nc

    def n_ops(self):
        return {e: len(v) for e, v in self.ops.items()}


def run(prog_nc, in_maps, trace=False):
    import sys, time
    t0 = time.time()
    res = run_bass_kernel_spmd(prog_nc, in_maps, core_ids=list(range(len(in_maps))), trace=trace)
    print("[kernel] launch done in %.1fs" % (time.time() - t0), file=sys.stderr, flush=True)
    return res


def build_L0():
    P = Prog("L0")
    nc = P.nc
    w = P.dram("w", [1024, 1536], F32, kind="ExternalInput")
    bT = P.dram("bT", [128, 12], F32, kind="ExternalInput")
    cT = P.dram("cT", [128, 8, 3], F32, kind="ExternalInput")
    out = P.dram("modT", [128, 12, 3], F32, kind="ExternalOutput")
    P.start()
    wt = P.sbuf("wt", [128, 8, 1536], F32)
    bt = P.sbuf("bt", [128, 12], F32)
    ct = P.sbuf("ct", [128, 8, 3], F32)
    st = P.sbuf("st", [128, 8, 3], F32)
    ot = P.sbuf("ot", [128, 12, 3], F32)
    ps = P.psum("ps", [128, 512], F32)
    P.dma(ct, ct[:], cT, cT[:])
    P.dma(bt, bt[:], bT, bT[:])
    for k in range(8):
        P.dma(wt, wt[:, k, :], w, w[k * 128:(k + 1) * 128, :])
    P.op("act", lambda e: e.activation(st[:], ct[:], AF.Silu), reads=[ct], writes=[st])
    for j in range(12):
        for k in range(8):
            P.mm(ps, ps[:, j * 3:(j + 1) * 3], wt, wt[:, k, j * 128:(j + 1) * 128], st, st[:, k, :],
                 start=(k == 0), stop=(k == 7))
    for j in range(12):
        P.op("dve", lambda e, j=j: e.tensor_scalar(ot[:, j, :], ps[:, j * 3:(j + 1) * 3], bt[:, j:j + 1], None, ALU.add),
             reads=[ps, bt], writes=[ot])
    P.dma(out, out[:], ot, ot[:])
    P.finish()
    return P.build()


def run_L0(inp):
    cvec = np.stack([inp["c"][0], inp["c"][1], inp["c_ctx"]], 0)
    cT = np.ascontiguousarray(cvec.T.reshape(8, 128, 3).transpose(1, 0, 2))
    in_maps = []
    for core in range(8):
        l, q = core // 4, core % 4
        cols = slice(q * 1536, (q + 1) * 1536)
        in_maps.append({
            "w": np.ascontiguousarray(inp["w_ada"][l][:, cols]),
            "bT": np.ascontiguousarray(inp["b_ada"][l][cols].reshape(12, 128).T),
            "cT": cT,
        })
    res = run(build_L0(), in_maps)
    mods = []
    for l in range(2):
        parts = []
        for q in range(4):
            m = np.asarray(res.results[l * 4 + q]["modT"])
            parts.append(m.transpose(1, 0, 2).reshape(1536, 3))
        mods.append(np.concatenate(parts, 0))
    return mods


NT = 16640
LN_EPS = 1e-6
NCH = 33


def chunk_rng(c):
    return (0, 256) if c == 0 else (256 + (c - 1) * 512, 512)


def build_A(lam_init, dbg=False):
    P = Prog("A")
    EI = "ExternalInput"
    xin = P.dram("xin", [NT, 1024], F32, EI)
    msT = P.dram("msT", [128, 8, 4], F32, EI)
    wfeat = P.dram("wfeat", [1024, 1216], F32, EI)
    wtok = P.dram("wtok", [1024, 256], F32, EI)
    wuq = P.dram("wuq", [256, 2, 192], F32, EI)
    wukvk = P.dram("wukvk", [128, 2, 64], F32, EI)
    wukvv = P.dram("wukvv", [128, 128], F32, EI)
    gq = P.dram("gq", [128, 2], F32, EI)
    gkv = P.dram("gkv", [128, 1], F32, EI)
    cs32 = P.dram("cs32", [32, 2, NT], F32, EI)
    cs128 = P.dram("cs128", [128, 2, NT], F32, EI)
    nabias = P.dram("nabias", [2, 3, 768, 256], F32, EI)
    lamv = P.dram("lamv", [128, 4, 64], F32, EI)
    gsub = P.dram("gsub", [128, 1], F32, EI)
    identd = P.dram("ident", [128, 128], F32, EI)
    EO = "ExternalOutput"
    yaT = P.dram("yaT", [128, NT], BF16, EO)
    ybT = P.dram("ybT", [128, NT], BF16, EO)
    ycT = P.dram("ycT", [128, NT], BF16, EO)
    sk = EO if dbg else "Internal"
    qaT = P.dram("qaT", [128, NT], BF16, sk)
    kaT = P.dram("kaT", [128, NT], BF16, sk)
    va = P.dram("va", [NT, 128], BF16, sk)
    qbT = P.dram("qbT", [2, 96, NT], BF16, sk)
    kbT = P.dram("kbT", [2, 96, NT], BF16, sk)
    vb = P.dram("vb", [NT, 128], BF16, sk)
    dqT = P.dram("dqT", [128, NT], BF16, sk)
    dkT = P.dram("dkT", [128, NT], BF16, sk)
    dvv = P.dram("dvv", [NT, 128], BF16, sk)
    P.start()
    P.banks(7)
    tp = P.psum("tp", [128, 1024], BF16)

    Wf = P.sbuf("Wf", [128, 8, 1216], BF16)
    Wt = P.sbuf("Wt", [128, 8, 256], BF16)
    for k in range(8):
        P.dma(Wf, Wf[:, k, :], wfeat, wfeat[k * 128:(k + 1) * 128, :], q="pool")
        P.dma(Wt, Wt[:, k, :], wtok, wtok[k * 128:(k + 1) * 128, :], q="pool")
    ident = P.sbuf("identb", [128, 128], BF16)
    P.dma(ident, ident[:], identd, identd[:], q="pool")
    ones = P.sbuf("ones", [128, 128], BF16)
    P.memset("dve", ones, ones[:], 1.0)
    onesf = P.sbuf("onesf", [128, 128], F32)
    epst = P.sbuf("epst", [128, 1], F32)
    P.memset("dve", epst, epst[:], LN_EPS)
    P.memset("dve", onesf, onesf[:], 1.0)
    ms = P.sbuf("ms", [128, 8, 4], F32)
    P.dma(ms, ms[:], msT, msT[:])
    scp = P.sbuf("scp", [128, 8, 2], F32)
    for m in range(2):
        P.ts("dve", scp, scp[:, :, m], ms, ms[:, :, 2 * m + 1], 1.0, None, ALU.add)
    gqt = P.sbuf("gqt", [128, 2], F32)
    P.dma(gqt, gqt[:], gq, gq[:])
    gkvt = P.sbuf("gkvt", [128, 1], F32)
    P.dma(gkvt, gkvt[:], gkv, gkv[:])
    wuq_r = P.sbuf("wuq_r", [128, 2, 2, 192], F32)
    for k in range(2):
        P.dma(wuq_r, wuq_r[:, k], wuq, wuq[k * 128:(k + 1) * 128])
    Wuq = P.sbuf("Wuq", [128, 2, 2, 192], BF16)
    for k in range(2):
        P.ts("dve", Wuq, Wuq[:, k], wuq_r, wuq_r[:, k], gqt[:, k:k + 1], None, ALU.mult, extra=[gqt])
    wk_r = P.sbuf("wk_r", [128, 2, 64], F32)
    P.dma(wk_r, wk_r[:], wukvk, wukvk[:])
    Wkk = P.sbuf("Wkk", [128, 2, 64], BF16)
    P.ts("dve", Wkk, Wkk[:], wk_r, wk_r[:], gkvt[:, 0:1], None, ALU.mult, extra=[gkvt])
    wv_r = P.sbuf("wv_r", [128, 128], F32)
    P.dma(wv_r, wv_r[:], wukvv, wukvv[:])
    Wkv = P.sbuf("Wkv", [128, 128], BF16)
    P.ts("dve", Wkv, Wkv[:], wv_r, wv_r[:], gkvt[:, 0:1], None, ALU.mult, extra=[gkvt])
    lv = P.sbuf("lv", [128, 4, 64], F32)
    P.dma(lv, lv[:], lamv, lamv[:])
    lpr = P.sbuf("lpr", [128, 2, 64], F32)
    P.tt("dve", lpr, lpr[:, 0], lv, lv[:, 0], lv, lv[:, 1], ALU.mult)
    P.tt("dve", lpr, lpr[:, 1], lv, lv[:, 2], lv, lv[:, 3], ALU.mult)
    lsum = P.sbuf("lsum", [128, 2], F32)
    P.op("dve", lambda e: e.reduce_sum(lsum[:], lpr[:], AX.X), reads=[lpr], writes=[lsum])
    lexp = P.sbuf("lexp", [128, 2], F32)
    P.act(lexp, lexp[:], lsum, lsum[:], AF.Exp)
    neglam = P.sbuf("neglam", [128, 1], F32)
    P.stt("dve", neglam, neglam[:], lexp, lexp[:, 1:2], -float(lam_init), lexp, lexp[:, 0:1], ALU.add, ALU.subtract)
    gsr = P.sbuf("gsr", [128, 1], F32)
    P.dma(gsr, gsr[:], gsub, gsub[:])
    gss = P.sbuf("gss", [128, 1], F32)
    P.ts("dve", gss, gss[:], gsr, gsr[:], 1.0 - float(lam_init), None, ALU.mult)

    xb = [P.sbuf("xb%d" % i, [128, 1024], F32) for i in range(2)]
    xnb = [P.sbuf("xn%d" % i, [128, 1024], BF16) for i in range(2)]
    st6 = P.sbuf("st6", [128, 2, 6], F32)
    mv = P.sbuf("mv", [128, 2], F32)
    rstd = P.sbuf("rstd", [128, 1], F32)
    htmp = P.sbuf("htmp", [128, 8, 128], F32)
    hTb = [P.sbuf("hT%d" % i, [128, 8, 512], BF16) for i in range(2)]
    t32 = [P.sbuf("t32_%d" % i, [128, 2, 512], F32) for i in range(2)]
    t128 = [P.sbuf("t128_%d" % i, [128, 2, 512], F32) for i in range(2)]
    stg = [P.sbuf("stg%d" % i, [128, 512], BF16) for i in range(3)]
    stgi = [0]

    def nstg():
        s_ = stg[stgi[0] % len(stg)]
        stgi[0] += 1
        return s_
    cqT = P.sbuf("cqT", [128, 2, 512], BF16)
    cqsq = P.sbuf("cqsq", [128, 2, 512], BF16)
    ckvT = P.sbuf("ckvT", [128, 512], BF16)
    ckvsq = P.sbuf("ckvsq", [128, 512], BF16)
    rq = P.sbuf("rq", [128, 512], F32)
    rkv = P.sbuf("rkv", [128, 512], F32)
    rkvt = P.sbuf("rkvt", [128, 4], F32)
    ra = [P.sbuf("ra%d" % i, [128, 512], F32) for i in range(1)]
    rb = [P.sbuf("rb%d" % i, [128, 512], F32) for i in range(1)]
    vst = [P.sbuf("vst%d" % i, [128, 256], BF16) for i in range(2)]
    krT = P.sbuf("krT", [32, 512], BF16)

    def rope(n, pa, pb_, rows, tab, ti, out_t, out_ap, mul_t=None, mul_ap=None):
        a = ra[0]
        b = rb[0]
        P.tt("dve", a, a[rows, :n], pa, pa[rows, :n], tab, tab[rows, 0, :n], ALU.mult)
        P.tt("dve", b, b[rows, :n], pb_, pb_[rows, :n], tab, tab[rows, 1, :n], ALU.mult)
        if mul_t is None:
            P.tt("pool", out_t, out_ap, a, a[rows, :n], b, b[rows, :n], ALU.add)
        else:
            P.tt("pool", a, a[rows, :n], a, a[rows, :n], b, b[rows, :n], ALU.add)
            P.tt("pool", out_t, out_ap, a, a[rows, :n], mul_t, mul_ap, ALU.mult)

    FO = {"qa": 0, "ka": 128, "cq0": 256, "cq1": 384, "ckv": 512, "kr": 640, "krp": 672,
          "dq": 704, "dqp": 832, "dk": 960, "dkp": 1088}

    def proj(hT, name, M, n):
        pbk = P.bank()
        off = FO[name]
        for k in range(8):
            P.mm(pbk, pbk[:M, :n], Wf, Wf[:, k, off:off + M], hT, hT[:, k, :n], start=(k == 0), stop=(k == 7))
        return pbk

    for c in range(NCH):
        tok0, n = chunk_rng(c)
        m = 1 if c == 0 else 0
        nt = n // 128
        hT = hTb[c % 2]
        tb32 = t32[c % 2]
        tb128 = t128[c % 2]
        P.dma(tb32, tb32[0:32, :, :n], cs32, cs32[:, :, tok0:tok0 + n])
        P.dma(tb32, tb32[64:96, :, :n], cs32, cs32[:, :, tok0:tok0 + n])
        P.dma(tb128, tb128[:, :, :n], cs128, cs128[:, :, tok0:tok0 + n])
        for i in range(nt):
            xt = xb[i % 2]
            xn = xnb[i % 2]
            P.dma(xt, xt[:], xin, xin[tok0 + i * 128: tok0 + (i + 1) * 128, :])
            for a_ in range(2):
                P.op("dve", lambda e, xt=xt, a_=a_: e.bn_stats(st6[:, a_, :], xt[:, a_ * 512:(a_ + 1) * 512]),
                     reads=[xt], writes=[st6])
            P.op("dve", lambda e: e.bn_aggr(mv[:], st6[:].rearrange("p a b -> p (a b)")), reads=[st6], writes=[mv])
            P.rsqrt(rstd, rstd[:], mv, mv[:, 1:2], 1.0, epst, epst[:, 0:1])
            P.ts("dve", xn, xn[:], xt, xt[:], mv[:, 0:1], rstd[:, 0:1], ALU.subtract, ALU.mult, extra=[mv, rstd])
            for k in range(8):
                P.tr(tp, tp[:, k * 128:(k + 1) * 128], xn, xn[:, k * 128:(k + 1) * 128], ident, ident[:])
            P.tt("dve", htmp, htmp[:], tp, tp[:].rearrange("p (a b) -> p a b", a=8),
                 scp, scp[:, :, m:m + 1].to_broadcast([128, 8, 128]), ALU.mult)
            P.tt("pool", hT, hT[:, :, i * 128:(i + 1) * 128], htmp, htmp[:],
                 ms, ms[:, :, 2 * m:2 * m + 1].to_broadcast([128, 8, 128]), ALU.add)
        for name, dst in (("qa", qaT), ("ka", kaT)):
            pbk = proj(hT, name, 128, n)
            s_ = nstg()
            P.cp("act", s_, s_[:, :n], pbk, pbk[:, :n])
            P.dma(dst, dst[:, tok0:tok0 + n], s_, s_[:, :n])
        for j, name in enumerate(("cq0", "cq1")):
            pbk = proj(hT, name, 128, n)
            P.cp("act", cqT, cqT[:, j, :n], pbk, pbk[:, :n])
            P.act(cqsq, cqsq[:, j, :n], pbk, pbk[:, :n], AF.Square)
        pbk = proj(hT, "ckv", 128, n)
        P.cp("act", ckvT, ckvT[:, :n], pbk, pbk[:, :n])
        P.act(ckvsq, ckvsq[:, :n], pbk, pbk[:, :n], AF.Square)
        pa = proj(hT, "kr", 32, n)
        pb_ = proj(hT, "krp", 32, n)
        rope(n, pa, pb_, slice(0, 32), tb32, 0, krT, krT[:, :n])
        for h in range(2):
            P.dma(kbT, kbT[h, 64:96, tok0:tok0 + n], krT, krT[:, :n])
        for (nm, nmp, dst, ti) in (("dq", "dqp", dqT, 0), ("dk", "dkp", dkT, 1)):
            pa = proj(hT, nm, 128, n)
            pb_ = proj(hT, nmp, 128, n)
            s_ = nstg()
            rope(n, pa, pb_, slice(0, 128), tb128, ti, s_, s_[:, :n])
            P.dma(dst, dst[:, tok0:tok0 + n], s_, s_[:, :n])
        for i in range(nt):
            pbk = P.bank()
            for k in range(8):
                P.mm(pbk, pbk[:, 0:256], hT, hT[:, k, i * 128:(i + 1) * 128], Wt, Wt[:, k, :], start=(k == 0), stop=(k == 7))
            v_ = vst[i % 2]
            P.cp("act", v_, v_[:], pbk, pbk[:, 0:256])
            r0 = tok0 + i * 128
            P.dma(va, va[r0:r0 + 128, :], v_, v_[:, 0:128])
            P.dma(dvv, dvv[r0:r0 + 128, :], v_, v_[:, 128:256])
        pbk = P.bank()
        P.mm(pbk, pbk[:, :n], ones, ones[:], cqsq, cqsq[:, 0, :n], start=True, stop=False)
        P.mm(pbk, pbk[:, :n], ones, ones[:], cqsq, cqsq[:, 1, :n], start=False, stop=True)
        P.rsqrt(rq, rq[:, :n], pbk, pbk[:, :n], 1.0 / 256, epst, epst[:, 0:1])
        pbk = P.bank()
        P.mm(pbk, pbk[:, :n], ones, ones[:], ckvsq, ckvsq[:, :n])
        P.rsqrt(rkv, rkv[:, :n], pbk, pbk[:, :n], 1.0 / 128, epst, epst[:, 0:1])
        pbk = P.bank()
        for i in range(nt):
            P.mm(pbk, pbk[:, i:i + 1], ckvsq, ckvsq[:, i * 128:(i + 1) * 128], ones, ones[:, 0:1])
        P.rsqrt(rkvt, rkvt[:, :nt], pbk, pbk[:, :nt], 1.0 / 128, epst, epst[:, 0:1])
        for h in range(2):
            qa_ = P.bank()
            for k in range(2):
                P.mm(qa_, qa_[:96, :n], Wuq, Wuq[:, k, h, 0:96], cqT, cqT[:, k, :n], start=(k == 0), stop=(k == 1))
            qb_ = P.bank()
            for k in range(2):
                P.mm(qb_, qb_[:96, :n], Wuq, Wuq[:, k, h, 96:192], cqT, cqT[:, k, :n], start=(k == 0), stop=(k == 1))
            s_ = nstg()
            P.tt("dve", s_, s_[0:64, :n], qa_, qa_[0:64, :n], rq, rq[0:64, :n], ALU.mult)
            rope(n, qa_, qb_, slice(64, 96), tb32, h, s_, s_[64:96, :n], rq, rq[64:96, :n])
            P.dma(qbT, qbT[h, :, tok0:tok0 + n], s_, s_[0:96, :n])
            kn = P.bank()
            P.mm(kn, kn[:64, :n], Wkk, Wkk[:, h, :], ckvT, ckvT[:, :n])
            s_ = nstg()
            P.tt("dve", s_, s_[0:64, :n], kn, kn[0:64, :n], rkv, rkv[0:64, :n], ALU.mult)
            P.dma(kbT, kbT[h, 0:64, tok0:tok0 + n], s_, s_[0:64, :n])
        for i in range(nt):
            pbk = P.bank()
            P.mm(pbk, pbk[:, 0:128], ckvT, ckvT[:, i * 128:(i + 1) * 128], Wkv, Wkv[:])
            v_ = vst[i % 2]
            P.ts("dve", v_, v_[:, 0:128], pbk, pbk[:, 0:128], rkvt[:, i:i + 1], None, ALU.mult, extra=[rkvt])
            r0 = tok0 + i * 128
            P.dma(vb, vb[r0:r0 + 128, :], v_, v_[:, 0:128])
    if dbg == "A1":
        P.finish()
        return P.build()

    BK = P.sbuf("BK", [128, NT], BF16)
    BV = P.sbuf("BV", [128, 130, 130], BF16)
    LOOK = 2
    pts = [P.sbuf("pt%d" % i, [128, 512], BF16) for i in range(5)]
    pti = [0]

    def npt():
        t_ = pts[pti[0] % 5]
        pti[0] += 1
        return t_
    qcs = [P.sbuf("qc%d" % i, [128, 512], BF16) for i in range(2)]
    rec = P.sbuf("rec", [128, 512], F32)
    rec2 = P.sbuf("rec2", [128, 512], F32)
    bsb = P.sbuf("bsb", [128, 512], F32)
    ysb = [P.sbuf("ysb%d" % i, [128, 512], BF16) for i in range(2)]
    yf = P.sbuf("yf", [128, 512], F32)
    yf2 = P.sbuf("yf2", [128, 512], F32)
    tmpf = [P.sbuf("tmpf%d" % i, [128, 256], F32) for i in range(2)]
    nb = P.sbuf("nb", [128, 3, 6, 256], F32)
    acc_banks = [P.pb[0], P.pb[1]]
    bb_bank = P.pb[2]
    s_banks = [P.pb[3], P.pb[4], P.pb[5], P.pb[6]]
    cnt = {"acc": 0, "s": 0, "y": 0}

    def sbank():
        b = s_banks[cnt["s"] % 4]
        cnt["s"] += 1
        return b

    def finalize(acc, n, dst, drow, tok0):
        P.op("dve", lambda e: e.reciprocal(rec[64:65, :n], acc[64:65, :n]), reads=[acc], writes=[rec])
        P.mm(bb_bank, bb_bank[:64, :n], onesf, onesf[64:65, 0:64], rec, rec[64:65, :n])
        P.cp("act", bsb, bsb[:64, :n], bb_bank, bb_bank[:64, :n])
        y_ = ysb[cnt["y"] % 2]
        cnt["y"] += 1
        P.tt("dve", y_, y_[:64, :n], acc, acc[:64, :n], bsb, bsb[:64, :n], ALU.mult)
        P.dma(dst, dst[drow:drow + 64, tok0:tok0 + n], y_, y_[:64, :n])

    for q4 in range(4):
        P.dma(BK, BK[:, q4 * 4160:(q4 + 1) * 4160], kaT, kaT[:, q4 * 4160:(q4 + 1) * 4160])
    BV4 = BV[:].rearrange("p t (h d) -> p t h d", h=2)
    for h in range(2):
        P.dma(BV, BV4[:, :, h, 0:64], va, va[:, h * 64:(h + 1) * 64].rearrange("(t p) d -> p t d", p=128))
    P.memset("pool", BV, BV4[:, :, :, 64:65], 1.0)
    for h in range(2):
        P.dma(nb, nb[:], nabias, nabias[h].rearrange("v (j p) q -> p v j q", p=128))
        for blk in range(-1, 64):
            if blk < 0:
                qtok0, tiles = 0, [(0, 0, None), (128, 1, None)]
            else:
                r0 = blk * 4
                var = 0 if blk == 0 else (2 if blk == 63 else 1)
                se = 0 if blk == 0 else (244 if blk == 63 else r0 - 4)
                qtok0 = 256 + r0 * 64
                tiles = [(0, 0, None), (128, 1, None)]
                for j in range(6):
                    tiles.append((256 + se * 64 + j * 128, 2 + se // 2 + j, (var, j)))
            qc = qcs[cnt["acc"] % 2]
            P.dma(qc, qc[:, :256], qaT, qaT[:, qtok0:qtok0 + 256])
            acc = acc_banks[cnt["acc"] % 2]
            cnt["acc"] += 1
            pend = []
            for idx in range(len(tiles) + LOOK):
                if idx < len(tiles):
                    kcol, vt, bi = tiles[idx]
                    sb = sbank()
                    P.mm(sb, sb[:, :256], BK, BK[h * 64:(h + 1) * 64, kcol:kcol + 128], qc, qc[h * 64:(h + 1) * 64, :256])
                    pt = npt()
                    if bi is None:
                        P.act(pt, pt[:, :256], sb, sb[:, :256], AF.Exp, scale=0.125)
                    else:
                        tf = tmpf[idx % 2]
                        P.stt("dve", tf, tf[:], sb, sb[:, :256], 0.125, nb, nb[:, bi[0], bi[1], :], ALU.mult, ALU.add)
                        P.act(pt, pt[:, :256], tf, tf[:], AF.Exp)
                    pend.append(pt)
                j = idx - LOOK
                if j >= 0:
                    pt = pend[j]
                    P.mm(acc, acc[:65, :256], BV, BV4[:, tiles[j][1], h, :], pt, pt[:, :256], start=(j == 0), stop=(j == len(tiles) - 1))
            finalize(acc, 256, yaT, h * 64, qtok0)

    BV3 = BV[:]
    sc_b = 96.0 ** -0.5
    for h in range(2):
        for q4 in range(4):
            P.dma(BK, BK[0:96, q4 * 4160:(q4 + 1) * 4160], kbT, kbT[h, :, q4 * 4160:(q4 + 1) * 4160])
        P.dma(BV, BV3[:, :, 0:64], vb, vb[:, h * 64:(h + 1) * 64].rearrange("(t p) d -> p t d", p=128))
        P.memset("pool", BV, BV3[:, :, 64:65], 1.0)
        for c in range(NCH):
            tok0, n = chunk_rng(c)
            kts = [0, 1] if c == 0 else list(range(130))
            qc = qcs[cnt["acc"] % 2]
            P.dma(qc, qc[0:96, :n], qbT, qbT[h, :, tok0:tok0 + n])
            acc = acc_banks[cnt["acc"] % 2]
            cnt["acc"] += 1
            pend = []
            for idx in range(len(kts) + LOOK):
                if idx < len(kts):
                    kt = kts[idx]
                    sb = sbank()
                    P.mm(sb, sb[:, :n], BK, BK[0:96, kt * 128:(kt + 1) * 128], qc, qc[0:96, :n])
                    pt = npt()
                    P.act(pt, pt[:, :n], sb, sb[:, :n], AF.Exp, scale=sc_b)
                    pend.append(pt)
                j = idx - LOOK
                if j >= 0:
                    pt = pend[j]
                    P.mm(acc, acc[:65, :n], BV, BV3[:, kts[j], 0:65], pt, pt[:, :n], start=(j == 0), stop=(j == len(kts) - 1))
            finalize(acc, n, ybT, h * 64, tok0)

    for q4 in range(4):
        P.dma(BK, BK[:, q4 * 4160:(q4 + 1) * 4160], dkT, dkT[:, q4 * 4160:(q4 + 1) * 4160])
    P.dma(BV, BV3[:, :, 0:128], dvv, dvv[:, :].rearrange("(t p) d -> p t d", p=128))
    O1, O2 = P.pb[0], P.pb[1]
    s3 = [P.pb[2], P.pb[3], P.pb[4], P.pb[5], P.pb[6]]
    sacc = [P.sbuf("sacc%d" % i, [128, 512], F32) for i in range(2)]
    S1 = S2 = None
    for c in range(NCH):
        tok0, n = chunk_rng(c)
        kts = [0, 1] if c == 0 else list(range(130))
        qc = qcs[c % 2]
        P.dma(qc, qc[:, :n], dqT, dqT[:, tok0:tok0 + n])
        pend = []
        for idx in range(len(kts) + 1):
            if idx < len(kts):
                kt = kts[idx]
                pp = []
                for half in range(2):
                    sb = s3[cnt["s"] % 5]
                    cnt["s"] += 1
                    rows = slice(half * 64, (half + 1) * 64)
                    P.mm(sb, sb[:, :n], BK, BK[rows, kt * 128:(kt + 1) * 128], qc, qc[rows, :n])
                    pt = npt()
                    P.act(pt, pt[:, :n], sb, sb[:, :n], AF.Exp, scale=0.125)
                    pp.append(pt)
                pend.append(pp)
            j = idx - 1
            if j >= 0:
                st_, sp_ = (j == 0), (j == len(kts) - 1)
                for half, O in enumerate((O1, O2)):
                    pt = pend[j][half]
                    P.mm(O, O[:, :n], BV, BV3[:, kts[j], 0:128], pt, pt[:, :n], start=st_, stop=sp_)
                    sa = sacc[half]
                    if j == 0:
                        P.cp("dve", sa, sa[:, :n], pt, pt[:, :n])
                    else:
                        P.tt("dve", sa, sa[:, :n], sa, sa[:, :n], pt, pt[:, :n], ALU.add)
        for half, rr in enumerate((rec, rec2)):
            sb = s3[cnt["s"] % 5]
            cnt["s"] += 1
            P.mm(sb, sb[:, :n], onesf, onesf[:, :], sacc[half], sacc[half][:, :n])
            P.op("dve", lambda e, n=n, rr=rr, sb=sb: e.reciprocal(rr[:, :n], sb[:, :n]), reads=[sb], writes=[rr])
        P.tt("dve", yf, yf[:, :n], O1, O1[:, :n], rec, rec[:, :n], ALU.mult)
        P.tt("dve", yf2, yf2[:, :n], O2, O2[:, :n], rec2, rec2[:, :n], ALU.mult)
        P.stt("dve", yf, yf[:, :n], yf2, yf2[:, :n], neglam[:, 0:1], yf, yf[:, :n], ALU.mult, ALU.add, extra=[neglam])
        pt = npt()
        P.act(pt, pt[:, :n], yf, yf[:, :n], AF.Square)
        sb = s3[cnt["s"] % 5]
        cnt["s"] += 1
        P.mm(sb, sb[:, :n], ones, ones[:], pt, pt[:, :n])
        P.rsqrt(bsb, bsb[:, :n], sb, sb[:, :n], 1.0 / 128, epst, epst[:, 0:1])
        P.tt("dve", yf, yf[:, :n], yf, yf[:, :n], bsb, bsb[:, :n], ALU.mult)
        y_ = ysb[c % 2]
        P.ts("pool", y_, y_[:, :n], yf, yf[:, :n], gss[:, 0:1], None, ALU.mult, extra=[gss])
        P.dma(ycT, ycT[:, tok0:tok0 + n], y_, y_[:, :n])
    P.finish()
    return P.build()


def _pm(vec):
    return np.ascontiguousarray(np.asarray(vec).reshape(-1, 128).T)


def _partner_perm(D):
    m = D // 2
    half = m // 2
    perm = np.zeros(D, np.int64)
    sign = np.zeros(D, np.float32)
    for d in range(D):
        seg, i = d // m, d % m
        if i < half:
            perm[d] = seg * m + i + half
            sign[d] = -1.0
        else:
            perm[d] = seg * m + i - half
            sign[d] = 1.0
    return perm, sign


def _rope_table(D):
    m = D // 2
    half = m // 2
    inv = (np.float32(10000.0) ** (-np.arange(0, m, 2, dtype=np.float32) / np.float32(m))).astype(np.float32)
    t = np.arange(16384)
    row = (t // 64).astype(np.float32)
    col = (t % 64).astype(np.float32)
    _, sign = _partner_perm(D)
    tab = np.zeros((D, 2, NT), np.float32)
    tab[:, 0, :256] = 1.0
    for d in range(D):
        seg, i = d // m, d % m
        fi = i % half
        pos = row if seg == 0 else col
        ang = (pos * inv[fi]).astype(np.float32)
        tab[d, 0, 256:] = np.cos(ang)
        tab[d, 1, 256:] = sign[d] * np.sin(ang)
    return tab


def _na_bias(rpb2):
    out = np.full((2, 3, 768, 256), -30000.0, np.float32)
    for var, (r0, se) in enumerate(((0, 0), (8, 4), (252, 244))):
        kr = se + np.arange(768) // 64
        kc = np.arange(768) % 64
        r = r0 + np.arange(256) // 64
        c = np.arange(256) % 64
        rs = np.clip(r - 4, 0, 248)
        cs = np.clip(c - 8, 0, 48)
        okr = (kr[:, None] >= rs[None, :]) & (kr[:, None] < rs[None, :] + 8)
        okc = (kc[:, None] >= cs[None, :]) & (kc[:, None] < cs[None, :] + 16)
        ok = okr & okc
        ro = np.clip(kr[:, None] - r[None, :] + 7, 0, 14)
        co = np.clip(kc[:, None] - c[None, :] + 15, 0, 30)
        for h in range(2):
            g = rpb2[h][ro, co]
            out[h, var] = np.where(ok, g, np.float32(-30000.0))
    return out


_CONST = {}


def _consts():
    if not _CONST:
        _CONST["cs32"] = _rope_table(32)
        t64 = _rope_table(64)
        _CONST["cs128"] = np.ascontiguousarray(np.concatenate([t64, t64], 0))
        _CONST["ident"] = np.eye(128, dtype=np.float32)
    return _CONST


def prep_A(inp, l, mod, x_cur, ctx_cur):
    C = _consts()
    w_in = inp["w_in"][l]
    p32, _ = _partner_perm(32)
    p64, _ = _partner_perm(64)
    p128 = np.concatenate([p64, 64 + p64])
    maps = []
    for core in range(8):
        b, g = core // 4, core % 4
        xin = np.ascontiguousarray(np.concatenate([ctx_cur[b], x_cur[b]], 0))
        msT = np.stack([_pm(mod[0:1024, b]), _pm(mod[1024:2048, b]), _pm(mod[0:1024, 2]), _pm(mod[1024:2048, 2])], -1)
        kr = w_in[:, 1920:1952]
        dq = w_in[:, 1952 + 128 * g: 1952 + 128 * (g + 1)]
        dk = w_in[:, 2464 + 128 * g: 2464 + 128 * (g + 1)]
        wfeat = np.concatenate([
            w_in[:, 128 * g:128 * (g + 1)], w_in[:, 512 + 128 * g:512 + 128 * (g + 1)],
            w_in[:, 1536:1664], w_in[:, 1664:1792], w_in[:, 1792:1920],
            kr, kr[:, p32], dq, dq[:, p128], dk, dk[:, p128]], 1)
        wtok = np.concatenate([w_in[:, 1024 + 128 * g:1024 + 128 * (g + 1)], w_in[:, 2976 + 128 * g:2976 + 128 * (g + 1)]], 1)
        wuq = np.zeros((256, 2, 192), np.float32)
        wukvk = np.zeros((128, 2, 64), np.float32)
        wukvv = np.zeros((128, 128), np.float32)
        for hh in range(2):
            H = 2 * g + hh
            uq = inp["mla_w_uq"][l][:, H * 96:(H + 1) * 96]
            rope_c = uq[:, 64:96]
            wuq[:, hh, 0:96] = uq
            wuq[:, hh, 160:192] = rope_c[:, p32]
            ukv = inp["mla_w_ukv"][l][:, H * 128:(H + 1) * 128]
            wukvk[:, hh, :] = ukv[:, 0:64]
            wukvv[:, hh * 64:(hh + 1) * 64] = ukv[:, 64:128]
        lamv = np.stack([inp["diff_lq1"][l], inp["diff_lk1"][l], inp["diff_lq2"][l], inp["diff_lk2"][l]], 0)
        maps.append({
            "xin": xin, "msT": np.ascontiguousarray(msT), "wfeat": np.ascontiguousarray(wfeat),
            "wtok": np.ascontiguousarray(wtok), "wuq": wuq, "wukvk": wukvk, "wukvv": wukvv,
            "gq": _pm(inp["mla_g_q"][l]), "gkv": _pm(inp["mla_g_kv"][l]),
            "cs32": C["cs32"], "cs128": C["cs128"],
            "nabias": _na_bias(inp["na_rpb"][l][2 * g:2 * g + 2]),
            "lamv": np.ascontiguousarray(np.broadcast_to(lamv[None], (128, 4, 64))),
            "gsub": _pm(inp["diff_g_sub"][l]), "ident": C["ident"],
        })
    return maps


NB = 4160
ALPHA = 4.0 ** 0.25


def chunkB(c):
    return (0, 64) if c == 0 else (64 + (c - 1) * 512, 512)


def ln_tile(P, src_t, src_ap, np_, st6, mv, rstd, epst, out_t, out_ap):
    for a_ in range(2):
        P.op("dve", lambda e, a_=a_: e.bn_stats(st6[:np_, a_, :], src_ap[:, a_ * 512:(a_ + 1) * 512]),
             reads=[src_t], writes=[st6])
    P.op("dve", lambda e: e.bn_aggr(mv[:np_, :], st6[:np_].rearrange("p a b -> p (a b)")), reads=[st6], writes=[mv])
    P.rsqrt(rstd, rstd[:np_, :], mv, mv[:np_, 1:2], 1.0, epst, epst[:np_, 0:1])
    P.ts("dve", out_t, out_ap, src_t, src_ap, mv[:np_, 0:1], rstd[:np_, 0:1], ALU.subtract, ALU.mult, extra=[mv, rstd])


def build_B():
    P = Prog("B")
    EI, EO = "ExternalInput", "ExternalOutput"
    xin = P.dram("xin", [NB, 1024], F32, EI)
    yT = P.dram("yT", [1536, NB], BF16, EI)
    msT = P.dram("msT", [128, 8, 4], F32, EI)
    wgate = P.dram("wgate", [1024, 3072], F32, EI)
    bgT = P.dram("bgT", [128, 24], F32, EI)
    wbr = P.dram("wbr", [1536, 1024], F32, EI)
    wout = P.dram("wout", [1024, 1024], F32, EI)
    wrt = P.dram("wrt", [128, 8, 16], F32, EI)
    rows = P.dram("rows", [128, 8, 1024], F32, EI)
    identd = P.dram("ident", [128, 128], F32, EI)
    x1o = P.dram("x1", [NB, 1024], F32, EO)
    h2To = P.dram("h2T", [1024, NB], BF16, EO)
    affo = P.dram("aff", [NB, 16], F32, EO)
    P.start()
    P.banks(7)
    tp = P.psum("tp", [128, 1024], BF16)
    Wg = P.sbuf("Wg", [128, 8, 3072], BF16)
    Wb = P.sbuf("Wb", [128, 12, 1024], BF16)
    Wo = P.sbuf("Wo", [128, 8, 1024], BF16)
    for k in range(8):
        P.dma(Wg, Wg[:, k, :], wgate, wgate[k * 128:(k + 1) * 128, :], q="pool")
        P.dma(Wo, Wo[:, k, :], wout, wout[k * 128:(k + 1) * 128, :], q="pool")
    for k in range(12):
        P.dma(Wb, Wb[:, k, :], wbr, wbr[k * 128:(k + 1) * 128, :], q="pool")
    ident = P.sbuf("identb", [128, 128], BF16)
    P.dma(ident, ident[:], identd, identd[:], q="pool")
    bg = P.sbuf("bg", [128, 24], F32)
    P.dma(bg, bg[:], bgT, bgT[:])
    ms = P.sbuf("ms", [128, 8, 4], F32)
    P.dma(ms, ms[:], msT, msT[:])
    scp = P.sbuf("scp", [128, 8, 2], F32)
    for m in range(2):
        P.ts("dve", scp, scp[:, :, m], ms, ms[:, :, 2 * m + 1], 1.0, None, ALU.add)
    rw = P.sbuf("rw", [128, 8, 1024], F32)
    P.dma(rw, rw[:], rows, rows[:])
    for j in (4, 6):
        P.ts("dve", rw, rw[:, j, :], rw, rw[:, j, :], 1.0, None, ALU.add)
    wr_f = P.sbuf("wr_f", [128, 8, 16], F32)
    P.dma(wr_f, wr_f[:], wrt, wrt[:])
    wr_hi = P.sbuf("wr_hi", [128, 8, 16], BF16)
    wr_lo = P.sbuf("wr_lo", [128, 8, 16], BF16)
    P.cp("dve", wr_hi, wr_hi[:], wr_f, wr_f[:])
    P.tt("dve", wr_lo, wr_lo[:], wr_f, wr_f[:], wr_hi, wr_hi[:], ALU.subtract)
    epst = P.sbuf("epst", [128, 1], F32)
    P.memset("dve", epst, epst[:], LN_EPS)

    xc = P.sbuf("xc", [128, 4, 1024], F32)
    xnb = [P.sbuf("xn%d" % i, [128, 1024], BF16) for i in range(2)]
    st6 = P.sbuf("st6", [128, 2, 6], F32)
    mv = P.sbuf("mv", [128, 2], F32)
    rstd = P.sbuf("rstd", [128, 1], F32)
    htmp = P.sbuf("htmp", [128, 8, 128], F32)
    hT = P.sbuf("hT", [128, 8, 512], BF16)
    yt = P.sbuf("yt", [128, 12, 512], BF16)
    sig = [P.sbuf("sig%d" % i, [128, 512], F32) for i in range(2)]
    mf = P.sbuf("mf", [128, 512], F32)
    mt2 = P.sbuf("mt2", [128, 512], F32)
    mT = P.sbuf("mT", [128, 8, 512], BF16)
    u = P.sbuf("u", [128, 1024], F32)
    ut = P.sbuf("ut", [128, 1024], F32)
    x1 = [P.sbuf("x1_%d" % i, [128, 1024], F32) for i in range(1)]
    h2 = P.sbuf("h2", [128, 1024], F32)
    h2hi = P.sbuf("h2hi", [128, 1024], BF16)
    h2lo = P.sbuf("h2lo", [128, 1024], BF16)
    h2Ts = [P.sbuf("h2Ts%d" % i, [128, 8, 128], BF16) for i in range(1)]
    h2Tl = P.sbuf("h2Tl", [128, 8, 128], BF16)
    lg = P.sbuf("lg", [128, 16], F32)
    mx = P.sbuf("mx", [128, 1], F32)
    sm = P.sbuf("sm", [128, 1], F32)
    affs = [P.sbuf("affs%d" % i, [128, 16], F32) for i in range(2)]

    for c in range(9):
        tok0, n = chunkB(c)
        m = 1 if c == 0 else 0
        tsz = 64 if c == 0 else 128
        nt = n // tsz
        P.dma(yt, yt[:, :, :n], yT, yT[:, tok0:tok0 + n].rearrange("(j p) t -> p j t", p=128))
        for i in range(nt):
            xn = xnb[i % 2]
            P.dma(xc, xc[:tsz, i, :], xin, xin[tok0 + i * tsz: tok0 + (i + 1) * tsz, :])
            ln_tile(P, xc, xc[:tsz, i, :], tsz, st6, mv, rstd, epst, xn, xn[:tsz, :])
            for k in range(8):
                P.tr(tp, tp[:, k * 128:k * 128 + tsz], xn, xn[:tsz, k * 128:(k + 1) * 128], ident, ident[:tsz, :tsz])
            P.tt("dve", htmp, htmp[:, :, :tsz], tp, tp[:].rearrange("p (a b) -> p a b", a=8)[:, :, :tsz],
                 scp, scp[:, :, m:m + 1].to_broadcast([128, 8, tsz]), ALU.mult)
            P.tt("pool", hT, hT[:, :, i * tsz:(i + 1) * tsz], htmp, htmp[:, :, :tsz],
                 ms, ms[:, :, 2 * m:2 * m + 1].to_broadcast([128, 8, tsz]), ALU.add)
        for oc in range(8):
            for br in range(3):
                G = P.bank()
                for k in range(8):
                    P.mm(G, G[:, :n], Wg, Wg[:, k, br * 1024 + oc * 128: br * 1024 + (oc + 1) * 128], hT, hT[:, k, :n],
                         start=(k == 0), stop=(k == 7))
                sg = sig[br % 2]
                P.act(sg, sg[:, :n], G, G[:, :n], AF.Sigmoid, extra=[bg], bias=bg[:, br * 8 + oc: br * 8 + oc + 1])
                Pb = P.bank()
                for k in range(4):
                    P.mm(Pb, Pb[:, :n], Wb, Wb[:, br * 4 + k, oc * 128:(oc + 1) * 128], yt, yt[:, br * 4 + k, :n],
                         start=(k == 0), stop=(k == 3))
                if br == 0:
                    P.tt("dve", mf, mf[:, :n], Pb, Pb[:, :n], sg, sg[:, :n], ALU.mult)
                else:
                    P.tt("dve", mt2, mt2[:, :n], Pb, Pb[:, :n], sg, sg[:, :n], ALU.mult)
                    P.tt("pool", mf, mf[:, :n], mf, mf[:, :n], mt2, mt2[:, :n], ALU.add)
            P.cp("pool", mT, mT[:, oc, :n], mf, mf[:, :n])
        for i in range(nt):
            for half in range(2):
                O = P.bank()
                for k in range(8):
                    P.mm(O, O[:tsz, :], mT, mT[:, k, i * tsz:(i + 1) * tsz], Wo, Wo[:, k, half * 512:(half + 1) * 512],
                         start=(k == 0), stop=(k == 7))
                hs = slice(half * 512, (half + 1) * 512)
                P.tt("dve", ut, ut[:tsz, hs], O, O[:tsz, :], rw, rw[:tsz, m, hs], ALU.mult)
            P.stt("dve", u, u[:tsz, :], xc, xc[:tsz, i, :], ALPHA, ut, ut[:tsz, :], ALU.mult, ALU.add)
            x1t = x1[0]
            ln_tile(P, u, u[:tsz, :], tsz, st6, mv, rstd, epst, x1t, x1t[:tsz, :])
            P.tt("dve", x1t, x1t[:tsz, :], x1t, x1t[:tsz, :], rw, rw[:tsz, 2, :], ALU.mult)
            P.tt("pool", x1t, x1t[:tsz, :], x1t, x1t[:tsz, :], rw, rw[:tsz, 3, :], ALU.add)
            r0 = tok0 + i * tsz
            P.dma(x1o, x1o[r0:r0 + tsz, :], x1t, x1t[:tsz, :])
            ln_tile(P, x1t, x1t[:tsz, :], tsz, st6, mv, rstd, epst, h2, h2[:tsz, :])
            P.tt("dve", h2, h2[:tsz, :], h2, h2[:tsz, :], rw, rw[:tsz, 4 + 2 * m, :], ALU.mult)
            P.tt("pool", h2, h2[:tsz, :], h2, h2[:tsz, :], rw, rw[:tsz, 5 + 2 * m, :], ALU.add)
            P.cp("act", h2hi, h2hi[:tsz, :], h2, h2[:tsz, :])
            P.tt("dve", h2lo, h2lo[:tsz, :], h2, h2[:tsz, :], h2hi, h2hi[:tsz, :], ALU.subtract)
            hs_ = h2Ts[0]
            for k in range(8):
                P.tr(tp, tp[:, k * 128:k * 128 + tsz], h2hi, h2hi[:tsz, k * 128:(k + 1) * 128], ident, ident[:tsz, :tsz])
            P.cp("act", hs_, hs_[:, :, :tsz], tp, tp[:].rearrange("p (a b) -> p a b", a=8)[:, :, :tsz])
            P.dma(h2To, h2To[:, r0:r0 + tsz].rearrange("(k p) t -> p k t", p=128), hs_, hs_[:, :, :tsz])
            for k in range(8):
                P.tr(tp, tp[:, k * 128:k * 128 + tsz], h2lo, h2lo[:tsz, k * 128:(k + 1) * 128], ident, ident[:tsz, :tsz])
            P.cp("dve", h2Tl, h2Tl[:, :, :tsz], tp, tp[:].rearrange("p (a b) -> p a b", a=8)[:, :, :tsz])
            L = P.bank()
            for k in range(8):
                P.mm(L, L[:tsz, 0:16], hs_, hs_[:, k, :tsz], wr_hi, wr_hi[:, k, :], start=(k == 0), stop=False)
                P.mm(L, L[:tsz, 0:16], hs_, hs_[:, k, :tsz], wr_lo, wr_lo[:, k, :], start=False, stop=False)
                P.mm(L, L[:tsz, 0:16], h2Tl, h2Tl[:, k, :tsz], wr_hi, wr_hi[:, k, :], start=False, stop=(k == 7))
            P.op("dve", lambda e, L=L, tsz=tsz: e.reduce_max(mx[:tsz, :], L[:tsz, 0:16], AX.X), reads=[L], writes=[mx])
            P.ts("dve", lg, lg[:tsz, :], L, L[:tsz, 0:16], mx[:tsz, 0:1], None, ALU.subtract, extra=[mx])
            P.act(lg, lg[:tsz, :], lg, lg[:tsz, :], AF.Exp)
            P.op("dve", lambda e, tsz=tsz: e.reduce_sum(sm[:tsz, :], lg[:tsz, :], AX.X), reads=[lg], writes=[sm])
            P.op("dve", lambda e, tsz=tsz: e.reciprocal(sm[:tsz, :], sm[:tsz, :]), reads=[sm], writes=[sm])
            af_ = affs[i % 2]
            P.ts("dve", af_, af_[:tsz, :], lg, lg[:tsz, :], sm[:tsz, 0:1], None, ALU.mult, extra=[sm])
            P.dma(affo, affo[r0:r0 + tsz, :], af_, af_[:tsz, :])
    P.finish()
    return P.build()


def prep_B(inp, l, mod, x_cur, ctx_cur, yTs):
    C = _consts()
    wbr = np.ascontiguousarray(np.concatenate([inp["w_br_a"][l], inp["w_br_b"][l], inp["w_br_c"][l]], 0))
    maps = []
    for core in range(8):
        b, s = core // 4, core % 4
        xin = np.concatenate([ctx_cur[b][64 * s:64 * (s + 1)], x_cur[b][4096 * s:4096 * (s + 1)]], 0)
        yT = np.concatenate([yTs[b][:, 64 * s:64 * (s + 1)], yTs[b][:, 256 + 4096 * s:256 + 4096 * (s + 1)]], 1)
        msT = np.stack([_pm(mod[0:1024, b]), _pm(mod[1024:2048, b]), _pm(mod[0:1024, 2]), _pm(mod[1024:2048, 2])], -1)
        rows = np.stack([mod[2048:3072, b], mod[2048:3072, 2], inp["ln1_g"][l], inp["ln1_b"][l],
                         mod[4096:5120, b], mod[3072:4096, b], mod[4096:5120, 2], mod[3072:4096, 2]], 0)
        maps.append({
            "xin": np.ascontiguousarray(xin), "yT": np.ascontiguousarray(yT), "msT": np.ascontiguousarray(msT),
            "wgate": inp["w_gate"][l], "bgT": _pm(inp["b_gate"][l]), "wbr": wbr, "wout": inp["w_out"][l],
            "wrt": np.ascontiguousarray(inp["w_router"][l].reshape(8, 128, 16).transpose(1, 0, 2)),
            "rows": np.ascontiguousarray(np.broadcast_to(rows[None], (128, 8, 1024))), "ident": C["ident"],
        })
    return maps


def chunkC(c):
    return (0, 256) if c == 0 else (256 + (c - 1) * 1024, 1024)


def build_C():
    P = Prog("C")
    EI, EO = "ExternalInput", "ExternalOutput"
    h2T = P.dram("h2T", [1024, NT], BF16, EI)
    affd = P.dram("affL", [128, 4, 130], F32, EI)
    wg = P.dram("wg", [4, 1024, 2816], F32, EI)
    wu = P.dram("wu", [4, 1024, 2816], F32, EI)
    wd = P.dram("wd", [4, 2816, 1024], F32, EI)
    fpo = P.dram("fp", [NT, 1024], F32, EO)
    P.start()
    P.banks(7)
    onesf = P.sbuf("onesf", [128, 128], F32)
    P.memset("dve", onesf, onesf[:], 1.0)
    aff = P.sbuf("aff", [128, 4, 130], F32)
    P.dma(aff, aff[:], affd, affd[:])
    lo = P.sbuf("lo", [128, 8], F32)
    hi = P.sbuf("hi", [128, 8], F32)
    mid = P.sbuf("mid", [128, 8], F32)
    kv = P.sbuf("kv", [128, 8], F32)
    cnt = P.sbuf("cnt", [128, 8], F32)
    ge = P.sbuf("ge", [128, 8], F32)
    d1 = P.sbuf("d1", [128, 8], F32)
    cmp_ = P.sbuf("cmp", [128, 128], F32)
    P.memset("dve", lo, lo[:], 0.0)
    P.memset("dve", hi, hi[:], 1.0)
    P.memset("dve", kv, kv[:, 0:4], 2048.0)
    P.memset("dve", kv, kv[:, 4:8], 32.0)
    for it in range(30):
        P.tt("dve", mid, mid[:], lo, lo[:], hi, hi[:], ALU.add)
        P.ts("dve", mid, mid[:], mid, mid[:], 0.5, None, ALU.mult)
        for e in range(4):
            P.ts("dve", cmp_, cmp_[:, 0:128], aff, aff[:, e, 2:130], mid[:, e:e + 1], None, ALU.is_ge, extra=[mid])
            P.op("dve", lambda e_, e=e: e_.reduce_sum(cnt[:, e:e + 1], cmp_[:, 0:128], AX.X), reads=[cmp_], writes=[cnt])
            P.ts("dve", cmp_, cmp_[:, 0:2], aff, aff[:, e, 0:2], mid[:, 4 + e:5 + e], None, ALU.is_ge, extra=[mid])
            P.op("dve", lambda e_, e=e: e_.reduce_sum(cnt[:, 4 + e:5 + e], cmp_[:, 0:2], AX.X), reads=[cmp_], writes=[cnt])
        tb = P.bank()
        P.mm(tb, tb[:, 0:8], onesf, onesf[:], cnt, cnt[:])
        P.tt("dve", ge, ge[:], tb, tb[:, 0:8], kv, kv[:], ALU.is_ge)
        P.tt("dve", d1, d1[:], mid, mid[:], lo, lo[:], ALU.subtract)
        P.tt("dve", d1, d1[:], d1, d1[:], ge, ge[:], ALU.mult)
        P.tt("dve", lo, lo[:], lo, lo[:], d1, d1[:], ALU.add)
        P.tt("dve", d1, d1[:], hi, hi[:], mid, mid[:], ALU.subtract)
        P.tt("dve", d1, d1[:], d1, d1[:], ge, ge[:], ALU.mult)
        P.tt("dve", hi, hi[:], mid, mid[:], d1, d1[:], ALU.add)
    gm = P.sbuf("gm", [128, 4, 130], F32)
    for e in range(4):
        P.ts("dve", gm, gm[:, e, 2:130], aff, aff[:, e, 2:130], lo[:, e:e + 1], None, ALU.is_ge, extra=[lo])
        P.ts("dve", gm, gm[:, e, 0:2], aff, aff[:, e, 0:2], lo[:, 4 + e:5 + e], None, ALU.is_ge, extra=[lo])
    P.tt("dve", gm, gm[:], gm, gm[:], aff, aff[:], ALU.mult)

    hxs = [P.sbuf("hx%d" % i, [128, 8, 1024], BF16) for i in range(2)]
    facc = P.sbuf("facc", [128, 8, 1024], F32)
    NWB = 4
    wgs = [P.sbuf("wgu%d" % i, [128, 8, 256], BF16) for i in range(NWB)]
    wus = [P.sbuf("wuu%d" % i, [128, 8, 256], BF16) for i in range(NWB)]
    wds = [P.sbuf("wdu%d" % i, [128, 2, 1024], BF16) for i in range(NWB)]
    hids = [P.sbuf("hid%d" % i, [128, 2, 1024], BF16) for i in range(2)]
    sgs = [P.sbuf("sg%d" % i, [128, 512], F32) for i in range(2)]
    ui = 0
    si = 0
    for c in range(17):
        tok0, n = chunkC(c)
        hx = hxs[c % 2]
        P.dma(hx, hx[:, :, :n], h2T, h2T[:, tok0:tok0 + n].rearrange("(k p) t -> p k t", p=128))
        P.memset("pool", facc, facc[:], 0.0)
        units = [(e, un) for e in range(4) for un in range(11)]
        pend = []
        for idx in range(len(units) + 1):
            if idx < len(units):
                e, un = units[idx]
                f0 = un * 256
                wgu, wuu, wdu, hid = wgs[ui % NWB], wus[ui % NWB], wds[ui % NWB], hids[ui % 2]
                ui += 1
                P.dma(wgu, wgu[:], wg, wg[e][:, f0:f0 + 256].rearrange("(k p) f -> p k f", p=128), q="pool")
                P.dma(wuu, wuu[:], wu, wu[e][:, f0:f0 + 256].rearrange("(k p) f -> p k f", p=128), q="pool")
                P.dma(wdu, wdu[:], wd, wd[e][f0:f0 + 256, :].rearrange("(c p) o -> p c o", p=128), q="pool")
                for fc in range(2):
                    for th in range(n // 512 if n >= 512 else 1):
                        w_ = min(512, n)
                        ts_ = slice(th * 512, th * 512 + w_)
                        G = P.bank()
                        for k in range(8):
                            P.mm(G, G[:, :w_], wgu, wgu[:, k, fc * 128:(fc + 1) * 128], hx, hx[:, k, ts_], start=(k == 0), stop=(k == 7))
                        U = P.bank()
                        for k in range(8):
                            P.mm(U, U[:, :w_], wuu, wuu[:, k, fc * 128:(fc + 1) * 128], hx, hx[:, k, ts_], start=(k == 0), stop=(k == 7))
                        sg = sgs[si % 2]
                        si += 1
                        P.act(sg, sg[:, :w_], G, G[:, :w_], AF.Silu)
                        P.tt("dve", hid, hid[:, fc, ts_], U, U[:, :w_], sg, sg[:, :w_], ALU.mult)
                pend.append((e, wdu, hid))
            j = idx - 1
            if j >= 0:
                e, wdu, hid = pend[j]
                for tl in range(n // 128):
                    gt = tok0 // 128 + tl
                    for oh in range(2):
                        O = P.bank()
                        for fc in range(2):
                            P.mm(O, O[:, :], hid, hid[:, fc, tl * 128:(tl + 1) * 128], wdu, wdu[:, fc, oh * 512:(oh + 1) * 512],
                                 start=(fc == 0), stop=(fc == 1))
                        fs = facc[:, tl, oh * 512:(oh + 1) * 512]
                        P.stt("dve", facc, fs, O, O[:, :], gm[:, e, gt:gt + 1], facc, fs, ALU.mult, ALU.add, extra=[gm])
        P.dma(fpo, fpo[tok0:tok0 + n, :].rearrange("(t p) o -> p t o", p=128), facc, facc[:, :n // 128, :])
    P.finish()
    return P.build()


def build_D():
    P = Prog("D")
    EI, EO = "ExternalInput", "ExternalOutput"
    x1d = P.dram("x1", [NB, 1024], F32, EI)
    fps = P.dram("fps", [4, NB, 1024], F32, EI)
    rows = P.dram("rows", [128, 4, 1024], F32, EI)
    x2o = P.dram("x2", [NB, 1024], F32, EO)
    P.start()
    rw = P.sbuf("rw", [128, 4, 1024], F32)
    P.dma(rw, rw[:], rows, rows[:])
    epst = P.sbuf("epst", [128, 1], F32)
    P.memset("dve", epst, epst[:], LN_EPS)
    st6 = P.sbuf("st6", [128, 2, 6], F32)
    mv = P.sbuf("mv", [128, 2], F32)
    rstd = P.sbuf("rstd", [128, 1], F32)
    xts = [P.sbuf("xt%d" % i, [128, 1024], F32) for i in range(2)]
    pts = [P.sbuf("pp%d" % i, [128, 4, 1024], F32) for i in range(2)]
    u = P.sbuf("u", [128, 1024], F32)
    ots = [P.sbuf("ot%d" % i, [128, 1024], F32) for i in range(2)]
    for t in range(33):
        r0, tsz = (0, 64) if t == 0 else (64 + (t - 1) * 128, 128)
        m = 1 if t == 0 else 0
        xt, pp, ot = xts[t % 2], pts[t % 2], ots[t % 2]
        P.dma(xt, xt[:tsz, :], x1d, x1d[r0:r0 + tsz, :])
        for j in range(4):
            P.dma(pp, pp[:tsz, j, :], fps, fps[j, r0:r0 + tsz, :])
        P.tt("dve", pp, pp[:tsz, 0, :], pp, pp[:tsz, 0, :], pp, pp[:tsz, 1, :], ALU.add)
        P.tt("pool", pp, pp[:tsz, 2, :], pp, pp[:tsz, 2, :], pp, pp[:tsz, 3, :], ALU.add)
        P.tt("dve", pp, pp[:tsz, 0, :], pp, pp[:tsz, 0, :], pp, pp[:tsz, 2, :], ALU.add)
        P.tt("dve", pp, pp[:tsz, 0, :], pp, pp[:tsz, 0, :], rw, rw[:tsz, m, :], ALU.mult)
        P.stt("dve", u, u[:tsz, :], xt, xt[:tsz, :], ALPHA, pp, pp[:tsz, 0, :], ALU.mult, ALU.add)
        ln_tile(P, u, u[:tsz, :], tsz, st6, mv, rstd, epst, ot, ot[:tsz, :])
        P.tt("dve", ot, ot[:tsz, :], ot, ot[:tsz, :], rw, rw[:tsz, 2, :], ALU.mult)
        P.tt("pool", ot, ot[:tsz, :], ot, ot[:tsz, :], rw, rw[:tsz, 3, :], ALU.add)
        P.dma(x2o, x2o[r0:r0 + tsz, :], ot, ot[:tsz, :])
    P.finish()
    return P.build()


def _lam_init(l):
    import math
    return 0.8 - 0.6 * math.exp(-0.3 * l)


def kernel(**inp):
    inp = {k: np.asarray(v) for k, v in inp.items()}
    mods = run_L0(inp)
    x_cur = [np.asarray(inp["x"][b], np.float32) for b in range(2)]
    ctx_cur = [np.asarray(inp["ctx"][b], np.float32) for b in range(2)]
    for l in range(2):
        mod = mods[l]
        resA = run(build_A(_lam_init(l)), prep_A(inp, l, mod, x_cur, ctx_cur)).results
        yTs = []
        for b in range(2):
            parts = [np.asarray(resA[b * 4 + g][nm]) for nm in ("yaT", "ybT", "ycT") for g in range(4)]
            yTs.append(np.concatenate(parts, 0))
        del resA
        resB = run(build_B(), prep_B(inp, l, mod, x_cur, ctx_cur, yTs)).results
        x1p = [np.asarray(resB[c]["x1"]) for c in range(8)]
        h2Tf, afff = [], []
        for b in range(2):
            hp = [np.asarray(resB[b * 4 + s]["h2T"]) for s in range(4)]
            ap = [np.asarray(resB[b * 4 + s]["aff"]) for s in range(4)]
            h2Tf.append(np.concatenate([p[:, :64] for p in hp] + [p[:, 64:] for p in hp], 1))
            afff.append(np.concatenate([p[:64] for p in ap] + [p[64:] for p in ap], 0))
        del resB
        mapsC = []
        for core in range(8):
            b, j = core // 4, core % 4
            a4 = afff[b][:, 4 * j:4 * j + 4].reshape(130, 128, 4).transpose(1, 2, 0)
            mapsC.append({
                "h2T": np.ascontiguousarray(h2Tf[b]), "affL": np.ascontiguousarray(a4),
                "wg": np.ascontiguousarray(inp["w_exp_gate"][l][4 * j:4 * j + 4]),
                "wu": np.ascontiguousarray(inp["w_exp_up"][l][4 * j:4 * j + 4]),
                "wd": np.ascontiguousarray(inp["w_exp_down"][l][4 * j:4 * j + 4]),
            })
        resC = run(build_C(), mapsC).results
        fpc = [np.asarray(resC[c]["fp"]) for c in range(8)]
        del resC, mapsC
        rowsD = lambda b: np.ascontiguousarray(np.broadcast_to(
            np.stack([mod[5120:6144, b], mod[5120:6144, 2], inp["ln2_g"][l], inp["ln2_b"][l]], 0)[None], (128, 4, 1024)))
        mapsD = []
        for core in range(8):
            b, s = core // 4, core % 4
            fps = np.stack([np.concatenate([fpc[b * 4 + jj][64 * s:64 * (s + 1)],
                                            fpc[b * 4 + jj][256 + 4096 * s:256 + 4096 * (s + 1)]], 0) for jj in range(4)], 0)
            mapsD.append({"x1": x1p[core], "fps": np.ascontiguousarray(fps), "rows": rowsD(b)})
        resD = run(build_D(), mapsD).results
        for b in range(2):
            xp = [np.asarray(resD[b * 4 + s]["x2"]) for s in range(4)]
            ctx_cur[b] = np.concatenate([p[:64] for p in xp], 0)
            x_cur[b] = np.concatenate([p[64:] for p in xp], 0)
        del resD, mapsD, fpc
    return np.stack(x_cur, 0).astype(np.float32)
```

```python
import numpy as np
import concourse.bass as bass
import concourse.mybir as mybir
from concourse.bass_utils import run_bass_kernel_spmd

F32 = mybir.dt.float32
BF16 = mybir.dt.bfloat16
I32 = mybir.dt.int32
U32 = mybir.dt.uint32
ALU = mybir.AluOpType
AF = mybir.ActivationFunctionType
AX = mybir.AxisListType

COMPUTE = ("pe", "act", "dve", "pool")
SAME_ENGINE_SYNC = True
import os
SIGALL = bool(int(os.environ.get('K_SIGALL', '0')))
SIGENG = set(os.environ.get('K_SIGENG', 'act,dve,pool').split(','))


class T:
    def __init__(self, prog, handle, name):
        self.p = prog
        self.h = handle
        self.name = name
        self.lastw = None
        self.reads = []
        self.dsem = None
        self.dcnt = 0

    def __getitem__(self, idx):
        return self.h[idx]


class Prog:
    def __init__(self, name):
        self.name = name
        self.nc = bass.Bass("TRN2", target_bir_lowering=False)
        self.ops = {e: [] for e in ("pe", "act", "dve", "pool", "sp")}
        self.cnt = {e: 0 for e in COMPUTE}
        self.opidx = {e: {} for e in COMPUTE}
        self.seen = {e: {} for e in self.ops}
        self.ctxs = []
        self.tiles = []
        self.sems = {}
        self.outs = []
        self.nsem = 0

    def _enter(self, cm):
        v = cm.__enter__()
        self.ctxs.append(cm)
        return v

    def sem(self, name):
        self.nsem += 1
        return self._enter(self.nc.semaphore(name))

    def sbuf(self, name, shape, dt):
        t = T(self, self._enter(self.nc.sbuf_tensor(name, list(shape), dt)), name)
        self.tiles.append(t)
        return t

    def psum(self, name, shape, dt=F32):
        t = T(self, self._enter(self.nc.psum_tensor(name, list(shape), dt)), name)
        self.tiles.append(t)
        return t

    def dram(self, name, shape, dt, kind="Internal", addr_space="Local"):
        h = self.nc.dram_tensor(name, list(shape), dt, kind=kind, addr_space=addr_space)
        t = T(self, h.ap(), name)
        self.tiles.append(t)
        if kind == "ExternalOutput":
            self.outs.append(t)
        return t

    def start(self):
        for e in COMPUTE:
            self.sems[e] = self.sem("s_" + e)

    def _need(self, eng, hz, waits):
        if hz is None:
            return
        kind, key, val = hz
        if kind == "eng" and key == eng and (eng == "pe" or not SAME_ENGINE_SYNC):
            return
        k = (kind, key if kind == "eng" else id(key))
        if self.seen[eng].get(k, 0) >= val:
            return
        self.seen[eng][k] = val
        if kind == "eng":
            rec = self.ops[key][self.opidx[key][val]]
            rec[3] = True
            waits.append(("eng", key, rec))
        else:
            waits.append(("dma", key.dsem, val))

    def op(self, eng, fn, reads=(), writes=(), dma_dst=None, ordered=False):
        waits = []
        for t in reads:
            self._need(eng, t.lastw, waits)
        for t in writes:
            if dma_dst is not None and t is dma_dst and not ordered and t.lastw is not None \
                    and t.lastw[0] == "dma" and not t.reads:
                pass
            else:
                self._need(eng, t.lastw, waits)
            for r in t.reads:
                self._need(eng, r, waits)
        if dma_dst is not None:
            if dma_dst.dsem is None:
                dma_dst.dsem = self.sem("d_" + dma_dst.name)
            dma_dst.dcnt += 16
            hz = ("dma", dma_dst, dma_dst.dcnt)
            inc = (dma_dst.dsem, 16)
        else:
            self.cnt[eng] += 1
            hz = ("eng", eng, self.cnt[eng])
            inc = None
            self.opidx[eng][self.cnt[eng]] = len(self.ops[eng])
        for t in reads:
            t.reads.append(hz)
        for t in writes:
            t.lastw = hz
            t.reads = []
        self.ops[eng].append([waits, fn, inc, False, 0])

    def dma(self, out_t, out_ap, in_t, in_ap, q="sp", ordered=False, **kw):
        self.op(q, lambda e: e.dma_start(out=out_ap, in_=in_ap, **kw),
                reads=[in_t], writes=[out_t], dma_dst=out_t, ordered=ordered)

    def mm(self, out_t, out_ap, a_t, lhsT, b_t, rhs, start=True, stop=True, **kw):
        self.op("pe", lambda e: e.matmul(out_ap, lhsT, rhs, start=start, stop=stop, **kw),
                reads=[a_t, b_t], writes=[out_t])

    def tr(self, out_t, out_ap, in_t, in_ap, ident_t, ident_ap):
        self.op("pe", lambda e: e.transpose(out_ap, in_ap, ident_ap), reads=[in_t, ident_t], writes=[out_t])


    def act(self, out_t, out_ap, in_t, in_ap, func, extra=(), **kw):
        self.op("act", lambda e: e.activation(out_ap, in_ap, func, **kw), reads=[in_t, *extra], writes=[out_t])

    def ts(self, eng, out_t, out_ap, in_t, in_ap, s1, s2, op0, op1=None, extra=()):
        if op1 is None:
            self.op(eng, lambda e: e.tensor_scalar(out_ap, in_ap, s1, None, op0), reads=[in_t, *extra], writes=[out_t])
        else:
            self.op(eng, lambda e: e.tensor_scalar(out_ap, in_ap, s1, s2, op0, op1), reads=[in_t, *extra], writes=[out_t])

    def tt(self, eng, out_t, out_ap, a_t, a_ap, b_t, b_ap, op):
        self.op(eng, lambda e: e.tensor_tensor(out_ap, a_ap, b_ap, op), reads=[a_t, b_t], writes=[out_t])

    def stt(self, eng, out_t, out_ap, a_t, a_ap, scalar, b_t, b_ap, op0, op1, extra=()):
        self.op(eng, lambda e: e.scalar_tensor_tensor(out_ap, a_ap, scalar, b_ap, op0, op1),
                reads=[a_t, b_t, *extra], writes=[out_t])

    def cp(self, eng, out_t, out_ap, in_t, in_ap):
        if eng == "act":
            self.op("act", lambda e: e.copy(out_ap, in_ap), reads=[in_t], writes=[out_t])
        else:
            self.op(eng, lambda e: e.tensor_copy(out_ap, in_ap), reads=[in_t], writes=[out_t])

    def memset(self, eng, t, ap, val):
        self.op(eng, lambda e: e.memset(ap, val), reads=[], writes=[t])

    def banks(self, n=7):
        self.pb = [self.psum("pb%d" % i, [128, 512], F32) for i in range(n)]
        self.pbi = 0

    def bank(self):
        b = self.pb[self.pbi % len(self.pb)]
        self.pbi += 1
        return b

    def rsqrt(self, out_t, out_ap, in_t, in_ap, scale, eps_t, eps_ap):
        self.op("act", lambda e: e.activation(out_ap, in_ap, AF.Sqrt, bias=eps_ap, scale=scale),
                reads=[in_t, eps_t], writes=[out_t])
        self.op("dve", lambda e: e.reciprocal(out_ap, out_ap), reads=[out_t], writes=[out_t])

    def finish(self):
        waits = []
        for t in self.tiles:
            if t.lastw is not None and t.lastw[0] == "dma":
                self._need("sp", t.lastw, waits)
        self.ops["sp"].append([waits, None, None, False, 0])

    def build(self):
        nc = self.nc
        engs = {"pe": "tensor", "act": "scalar", "dve": "vector", "pool": "gpsimd", "sp": "sync"}
        for e in COMPUTE:
            c = 0
            for rec in self.ops[e]:
                if (SIGALL or e in SIGENG) and rec[2] is None and rec[1] is not None:
                    rec[3] = True
                if rec[2] is None and rec[1] is not None and rec[3]:
                    c += 1
                    rec[4] = c
        with nc.Block() as block:
            for e, attr in engs.items():
                ops = self.ops[e]

                def body(engobj, ops=ops, e=e):
                    for waits, fn, inc, sig, _c in ops:
                        for w in waits:
                            if w[0] == "eng":
                                engobj.wait_ge(self.sems[w[1]], w[2][4])
                            else:
                                engobj.wait_ge(w[1], w[2])
                        if fn is not None:
                            ins = fn(engobj)
                            if inc is not None:
                                ins.then_inc(inc[0], inc[1])
                            elif sig:
                                ins.then_inc(self.sems[e], 1)
                getattr(block, attr)(body)
        for cm in reversed(self.ctxs):
            cm.__exit__(None, None, None)
        self.ctxs = []
        return nc

    def n_ops(self):
        return {e: len(v) for e, v in self.ops.items()}


def run(prog_nc, in_maps, trace=False):
    import sys, time
    t0 = time.time()
    res = run_bass_kernel_spmd(prog_nc, in_maps, core_ids=list(range(len(in_maps))), trace=trace)
    print("[kernel] launch done in %.1fs" % (time.time() - t0), file=sys.stderr, flush=True)
    return res


def build_L0():
    P = Prog("L0")
    nc = P.nc
    w = P.dram("w", [1024, 1536], F32, kind="ExternalInput")
    bT = P.dram("bT", [128, 12], F32, kind="ExternalInput")
    cT = P.dram("cT", [128, 8, 3], F32, kind="ExternalInput")
    out = P.dram("modT", [128, 12, 3], F32, kind="ExternalOutput")
    P.start()
    wt = P.sbuf("wt", [128, 8, 1536], F32)
    bt = P.sbuf("bt", [128, 12], F32)
    ct = P.sbuf("ct", [128, 8, 3], F32)
    st = P.sbuf("st", [128, 8, 3], F32)
    ot = P.sbuf("ot", [128, 12, 3], F32)
    ps = P.psum("ps", [128, 512], F32)
    P.dma(ct, ct[:], cT, cT[:])
    P.dma(bt, bt[:], bT, bT[:])
    for k in range(8):
        P.dma(wt, wt[:, k, :], w, w[k * 128:(k + 1) * 128, :])
    P.op("act", lambda e: e.activation(st[:], ct[:], AF.Silu), reads=[ct], writes=[st])
    for j in range(12):
        for k in range(8):
            P.mm(ps, ps[:, j * 3:(j + 1) * 3], wt, wt[:, k, j * 128:(j + 1) * 128], st, st[:, k, :],
                 start=(k == 0), stop=(k == 7))
    for j in range(12):
        P.op("dve", lambda e, j=j: e.tensor_scalar(ot[:, j, :], ps[:, j * 3:(j + 1) * 3], bt[:, j:j + 1], None, ALU.add),
             reads=[ps, bt], writes=[ot])
    P.dma(out, out[:], ot, ot[:])
    P.finish()
    return P.build()


def run_L0(inp):
    cvec = np.stack([inp["c"][0], inp["c"][1], inp["c_ctx"]], 0)
    cT = np.ascontiguousarray(cvec.T.reshape(8, 128, 3).transpose(1, 0, 2))
    in_maps = []
    for core in range(8):
        l, q = core // 4, core % 4
        cols = slice(q * 1536, (q + 1) * 1536)
        in_maps.append({
            "w": np.ascontiguousarray(inp["w_ada"][l][:, cols]),
            "bT": np.ascontiguousarray(inp["b_ada"][l][cols].reshape(12, 128).T),
            "cT": cT,
        })
    res = run(build_L0(), in_maps)
    mods = []
    for l in range(2):
        parts = []
        for q in range(4):
            m = np.asarray(res.results[l * 4 + q]["modT"])
            parts.append(m.transpose(1, 0, 2).reshape(1536, 3))
        mods.append(np.concatenate(parts, 0))
    return mods


NT = 16640
LN_EPS = 1e-6
NCH = 33


def chunk_rng(c):
    return (0, 256) if c == 0 else (256 + (c - 1) * 512, 512)


def build_A(lam_init, dbg=False):
    P = Prog("A")
    EI = "ExternalInput"
    xin = P.dram("xin", [NT, 1024], F32, EI)
    msT = P.dram("msT", [128, 8, 4], F32, EI)
    wfeat = P.dram("wfeat", [1024, 1216], F32, EI)
    wtok = P.dram("wtok", [1024, 256], F32, EI)
    wuq = P.dram("wuq", [256, 2, 192], F32, EI)
    wukvk = P.dram("wukvk", [128, 2, 64], F32, EI)
    wukvv = P.dram("wukvv", [128, 128], F32, EI)
    gq = P.dram("gq", [128, 2], F32, EI)
    gkv = P.dram("gkv", [128, 1], F32, EI)
    cs32 = P.dram("cs32", [32, 2, NT], F32, EI)
    cs128 = P.dram("cs128", [128, 2, NT], F32, EI)
    nabias = P.dram("nabias", [2, 3, 768, 256], F32, EI)
    lamv = P.dram("lamv", [128, 4, 64], F32, EI)
    gsub = P.dram("gsub", [128, 1], F32, EI)
    identd = P.dram("ident", [128, 128], F32, EI)
    EO = "ExternalOutput"
    yaT = P.dram("yaT", [128, NT], BF16, EO)
    ybT = P.dram("ybT", [128, NT], BF16, EO)
    ycT = P.dram("ycT", [128, NT], BF16, EO)
    sk = EO if dbg else "Internal"
    qaT = P.dram("qaT", [128, NT], BF16, sk)
    kaT = P.dram("kaT", [128, NT], BF16, sk)
    va = P.dram("va", [NT, 128], BF16, sk)
    qbT = P.dram("qbT", [2, 96, NT], BF16, sk)
    kbT = P.dram("kbT", [2, 96, NT], BF16, sk)
    vb = P.dram("vb", [NT, 128], BF16, sk)
    dqT = P.dram("dqT", [128, NT], BF16, sk)
    dkT = P.dram("dkT", [128, NT], BF16, sk)
    dvv = P.dram("dvv", [NT, 128], BF16, sk)
    P.start()
    P.banks(7)
    tp = P.psum("tp", [128, 1024], BF16)

    Wf = P.sbuf("Wf", [128, 8, 1216], BF16)
    Wt = P.sbuf("Wt", [128, 8, 256], BF16)
    for k in range(8):
        P.dma(Wf, Wf[:, k, :], wfeat, wfeat[k * 128:(k + 1) * 128, :], q="pool")
        P.dma(Wt, Wt[:, k, :], wtok, wtok[k * 128:(k + 1) * 128, :], q="pool")
    ident = P.sbuf("identb", [128, 128], BF16)
    P.dma(ident, ident[:], identd, identd[:], q="pool")
    ones = P.sbuf("ones", [128, 128], BF16)
    P.memset("dve", ones, ones[:], 1.0)
    onesf = P.sbuf("onesf", [128, 128], F32)
    epst = P.sbuf("epst", [128, 1], F32)
    P.memset("dve", epst, epst[:], LN_EPS)
    P.memset("dve", onesf, onesf[:], 1.0)
    ms = P.sbuf("ms", [128, 8, 4], F32)
    P.dma(ms, ms[:], msT, msT[:])
    scp = P.sbuf("scp", [128, 8, 2], F32)
    for m in range(2):
        P.ts("dve", scp, scp[:, :, m], ms, ms[:, :, 2 * m + 1], 1.0, None, ALU.add)
    gqt = P.sbuf("gqt", [128, 2], F32)
    P.dma(gqt, gqt[:], gq, gq[:])
    gkvt = P.sbuf("gkvt", [128, 1], F32)
    P.dma(gkvt, gkvt[:], gkv, gkv[:])
    wuq_r = P.sbuf("wuq_r", [128, 2, 2, 192], F32)
    for k in range(2):
        P.dma(wuq_r, wuq_r[:, k], wuq, wuq[k * 128:(k + 1) * 128])
    Wuq = P.sbuf("Wuq", [128, 2, 2, 192], BF16)
    for k in range(2):
        P.ts("dve", Wuq, Wuq[:, k], wuq_r, wuq_r[:, k], gqt[:, k:k + 1], None, ALU.mult, extra=[gqt])
    wk_r = P.sbuf("wk_r", [128, 2, 64], F32)
    P.dma(wk_r, wk_r[:], wukvk, wukvk[:])
    Wkk = P.sbuf("Wkk", [128, 2, 64], BF16)
    P.ts("dve", Wkk, Wkk[:], wk_r, wk_r[:], gkvt[:, 0:1], None, ALU.mult, extra=[gkvt])
    wv_r = P.sbuf("wv_r", [128, 128], F32)
    P.dma(wv_r, wv_r[:], wukvv, wukvv[:])
    Wkv = P.sbuf("Wkv", [128, 128], BF16)
    P.ts("dve", Wkv, Wkv[:], wv_r, wv_r[:], gkvt[:, 0:1], None, ALU.mult, extra=[gkvt])
    lv = P.sbuf("lv", [128, 4, 64], F32)
    P.dma(lv, lv[:], lamv, lamv[:])
    lpr = P.sbuf("lpr", [128, 2, 64], F32)
    P.tt("dve", lpr, lpr[:, 0], lv, lv[:, 0], lv, lv[:, 1], ALU.mult)
    P.tt("dve", lpr, lpr[:, 1], lv, lv[:, 2], lv, lv[:, 3], ALU.mult)
    lsum = P.sbuf("lsum", [128, 2], F32)
    P.op("dve", lambda e: e.reduce_sum(lsum[:], lpr[:], AX.X), reads=[lpr], writes=[lsum])
    lexp = P.sbuf("lexp", [128, 2], F32)
    P.act(lexp, lexp[:], lsum, lsum[:], AF.Exp)
    neglam = P.sbuf("neglam", [128, 1], F32)
    P.stt("dve", neglam, neglam[:], lexp, lexp[:, 1:2], -float(lam_init), lexp, lexp[:, 0:1], ALU.add, ALU.subtract)
    gsr = P.sbuf("gsr", [128, 1], F32)
    P.dma(gsr, gsr[:], gsub, gsub[:])
    gss = P.sbuf("gss", [128, 1], F32)
    P.ts("dve", gss, gss[:], gsr, gsr[:], 1.0 - float(lam_init), None, ALU.mult)

    xb = [P.sbuf("xb%d" % i, [128, 1024], F32) for i in range(2)]
    xnb = [P.sbuf("xn%d" % i, [128, 1024], BF16) for i in range(2)]
    st6 = P.sbuf("st6", [128, 2, 6], F32)
    mv = P.sbuf("mv", [128, 2], F32)
    rstd = P.sbuf("rstd", [128, 1], F32)
    htmp = P.sbuf("htmp", [128, 8, 128], F32)
    hTb = [P.sbuf("hT%d" % i, [128, 8, 512], BF16) for i in range(2)]
    t32 = [P.sbuf("t32_%d" % i, [128, 2, 512], F32) for i in range(2)]
    t128 = [P.sbuf("t128_%d" % i, [128, 2, 512], F32) for i in range(2)]
    stg = [P.sbuf("stg%d" % i, [128, 512], BF16) for i in range(3)]
    stgi = [0]

    def nstg():
        s_ = stg[stgi[0] % len(stg)]
        stgi[0] += 1
        return s_
    cqT = P.sbuf("cqT", [128, 2, 512], BF16)
    cqsq = P.sbuf("cqsq", [128, 2, 512], BF16)
    ckvT = P.sbuf("ckvT", [128, 512], BF16)
    ckvsq = P.sbuf("ckvsq", [128, 512], BF16)
    rq = P.sbuf("rq", [128, 512], F32)
    rkv = P.sbuf("rkv", [128, 512], F32)
    rkvt = P.sbuf("rkvt", [128, 4], F32)
    ra = [P.sbuf("ra%d" % i, [128, 512], F32) for i in range(1)]
    rb = [P.sbuf("rb%d" % i, [128, 512], F32) for i in range(1)]
    vst = [P.sbuf("vst%d" % i, [128, 256], BF16) for i in range(2)]
    krT = P.sbuf("krT", [32, 512], BF16)

    def rope(n, pa, pb_, rows, tab, ti, out_t, out_ap, mul_t=None, mul_ap=None):
        a = ra[0]
        b = rb[0]
        P.tt("dve", a, a[rows, :n], pa, pa[rows, :n], tab, tab[rows, 0, :n], ALU.mult)
        P.tt("dve", b, b[rows, :n], pb_, pb_[rows, :n], tab, tab[rows, 1, :n], ALU.mult)
        if mul_t is None:
            P.tt("pool", out_t, out_ap, a, a[rows, :n], b, b[rows, :n], ALU.add)
        else:
            P.tt("pool", a, a[rows, :n], a, a[rows, :n], b, b[rows, :n], ALU.add)
            P.tt("pool", out_t, out_ap, a, a[rows, :n], mul_t, mul_ap, ALU.mult)

    FO = {"qa": 0, "ka": 128, "cq0": 256, "cq1": 384, "ckv": 512, "kr": 640, "krp": 672,
          "dq": 704, "dqp": 832, "dk": 960, "dkp": 1088}

    def proj(hT, name, M, n):
        pbk = P.bank()
        off = FO[name]
        for k in range(8):
            P.mm(pbk, pbk[:M, :n], Wf, Wf[:, k, off:off + M], hT, hT[:, k, :n], start=(k == 0), stop=(k == 7))
        return pbk

    for c in range(NCH):
        tok0, n = chunk_rng(c)
        m = 1 if c == 0 else 0
        nt = n // 128
        hT = hTb[c % 2]
        tb32 = t32[c % 2]
        tb128 = t128[c % 2]
        P.dma(tb32, tb32[0:32, :, :n], cs32, cs32[:, :, tok0:tok0 + n])
        P.dma(tb32, tb32[64:96, :, :n], cs32, cs32[:, :, tok0:tok0 + n])
        P.dma(tb128, tb128[:, :, :n], cs128, cs128[:, :, tok0:tok0 + n])
        for i in range(nt):
            xt = xb[i % 2]
            xn = xnb[i % 2]
            P.dma(xt, xt[:], xin, xin[tok0 + i * 128: tok0 + (i + 1) * 128, :])
            for a_ in range(2):
                P.op("dve", lambda e, xt=xt, a_=a_: e.bn_stats(st6[:, a_, :], xt[:, a_ * 512:(a_ + 1) * 512]),
                     reads=[xt], writes=[st6])
            P.op("dve", lambda e: e.bn_aggr(mv[:], st6[:].rearrange("p a b -> p (a b)")), reads=[st6], writes=[mv])
            P.rsqrt(rstd, rstd[:], mv, mv[:, 1:2], 1.0, epst, epst[:, 0:1])
            P.ts("dve", xn, xn[:], xt, xt[:], mv[:, 0:1], rstd[:, 0:1], ALU.subtract, ALU.mult, extra=[mv, rstd])
            for k in range(8):
                P.tr(tp, tp[:, k * 128:(k + 1) * 128], xn, xn[:, k * 128:(k + 1) * 128], ident, ident[:])
            P.tt("dve", htmp, htmp[:], tp, tp[:].rearrange("p (a b) -> p a b", a=8),
                 scp, scp[:, :, m:m + 1].to_broadcast([128, 8, 128]), ALU.mult)
            P.tt("pool", hT, hT[:, :, i * 128:(i + 1) * 128], htmp, htmp[:],
                 ms, ms[:, :, 2 * m:2 * m + 1].to_broadcast([128, 8, 128]), ALU.add)
        for name, dst in (("qa", qaT), ("ka", kaT)):
            pbk = proj(hT, name, 128, n)
            s_ = nstg()
            P.cp("act", s_, s_[:, :n], pbk, pbk[:, :n])
            P.dma(dst, dst[:, tok0:tok0 + n], s_, s_[:, :n])
        for j, name in enumerate(("cq0", "cq1")):
            pbk = proj(hT, name, 128, n)
            P.cp("act", cqT, cqT[:, j, :n], pbk, pbk[:, :n])
            P.act(cqsq, cqsq[:, j, :n], pbk, pbk[:, :n], AF.Square)
        pbk = proj(hT, "ckv", 128, n)
        P.cp("act", ckvT, ckvT[:, :n], pbk, pbk[:, :n])
        P.act(ckvsq, ckvsq[:, :n], pbk, pbk[:, :n], AF.Square)
        pa = proj(hT, "kr", 32, n)
        pb_ = proj(hT, "krp", 32, n)
        rope(n, pa, pb_, slice(0, 32), tb32, 0, krT, krT[:, :n])
        for h in range(2):
            P.dma(kbT, kbT[h, 64:96, tok0:tok0 + n], krT, krT[:, :n])
        for (nm, nmp, dst, ti) in (("dq", "dqp", dqT, 0), ("dk", "dkp", dkT, 1)):
            pa = proj(hT, nm, 128, n)
            pb_ = proj(hT, nmp, 128, n)
            s_ = nstg()
            rope(n, pa, pb_, slice(0, 128), tb128, ti, s_, s_[:, :n])
            P.dma(dst, dst[:, tok0:tok0 + n], s_, s_[:, :n])
        for i in range(nt):
            pbk = P.bank()
            for k in range(8):
                P.mm(pbk, pbk[:, 0:256], hT, hT[:, k, i * 128:(i + 1) * 128], Wt, Wt[:, k, :], start=(k == 0), stop=(k == 7))
            v_ = vst[i % 2]
            P.cp("act", v_, v_[:], pbk, pbk[:, 0:256])
            r0 = tok0 + i * 128
            P.dma(va, va[r0:r0 + 128, :], v_, v_[:, 0:128])
            P.dma(dvv, dvv[r0:r0 + 128, :], v_, v_[:, 128:256])
        pbk = P.bank()
        P.mm(pbk, pbk[:, :n], ones, ones[:], cqsq, cqsq[:, 0, :n], start=True, stop=False)
        P.mm(pbk, pbk[:, :n], ones, ones[:], cqsq, cqsq[:, 1, :n], start=False, stop=True)
        P.rsqrt(rq, rq[:, :n], pbk, pbk[:, :n], 1.0 / 256, epst, epst[:, 0:1])
        pbk = P.bank()
        P.mm(pbk, pbk[:, :n], ones, ones[:], ckvsq, ckvsq[:, :n])
        P.rsqrt(rkv, rkv[:, :n], pbk, pbk[:, :n], 1.0 / 128, epst, epst[:, 0:1])
        pbk = P.bank()
        for i in range(nt):
            P.mm(pbk, pbk[:, i:i + 1], ckvsq, ckvsq[:, i * 128:(i + 1) * 128], ones, ones[:, 0:1])
        P.rsqrt(rkvt, rkvt[:, :nt], pbk, pbk[:, :nt], 1.0 / 128, epst, epst[:, 0:1])
        for h in range(2):
            qa_ = P.bank()
            for k in range(2):
                P.mm(qa_, qa_[:96, :n], Wuq, Wuq[:, k, h, 0:96], cqT, cqT[:, k, :n], start=(k == 0), stop=(k == 1))
            qb_ = P.bank()
            for k in range(2):
                P.mm(qb_, qb_[:96, :n], Wuq, Wuq[:, k, h, 96:192], cqT, cqT[:, k, :n], start=(k == 0), stop=(k == 1))
            s_ = nstg()
            P.tt("dve", s_, s_[0:64, :n], qa_, qa_[0:64, :n], rq, rq[0:64, :n], ALU.mult)
            rope(n, qa_, qb_, slice(64, 96), tb32, h, s_, s_[64:96, :n], rq, rq[64:96, :n])
            P.dma(qbT, qbT[h, :, tok0:tok0 + n], s_, s_[0:96, :n])
            kn = P.bank()
            P.mm(kn, kn[:64, :n], Wkk, Wkk[:, h, :], ckvT, ckvT[:, :n])
            s_ = nstg()
            P.tt("dve", s_, s_[0:64, :n], kn, kn[0:64, :n], rkv, rkv[0:64, :n], ALU.mult)
            P.dma(kbT, kbT[h, 0:64, tok0:tok0 + n], s_, s_[0:64, :n])
        for i in range(nt):
            pbk = P.bank()
            P.mm(pbk, pbk[:, 0:128], ckvT, ckvT[:, i * 128:(i + 1) * 128], Wkv, Wkv[:])
            v_ = vst[i % 2]
            P.ts("dve", v_, v_[:, 0:128], pbk, pbk[:, 0:128], rkvt[:, i:i + 1], None, ALU.mult, extra=[rkvt])
            r0 = tok0 + i * 128
            P.dma(vb, vb[r0:r0 + 128, :], v_, v_[:, 0:128])
    if dbg == "A1":
        P.finish()
        return P.build()

    BK = P.sbuf("BK", [128, NT], BF16)
    BV = P.sbuf("BV", [128, 130, 130], BF16)
    LOOK = 2
    pts = [P.sbuf("pt%d" % i, [128, 512], BF16) for i in range(5)]
    pti = [0]

    def npt():
        t_ = pts[pti[0] % 5]
        pti[0] += 1
        return t_
    qcs = [P.sbuf("qc%d" % i, [128, 512], BF16) for i in range(2)]
    rec = P.sbuf("rec", [128, 512], F32)
    rec2 = P.sbuf("rec2", [128, 512], F32)
    bsb = P.sbuf("bsb", [128, 512], F32)
    ysb = [P.sbuf("ysb%d" % i, [128, 512], BF16) for i in range(2)]
    yf = P.sbuf("yf", [128, 512], F32)
    yf2 = P.sbuf("yf2", [128, 512], F32)
    tmpf = [P.sbuf("tmpf%d" % i, [128, 256], F32) for i in range(2)]
    nb = P.sbuf("nb", [128, 3, 6, 256], F32)
    acc_banks = [P.pb[0], P.pb[1]]
    bb_bank = P.pb[2]
    s_banks = [P.pb[3], P.pb[4], P.pb[5], P.pb[6]]
    cnt = {"acc": 0, "s": 0, "y": 0}

    def sbank():
        b = s_banks[cnt["s"] % 4]
        cnt["s"] += 1
        return b

    def finalize(acc, n, dst, drow, tok0):
        P.op("dve", lambda e: e.reciprocal(rec[64:65, :n], acc[64:65, :n]), reads=[acc], writes=[rec])
        P.mm(bb_bank, bb_bank[:64, :n], onesf, onesf[64:65, 0:64], rec, rec[64:65, :n])
        P.cp("act", bsb, bsb[:64, :n], bb_bank, bb_bank[:64, :n])
        y_ = ysb[cnt["y"] % 2]
        cnt["y"] += 1
        P.tt("dve", y_, y_[:64, :n], acc, acc[:64, :n], bsb, bsb[:64, :n], ALU.mult)
        P.dma(dst, dst[drow:drow + 64, tok0:tok0 + n], y_, y_[:64, :n])

    for q4 in range(4):
        P.dma(BK, BK[:, q4 * 4160:(q4 + 1) * 4160], kaT, kaT[:, q4 * 4160:(q4 + 1) * 4160])
    BV4 = BV[:].rearrange("p t (h d) -> p t h d", h=2)
    for h in range(2):
        P.dma(BV, BV4[:, :, h, 0:64], va, va[:, h * 64:(h + 1) * 64].rearrange("(t p) d -> p t d", p=128))
    P.memset("pool", BV, BV4[:, :, :, 64:65], 1.0)
    for h in range(2):
        P.dma(nb, nb[:], nabias, nabias[h].rearrange("v (j p) q -> p v j q", p=128))
        for blk in range(-1, 64):
            if blk < 0:
                qtok0, tiles = 0, [(0, 0, None), (128, 1, None)]
            else:
                r0 = blk * 4
                var = 0 if blk == 0 else (2 if blk == 63 else 1)
                se = 0 if blk == 0 else (244 if blk == 63 else r0 - 4)
                qtok0 = 256 + r0 * 64
                tiles = [(0, 0, None), (128, 1, None)]
                for j in range(6):
                    tiles.append((256 + se * 64 + j * 128, 2 + se // 2 + j, (var, j)))
            qc = qcs[cnt["acc"] % 2]
            P.dma(qc, qc[:, :256], qaT, qaT[:, qtok0:qtok0 + 256])
            acc = acc_banks[cnt["acc"] % 2]
            cnt["acc"] += 1
            pend = []
            for idx in range(len(tiles) + LOOK):
                if idx < len(tiles):
                    kcol, vt, bi = tiles[idx]
                    sb = sbank()
                    P.mm(sb, sb[:, :256], BK, BK[h * 64:(h + 1) * 64, kcol:kcol + 128], qc, qc[h * 64:(h + 1) * 64, :256])
                    pt = npt()
                    if bi is None:
                        P.act(pt, pt[:, :256], sb, sb[:, :256], AF.Exp, scale=0.125)
                    else:
                        tf = tmpf[idx % 2]
                        P.stt("dve", tf, tf[:], sb, sb[:, :256], 0.125, nb, nb[:, bi[0], bi[1], :], ALU.mult, ALU.add)
                        P.act(pt, pt[:, :256], tf, tf[:], AF.Exp)
                    pend.append(pt)
                j = idx - LOOK
                if j >= 0:
                    pt = pend[j]
                    P.mm(acc, acc[:65, :256], BV, BV4[:, tiles[j][1], h, :], pt, pt[:, :256], start=(j == 0), stop=(j == len(tiles) - 1))
            finalize(acc, 256, yaT, h * 64, qtok0)

    BV3 = BV[:]
    sc_b = 96.0 ** -0.5
    for h in range(2):
        for q4 in range(4):
            P.dma(BK, BK[0:96, q4 * 4160:(q4 + 1) * 4160], kbT, kbT[h, :, q4 * 4160:(q4 + 1) * 4160])
        P.dma(BV, BV3[:, :, 0:64], vb, vb[:, h * 64:(h + 1) * 64].rearrange("(t p) d -> p t d", p=128))
        P.memset("pool", BV, BV3[:, :, 64:65], 1.0)
        for c in range(NCH):
            tok0, n = chunk_rng(c)
            kts = [0, 1] if c == 0 else list(range(130))
            qc = qcs[cnt["acc"] % 2]
            P.dma(qc, qc[0:96, :n], qbT, qbT[h, :, tok0:tok0 + n])
            acc = acc_banks[cnt["acc"] % 2]
            cnt["acc"] += 1
            pend = []
            for idx in range(len(kts) + LOOK):
                if idx < len(kts):
                    kt = kts[idx]
                    sb = sbank()
                    P.mm(sb, sb[:, :n], BK, BK[0:96, kt * 128:(kt + 1) * 128], qc, qc[0:96, :n])
                    pt = npt()
                    P.act(pt, pt[:, :n], sb, sb[:, :n], AF.Exp, scale=sc_b)
                    pend.append(pt)
                j = idx - LOOK
                if j >= 0:
                    pt = pend[j]
                    P.mm(acc, acc[:65, :n], BV, BV3[:, kts[j], 0:65], pt, pt[:, :n], start=(j == 0), stop=(j == len(kts) - 1))
            finalize(acc, n, ybT, h * 64, tok0)

    for q4 in range(4):
        P.dma(BK, BK[:, q4 * 4160:(q4 + 1) * 4160], dkT, dkT[:, q4 * 4160:(q4 + 1) * 4160])
    P.dma(BV, BV3[:, :, 0:128], dvv, dvv[:, :].rearrange("(t p) d -> p t d", p=128))
    O1, O2 = P.pb[0], P.pb[1]
    s3 = [P.pb[2], P.pb[3], P.pb[4], P.pb[5], P.pb[6]]
    sacc = [P.sbuf("sacc%d" % i, [128, 512], F32) for i in range(2)]
    S1 = S2 = None
    for c in range(NCH):
        tok0, n = chunk_rng(c)
        kts = [0, 1] if c == 0 else list(range(130))
        qc = qcs[c % 2]
        P.dma(qc, qc[:, :n], dqT, dqT[:, tok0:tok0 + n])
        pend = []
        for idx in range(len(kts) + 1):
            if idx < len(kts):
                kt = kts[idx]
                pp = []
                for half in range(2):
                    sb = s3[cnt["s"] % 5]
                    cnt["s"] += 1
                    rows = slice(half * 64, (half + 1) * 64)
                    P.mm(sb, sb[:, :n], BK, BK[rows, kt * 128:(kt + 1) * 128], qc, qc[rows, :n])
                    pt = npt()
                    P.act(pt, pt[:, :n], sb, sb[:, :n], AF.Exp, scale=0.125)
                    pp.append(pt)
                pend.append(pp)
            j = idx - 1
            if j >= 0:
                st_, sp_ = (j == 0), (j == len(kts) - 1)
                for half, O in enumerate((O1, O2)):
                    pt = pend[j][half]
                    P.mm(O, O[:, :n], BV, BV3[:, kts[j], 0:128], pt, pt[:, :n], start=st_, stop=sp_)
                    sa = sacc[half]
                    if j == 0:
                        P.cp("dve", sa, sa[:, :n], pt, pt[:, :n])
                    else:
                        P.tt("dve", sa, sa[:, :n], sa, sa[:, :n], pt, pt[:, :n], ALU.add)
        for half, rr in enumerate((rec, rec2)):
            sb = s3[cnt["s"] % 5]
            cnt["s"] += 1
            P.mm(sb, sb[:, :n], onesf, onesf[:, :], sacc[half], sacc[half][:, :n])
            P.op("dve", lambda e, n=n, rr=rr, sb=sb: e.reciprocal(rr[:, :n], sb[:, :n]), reads=[sb], writes=[rr])
        P.tt("dve", yf, yf[:, :n], O1, O1[:, :n], rec, rec[:, :n], ALU.mult)
        P.tt("dve", yf2, yf2[:, :n], O2, O2[:, :n], rec2, rec2[:, :n], ALU.mult)
        P.stt("dve", yf, yf[:, :n], yf2, yf2[:, :n], neglam[:, 0:1], yf, yf[:, :n], ALU.mult, ALU.add, extra=[neglam])
        pt = npt()
        P.act(pt, pt[:, :n], yf, yf[:, :n], AF.Square)
        sb = s3[cnt["s"] % 5]
        cnt["s"] += 1
        P.mm(sb, sb[:, :n], ones, ones[:], pt, pt[:, :n])
        P.rsqrt(bsb, bsb[:, :n], sb, sb[:, :n], 1.0 / 128, epst, epst[:, 0:1])
        P.tt("dve", yf, yf[:, :n], yf, yf[:, :n], bsb, bsb[:, :n], ALU.mult)
        y_ = ysb[c % 2]
        P.ts("pool", y_, y_[:, :n], yf, yf[:, :n], gss[:, 0:1], None, ALU.mult, extra=[gss])
        P.dma(ycT, ycT[:, tok0:tok0 + n], y_, y_[:, :n])
    P.finish()
    return P.build()


def _pm(vec):
    return np.ascontiguousarray(np.asarray(vec).reshape(-1, 128).T)


def _partner_perm(D):
    m = D // 2
    half = m // 2
    perm = np.zeros(D, np.int64)
    sign = np.zeros(D, np.float32)
    for d in range(D):
        seg, i = d // m, d % m
        if i < half:
            perm[d] = seg * m + i + half
            sign[d] = -1.0
        else:
            perm[d] = seg * m + i - half
            sign[d] = 1.0
    return perm, sign


def _rope_table(D):
    m = D // 2
    half = m // 2
    inv = (np.float32(10000.0) ** (-np.arange(0, m, 2, dtype=np.float32) / np.float32(m))).astype(np.float32)
    t = np.arange(16384)
    row = (t // 64).astype(np.float32)
    col = (t % 64).astype(np.float32)
    _, sign = _partner_perm(D)
    tab = np.zeros((D, 2, NT), np.float32)
    tab[:, 0, :256] = 1.0
    for d in range(D):
        seg, i = d // m, d % m
        fi = i % half
        pos = row if seg == 0 else col
        ang = (pos * inv[fi]).astype(np.float32)
        tab[d, 0, 256:] = np.cos(ang)
        tab[d, 1, 256:] = sign[d] * np.sin(ang)
    return tab


def _na_bias(rpb2):
    out = np.full((2, 3, 768, 256), -30000.0, np.float32)
    for var, (r0, se) in enumerate(((0, 0), (8, 4), (252, 244))):
        kr = se + np.arange(768) // 64
        kc = np.arange(768) % 64
        r = r0 + np.arange(256) // 64
        c = np.arange(256) % 64
        rs = np.clip(r - 4, 0, 248)
        cs = np.clip(c - 8, 0, 48)
        okr = (kr[:, None] >= rs[None, :]) & (kr[:, None] < rs[None, :] + 8)
        okc = (kc[:, None] >= cs[None, :]) & (kc[:, None] < cs[None, :] + 16)
        ok = okr & okc
        ro = np.clip(kr[:, None] - r[None, :] + 7, 0, 14)
        co = np.clip(kc[:, None] - c[None, :] + 15, 0, 30)
        for h in range(2):
            g = rpb2[h][ro, co]
            out[h, var] = np.where(ok, g, np.float32(-30000.0))
    return out


_CONST = {}


def _consts():
    if not _CONST:
        _CONST["cs32"] = _rope_table(32)
        t64 = _rope_table(64)
        _CONST["cs128"] = np.ascontiguousarray(np.concatenate([t64, t64], 0))
        _CONST["ident"] = np.eye(128, dtype=np.float32)
    return _CONST


def prep_A(inp, l, mod, x_cur, ctx_cur):
    C = _consts()
    w_in = inp["w_in"][l]
    p32, _ = _partner_perm(32)
    p64, _ = _partner_perm(64)
    p128 = np.concatenate([p64, 64 + p64])
    maps = []
    for core in range(8):
        b, g = core // 4, core % 4
        xin = np.ascontiguousarray(np.concatenate([ctx_cur[b], x_cur[b]], 0))
        msT = np.stack([_pm(mod[0:1024, b]), _pm(mod[1024:2048, b]), _pm(mod[0:1024, 2]), _pm(mod[1024:2048, 2])], -1)
        kr = w_in[:, 1920:1952]
        dq = w_in[:, 1952 + 128 * g: 1952 + 128 * (g + 1)]
        dk = w_in[:, 2464 + 128 * g: 2464 + 128 * (g + 1)]
        wfeat = np.concatenate([
            w_in[:, 128 * g:128 * (g + 1)], w_in[:, 512 + 128 * g:512 + 128 * (g + 1)],
            w_in[:, 1536:1664], w_in[:, 1664:1792], w_in[:, 1792:1920],
            kr, kr[:, p32], dq, dq[:, p128], dk, dk[:, p128]], 1)
        wtok = np.concatenate([w_in[:, 1024 + 128 * g:1024 + 128 * (g + 1)], w_in[:, 2976 + 128 * g:2976 + 128 * (g + 1)]], 1)
        wuq = np.zeros((256, 2, 192), np.float32)
        wukvk = np.zeros((128, 2, 64), np.float32)
        wukvv = np.zeros((128, 128), np.float32)
        for hh in range(2):
            H = 2 * g + hh
            uq = inp["mla_w_uq"][l][:, H * 96:(H + 1) * 96]
            rope_c = uq[:, 64:96]
            wuq[:, hh, 0:96] = uq
            wuq[:, hh, 160:192] = rope_c[:, p32]
            ukv = inp["mla_w_ukv"][l][:, H * 128:(H + 1) * 128]
            wukvk[:, hh, :] = ukv[:, 0:64]
            wukvv[:, hh * 64:(hh + 1) * 64] = ukv[:, 64:128]
        lamv = np.stack([inp["diff_lq1"][l], inp["diff_lk1"][l], inp["diff_lq2"][l], inp["diff_lk2"][l]], 0)
        maps.append({
            "xin": xin, "msT": np.ascontiguousarray(msT), "wfeat": np.ascontiguousarray(wfeat),
            "wtok": np.ascontiguousarray(wtok), "wuq": wuq, "wukvk": wukvk, "wukvv": wukvv,
            "gq": _pm(inp["mla_g_q"][l]), "gkv": _pm(inp["mla_g_kv"][l]),
            "cs32": C["cs32"], "cs128": C["cs128"],
            "nabias": _na_bias(inp["na_rpb"][l][2 * g:2 * g + 2]),
            "lamv": np.ascontiguousarray(np.broadcast_to(lamv[None], (128, 4, 64))),
            "gsub": _pm(inp["diff_g_sub"][l]), "ident": C["ident"],
        })
    return maps


NB = 4160
ALPHA = 4.0 ** 0.25


def chunkB(c):
    return (0, 64) if c == 0 else (64 + (c - 1) * 512, 512)


def ln_tile(P, src_t, src_ap, np_, st6, mv, rstd, epst, out_t, out_ap):
    for a_ in range(2):
        P.op("dve", lambda e, a_=a_: e.bn_stats(st6[:np_, a_, :], src_ap[:, a_ * 512:(a_ + 1) * 512]),
             reads=[src_t], writes=[st6])
    P.op("dve", lambda e: e.bn_aggr(mv[:np_, :], st6[:np_].rearrange("p a b -> p (a b)")), reads=[st6], writes=[mv])
    P.rsqrt(rstd, rstd[:np_, :], mv, mv[:np_, 1:2], 1.0, epst, epst[:np_, 0:1])
    P.ts("dve", out_t, out_ap, src_t, src_ap, mv[:np_, 0:1], rstd[:np_, 0:1], ALU.subtract, ALU.mult, extra=[mv, rstd])


def build_B():
    P = Prog("B")
    EI, EO = "ExternalInput", "ExternalOutput"
    xin = P.dram("xin", [NB, 1024], F32, EI)
    yT = P.dram("yT", [1536, NB], BF16, EI)
    msT = P.dram("msT", [128, 8, 4], F32, EI)
    wgate = P.dram("wgate", [1024, 3072], F32, EI)
    bgT = P.dram("bgT", [128, 24], F32, EI)
    wbr = P.dram("wbr", [1536, 1024], F32, EI)
    wout = P.dram("wout", [1024, 1024], F32, EI)
    wrt = P.dram("wrt", [128, 8, 16], F32, EI)
    rows = P.dram("rows", [128, 8, 1024], F32, EI)
    identd = P.dram("ident", [128, 128], F32, EI)
    x1o = P.dram("x1", [NB, 1024], F32, EO)
    h2tmo = P.dram("h2tm", [NB, 1024], BF16, EO)
    affo = P.dram("aff", [NB, 16], F32, EO)
    P.start()
    P.banks(7)
    tp = P.psum("tp", [128, 1024], BF16)
    Wg = P.sbuf("Wg", [128, 8, 3072], BF16)
    Wb = P.sbuf("Wb", [128, 12, 1024], BF16)
    Wo = P.sbuf("Wo", [128, 8, 1024], BF16)
    for k in range(8):
        P.dma(Wg, Wg[:, k, :], wgate, wgate[k * 128:(k + 1) * 128, :], q="pool")
        P.dma(Wo, Wo[:, k, :], wout, wout[k * 128:(k + 1) * 128, :], q="pool")
    for k in range(12):
        P.dma(Wb, Wb[:, k, :], wbr, wbr[k * 128:(k + 1) * 128, :], q="pool")
    ident = P.sbuf("identb", [128, 128], BF16)
    P.dma(ident, ident[:], identd, identd[:], q="pool")
    bg = P.sbuf("bg", [128, 24], F32)
    P.dma(bg, bg[:], bgT, bgT[:])
    ms = P.sbuf("ms", [128, 8, 4], F32)
    P.dma(ms, ms[:], msT, msT[:])
    scp = P.sbuf("scp", [128, 8, 2], F32)
    for m in range(2):
        P.ts("dve", scp, scp[:, :, m], ms, ms[:, :, 2 * m + 1], 1.0, None, ALU.add)
    rw = P.sbuf("rw", [128, 8, 1024], F32)
    P.dma(rw, rw[:], rows, rows[:])
    for j in (4, 6):
        P.ts("dve", rw, rw[:, j, :], rw, rw[:, j, :], 1.0, None, ALU.add)
    wr_f = P.sbuf("wr_f", [128, 8, 16], F32)
    P.dma(wr_f, wr_f[:], wrt, wrt[:])
    wr_hi = P.sbuf("wr_hi", [128, 8, 16], BF16)
    wr_lo = P.sbuf("wr_lo", [128, 8, 16], BF16)
    P.cp("dve", wr_hi, wr_hi[:], wr_f, wr_f[:])
    P.tt("dve", wr_lo, wr_lo[:], wr_f, wr_f[:], wr_hi, wr_hi[:], ALU.subtract)
    epst = P.sbuf("epst", [128, 1], F32)
    P.memset("dve", epst, epst[:], LN_EPS)

    xc = P.sbuf("xc", [128, 4, 1024], F32)
    xnb = [P.sbuf("xn%d" % i, [128, 1024], BF16) for i in range(2)]
    st6 = P.sbuf("st6", [128, 2, 6], F32)
    mv = P.sbuf("mv", [128, 2], F32)
    rstd = P.sbuf("rstd", [128, 1], F32)
    htmp = P.sbuf("htmp", [128, 8, 128], F32)
    hT = P.sbuf("hT", [128, 8, 512], BF16)
    yt = P.sbuf("yt", [128, 12, 512], BF16)
    sig = [P.sbuf("sig%d" % i, [128, 512], F32) for i in range(2)]
    mf = P.sbuf("mf", [128, 512], F32)
    mt2 = P.sbuf("mt2", [128, 512], F32)
    mT = P.sbuf("mT", [128, 8, 512], BF16)
    u = P.sbuf("u", [128, 1024], F32)
    ut = P.sbuf("ut", [128, 1024], F32)
    x1 = [P.sbuf("x1_%d" % i, [128, 1024], F32) for i in range(1)]
    h2 = P.sbuf("h2", [128, 1024], F32)
    h2hi = P.sbuf("h2hi", [128, 1024], BF16)
    h2lo = P.sbuf("h2lo", [128, 1024], BF16)
    h2Ts = [P.sbuf("h2Ts%d" % i, [128, 8, 128], BF16) for i in range(1)]
    h2Tl = P.sbuf("h2Tl", [128, 8, 128], BF16)
    lg = P.sbuf("lg", [128, 16], F32)
    mx = P.sbuf("mx", [128, 1], F32)
    sm = P.sbuf("sm", [128, 1], F32)
    affs = [P.sbuf("affs%d" % i, [128, 16], F32) for i in range(2)]

    for c in range(9):
        tok0, n = chunkB(c)
        m = 1 if c == 0 else 0
        tsz = 64 if c == 0 else 128
        nt = n // tsz
        P.dma(yt, yt[:, :, :n], yT, yT[:, tok0:tok0 + n].rearrange("(j p) t -> p j t", p=128))
        for i in range(nt):
            xn = xnb[i % 2]
            P.dma(xc, xc[:tsz, i, :], xin, xin[tok0 + i * tsz: tok0 + (i + 1) * tsz, :])
            ln_tile(P, xc, xc[:tsz, i, :], tsz, st6, mv, rstd, epst, xn, xn[:tsz, :])
            for k in range(8):
                P.tr(tp, tp[:, k * 128:k * 128 + tsz], xn, xn[:tsz, k * 128:(k + 1) * 128], ident, ident[:tsz, :tsz])
            P.tt("dve", htmp, htmp[:, :, :tsz], tp, tp[:].rearrange("p (a b) -> p a b", a=8)[:, :, :tsz],
                 scp, scp[:, :, m:m + 1].to_broadcast([128, 8, tsz]), ALU.mult)
            P.tt("pool", hT, hT[:, :, i * tsz:(i + 1) * tsz], htmp, htmp[:, :, :tsz],
                 ms, ms[:, :, 2 * m:2 * m + 1].to_broadcast([128, 8, tsz]), ALU.add)
        for oc in range(8):
            for br in range(3):
                G = P.bank()
                for k in range(8):
                    P.mm(G, G[:, :n], Wg, Wg[:, k, br * 1024 + oc * 128: br * 1024 + (oc + 1) * 128], hT, hT[:, k, :n],
                         start=(k == 0), stop=(k == 7))
                sg = sig[br % 2]
                P.act(sg, sg[:, :n], G, G[:, :n], AF.Sigmoid, extra=[bg], bias=bg[:, br * 8 + oc: br * 8 + oc + 1])
                Pb = P.bank()
                for k in range(4):
                    P.mm(Pb, Pb[:, :n], Wb, Wb[:, br * 4 + k, oc * 128:(oc + 1) * 128], yt, yt[:, br * 4 + k, :n],
                         start=(k == 0), stop=(k == 3))
                if br == 0:
                    P.tt("dve", mf, mf[:, :n], Pb, Pb[:, :n], sg, sg[:, :n], ALU.mult)
                else:
                    P.tt("dve", mt2, mt2[:, :n], Pb, Pb[:, :n], sg, sg[:, :n], ALU.mult)
                    P.tt("pool", mf, mf[:, :n], mf, mf[:, :n], mt2, mt2[:, :n], ALU.add)
            P.cp("pool", mT, mT[:, oc, :n], mf, mf[:, :n])
        for i in range(nt):
            for half in range(2):
                O = P.bank()
                for k in range(8):
                    P.mm(O, O[:tsz, :], mT, mT[:, k, i * tsz:(i + 1) * tsz], Wo, Wo[:, k, half * 512:(half + 1) * 512],
                         start=(k == 0), stop=(k == 7))
                hs = slice(half * 512, (half + 1) * 512)
                P.tt("dve", ut, ut[:tsz, hs], O, O[:tsz, :], rw, rw[:tsz, m, hs], ALU.mult)
            P.stt("dve", u, u[:tsz, :], xc, xc[:tsz, i, :], ALPHA, ut, ut[:tsz, :], ALU.mult, ALU.add)
            x1t = x1[0]
            ln_tile(P, u, u[:tsz, :], tsz, st6, mv, rstd, epst, x1t, x1t[:tsz, :])
            P.tt("dve", x1t, x1t[:tsz, :], x1t, x1t[:tsz, :], rw, rw[:tsz, 2, :], ALU.mult)
            P.tt("pool", x1t, x1t[:tsz, :], x1t, x1t[:tsz, :], rw, rw[:tsz, 3, :], ALU.add)
            r0 = tok0 + i * tsz
            P.dma(x1o, x1o[r0:r0 + tsz, :], x1t, x1t[:tsz, :])
            ln_tile(P, x1t, x1t[:tsz, :], tsz, st6, mv, rstd, epst, h2, h2[:tsz, :])
            P.tt("dve", h2, h2[:tsz, :], h2, h2[:tsz, :], rw, rw[:tsz, 4 + 2 * m, :], ALU.mult)
            P.tt("pool", h2, h2[:tsz, :], h2, h2[:tsz, :], rw, rw[:tsz, 5 + 2 * m, :], ALU.add)
            P.cp("act", h2hi, h2hi[:tsz, :], h2, h2[:tsz, :])
            P.tt("dve", h2lo, h2lo[:tsz, :], h2, h2[:tsz, :], h2hi, h2hi[:tsz, :], ALU.subtract)
            hs_ = h2Ts[0]
            for k in range(8):
                P.tr(tp, tp[:, k * 128:k * 128 + tsz], h2hi, h2hi[:tsz, k * 128:(k + 1) * 128], ident, ident[:tsz, :tsz])
            P.cp("act", hs_, hs_[:, :, :tsz], tp, tp[:].rearrange("p (a b) -> p a b", a=8)[:, :, :tsz])
            P.dma(h2tmo, h2tmo[r0:r0 + tsz, :], h2hi, h2hi[:tsz, :])
            for k in range(8):
                P.tr(tp, tp[:, k * 128:k * 128 + tsz], h2lo, h2lo[:tsz, k * 128:(k + 1) * 128], ident, ident[:tsz, :tsz])
            P.cp("dve", h2Tl, h2Tl[:, :, :tsz], tp, tp[:].rearrange("p (a b) -> p a b", a=8)[:, :, :tsz])
            L = P.bank()
            for k in range(8):
                P.mm(L, L[:tsz, 0:16], hs_, hs_[:, k, :tsz], wr_hi, wr_hi[:, k, :], start=(k == 0), stop=False)
                P.mm(L, L[:tsz, 0:16], hs_, hs_[:, k, :tsz], wr_lo, wr_lo[:, k, :], start=False, stop=False)
                P.mm(L, L[:tsz, 0:16], h2Tl, h2Tl[:, k, :tsz], wr_hi, wr_hi[:, k, :], start=False, stop=(k == 7))
            P.op("dve", lambda e, L=L, tsz=tsz: e.reduce_max(mx[:tsz, :], L[:tsz, 0:16], AX.X), reads=[L], writes=[mx])
            P.ts("dve", lg, lg[:tsz, :], L, L[:tsz, 0:16], mx[:tsz, 0:1], None, ALU.subtract, extra=[mx])
            P.act(lg, lg[:tsz, :], lg, lg[:tsz, :], AF.Exp)
            P.op("dve", lambda e, tsz=tsz: e.reduce_sum(sm[:tsz, :], lg[:tsz, :], AX.X), reads=[lg], writes=[sm])
            P.op("dve", lambda e, tsz=tsz: e.reciprocal(sm[:tsz, :], sm[:tsz, :]), reads=[sm], writes=[sm])
            af_ = affs[i % 2]
            P.ts("dve", af_, af_[:tsz, :], lg, lg[:tsz, :], sm[:tsz, 0:1], None, ALU.mult, extra=[sm])
            P.dma(affo, affo[r0:r0 + tsz, :], af_, af_[:tsz, :])
    P.finish()
    return P.build()


def prep_B(inp, l, mod, x_cur, ctx_cur, yTs):
    C = _consts()
    wbr = np.ascontiguousarray(np.concatenate([inp["w_br_a"][l], inp["w_br_b"][l], inp["w_br_c"][l]], 0))
    maps = []
    for core in range(8):
        b, s = core // 4, core % 4
        xin = np.concatenate([ctx_cur[b][64 * s:64 * (s + 1)], x_cur[b][4096 * s:4096 * (s + 1)]], 0)
        yT = np.concatenate([yTs[b][:, 64 * s:64 * (s + 1)], yTs[b][:, 256 + 4096 * s:256 + 4096 * (s + 1)]], 1)
        msT = np.stack([_pm(mod[0:1024, b]), _pm(mod[1024:2048, b]), _pm(mod[0:1024, 2]), _pm(mod[1024:2048, 2])], -1)
        rows = np.stack([mod[2048:3072, b], mod[2048:3072, 2], inp["ln1_g"][l], inp["ln1_b"][l],
                         mod[4096:5120, b], mod[3072:4096, b], mod[4096:5120, 2], mod[3072:4096, 2]], 0)
        maps.append({
            "xin": np.ascontiguousarray(xin), "yT": np.ascontiguousarray(yT), "msT": np.ascontiguousarray(msT),
            "wgate": inp["w_gate"][l], "bgT": _pm(inp["b_gate"][l]), "wbr": wbr, "wout": inp["w_out"][l],
            "wrt": np.ascontiguousarray(inp["w_router"][l].reshape(8, 128, 16).transpose(1, 0, 2)),
            "rows": np.ascontiguousarray(np.broadcast_to(rows[None], (128, 8, 1024))), "ident": C["ident"],
        })
    return maps


def chunkC(c):
    return (0, 256) if c == 0 else (256 + (c - 1) * 1024, 1024)


def build_C_dense():
    P = Prog("C")
    EI, EO = "ExternalInput", "ExternalOutput"
    h2T = P.dram("h2T", [1024, NT], BF16, EI)
    affd = P.dram("affL", [128, 4, 130], F32, EI)
    wg = P.dram("wg", [4, 1024, 2816], F32, EI)
    wu = P.dram("wu", [4, 1024, 2816], F32, EI)
    wd = P.dram("wd", [4, 2816, 1024], F32, EI)
    fpo = P.dram("fp", [NT, 1024], F32, EO)
    P.start()
    P.banks(7)
    onesf = P.sbuf("onesf", [128, 128], F32)
    P.memset("dve", onesf, onesf[:], 1.0)
    aff = P.sbuf("aff", [128, 4, 130], F32)
    P.dma(aff, aff[:], affd, affd[:])
    lo = P.sbuf("lo", [128, 8], F32)
    hi = P.sbuf("hi", [128, 8], F32)
    mid = P.sbuf("mid", [128, 8], F32)
    kv = P.sbuf("kv", [128, 8], F32)
    cnt = P.sbuf("cnt", [128, 8], F32)
    ge = P.sbuf("ge", [128, 8], F32)
    d1 = P.sbuf("d1", [128, 8], F32)
    cmp_ = P.sbuf("cmp", [128, 128], F32)
    P.memset("dve", lo, lo[:], 0.0)
    P.memset("dve", hi, hi[:], 1.0)
    P.memset("dve", kv, kv[:, 0:4], 2048.0)
    P.memset("dve", kv, kv[:, 4:8], 32.0)
    for it in range(30):
        P.tt("dve", mid, mid[:], lo, lo[:], hi, hi[:], ALU.add)
        P.ts("dve", mid, mid[:], mid, mid[:], 0.5, None, ALU.mult)
        for e in range(4):
            P.ts("dve", cmp_, cmp_[:, 0:128], aff, aff[:, e, 2:130], mid[:, e:e + 1], None, ALU.is_ge, extra=[mid])
            P.op("dve", lambda e_, e=e: e_.reduce_sum(cnt[:, e:e + 1], cmp_[:, 0:128], AX.X), reads=[cmp_], writes=[cnt])
            P.ts("dve", cmp_, cmp_[:, 0:2], aff, aff[:, e, 0:2], mid[:, 4 + e:5 + e], None, ALU.is_ge, extra=[mid])
            P.op("dve", lambda e_, e=e: e_.reduce_sum(cnt[:, 4 + e:5 + e], cmp_[:, 0:2], AX.X), reads=[cmp_], writes=[cnt])
        tb = P.bank()
        P.mm(tb, tb[:, 0:8], onesf, onesf[:], cnt, cnt[:])
        P.tt("dve", ge, ge[:], tb, tb[:, 0:8], kv, kv[:], ALU.is_ge)
        P.tt("dve", d1, d1[:], mid, mid[:], lo, lo[:], ALU.subtract)
        P.tt("dve", d1, d1[:], d1, d1[:], ge, ge[:], ALU.mult)
        P.tt("dve", lo, lo[:], lo, lo[:], d1, d1[:], ALU.add)
        P.tt("dve", d1, d1[:], hi, hi[:], mid, mid[:], ALU.subtract)
        P.tt("dve", d1, d1[:], d1, d1[:], ge, ge[:], ALU.mult)
        P.tt("dve", hi, hi[:], mid, mid[:], d1, d1[:], ALU.add)
    gm = P.sbuf("gm", [128, 4, 130], F32)
    for e in range(4):
        P.ts("dve", gm, gm[:, e, 2:130], aff, aff[:, e, 2:130], lo[:, e:e + 1], None, ALU.is_ge, extra=[lo])
        P.ts("dve", gm, gm[:, e, 0:2], aff, aff[:, e, 0:2], lo[:, 4 + e:5 + e], None, ALU.is_ge, extra=[lo])
    P.tt("dve", gm, gm[:], gm, gm[:], aff, aff[:], ALU.mult)

    hxs = [P.sbuf("hx%d" % i, [128, 8, 1024], BF16) for i in range(2)]
    facc = P.sbuf("facc", [128, 8, 1024], F32)
    NWB = 4
    wgs = [P.sbuf("wgu%d" % i, [128, 8, 256], BF16) for i in range(NWB)]
    wus = [P.sbuf("wuu%d" % i, [128, 8, 256], BF16) for i in range(NWB)]
    wds = [P.sbuf("wdu%d" % i, [128, 2, 1024], BF16) for i in range(NWB)]
    hids = [P.sbuf("hid%d" % i, [128, 2, 1024], BF16) for i in range(2)]
    sgs = [P.sbuf("sg%d" % i, [128, 512], F32) for i in range(2)]
    ui = 0
    si = 0
    for c in range(17):
        tok0, n = chunkC(c)
        hx = hxs[c % 2]
        P.dma(hx, hx[:, :, :n], h2T, h2T[:, tok0:tok0 + n].rearrange("(k p) t -> p k t", p=128))
        P.memset("pool", facc, facc[:], 0.0)
        units = [(e, un) for e in range(4) for un in range(11)]
        pend = []
        for idx in range(len(units) + 1):
            if idx < len(units):
                e, un = units[idx]
                f0 = un * 256
                wgu, wuu, wdu, hid = wgs[ui % NWB], wus[ui % NWB], wds[ui % NWB], hids[ui % 2]
                ui += 1
                P.dma(wgu, wgu[:], wg, wg[e][:, f0:f0 + 256].rearrange("(k p) f -> p k f", p=128), q="pool")
                P.dma(wuu, wuu[:], wu, wu[e][:, f0:f0 + 256].rearrange("(k p) f -> p k f", p=128), q="pool")
                P.dma(wdu, wdu[:], wd, wd[e][f0:f0 + 256, :].rearrange("(c p) o -> p c o", p=128), q="pool")
                for fc in range(2):
                    for th in range(n // 512 if n >= 512 else 1):
                        w_ = min(512, n)
                        ts_ = slice(th * 512, th * 512 + w_)
                        G = P.bank()
                        for k in range(8):
                            P.mm(G, G[:, :w_], wgu, wgu[:, k, fc * 128:(fc + 1) * 128], hx, hx[:, k, ts_], start=(k == 0), stop=(k == 7))
                        U = P.bank()
                        for k in range(8):
                            P.mm(U, U[:, :w_], wuu, wuu[:, k, fc * 128:(fc + 1) * 128], hx, hx[:, k, ts_], start=(k == 0), stop=(k == 7))
                        sg = sgs[si % 2]
                        si += 1
                        P.act(sg, sg[:, :w_], G, G[:, :w_], AF.Silu)
                        P.tt("dve", hid, hid[:, fc, ts_], U, U[:, :w_], sg, sg[:, :w_], ALU.mult)
                pend.append((e, wdu, hid))
            j = idx - 1
            if j >= 0:
                e, wdu, hid = pend[j]
                for tl in range(n // 128):
                    gt = tok0 // 128 + tl
                    for oh in range(2):
                        O = P.bank()
                        for fc in range(2):
                            P.mm(O, O[:, :], hid, hid[:, fc, tl * 128:(tl + 1) * 128], wdu, wdu[:, fc, oh * 512:(oh + 1) * 512],
                                 start=(fc == 0), stop=(fc == 1))
                        fs = facc[:, tl, oh * 512:(oh + 1) * 512]
                        P.stt("dve", facc, fs, O, O[:, :], gm[:, e, gt:gt + 1], facc, fs, ALU.mult, ALU.add, extra=[gm])
        P.dma(fpo, fpo[tok0:tok0 + n, :].rearrange("(t p) o -> p t o", p=128), facc, facc[:, :n // 128, :])
    P.finish()
    return P.build()


XROWS = 2304
TRASH0 = 2176


def build_C():
    P = Prog("C")
    EI, EO = "ExternalInput", "ExternalOutput"
    h2tm = P.dram("h2tm", [NT, 1024], BF16, EI)
    affd = P.dram("affL", [128, 4, 130], F32, EI)
    trid = P.dram("tri", [128, 128], F32, EI)
    trashd = P.dram("trashc", [128, 1], F32, EI)
    identd = P.dram("ident", [128, 128], F32, EI)
    wg = P.dram("wg", [4, 1024, 2816], F32, EI)
    wu = P.dram("wu", [4, 1024, 2816], F32, EI)
    wd = P.dram("wd", [4, 2816, 1024], F32, EI)
    fpo = P.dram("fp", [NT, 1024], F32, EO)
    xsel = [P.dram("xsel%d" % e, [XROWS, 1024], BF16) for e in range(4)]
    osel = [P.dram("osel%d" % e, [XROWS, 1024], F32) for e in range(4)]
    P.start()
    P.banks(7)
    tp = P.psum("tp", [128, 1024], BF16)
    onesf = P.sbuf("onesf", [128, 128], F32)
    P.memset("dve", onesf, onesf[:], 1.0)
    onesb = P.sbuf("onesb", [128, 128], BF16)
    P.memset("dve", onesb, onesb[:], 1.0)
    trib = P.sbuf("trib", [128, 128], BF16)
    P.dma(trib, trib[:], trid, trid[:], q="pool")
    ident = P.sbuf("identb", [128, 128], BF16)
    P.dma(ident, ident[:], identd, identd[:], q="pool")
    trc = P.sbuf("trc", [128, 1], F32)
    P.dma(trc, trc[:], trashd, trashd[:])
    aff = P.sbuf("aff", [128, 4, 130], F32)
    P.dma(aff, aff[:], affd, affd[:])
    lo = P.sbuf("lo", [128, 8], F32)
    hi = P.sbuf("hi", [128, 8], F32)
    mid = P.sbuf("mid", [128, 8], F32)
    kv = P.sbuf("kv", [128, 8], F32)
    cnt = P.sbuf("cnt", [128, 8], F32)
    ge = P.sbuf("ge", [128, 8], F32)
    d1 = P.sbuf("d1", [128, 8], F32)
    cmp_ = P.sbuf("cmp", [128, 128], F32)
    P.memset("dve", lo, lo[:], 0.0)
    P.memset("dve", hi, hi[:], 1.0)
    P.memset("dve", kv, kv[:, 0:4], 2048.0)
    P.memset("dve", kv, kv[:, 4:8], 32.0)
    for it in range(30):
        P.tt("dve", mid, mid[:], lo, lo[:], hi, hi[:], ALU.add)
        P.ts("dve", mid, mid[:], mid, mid[:], 0.5, None, ALU.mult)
        for e in range(4):
            P.ts("dve", cmp_, cmp_[:, 0:128], aff, aff[:, e, 2:130], mid[:, e:e + 1], None, ALU.is_ge, extra=[mid])
            P.op("dve", lambda e_, e=e: e_.reduce_sum(cnt[:, e:e + 1], cmp_[:, 0:128], AX.X), reads=[cmp_], writes=[cnt])
            P.ts("dve", cmp_, cmp_[:, 0:2], aff, aff[:, e, 0:2], mid[:, 4 + e:5 + e], None, ALU.is_ge, extra=[mid])
            P.op("dve", lambda e_, e=e: e_.reduce_sum(cnt[:, 4 + e:5 + e], cmp_[:, 0:2], AX.X), reads=[cmp_], writes=[cnt])
        tb = P.bank()
        P.mm(tb, tb[:, 0:8], onesf, onesf[:], cnt, cnt[:])
        P.tt("dve", ge, ge[:], tb, tb[:, 0:8], kv, kv[:], ALU.is_ge)
        P.tt("dve", d1, d1[:], mid, mid[:], lo, lo[:], ALU.subtract)
        P.tt("dve", d1, d1[:], d1, d1[:], ge, ge[:], ALU.mult)
        P.tt("dve", lo, lo[:], lo, lo[:], d1, d1[:], ALU.add)
        P.tt("dve", d1, d1[:], hi, hi[:], mid, mid[:], ALU.subtract)
        P.tt("dve", d1, d1[:], d1, d1[:], ge, ge[:], ALU.mult)
        P.tt("dve", hi, hi[:], mid, mid[:], d1, d1[:], ALU.add)
    m = P.sbuf("m", [128, 4, 130], F32)
    for e in range(4):
        P.ts("dve", m, m[:, e, 2:130], aff, aff[:, e, 2:130], lo[:, e:e + 1], None, ALU.is_ge, extra=[lo])
        P.ts("dve", m, m[:, e, 0:2], aff, aff[:, e, 0:2], lo[:, 4 + e:5 + e], None, ALU.is_ge, extra=[lo])
    mb = P.sbuf("mb", [128, 4, 130], BF16)
    P.cp("dve", mb, mb[:], m, m[:])
    posf = P.sbuf("posf", [128, 4, 130], F32)
    posi = P.sbuf("posi", [128, 4, 130], I32)
    cT = P.sbuf("cT", [128, 128], BF16)
    offs = P.sbuf("offs", [128, 130], F32)
    ltm = P.sbuf("ltm", [128, 130], F32)
    for e in range(4):
        R = P.bank()
        P.mm(R, R[:, 0:130], trib, trib[:], mb, mb[:, e, :])
        CT = P.bank()
        P.mm(CT, CT[:, 0:128], mb, mb[:, e, 2:130], onesb, onesb[:])
        P.cp("act", cT, cT[:], CT, CT[:, 0:128])
        OF = P.bank()
        P.mm(OF, OF[:, 0:128], cT, cT[:], trib, trib[:])
        C0 = P.bank()
        P.mm(C0, C0[:, 0:2], onesb, onesb[:], mb, mb[:, e, 0:2])
        P.cp("act", offs, offs[:, 2:130], OF, OF[:, 0:128])
        P.memset("dve", offs, offs[:, 0:1], 2048.0)
        P.ts("dve", offs, offs[:, 1:2], C0, C0[:, 0:1], 2048.0, None, ALU.add)
        P.tt("dve", posf, posf[:, e, :], R, R[:, 0:130], offs, offs[:], ALU.add)
        P.ts("dve", ltm, ltm[:, 2:130], posf, posf[:, e, 2:130], 2047.5, None, ALU.is_lt)
        P.ts("dve", ltm, ltm[:, 0:2], posf, posf[:, e, 0:2], 2079.5, None, ALU.is_lt)
        P.tt("dve", m, m[:, e, :], m, m[:, e, :], ltm, ltm[:], ALU.mult)
        P.ts("dve", posf, posf[:, e, :], posf, posf[:, e, :], trc[:, 0:1], None, ALU.subtract, extra=[trc])
        P.tt("dve", posf, posf[:, e, :], posf, posf[:, e, :], m, m[:, e, :], ALU.mult)
        P.ts("dve", posf, posf[:, e, :], posf, posf[:, e, :], trc[:, 0:1], None, ALU.add, extra=[trc])
    P.cp("dve", posi, posi[:], posf, posf[:])
    gm = P.sbuf("gm", [128, 4, 130], F32)
    P.tt("dve", gm, gm[:], m, m[:], aff, aff[:], ALU.mult)

    htb = [P.sbuf("ht%d" % i, [128, 1024], BF16) for i in range(2)]
    for i in range(130):
        ht = htb[i % 2]
        P.dma(ht, ht[:], h2tm, h2tm[i * 128:(i + 1) * 128, :])
        for e in range(4):
            P.op("pool", lambda e_, e=e, i=i, ht=ht: e_.indirect_dma_start(
                out=xsel[e][:, :], out_offset=bass.IndirectOffsetOnAxis(ap=posi[:, e, i:i + 1], axis=0),
                in_=ht[:, :], in_offset=None),
                reads=[ht, posi], writes=[xsel[e]], dma_dst=xsel[e])
    zt = P.sbuf("zt", [128, 1024], F32)
    P.memset("pool", zt, zt[:], 0.0)
    for e in range(4):
        P.dma(osel[e], osel[e][TRASH0:TRASH0 + 128, :], zt, zt[:])

    NSL = 2080
    hx = P.sbuf("hx", [128, 8, NSL], BF16)
    oacc = P.sbuf("oacc", [128, 17, 1024], F32)
    NWB = 3
    wgs = [P.sbuf("wgu%d" % i, [128, 8, 256], BF16) for i in range(NWB)]
    wus = [P.sbuf("wuu%d" % i, [128, 8, 256], BF16) for i in range(NWB)]
    wds = [P.sbuf("wdu%d" % i, [128, 2, 1024], BF16) for i in range(NWB)]
    hids = [P.sbuf("hid%d" % i, [128, 2, NSL], BF16) for i in range(2)]
    sgs = [P.sbuf("sg%d" % i, [128, 512], F32) for i in range(2)]
    xts = [P.sbuf("xts%d" % i, [128, 1024], BF16) for i in range(2)]
    cchunks = [(j * 512, 512) for j in range(4)] + [(2048, 32)]
    ui = 0
    si = 0
    for e in range(4):
        for st in range(17):
            rows = 128 if st < 16 else 32
            xt = xts[st % 2]
            P.dma(xt, xt[:rows, :], xsel[e], xsel[e][st * 128:st * 128 + rows, :])
            for k in range(8):
                P.tr(tp, tp[:, k * 128:k * 128 + rows], xt, xt[:rows, k * 128:(k + 1) * 128], ident, ident[:rows, :rows])
            P.cp("act" if st % 2 else "dve", hx, hx[:, :, st * 128:st * 128 + rows],
                 tp, tp[:].rearrange("p (a b) -> p a b", a=8)[:, :, :rows])
        pend = []
        for idx in range(12):
            if idx < 11:
                f0 = idx * 256
                wgu, wuu, wdu, hid = wgs[ui % NWB], wus[ui % NWB], wds[ui % NWB], hids[ui % 2]
                ui += 1
                P.dma(wgu, wgu[:], wg, wg[e][:, f0:f0 + 256].rearrange("(k p) f -> p k f", p=128), q="pool")
                P.dma(wuu, wuu[:], wu, wu[e][:, f0:f0 + 256].rearrange("(k p) f -> p k f", p=128), q="pool")
                P.dma(wdu, wdu[:], wd, wd[e][f0:f0 + 256, :].rearrange("(c p) o -> p c o", p=128), q="pool")
                for fc in range(2):
                    for (c0, w_) in cchunks:
                        ts_ = slice(c0, c0 + w_)
                        G = P.bank()
                        for k in range(8):
                            P.mm(G, G[:, :w_], wgu, wgu[:, k, fc * 128:(fc + 1) * 128], hx, hx[:, k, ts_], start=(k == 0), stop=(k == 7))
                        U = P.bank()
                        for k in range(8):
                            P.mm(U, U[:, :w_], wuu, wuu[:, k, fc * 128:(fc + 1) * 128], hx, hx[:, k, ts_], start=(k == 0), stop=(k == 7))
                        sg = sgs[si % 2]
                        si += 1
                        P.act(sg, sg[:, :w_], G, G[:, :w_], AF.Silu)
                        P.tt("dve", hid, hid[:, fc, ts_], U, U[:, :w_], sg, sg[:, :w_], ALU.mult)
                pend.append((wdu, hid))
            j = idx - 1
            if j >= 0:
                wdu, hid = pend[j]
                for st in range(17):
                    rows = 128 if st < 16 else 32
                    for oh in range(2):
                        O = P.bank()
                        for fc in range(2):
                            P.mm(O, O[:rows, :], hid, hid[:, fc, st * 128:st * 128 + rows], wdu, wdu[:, fc, oh * 512:(oh + 1) * 512],
                                 start=(fc == 0), stop=(fc == 1))
                        os_ = oacc[:rows, st, oh * 512:(oh + 1) * 512]
                        if j == 0:
                            P.cp("act", oacc, os_, O, O[:rows, :])
                        else:
                            P.tt("dve", oacc, os_, O, O[:rows, :], oacc, os_, ALU.add)
        P.dma(osel[e], osel[e][0:2048, :].rearrange("(t p) o -> p t o", p=128), oacc, oacc[:, 0:16, :])
        P.dma(osel[e], osel[e][2048:2080, :], oacc, oacc[:32, 16, :])

    gts = [P.sbuf("gt%d" % i, [128, 4, 1024], F32) for i in range(1)]
    fos = [P.sbuf("fo0", [128, 1024], F32), zt]
    for i in range(130):
        gt, fo = gts[0], fos[i % 2]
        for e in range(4):
            P.op("pool", lambda e_, e=e, i=i, gt=gt: e_.indirect_dma_start(
                out=gt[:, e, :], out_offset=None, in_=osel[e][:, :],
                in_offset=bass.IndirectOffsetOnAxis(ap=posi[:, e, i:i + 1], axis=0)),
                reads=[osel[e], posi], writes=[gt], dma_dst=gt)
        P.ts("dve", fo, fo[:], gt, gt[:, 0, :], gm[:, 0, i:i + 1], None, ALU.mult, extra=[gm])
        for e in range(1, 4):
            P.stt("dve", fo, fo[:], gt, gt[:, e, :], gm[:, e, i:i + 1], fo, fo[:], ALU.mult, ALU.add, extra=[gm])
        P.dma(fpo, fpo[i * 128:(i + 1) * 128, :], fo, fo[:])
    P.finish()
    return P.build()


def build_D():
    P = Prog("D")
    EI, EO = "ExternalInput", "ExternalOutput"
    x1d = P.dram("x1", [NB, 1024], F32, EI)
    fps = P.dram("fps", [4, NB, 1024], F32, EI)
    rows = P.dram("rows", [128, 4, 1024], F32, EI)
    x2o = P.dram("x2", [NB, 1024], F32, EO)
    P.start()
    rw = P.sbuf("rw", [128, 4, 1024], F32)
    P.dma(rw, rw[:], rows, rows[:])
    epst = P.sbuf("epst", [128, 1], F32)
    P.memset("dve", epst, epst[:], LN_EPS)
    st6 = P.sbuf("st6", [128, 2, 6], F32)
    mv = P.sbuf("mv", [128, 2], F32)
    rstd = P.sbuf("rstd", [128, 1], F32)
    xts = [P.sbuf("xt%d" % i, [128, 1024], F32) for i in range(2)]
    pts = [P.sbuf("pp%d" % i, [128, 4, 1024], F32) for i in range(2)]
    u = P.sbuf("u", [128, 1024], F32)
    ots = [P.sbuf("ot%d" % i, [128, 1024], F32) for i in range(2)]
    for t in range(33):
        r0, tsz = (0, 64) if t == 0 else (64 + (t - 1) * 128, 128)
        m = 1 if t == 0 else 0
        xt, pp, ot = xts[t % 2], pts[t % 2], ots[t % 2]
        P.dma(xt, xt[:tsz, :], x1d, x1d[r0:r0 + tsz, :])
        for j in range(4):
            P.dma(pp, pp[:tsz, j, :], fps, fps[j, r0:r0 + tsz, :])
        P.tt("dve", pp, pp[:tsz, 0, :], pp, pp[:tsz, 0, :], pp, pp[:tsz, 1, :], ALU.add)
        P.tt("pool", pp, pp[:tsz, 2, :], pp, pp[:tsz, 2, :], pp, pp[:tsz, 3, :], ALU.add)
        P.tt("dve", pp, pp[:tsz, 0, :], pp, pp[:tsz, 0, :], pp, pp[:tsz, 2, :], ALU.add)
        P.tt("dve", pp, pp[:tsz, 0, :], pp, pp[:tsz, 0, :], rw, rw[:tsz, m, :], ALU.mult)
        P.stt("dve", u, u[:tsz, :], xt, xt[:tsz, :], ALPHA, pp, pp[:tsz, 0, :], ALU.mult, ALU.add)
        ln_tile(P, u, u[:tsz, :], tsz, st6, mv, rstd, epst, ot, ot[:tsz, :])
        P.tt("dve", ot, ot[:tsz, :], ot, ot[:tsz, :], rw, rw[:tsz, 2, :], ALU.mult)
        P.tt("pool", ot, ot[:tsz, :], ot, ot[:tsz, :], rw, rw[:tsz, 3, :], ALU.add)
        P.dma(x2o, x2o[r0:r0 + tsz, :], ot, ot[:tsz, :])
    P.finish()
    return P.build()


def _lam_init(l):
    import math
    return 0.8 - 0.6 * math.exp(-0.3 * l)


def kernel(**inp):
    inp = {k: np.asarray(v) for k, v in inp.items()}
    mods = run_L0(inp)
    x_cur = [np.asarray(inp["x"][b], np.float32) for b in range(2)]
    ctx_cur = [np.asarray(inp["ctx"][b], np.float32) for b in range(2)]
    for l in range(2):
        mod = mods[l]
        resA = run(build_A(_lam_init(l)), prep_A(inp, l, mod, x_cur, ctx_cur)).results
        yTs = []
        for b in range(2):
            parts = [np.asarray(resA[b * 4 + g][nm]) for nm in ("yaT", "ybT", "ycT") for g in range(4)]
            yTs.append(np.concatenate(parts, 0))
        del resA
        resB = run(build_B(), prep_B(inp, l, mod, x_cur, ctx_cur, yTs)).results
        x1p = [np.asarray(resB[c]["x1"]) for c in range(8)]
        h2Tf, afff = [], []
        for b in range(2):
            hp = [np.asarray(resB[b * 4 + s]["h2tm"]) for s in range(4)]
            ap = [np.asarray(resB[b * 4 + s]["aff"]) for s in range(4)]
            h2Tf.append(np.concatenate([p[:64] for p in hp] + [p[64:] for p in hp], 0))
            afff.append(np.concatenate([p[:64] for p in ap] + [p[64:] for p in ap], 0))
        del resB
        mapsC = []
        for core in range(8):
            b, j = core // 4, core % 4
            a4 = afff[b][:, 4 * j:4 * j + 4].reshape(130, 128, 4).transpose(1, 2, 0)
            mapsC.append({
                "h2tm": np.ascontiguousarray(h2Tf[b]), "affL": np.ascontiguousarray(a4),
                "tri": np.triu(np.ones((128, 128), np.float32), 1),
                "trashc": (TRASH0 + np.arange(128, dtype=np.float32)).reshape(128, 1),
                "ident": _consts()["ident"],
                "wg": np.ascontiguousarray(inp["w_exp_gate"][l][4 * j:4 * j + 4]),
                "wu": np.ascontiguousarray(inp["w_exp_up"][l][4 * j:4 * j + 4]),
                "wd": np.ascontiguousarray(inp["w_exp_down"][l][4 * j:4 * j + 4]),
            })
        resC = run(build_C(), mapsC).results
        fpc = [np.asarray(resC[c]["fp"]) for c in range(8)]
        del resC, mapsC
        rowsD = lambda b: np.ascontiguousarray(np.broadcast_to(
            np.stack([mod[5120:6144, b], mod[5120:6144, 2], inp["ln2_g"][l], inp["ln2_b"][l]], 0)[None], (128, 4, 1024)))
        mapsD = []
        for core in range(8):
            b, s = core // 4, core % 4
            fps = np.stack([np.concatenate([fpc[b * 4 + jj][64 * s:64 * (s + 1)],
                                            fpc[b * 4 + jj][256 + 4096 * s:256 + 4096 * (s + 1)]], 0) for jj in range(4)], 0)
            mapsD.append({"x1": x1p[core], "fps": np.ascontiguousarray(fps), "rows": rowsD(b)})
        resD = run(build_D(), mapsD).results
        for b in range(2):
            xp = [np.asarray(resD[b * 4 + s]["x2"]) for s in range(4)]
            ctx_cur[b] = np.concatenate([p[:64] for p in xp], 0)
            x_cur[b] = np.concatenate([p[64:] for p in xp], 0)
        del resD, mapsD, fpc
    return np.stack(x_cur, 0).astype(np.float32)
```

```python
import numpy as np
import concourse.bass as bass
import concourse.mybir as mybir
from concourse.bass_utils import run_bass_kernel_spmd

F32 = mybir.dt.float32
BF16 = mybir.dt.bfloat16
I32 = mybir.dt.int32
U32 = mybir.dt.uint32
ALU = mybir.AluOpType
AF = mybir.ActivationFunctionType
AX = mybir.AxisListType

COMPUTE = ("pe", "act", "dve", "pool")
SAME_ENGINE_SYNC = True
import os
SIGALL = bool(int(os.environ.get('K_SIGALL', '0')))
SIGENG = set(os.environ.get('K_SIGENG', 'act,dve,pool').split(','))


class T:
    def __init__(self, prog, handle, name):
        self.p = prog
        self.h = handle
        self.name = name
        self.lastw = None
        self.reads = []
        self.dsem = None
        self.dcnt = 0

    def __getitem__(self, idx):
        return self.h[idx]


class Prog:
    def __init__(self, name):
        self.name = name
        self.nc = bass.Bass("TRN2", target_bir_lowering=False)
        self.ops = {e: [] for e in ("pe", "act", "dve", "pool", "sp")}
        self.cnt = {e: 0 for e in COMPUTE}
        self.opidx = {e: {} for e in COMPUTE}
        self.seen = {e: {} for e in self.ops}
        self.ctxs = []
        self.tiles = []
        self.sems = {}
        self.outs = []
        self.nsem = 0

    def _enter(self, cm):
        v = cm.__enter__()
        self.ctxs.append(cm)
        return v

    def sem(self, name):
        self.nsem += 1
        return self._enter(self.nc.semaphore(name))

    def sbuf(self, name, shape, dt):
        t = T(self, self._enter(self.nc.sbuf_tensor(name, list(shape), dt)), name)
        self.tiles.append(t)
        return t

    def psum(self, name, shape, dt=F32):
        t = T(self, self._enter(self.nc.psum_tensor(name, list(shape), dt)), name)
        self.tiles.append(t)
        return t

    def dram(self, name, shape, dt, kind="Internal", addr_space="Local"):
        h = self.nc.dram_tensor(name, list(shape), dt, kind=kind, addr_space=addr_space)
        t = T(self, h.ap(), name)
        self.tiles.append(t)
        if kind == "ExternalOutput":
            self.outs.append(t)
        return t

    def start(self):
        for e in COMPUTE:
            self.sems[e] = self.sem("s_" + e)

    def _need(self, eng, hz, waits):
        if hz is None:
            return
        kind, key, val = hz
        if kind == "eng" and key == eng and (eng == "pe" or not SAME_ENGINE_SYNC):
            return
        k = (kind, key if kind == "eng" else id(key))
        if self.seen[eng].get(k, 0) >= val:
            return
        self.seen[eng][k] = val
        if kind == "eng":
            rec = self.ops[key][self.opidx[key][val]]
            rec[3] = True
            waits.append(("eng", key, rec))
        else:
            waits.append(("dma", key.dsem, val))

    def op(self, eng, fn, reads=(), writes=(), dma_dst=None, ordered=False):
        waits = []
        for t in reads:
            self._need(eng, t.lastw, waits)
        for t in writes:
            if dma_dst is not None and t is dma_dst and not ordered and t.lastw is not None \
                    and t.lastw[0] == "dma" and not t.reads:
                pass
            else:
                self._need(eng, t.lastw, waits)
            for r in t.reads:
                self._need(eng, r, waits)
        if dma_dst is not None:
            if dma_dst.dsem is None:
                dma_dst.dsem = self.sem("d_" + dma_dst.name)
            dma_dst.dcnt += 16
            hz = ("dma", dma_dst, dma_dst.dcnt)
            inc = (dma_dst.dsem, 16)
        else:
            self.cnt[eng] += 1
            hz = ("eng", eng, self.cnt[eng])
            inc = None
            self.opidx[eng][self.cnt[eng]] = len(self.ops[eng])
        for t in reads:
            t.reads.append(hz)
        for t in writes:
            t.lastw = hz
            t.reads = []
        self.ops[eng].append([waits, fn, inc, False, 0])

    def dma(self, out_t, out_ap, in_t, in_ap, q="sp", ordered=False, **kw):
        self.op(q, lambda e: e.dma_start(out=out_ap, in_=in_ap, **kw),
                reads=[in_t], writes=[out_t], dma_dst=out_t, ordered=ordered)

    def mm(self, out_t, out_ap, a_t, lhsT, b_t, rhs, start=True, stop=True, **kw):
        self.op("pe", lambda e: e.matmul(out_ap, lhsT, rhs, start=start, stop=stop, **kw),
                reads=[a_t, b_t], writes=[out_t])

    def tr(self, out_t, out_ap, in_t, in_ap, ident_t, ident_ap):
        self.op("pe", lambda e: e.transpose(out_ap, in_ap, ident_ap), reads=[in_t, ident_t], writes=[out_t])


    def act(self, out_t, out_ap, in_t, in_ap, func, extra=(), **kw):
        self.op("act", lambda e: e.activation(out_ap, in_ap, func, **kw), reads=[in_t, *extra], writes=[out_t])

    def ts(self, eng, out_t, out_ap, in_t, in_ap, s1, s2, op0, op1=None, extra=()):
        if op1 is None:
            self.op(eng, lambda e: e.tensor_scalar(out_ap, in_ap, s1, None, op0), reads=[in_t, *extra], writes=[out_t])
        else:
            self.op(eng, lambda e: e.tensor_scalar(out_ap, in_ap, s1, s2, op0, op1), reads=[in_t, *extra], writes=[out_t])

    def tt(self, eng, out_t, out_ap, a_t, a_ap, b_t, b_ap, op):
        self.op(eng, lambda e: e.tensor_tensor(out_ap, a_ap, b_ap, op), reads=[a_t, b_t], writes=[out_t])

    def stt(self, eng, out_t, out_ap, a_t, a_ap, scalar, b_t, b_ap, op0, op1, extra=()):
        self.op(eng, lambda e: e.scalar_tensor_tensor(out_ap, a_ap, scalar, b_ap, op0, op1),
                reads=[a_t, b_t, *extra], writes=[out_t])

    def cp(self, eng, out_t, out_ap, in_t, in_ap):
        if eng == "act":
            self.op("act", lambda e: e.copy(out_ap, in_ap), reads=[in_t], writes=[out_t])
        else:
            self.op(eng, lambda e: e.tensor_copy(out_ap, in_ap), reads=[in_t], writes=[out_t])

    def memset(self, eng, t, ap, val):
        self.op(eng, lambda e: e.memset(ap, val), reads=[], writes=[t])

    def banks(self, n=7):
        self.pb = [self.psum("pb%d" % i, [128, 512], F32) for i in range(n)]
        self.pbi = 0

    def bank(self):
        b = self.pb[self.pbi % len(self.pb)]
        self.pbi += 1
        return b

    def rsqrt(self, out_t, out_ap, in_t, in_ap, scale, eps_t, eps_ap):
        self.op("act", lambda e: e.activation(out_ap, in_ap, AF.Sqrt, bias=eps_ap, scale=scale),
                reads=[in_t, eps_t], writes=[out_t])
        self.op("dve", lambda e: e.reciprocal(out_ap, out_ap), reads=[out_t], writes=[out_t])

    def finish(self):
        waits = []
        for t in self.tiles:
            if t.lastw is not None and t.lastw[0] == "dma":
                self._need("sp", t.lastw, waits)
        self.ops["sp"].append([waits, None, None, False, 0])

    def build(self):
        nc = self.nc
        engs = {"pe": "tensor", "act": "scalar", "dve": "vector", "pool": "gpsimd", "sp": "sync"}
        for e in COMPUTE:
            c = 0
            for rec in self.ops[e]:
                if (SIGALL or e in SIGENG) and rec[2] is None and rec[1] is not None:
                    rec[3] = True
                if rec[2] is None and rec[1] is not None and rec[3]:
                    c += 1
                    rec[4] = c
        with nc.Block() as block:
            for e, attr in engs.items():
                ops = self.ops[e]

                def body(engobj, ops=ops, e=e):
                    for waits, fn, inc, sig, _c in ops:
                        for w in waits:
                            if w[0] == "eng":
                                engobj.wait_ge(self.sems[w[1]], w[2][4])
                            else:
                                engobj.wait_ge(w[1], w[2])
                        if fn is not None:
                            ins = fn(engobj)
                            if inc is not None:
                                ins.then_inc(inc[0], inc[1])
                            elif sig:
                                ins.then_inc(self.sems[e], 1)
                getattr(block, attr)(body)
        for cm in reversed(self.ctxs):
            cm.__exit__(None, None, None)
        self.ctxs = []
        return nc

    def n_ops(self):
        return {e: len(v) for e, v in self.ops.items()}


def run(prog_nc, in_maps, trace=False):
    import sys, time
    t0 = time.time()
    res = run_bass_kernel_spmd(prog_nc, in_maps, core_ids=list(range(len(in_maps))), trace=trace)
    print("[kernel] launch done in %.1fs" % (time.time() - t0), file=sys.stderr, flush=True)
    return res


def build_L0():
    P = Prog("L0")
    nc = P.nc
    w = P.dram("w", [1024, 1536], F32, kind="ExternalInput")
    bT = P.dram("bT", [128, 12], F32, kind="ExternalInput")
    cT = P.dram("cT", [128, 8, 3], F32, kind="ExternalInput")
    out = P.dram("modT", [128, 12, 3], F32, kind="ExternalOutput")
    P.start()
    wt = P.sbuf("wt", [128, 8, 1536], F32)
    bt = P.sbuf("bt", [128, 12], F32)
    ct = P.sbuf("ct", [128, 8, 3], F32)
    st = P.sbuf("st", [128, 8, 3], F32)
    ot = P.sbuf("ot", [128, 12, 3], F32)
    ps = P.psum("ps", [128, 512], F32)
    P.dma(ct, ct[:], cT, cT[:])
    P.dma(bt, bt[:], bT, bT[:])
    for k in range(8):
        P.dma(wt, wt[:, k, :], w, w[k * 128:(k + 1) * 128, :])
    P.op("act", lambda e: e.activation(st[:], ct[:], AF.Silu), reads=[ct], writes=[st])
    for j in range(12):
        for k in range(8):
            P.mm(ps, ps[:, j * 3:(j + 1) * 3], wt, wt[:, k, j * 128:(j + 1) * 128], st, st[:, k, :],
                 start=(k == 0), stop=(k == 7))
    for j in range(12):
        P.op("dve", lambda e, j=j: e.tensor_scalar(ot[:, j, :], ps[:, j * 3:(j + 1) * 3], bt[:, j:j + 1], None, ALU.add),
             reads=[ps, bt], writes=[ot])
    P.dma(out, out[:], ot, ot[:])
    P.finish()
    return P.build()


def run_L0(inp):
    cvec = np.stack([inp["c"][0], inp["c"][1], inp["c_ctx"]], 0)
    cT = np.ascontiguousarray(cvec.T.reshape(8, 128, 3).transpose(1, 0, 2))
    in_maps = []
    for core in range(8):
        l, q = core // 4, core % 4
        cols = slice(q * 1536, (q + 1) * 1536)
        in_maps.append({
            "w": np.ascontiguousarray(inp["w_ada"][l][:, cols]),
            "bT": np.ascontiguousarray(inp["b_ada"][l][cols].reshape(12, 128).T),
            "cT": cT,
        })
    res = run(build_L0(), in_maps)
    mods = []
    for l in range(2):
        parts = []
        for q in range(4):
            m = np.asarray(res.results[l * 4 + q]["modT"])
            parts.append(m.transpose(1, 0, 2).reshape(1536, 3))
        mods.append(np.concatenate(parts, 0))
    return mods


NT = 16640
LN_EPS = 1e-6
NCH = 33


def chunk_rng(c):
    return (0, 256) if c == 0 else (256 + (c - 1) * 512, 512)


def build_A(lam_init, dbg=False):
    P = Prog("A")
    EI = "ExternalInput"
    xin = P.dram("xin", [NT, 1024], F32, EI)
    msT = P.dram("msT", [128, 8, 4], F32, EI)
    wfeat = P.dram("wfeat", [1024, 1216], F32, EI)
    wtok = P.dram("wtok", [1024, 256], F32, EI)
    wuq = P.dram("wuq", [256, 2, 192], F32, EI)
    wukvk = P.dram("wukvk", [128, 2, 64], F32, EI)
    wukvv = P.dram("wukvv", [128, 128], F32, EI)
    gq = P.dram("gq", [128, 2], F32, EI)
    gkv = P.dram("gkv", [128, 1], F32, EI)
    cs32 = P.dram("cs32", [32, 2, NT], F32, EI)
    cs128 = P.dram("cs128", [128, 2, NT], F32, EI)
    nabias = P.dram("nabias", [2, 3, 768, 256], F32, EI)
    lamv = P.dram("lamv", [128, 4, 64], F32, EI)
    gsub = P.dram("gsub", [128, 1], F32, EI)
    identd = P.dram("ident", [128, 128], F32, EI)
    EO = "ExternalOutput"
    yaT = P.dram("yaT", [128, NT], BF16, EO)
    ybT = P.dram("ybT", [128, NT], BF16, EO)
    ycT = P.dram("ycT", [128, NT], BF16, EO)
    sk = EO if dbg else "Internal"
    qaT = P.dram("qaT", [128, NT], BF16, sk)
    kaT = P.dram("kaT", [128, NT], BF16, sk)
    va = P.dram("va", [NT, 128], BF16, sk)
    qbT = P.dram("qbT", [2, 96, NT], BF16, sk)
    kbT = P.dram("kbT", [2, 96, NT], BF16, sk)
    vb = P.dram("vb", [NT, 128], BF16, sk)
    dqT = P.dram("dqT", [128, NT], BF16, sk)
    dkT = P.dram("dkT", [128, NT], BF16, sk)
    dvv = P.dram("dvv", [NT, 128], BF16, sk)
    P.start()
    P.banks(7)
    tp = P.psum("tp", [128, 1024], BF16)

    Wf = P.sbuf("Wf", [128, 8, 1216], BF16)
    Wt = P.sbuf("Wt", [128, 8, 256], BF16)
    for k in range(8):
        P.dma(Wf, Wf[:, k, :], wfeat, wfeat[k * 128:(k + 1) * 128, :], q="pool")
        P.dma(Wt, Wt[:, k, :], wtok, wtok[k * 128:(k + 1) * 128, :], q="pool")
    ident = P.sbuf("identb", [128, 128], BF16)
    P.dma(ident, ident[:], identd, identd[:], q="pool")
    ones = P.sbuf("ones", [128, 128], BF16)
    P.memset("dve", ones, ones[:], 1.0)
    onesf = P.sbuf("onesf", [128, 128], F32)
    epst = P.sbuf("epst", [128, 1], F32)
    P.memset("dve", epst, epst[:], LN_EPS)
    P.memset("dve", onesf, onesf[:], 1.0)
    ms = P.sbuf("ms", [128, 8, 4], F32)
    P.dma(ms, ms[:], msT, msT[:])
    scp = P.sbuf("scp", [128, 8, 2], F32)
    for m in range(2):
        P.ts("dve", scp, scp[:, :, m], ms, ms[:, :, 2 * m + 1], 1.0, None, ALU.add)
    gqt = P.sbuf("gqt", [128, 2], F32)
    P.dma(gqt, gqt[:], gq, gq[:])
    gkvt = P.sbuf("gkvt", [128, 1], F32)
    P.dma(gkvt, gkvt[:], gkv, gkv[:])
    wuq_r = P.sbuf("wuq_r", [128, 2, 2, 192], F32)
    for k in range(2):
        P.dma(wuq_r, wuq_r[:, k], wuq, wuq[k * 128:(k + 1) * 128])
    Wuq = P.sbuf("Wuq", [128, 2, 2, 192], BF16)
    for k in range(2):
        P.ts("dve", Wuq, Wuq[:, k], wuq_r, wuq_r[:, k], gqt[:, k:k + 1], None, ALU.mult, extra=[gqt])
    wk_r = P.sbuf("wk_r", [128, 2, 64], F32)
    P.dma(wk_r, wk_r[:], wukvk, wukvk[:])
    Wkk = P.sbuf("Wkk", [128, 2, 64], BF16)
    P.ts("dve", Wkk, Wkk[:], wk_r, wk_r[:], gkvt[:, 0:1], None, ALU.mult, extra=[gkvt])
    wv_r = P.sbuf("wv_r", [128, 128], F32)
    P.dma(wv_r, wv_r[:], wukvv, wukvv[:])
    Wkv = P.sbuf("Wkv", [128, 128], BF16)
    P.ts("dve", Wkv, Wkv[:], wv_r, wv_r[:], gkvt[:, 0:1], None, ALU.mult, extra=[gkvt])
    lv = P.sbuf("lv", [128, 4, 64], F32)
    P.dma(lv, lv[:], lamv, lamv[:])
    lpr = P.sbuf("lpr", [128, 2, 64], F32)
    P.tt("dve", lpr, lpr[:, 0], lv, lv[:, 0], lv, lv[:, 1], ALU.mult)
    P.tt("dve", lpr, lpr[:, 1], lv, lv[:, 2], lv, lv[:, 3], ALU.mult)
    lsum = P.sbuf("lsum", [128, 2], F32)
    P.op("dve", lambda e: e.reduce_sum(lsum[:], lpr[:], AX.X), reads=[lpr], writes=[lsum])
    lexp = P.sbuf("lexp", [128, 2], F32)
    P.act(lexp, lexp[:], lsum, lsum[:], AF.Exp)
    neglam = P.sbuf("neglam", [128, 1], F32)
    P.stt("dve", neglam, neglam[:], lexp, lexp[:, 1:2], -float(lam_init), lexp, lexp[:, 0:1], ALU.add, ALU.subtract)
    gsr = P.sbuf("gsr", [128, 1], F32)
    P.dma(gsr, gsr[:], gsub, gsub[:])
    gss = P.sbuf("gss", [128, 1], F32)
    P.ts("dve", gss, gss[:], gsr, gsr[:], 1.0 - float(lam_init), None, ALU.mult)

    xb = [P.sbuf("xb%d" % i, [128, 1024], F32) for i in range(2)]
    xnb = [P.sbuf("xn%d" % i, [128, 1024], BF16) for i in range(2)]
    st6 = P.sbuf("st6", [128, 2, 6], F32)
    mv = P.sbuf("mv", [128, 2], F32)
    rstd = P.sbuf("rstd", [128, 1], F32)
    htmp = P.sbuf("htmp", [128, 8, 128], F32)
    hTb = [P.sbuf("hT%d" % i, [128, 8, 512], BF16) for i in range(2)]
    t32 = [P.sbuf("t32_%d" % i, [128, 2, 512], F32) for i in range(2)]
    t128 = [P.sbuf("t128_%d" % i, [128, 2, 512], F32) for i in range(2)]
    stg = [P.sbuf("stg%d" % i, [128, 512], BF16) for i in range(3)]
    stgi = [0]

    def nstg():
        s_ = stg[stgi[0] % len(stg)]
        stgi[0] += 1
        return s_
    cqT = P.sbuf("cqT", [128, 2, 512], BF16)
    cqsq = P.sbuf("cqsq", [128, 2, 512], BF16)
    ckvT = P.sbuf("ckvT", [128, 512], BF16)
    ckvsq = P.sbuf("ckvsq", [128, 512], BF16)
    rq = P.sbuf("rq", [128, 512], F32)
    rkv = P.sbuf("rkv", [128, 512], F32)
    rkvt = P.sbuf("rkvt", [128, 4], F32)
    ra = [P.sbuf("ra%d" % i, [128, 512], F32) for i in range(1)]
    rb = [P.sbuf("rb%d" % i, [128, 512], F32) for i in range(1)]
    vst = [P.sbuf("vst%d" % i, [128, 256], BF16) for i in range(2)]
    krT = P.sbuf("krT", [32, 512], BF16)

    def rope(n, pa, pb_, rows, tab, ti, out_t, out_ap, mul_t=None, mul_ap=None):
        a = ra[0]
        b = rb[0]
        P.tt("dve", a, a[rows, :n], pa, pa[rows, :n], tab, tab[rows, 0, :n], ALU.mult)
        P.tt("dve", b, b[rows, :n], pb_, pb_[rows, :n], tab, tab[rows, 1, :n], ALU.mult)
        if mul_t is None:
            P.tt("pool", out_t, out_ap, a, a[rows, :n], b, b[rows, :n], ALU.add)
        else:
            P.tt("pool", a, a[rows, :n], a, a[rows, :n], b, b[rows, :n], ALU.add)
            P.tt("pool", out_t, out_ap, a, a[rows, :n], mul_t, mul_ap, ALU.mult)

    FO = {"qa": 0, "ka": 128, "cq0": 256, "cq1": 384, "ckv": 512, "kr": 640, "krp": 672,
          "dq": 704, "dqp": 832, "dk": 960, "dkp": 1088}

    def proj(hT, name, M, n):
        pbk = P.bank()
        off = FO[name]
        for k in range(8):
            P.mm(pbk, pbk[:M, :n], Wf, Wf[:, k, off:off + M], hT, hT[:, k, :n], start=(k == 0), stop=(k == 7))
        return pbk

    for c in range(NCH):
        tok0, n = chunk_rng(c)
        m = 1 if c == 0 else 0
        nt = n // 128
        hT = hTb[c % 2]
        tb32 = t32[c % 2]
        tb128 = t128[c % 2]
        P.dma(tb32, tb32[0:32, :, :n], cs32, cs32[:, :, tok0:tok0 + n])
        P.dma(tb32, tb32[64:96, :, :n], cs32, cs32[:, :, tok0:tok0 + n])
        P.dma(tb128, tb128[:, :, :n], cs128, cs128[:, :, tok0:tok0 + n])
        for i in range(nt):
            xt = xb[i % 2]
            xn = xnb[i % 2]
            P.dma(xt, xt[:], xin, xin[tok0 + i * 128: tok0 + (i + 1) * 128, :])
            for a_ in range(2):
                P.op("dve", lambda e, xt=xt, a_=a_: e.bn_stats(st6[:, a_, :], xt[:, a_ * 512:(a_ + 1) * 512]),
                     reads=[xt], writes=[st6])
            P.op("dve", lambda e: e.bn_aggr(mv[:], st6[:].rearrange("p a b -> p (a b)")), reads=[st6], writes=[mv])
            P.rsqrt(rstd, rstd[:], mv, mv[:, 1:2], 1.0, epst, epst[:, 0:1])
            P.ts("dve", xn, xn[:], xt, xt[:], mv[:, 0:1], rstd[:, 0:1], ALU.subtract, ALU.mult, extra=[mv, rstd])
            for k in range(8):
                P.tr(tp, tp[:, k * 128:(k + 1) * 128], xn, xn[:, k * 128:(k + 1) * 128], ident, ident[:])
            P.tt("dve", htmp, htmp[:], tp, tp[:].rearrange("p (a b) -> p a b", a=8),
                 scp, scp[:, :, m:m + 1].to_broadcast([128, 8, 128]), ALU.mult)
            P.tt("pool", hT, hT[:, :, i * 128:(i + 1) * 128], htmp, htmp[:],
                 ms, ms[:, :, 2 * m:2 * m + 1].to_broadcast([128, 8, 128]), ALU.add)
        for name, dst in (("qa", qaT), ("ka", kaT)):
            pbk = proj(hT, name, 128, n)
            s_ = nstg()
            P.cp("act", s_, s_[:, :n], pbk, pbk[:, :n])
            P.dma(dst, dst[:, tok0:tok0 + n], s_, s_[:, :n])
        for j, name in enumerate(("cq0", "cq1")):
            pbk = proj(hT, name, 128, n)
            P.cp("act", cqT, cqT[:, j, :n], pbk, pbk[:, :n])
            P.act(cqsq, cqsq[:, j, :n], pbk, pbk[:, :n], AF.Square)
        pbk = proj(hT, "ckv", 128, n)
        P.cp("act", ckvT, ckvT[:, :n], pbk, pbk[:, :n])
        P.act(ckvsq, ckvsq[:, :n], pbk, pbk[:, :n], AF.Square)
        pa = proj(hT, "kr", 32, n)
        pb_ = proj(hT, "krp", 32, n)
        rope(n, pa, pb_, slice(0, 32), tb32, 0, krT, krT[:, :n])
        for h in range(2):
            P.dma(kbT, kbT[h, 64:96, tok0:tok0 + n], krT, krT[:, :n])
        for (nm, nmp, dst, ti) in (("dq", "dqp", dqT, 0), ("dk", "dkp", dkT, 1)):
            pa = proj(hT, nm, 128, n)
            pb_ = proj(hT, nmp, 128, n)
            s_ = nstg()
            rope(n, pa, pb_, slice(0, 128), tb128, ti, s_, s_[:, :n])
            P.dma(dst, dst[:, tok0:tok0 + n], s_, s_[:, :n])
        for i in range(nt):
            pbk = P.bank()
            for k in range(8):
                P.mm(pbk, pbk[:, 0:256], hT, hT[:, k, i * 128:(i + 1) * 128], Wt, Wt[:, k, :], start=(k == 0), stop=(k == 7))
            v_ = vst[i % 2]
            P.cp("act", v_, v_[:], pbk, pbk[:, 0:256])
            r0 = tok0 + i * 128
            P.dma(va, va[r0:r0 + 128, :], v_, v_[:, 0:128])
            P.dma(dvv, dvv[r0:r0 + 128, :], v_, v_[:, 128:256])
        pbk = P.bank()
        P.mm(pbk, pbk[:, :n], ones, ones[:], cqsq, cqsq[:, 0, :n], start=True, stop=False)
        P.mm(pbk, pbk[:, :n], ones, ones[:], cqsq, cqsq[:, 1, :n], start=False, stop=True)
        P.rsqrt(rq, rq[:, :n], pbk, pbk[:, :n], 1.0 / 256, epst, epst[:, 0:1])
        pbk = P.bank()
        P.mm(pbk, pbk[:, :n], ones, ones[:], ckvsq, ckvsq[:, :n])
        P.rsqrt(rkv, rkv[:, :n], pbk, pbk[:, :n], 1.0 / 128, epst, epst[:, 0:1])
        pbk = P.bank()
        for i in range(nt):
            P.mm(pbk, pbk[:, i:i + 1], ckvsq, ckvsq[:, i * 128:(i + 1) * 128], ones, ones[:, 0:1])
        P.rsqrt(rkvt, rkvt[:, :nt], pbk, pbk[:, :nt], 1.0 / 128, epst, epst[:, 0:1])
        for h in range(2):
            qa_ = P.bank()
            for k in range(2):
                P.mm(qa_, qa_[:96, :n], Wuq, Wuq[:, k, h, 0:96], cqT, cqT[:, k, :n], start=(k == 0), stop=(k == 1))
            qb_ = P.bank()
            for k in range(2):
                P.mm(qb_, qb_[:96, :n], Wuq, Wuq[:, k, h, 96:192], cqT, cqT[:, k, :n], start=(k == 0), stop=(k == 1))
            s_ = nstg()
            P.tt("dve", s_, s_[0:64, :n], qa_, qa_[0:64, :n], rq, rq[0:64, :n], ALU.mult)
            rope(n, qa_, qb_, slice(64, 96), tb32, h, s_, s_[64:96, :n], rq, rq[64:96, :n])
            P.dma(qbT, qbT[h, :, tok0:tok0 + n], s_, s_[0:96, :n])
            kn = P.bank()
            P.mm(kn, kn[:64, :n], Wkk, Wkk[:, h, :], ckvT, ckvT[:, :n])
            s_ = nstg()
            P.tt("dve", s_, s_[0:64, :n], kn, kn[0:64, :n], rkv, rkv[0:64, :n], ALU.mult)
            P.dma(kbT, kbT[h, 0:64, tok0:tok0 + n], s_, s_[0:64, :n])
        for i in range(nt):
            pbk = P.bank()
            P.mm(pbk, pbk[:, 0:128], ckvT, ckvT[:, i * 128:(i + 1) * 128], Wkv, Wkv[:])
            v_ = vst[i % 2]
            P.ts("dve", v_, v_[:, 0:128], pbk, pbk[:, 0:128], rkvt[:, i:i + 1], None, ALU.mult, extra=[rkvt])
            r0 = tok0 + i * 128
            P.dma(vb, vb[r0:r0 + 128, :], v_, v_[:, 0:128])
    if dbg == "A1":
        P.finish()
        return P.build()

    BK = P.sbuf("BK", [128, NT], BF16)
    BV = P.sbuf("BV", [128, 130, 130], BF16)
    LOOK = 2
    pts = [P.sbuf("pt%d" % i, [128, 512], BF16) for i in range(5)]
    pti = [0]

    def npt():
        t_ = pts[pti[0] % 5]
        pti[0] += 1
        return t_
    qcs = [P.sbuf("qc%d" % i, [128, 512], BF16) for i in range(2)]
    rec = P.sbuf("rec", [128, 512], F32)
    rec2 = P.sbuf("rec2", [128, 512], F32)
    bsb = P.sbuf("bsb", [128, 512], F32)
    ysb = [P.sbuf("ysb%d" % i, [128, 512], BF16) for i in range(2)]
    yf = P.sbuf("yf", [128, 512], F32)
    yf2 = P.sbuf("yf2", [128, 512], F32)
    tmpf = [P.sbuf("tmpf%d" % i, [128, 256], F32) for i in range(2)]
    nb = P.sbuf("nb", [128, 3, 6, 256], F32)
    acc_banks = [P.pb[0], P.pb[1]]
    bb_bank = P.pb[2]
    s_banks = [P.pb[3], P.pb[4], P.pb[5], P.pb[6]]
    cnt = {"acc": 0, "s": 0, "y": 0}

    def sbank():
        b = s_banks[cnt["s"] % 4]
        cnt["s"] += 1
        return b

    def finalize(acc, n, dst, drow, tok0):
        P.op("dve", lambda e: e.reciprocal(rec[64:65, :n], acc[64:65, :n]), reads=[acc], writes=[rec])
        P.mm(bb_bank, bb_bank[:64, :n], onesf, onesf[64:65, 0:64], rec, rec[64:65, :n])
        P.cp("act", bsb, bsb[:64, :n], bb_bank, bb_bank[:64, :n])
        y_ = ysb[cnt["y"] % 2]
        cnt["y"] += 1
        P.tt("dve", y_, y_[:64, :n], acc, acc[:64, :n], bsb, bsb[:64, :n], ALU.mult)
        P.dma(dst, dst[drow:drow + 64, tok0:tok0 + n], y_, y_[:64, :n])

    for q4 in range(4):
        P.dma(BK, BK[:, q4 * 4160:(q4 + 1) * 4160], kaT, kaT[:, q4 * 4160:(q4 + 1) * 4160])
    BV4 = BV[:].rearrange("p t (h d) -> p t h d", h=2)
    for h in range(2):
        P.dma(BV, BV4[:, :, h, 0:64], va, va[:, h * 64:(h + 1) * 64].rearrange("(t p) d -> p t d", p=128))
    P.memset("pool", BV, BV4[:, :, :, 64:65], 1.0)
    for h in range(2):
        P.dma(nb, nb[:], nabias, nabias[h].rearrange("v (j p) q -> p v j q", p=128))
        for blk in range(-1, 64):
            if blk < 0:
                qtok0, tiles = 0, [(0, 0, None), (128, 1, None)]
            else:
                r0 = blk * 4
                var = 0 if blk == 0 else (2 if blk == 63 else 1)
                se = 0 if blk == 0 else (244 if blk == 63 else r0 - 4)
                qtok0 = 256 + r0 * 64
                tiles = [(0, 0, None), (128, 1, None)]
                for j in range(6):
                    tiles.append((256 + se * 64 + j * 128, 2 + se // 2 + j, (var, j)))
            qc = qcs[cnt["acc"] % 2]
            P.dma(qc, qc[:, :256], qaT, qaT[:, qtok0:qtok0 + 256])
            acc = acc_banks[cnt["acc"] % 2]
            cnt["acc"] += 1
            pend = []
            for idx in range(len(tiles) + LOOK):
                if idx < len(tiles):
                    kcol, vt, bi = tiles[idx]
                    sb = sbank()
                    P.mm(sb, sb[:, :256], BK, BK[h * 64:(h + 1) * 64, kcol:kcol + 128], qc, qc[h * 64:(h + 1) * 64, :256])
                    pt = npt()
                    if bi is None:
                        P.act(pt, pt[:, :256], sb, sb[:, :256], AF.Exp, scale=0.125)
                    else:
                        tf = tmpf[idx % 2]
                        P.stt("dve", tf, tf[:], sb, sb[:, :256], 0.125, nb, nb[:, bi[0], bi[1], :], ALU.mult, ALU.add)
                        P.act(pt, pt[:, :256], tf, tf[:], AF.Exp)
                    pend.append(pt)
                j = idx - LOOK
                if j >= 0:
                    pt = pend[j]
                    P.mm(acc, acc[:65, :256], BV, BV4[:, tiles[j][1], h, :], pt, pt[:, :256], start=(j == 0), stop=(j == len(tiles) - 1))
            finalize(acc, 256, yaT, h * 64, qtok0)

    BV3 = BV[:]
    sc_b = 96.0 ** -0.5
    for h in range(2):
        for q4 in range(4):
            P.dma(BK, BK[0:96, q4 * 4160:(q4 + 1) * 4160], kbT, kbT[h, :, q4 * 4160:(q4 + 1) * 4160])
        P.dma(BV, BV3[:, :, 0:64], vb, vb[:, h * 64:(h + 1) * 64].rearrange("(t p) d -> p t d", p=128))
        P.memset("pool", BV, BV3[:, :, 64:65], 1.0)
        for c in range(NCH):
            tok0, n = chunk_rng(c)
            kts = [0, 1] if c == 0 else list(range(130))
            qc = qcs[cnt["acc"] % 2]
            P.dma(qc, qc[0:96, :n], qbT, qbT[h, :, tok0:tok0 + n])
            acc = acc_banks[cnt["acc"] % 2]
            cnt["acc"] += 1
            pend = []
            for idx in range(len(kts) + LOOK):
                if idx < len(kts):
                    kt = kts[idx]
                    sb = sbank()
                    P.mm(sb, sb[:, :n], BK, BK[0:96, kt * 128:(kt + 1) * 128], qc, qc[0:96, :n])
                    pt = npt()
                    P.act(pt, pt[:, :n], sb, sb[:, :n], AF.Exp, scale=sc_b)
                    pend.append(pt)
                j = idx - LOOK
                if j >= 0:
                    pt = pend[j]
                    P.mm(acc, acc[:65, :n], BV, BV3[:, kts[j], 0:65], pt, pt[:, :n], start=(j == 0), stop=(j == len(kts) - 1))
            finalize(acc, n, ybT, h * 64, tok0)

    for q4 in range(4):
        P.dma(BK, BK[:, q4 * 4160:(q4 + 1) * 4160], dkT, dkT[:, q4 * 4160:(q4 + 1) * 4160])
    P.dma(BV, BV3[:, :, 0:128], dvv, dvv[:, :].rearrange("(t p) d -> p t d", p=128))
    O1, O2 = P.pb[0], P.pb[1]
    s3 = [P.pb[2], P.pb[3], P.pb[4], P.pb[5]]
    S2 = P.pb[6]
    sacc = [P.sbuf("sacc%d" % i, [128, 512], F32) for i in range(1)]
    for c in range(NCH):
        tok0, n = chunk_rng(c)
        kts = [0, 1] if c == 0 else list(range(130))
        qc = qcs[c % 2]
        P.dma(qc, qc[:, :n], dqT, dqT[:, tok0:tok0 + n])
        pend = []
        for idx in range(len(kts) + 1):
            if idx < len(kts):
                kt = kts[idx]
                pp = []
                for half in range(2):
                    sb = s3[cnt["s"] % 4]
                    cnt["s"] += 1
                    rows = slice(half * 64, (half + 1) * 64)
                    P.mm(sb, sb[:, :n], BK, BK[rows, kt * 128:(kt + 1) * 128], qc, qc[rows, :n])
                    pt = npt()
                    P.act(pt, pt[:, :n], sb, sb[:, :n], AF.Exp, scale=0.125)
                    pp.append(pt)
                pend.append(pp)
            j = idx - 1
            if j >= 0:
                st_, sp_ = (j == 0), (j == len(kts) - 1)
                for half, O in enumerate((O1, O2)):
                    pt = pend[j][half]
                    P.mm(O, O[:, :n], BV, BV3[:, kts[j], 0:128], pt, pt[:, :n], start=st_, stop=sp_)
                    if half == 0:
                        sa = sacc[0]
                        if j == 0:
                            P.cp("dve", sa, sa[:, :n], pt, pt[:, :n])
                        else:
                            P.tt("dve", sa, sa[:, :n], sa, sa[:, :n], pt, pt[:, :n], ALU.add)
                    else:
                        P.mm(S2, S2[:, :n], ones, ones[:], pt, pt[:, :n], start=st_, stop=sp_)
        sb = s3[cnt["s"] % 4]
        cnt["s"] += 1
        P.mm(sb, sb[:, :n], onesf, onesf[:, :], sacc[0], sacc[0][:, :n])
        P.op("dve", lambda e, n=n, sb=sb: e.reciprocal(rec[:, :n], sb[:, :n]), reads=[sb], writes=[rec])
        P.op("dve", lambda e, n=n: e.reciprocal(rec2[:, :n], S2[:, :n]), reads=[S2], writes=[rec2])
        P.tt("dve", yf, yf[:, :n], O1, O1[:, :n], rec, rec[:, :n], ALU.mult)
        P.tt("dve", yf2, yf2[:, :n], O2, O2[:, :n], rec2, rec2[:, :n], ALU.mult)
        P.stt("dve", yf, yf[:, :n], yf2, yf2[:, :n], neglam[:, 0:1], yf, yf[:, :n], ALU.mult, ALU.add, extra=[neglam])
        pt = npt()
        P.act(pt, pt[:, :n], yf, yf[:, :n], AF.Square)
        sb = s3[cnt["s"] % 4]
        cnt["s"] += 1
        P.mm(sb, sb[:, :n], ones, ones[:], pt, pt[:, :n])
        P.rsqrt(bsb, bsb[:, :n], sb, sb[:, :n], 1.0 / 128, epst, epst[:, 0:1])
        P.tt("dve", yf, yf[:, :n], yf, yf[:, :n], bsb, bsb[:, :n], ALU.mult)
        y_ = ysb[c % 2]
        P.ts("pool", y_, y_[:, :n], yf, yf[:, :n], gss[:, 0:1], None, ALU.mult, extra=[gss])
        P.dma(ycT, ycT[:, tok0:tok0 + n], y_, y_[:, :n])
    P.finish()
    return P.build()


def _pm(vec):
    return np.ascontiguousarray(np.asarray(vec).reshape(-1, 128).T)


def _partner_perm(D):
    m = D // 2
    half = m // 2
    perm = np.zeros(D, np.int64)
    sign = np.zeros(D, np.float32)
    for d in range(D):
        seg, i = d // m, d % m
        if i < half:
            perm[d] = seg * m + i + half
            sign[d] = -1.0
        else:
            perm[d] = seg * m + i - half
            sign[d] = 1.0
    return perm, sign


def _rope_table(D):
    m = D // 2
    half = m // 2
    inv = (np.float32(10000.0) ** (-np.arange(0, m, 2, dtype=np.float32) / np.float32(m))).astype(np.float32)
    t = np.arange(16384)
    row = (t // 64).astype(np.float32)
    col = (t % 64).astype(np.float32)
    _, sign = _partner_perm(D)
    tab = np.zeros((D, 2, NT), np.float32)
    tab[:, 0, :256] = 1.0
    for d in range(D):
        seg, i = d // m, d % m
        fi = i % half
        pos = row if seg == 0 else col
        ang = (pos * inv[fi]).astype(np.float32)
        tab[d, 0, 256:] = np.cos(ang)
        tab[d, 1, 256:] = sign[d] * np.sin(ang)
    return tab


def _na_bias(rpb2):
    out = np.full((2, 3, 768, 256), -30000.0, np.float32)
    for var, (r0, se) in enumerate(((0, 0), (8, 4), (252, 244))):
        kr = se + np.arange(768) // 64
        kc = np.arange(768) % 64
        r = r0 + np.arange(256) // 64
        c = np.arange(256) % 64
        rs = np.clip(r - 4, 0, 248)
        cs = np.clip(c - 8, 0, 48)
        okr = (kr[:, None] >= rs[None, :]) & (kr[:, None] < rs[None, :] + 8)
        okc = (kc[:, None] >= cs[None, :]) & (kc[:, None] < cs[None, :] + 16)
        ok = okr & okc
        ro = np.clip(kr[:, None] - r[None, :] + 7, 0, 14)
        co = np.clip(kc[:, None] - c[None, :] + 15, 0, 30)
        for h in range(2):
            g = rpb2[h][ro, co]
            out[h, var] = np.where(ok, g, np.float32(-30000.0))
    return out


_CONST = {}


def _consts():
    if not _CONST:
        _CONST["cs32"] = _rope_table(32)
        t64 = _rope_table(64)
        _CONST["cs128"] = np.ascontiguousarray(np.concatenate([t64, t64], 0))
        _CONST["ident"] = np.eye(128, dtype=np.float32)
    return _CONST


def prep_A(inp, l, mod, x_cur, ctx_cur):
    C = _consts()
    w_in = inp["w_in"][l]
    p32, _ = _partner_perm(32)
    p64, _ = _partner_perm(64)
    p128 = np.concatenate([p64, 64 + p64])
    maps = []
    for core in range(8):
        b, g = core // 4, core % 4
        xin = np.ascontiguousarray(np.concatenate([ctx_cur[b], x_cur[b]], 0))
        msT = np.stack([_pm(mod[0:1024, b]), _pm(mod[1024:2048, b]), _pm(mod[0:1024, 2]), _pm(mod[1024:2048, 2])], -1)
        kr = w_in[:, 1920:1952]
        dq = w_in[:, 1952 + 128 * g: 1952 + 128 * (g + 1)]
        dk = w_in[:, 2464 + 128 * g: 2464 + 128 * (g + 1)]
        wfeat = np.concatenate([
            w_in[:, 128 * g:128 * (g + 1)], w_in[:, 512 + 128 * g:512 + 128 * (g + 1)],
            w_in[:, 1536:1664], w_in[:, 1664:1792], w_in[:, 1792:1920],
            kr, kr[:, p32], dq, dq[:, p128], dk, dk[:, p128]], 1)
        wtok = np.concatenate([w_in[:, 1024 + 128 * g:1024 + 128 * (g + 1)], w_in[:, 2976 + 128 * g:2976 + 128 * (g + 1)]], 1)
        wuq = np.zeros((256, 2, 192), np.float32)
        wukvk = np.zeros((128, 2, 64), np.float32)
        wukvv = np.zeros((128, 128), np.float32)
        for hh in range(2):
            H = 2 * g + hh
            uq = inp["mla_w_uq"][l][:, H * 96:(H + 1) * 96]
            rope_c = uq[:, 64:96]
            wuq[:, hh, 0:96] = uq
            wuq[:, hh, 160:192] = rope_c[:, p32]
            ukv = inp["mla_w_ukv"][l][:, H * 128:(H + 1) * 128]
            wukvk[:, hh, :] = ukv[:, 0:64]
            wukvv[:, hh * 64:(hh + 1) * 64] = ukv[:, 64:128]
        lamv = np.stack([inp["diff_lq1"][l], inp["diff_lk1"][l], inp["diff_lq2"][l], inp["diff_lk2"][l]], 0)
        maps.append({
            "xin": xin, "msT": np.ascontiguousarray(msT), "wfeat": np.ascontiguousarray(wfeat),
            "wtok": np.ascontiguousarray(wtok), "wuq": wuq, "wukvk": wukvk, "wukvv": wukvv,
            "gq": _pm(inp["mla_g_q"][l]), "gkv": _pm(inp["mla_g_kv"][l]),
            "cs32": C["cs32"], "cs128": C["cs128"],
            "nabias": _na_bias(inp["na_rpb"][l][2 * g:2 * g + 2]),
            "lamv": np.ascontiguousarray(np.broadcast_to(lamv[None], (128, 4, 64))),
            "gsub": _pm(inp["diff_g_sub"][l]), "ident": C["ident"],
        })
    return maps


NB = 4160
ALPHA = 4.0 ** 0.25


def chunkB(c):
    return (0, 64) if c == 0 else (64 + (c - 1) * 512, 512)


def ln_tile(P, src_t, src_ap, np_, st6, mv, rstd, epst, out_t, out_ap):
    for a_ in range(2):
        P.op("dve", lambda e, a_=a_: e.bn_stats(st6[:np_, a_, :], src_ap[:, a_ * 512:(a_ + 1) * 512]),
             reads=[src_t], writes=[st6])
    P.op("dve", lambda e: e.bn_aggr(mv[:np_, :], st6[:np_].rearrange("p a b -> p (a b)")), reads=[st6], writes=[mv])
    P.rsqrt(rstd, rstd[:np_, :], mv, mv[:np_, 1:2], 1.0, epst, epst[:np_, 0:1])
    P.ts("dve", out_t, out_ap, src_t, src_ap, mv[:np_, 0:1], rstd[:np_, 0:1], ALU.subtract, ALU.mult, extra=[mv, rstd])


def build_B():
    P = Prog("B")
    EI, EO = "ExternalInput", "ExternalOutput"
    xin = P.dram("xin", [NB, 1024], F32, EI)
    yT = P.dram("yT", [1536, NB], BF16, EI)
    msT = P.dram("msT", [128, 8, 4], F32, EI)
    wgate = P.dram("wgate", [1024, 3072], F32, EI)
    bgT = P.dram("bgT", [128, 24], F32, EI)
    wbr = P.dram("wbr", [1536, 1024], F32, EI)
    wout = P.dram("wout", [1024, 1024], F32, EI)
    wrt = P.dram("wrt", [128, 8, 16], F32, EI)
    rows = P.dram("rows", [128, 8, 1024], F32, EI)
    identd = P.dram("ident", [128, 128], F32, EI)
    x1o = P.dram("x1", [NB, 1024], F32, EO)
    h2tmo = P.dram("h2tm", [NB, 1024], BF16, EO)
    affo = P.dram("aff", [NB, 16], F32, EO)
    P.start()
    P.banks(7)
    tp = P.psum("tp", [128, 1024], BF16)
    Wg = P.sbuf("Wg", [128, 8, 3072], BF16)
    Wb = P.sbuf("Wb", [128, 12, 1024], BF16)
    Wo = P.sbuf("Wo", [128, 8, 1024], BF16)
    for k in range(8):
        P.dma(Wg, Wg[:, k, :], wgate, wgate[k * 128:(k + 1) * 128, :], q="pool")
        P.dma(Wo, Wo[:, k, :], wout, wout[k * 128:(k + 1) * 128, :], q="pool")
    for k in range(12):
        P.dma(Wb, Wb[:, k, :], wbr, wbr[k * 128:(k + 1) * 128, :], q="pool")
    ident = P.sbuf("identb", [128, 128], BF16)
    P.dma(ident, ident[:], identd, identd[:], q="pool")
    bg = P.sbuf("bg", [128, 24], F32)
    P.dma(bg, bg[:], bgT, bgT[:])
    ms = P.sbuf("ms", [128, 8, 4], F32)
    P.dma(ms, ms[:], msT, msT[:])
    scp = P.sbuf("scp", [128, 8, 2], F32)
    for m in range(2):
        P.ts("dve", scp, scp[:, :, m], ms, ms[:, :, 2 * m + 1], 1.0, None, ALU.add)
    rw = P.sbuf("rw", [128, 8, 1024], F32)
    P.dma(rw, rw[:], rows, rows[:])
    for j in (4, 6):
        P.ts("dve", rw, rw[:, j, :], rw, rw[:, j, :], 1.0, None, ALU.add)
    wr_f = P.sbuf("wr_f", [128, 8, 16], F32)
    P.dma(wr_f, wr_f[:], wrt, wrt[:])
    wr_hi = P.sbuf("wr_hi", [128, 8, 16], BF16)
    wr_lo = P.sbuf("wr_lo", [128, 8, 16], BF16)
    P.cp("dve", wr_hi, wr_hi[:], wr_f, wr_f[:])
    P.tt("dve", wr_lo, wr_lo[:], wr_f, wr_f[:], wr_hi, wr_hi[:], ALU.subtract)
    epst = P.sbuf("epst", [128, 1], F32)
    P.memset("dve", epst, epst[:], LN_EPS)

    xc = P.sbuf("xc", [128, 4, 1024], F32)
    xnb = [P.sbuf("xn%d" % i, [128, 1024], BF16) for i in range(2)]
    st6 = P.sbuf("st6", [128, 2, 6], F32)
    mv = P.sbuf("mv", [128, 2], F32)
    rstd = P.sbuf("rstd", [128, 1], F32)
    htmp = P.sbuf("htmp", [128, 8, 128], F32)
    hT = P.sbuf("hT", [128, 8, 512], BF16)
    yt = P.sbuf("yt", [128, 12, 512], BF16)
    sig = [P.sbuf("sig%d" % i, [128, 512], F32) for i in range(2)]
    mf = P.sbuf("mf", [128, 512], F32)
    mt2 = P.sbuf("mt2", [128, 512], F32)
    mT = P.sbuf("mT", [128, 8, 512], BF16)
    u = P.sbuf("u", [128, 1024], F32)
    ut = P.sbuf("ut", [128, 1024], F32)
    x1 = [P.sbuf("x1_%d" % i, [128, 1024], F32) for i in range(1)]
    h2 = P.sbuf("h2", [128, 1024], F32)
    h2hi = P.sbuf("h2hi", [128, 1024], BF16)
    h2lo = P.sbuf("h2lo", [128, 1024], BF16)
    h2Ts = [P.sbuf("h2Ts%d" % i, [128, 8, 128], BF16) for i in range(1)]
    h2Tl = P.sbuf("h2Tl", [128, 8, 128], BF16)
    lg = P.sbuf("lg", [128, 16], F32)
    mx = P.sbuf("mx", [128, 1], F32)
    sm = P.sbuf("sm", [128, 1], F32)
    affs = [P.sbuf("affs%d" % i, [128, 16], F32) for i in range(2)]

    for c in range(9):
        tok0, n = chunkB(c)
        m = 1 if c == 0 else 0
        tsz = 64 if c == 0 else 128
        nt = n // tsz
        P.dma(yt, yt[:, :, :n], yT, yT[:, tok0:tok0 + n].rearrange("(j p) t -> p j t", p=128))
        for i in range(nt):
            xn = xnb[i % 2]
            P.dma(xc, xc[:tsz, i, :], xin, xin[tok0 + i * tsz: tok0 + (i + 1) * tsz, :])
            ln_tile(P, xc, xc[:tsz, i, :], tsz, st6, mv, rstd, epst, xn, xn[:tsz, :])
            for k in range(8):
                P.tr(tp, tp[:, k * 128:k * 128 + tsz], xn, xn[:tsz, k * 128:(k + 1) * 128], ident, ident[:tsz, :tsz])
            P.tt("dve", htmp, htmp[:, :, :tsz], tp, tp[:].rearrange("p (a b) -> p a b", a=8)[:, :, :tsz],
                 scp, scp[:, :, m:m + 1].to_broadcast([128, 8, tsz]), ALU.mult)
            P.tt("pool", hT, hT[:, :, i * tsz:(i + 1) * tsz], htmp, htmp[:, :, :tsz],
                 ms, ms[:, :, 2 * m:2 * m + 1].to_broadcast([128, 8, tsz]), ALU.add)
        for oc in range(8):
            for br in range(3):
                G = P.bank()
                for k in range(8):
                    P.mm(G, G[:, :n], Wg, Wg[:, k, br * 1024 + oc * 128: br * 1024 + (oc + 1) * 128], hT, hT[:, k, :n],
                         start=(k == 0), stop=(k == 7))
                sg = sig[br % 2]
                P.act(sg, sg[:, :n], G, G[:, :n], AF.Sigmoid, extra=[bg], bias=bg[:, br * 8 + oc: br * 8 + oc + 1])
                Pb = P.bank()
                for k in range(4):
                    P.mm(Pb, Pb[:, :n], Wb, Wb[:, br * 4 + k, oc * 128:(oc + 1) * 128], yt, yt[:, br * 4 + k, :n],
                         start=(k == 0), stop=(k == 3))
                if br == 0:
                    P.tt("dve", mf, mf[:, :n], Pb, Pb[:, :n], sg, sg[:, :n], ALU.mult)
                else:
                    P.tt("dve", mt2, mt2[:, :n], Pb, Pb[:, :n], sg, sg[:, :n], ALU.mult)
                    P.tt("pool", mf, mf[:, :n], mf, mf[:, :n], mt2, mt2[:, :n], ALU.add)
            P.cp("pool", mT, mT[:, oc, :n], mf, mf[:, :n])
        for i in range(nt):
            for half in range(2):
                O = P.bank()
                for k in range(8):
                    P.mm(O, O[:tsz, :], mT, mT[:, k, i * tsz:(i + 1) * tsz], Wo, Wo[:, k, half * 512:(half + 1) * 512],
                         start=(k == 0), stop=(k == 7))
                hs = slice(half * 512, (half + 1) * 512)
                P.tt("dve", ut, ut[:tsz, hs], O, O[:tsz, :], rw, rw[:tsz, m, hs], ALU.mult)
            P.stt("dve", u, u[:tsz, :], xc, xc[:tsz, i, :], ALPHA, ut, ut[:tsz, :], ALU.mult, ALU.add)
            x1t = x1[0]
            ln_tile(P, u, u[:tsz, :], tsz, st6, mv, rstd, epst, x1t, x1t[:tsz, :])
            P.tt("dve", x1t, x1t[:tsz, :], x1t, x1t[:tsz, :], rw, rw[:tsz, 2, :], ALU.mult)
            P.tt("pool", x1t, x1t[:tsz, :], x1t, x1t[:tsz, :], rw, rw[:tsz, 3, :], ALU.add)
            r0 = tok0 + i * tsz
            P.dma(x1o, x1o[r0:r0 + tsz, :], x1t, x1t[:tsz, :])
            ln_tile(P, x1t, x1t[:tsz, :], tsz, st6, mv, rstd, epst, h2, h2[:tsz, :])
            P.tt("dve", h2, h2[:tsz, :], h2, h2[:tsz, :], rw, rw[:tsz, 4 + 2 * m, :], ALU.mult)
            P.tt("pool", h2, h2[:tsz, :], h2, h2[:tsz, :], rw, rw[:tsz, 5 + 2 * m, :], ALU.add)
            P.cp("act", h2hi, h2hi[:tsz, :], h2, h2[:tsz, :])
            P.tt("dve", h2lo, h2lo[:tsz, :], h2, h2[:tsz, :], h2hi, h2hi[:tsz, :], ALU.subtract)
            hs_ = h2Ts[0]
            for k in range(8):
                P.tr(tp, tp[:, k * 128:k * 128 + tsz], h2hi, h2hi[:tsz, k * 128:(k + 1) * 128], ident, ident[:tsz, :tsz])
            P.cp("act", hs_, hs_[:, :, :tsz], tp, tp[:].rearrange("p (a b) -> p a b", a=8)[:, :, :tsz])
            P.dma(h2tmo, h2tmo[r0:r0 + tsz, :], h2hi, h2hi[:tsz, :])
            for k in range(8):
                P.tr(tp, tp[:, k * 128:k * 128 + tsz], h2lo, h2lo[:tsz, k * 128:(k + 1) * 128], ident, ident[:tsz, :tsz])
            P.cp("dve", h2Tl, h2Tl[:, :, :tsz], tp, tp[:].rearrange("p (a b) -> p a b", a=8)[:, :, :tsz])
            L = P.bank()
            for k in range(8):
                P.mm(L, L[:tsz, 0:16], hs_, hs_[:, k, :tsz], wr_hi, wr_hi[:, k, :], start=(k == 0), stop=False)
                P.mm(L, L[:tsz, 0:16], hs_, hs_[:, k, :tsz], wr_lo, wr_lo[:, k, :], start=False, stop=False)
                P.mm(L, L[:tsz, 0:16], h2Tl, h2Tl[:, k, :tsz], wr_hi, wr_hi[:, k, :], start=False, stop=(k == 7))
            P.op("dve", lambda e, L=L, tsz=tsz: e.reduce_max(mx[:tsz, :], L[:tsz, 0:16], AX.X), reads=[L], writes=[mx])
            P.ts("dve", lg, lg[:tsz, :], L, L[:tsz, 0:16], mx[:tsz, 0:1], None, ALU.subtract, extra=[mx])
            P.act(lg, lg[:tsz, :], lg, lg[:tsz, :], AF.Exp)
            P.op("dve", lambda e, tsz=tsz: e.reduce_sum(sm[:tsz, :], lg[:tsz, :], AX.X), reads=[lg], writes=[sm])
            P.op("dve", lambda e, tsz=tsz: e.reciprocal(sm[:tsz, :], sm[:tsz, :]), reads=[sm], writes=[sm])
            af_ = affs[i % 2]
            P.ts("dve", af_, af_[:tsz, :], lg, lg[:tsz, :], sm[:tsz, 0:1], None, ALU.mult, extra=[sm])
            P.dma(affo, affo[r0:r0 + tsz, :], af_, af_[:tsz, :])
    P.finish()
    return P.build()


def prep_B(inp, l, mod, x_cur, ctx_cur, yTs):
    C = _consts()
    wbr = np.ascontiguousarray(np.concatenate([inp["w_br_a"][l], inp["w_br_b"][l], inp["w_br_c"][l]], 0))
    maps = []
    for core in range(8):
        b, s = core // 4, core % 4
        xin = np.concatenate([ctx_cur[b][64 * s:64 * (s + 1)], x_cur[b][4096 * s:4096 * (s + 1)]], 0)
        yT = np.concatenate([yTs[b][:, 64 * s:64 * (s + 1)], yTs[b][:, 256 + 4096 * s:256 + 4096 * (s + 1)]], 1)
        msT = np.stack([_pm(mod[0:1024, b]), _pm(mod[1024:2048, b]), _pm(mod[0:1024, 2]), _pm(mod[1024:2048, 2])], -1)
        rows = np.stack([mod[2048:3072, b], mod[2048:3072, 2], inp["ln1_g"][l], inp["ln1_b"][l],
                         mod[4096:5120, b], mod[3072:4096, b], mod[4096:5120, 2], mod[3072:4096, 2]], 0)
        maps.append({
            "xin": np.ascontiguousarray(xin), "yT": np.ascontiguousarray(yT), "msT": np.ascontiguousarray(msT),
            "wgate": inp["w_gate"][l], "bgT": _pm(inp["b_gate"][l]), "wbr": wbr, "wout": inp["w_out"][l],
            "wrt": np.ascontiguousarray(inp["w_router"][l].reshape(8, 128, 16).transpose(1, 0, 2)),
            "rows": np.ascontiguousarray(np.broadcast_to(rows[None], (128, 8, 1024))), "ident": C["ident"],
        })
    return maps


def chunkC(c):
    return (0, 256) if c == 0 else (256 + (c - 1) * 1024, 1024)


def build_C_dense():
    P = Prog("C")
    EI, EO = "ExternalInput", "ExternalOutput"
    h2T = P.dram("h2T", [1024, NT], BF16, EI)
    affd = P.dram("affL", [128, 4, 130], F32, EI)
    wg = P.dram("wg", [4, 1024, 2816], F32, EI)
    wu = P.dram("wu", [4, 1024, 2816], F32, EI)
    wd = P.dram("wd", [4, 2816, 1024], F32, EI)
    fpo = P.dram("fp", [NT, 1024], F32, EO)
    P.start()
    P.banks(7)
    onesf = P.sbuf("onesf", [128, 128], F32)
    P.memset("dve", onesf, onesf[:], 1.0)
    aff = P.sbuf("aff", [128, 4, 130], F32)
    P.dma(aff, aff[:], affd, affd[:])
    lo = P.sbuf("lo", [128, 8], F32)
    hi = P.sbuf("hi", [128, 8], F32)
    mid = P.sbuf("mid", [128, 8], F32)
    kv = P.sbuf("kv", [128, 8], F32)
    cnt = P.sbuf("cnt", [128, 8], F32)
    ge = P.sbuf("ge", [128, 8], F32)
    d1 = P.sbuf("d1", [128, 8], F32)
    cmp_ = P.sbuf("cmp", [128, 128], F32)
    P.memset("dve", lo, lo[:], 0.0)
    P.memset("dve", hi, hi[:], 1.0)
    P.memset("dve", kv, kv[:, 0:4], 2048.0)
    P.memset("dve", kv, kv[:, 4:8], 32.0)
    for it in range(30):
        P.tt("dve", mid, mid[:], lo, lo[:], hi, hi[:], ALU.add)
        P.ts("dve", mid, mid[:], mid, mid[:], 0.5, None, ALU.mult)
        for e in range(4):
            P.ts("dve", cmp_, cmp_[:, 0:128], aff, aff[:, e, 2:130], mid[:, e:e + 1], None, ALU.is_ge, extra=[mid])
            P.op("dve", lambda e_, e=e: e_.reduce_sum(cnt[:, e:e + 1], cmp_[:, 0:128], AX.X), reads=[cmp_], writes=[cnt])
            P.ts("dve", cmp_, cmp_[:, 0:2], aff, aff[:, e, 0:2], mid[:, 4 + e:5 + e], None, ALU.is_ge, extra=[mid])
            P.op("dve", lambda e_, e=e: e_.reduce_sum(cnt[:, 4 + e:5 + e], cmp_[:, 0:2], AX.X), reads=[cmp_], writes=[cnt])
        tb = P.bank()
        P.mm(tb, tb[:, 0:8], onesf, onesf[:], cnt, cnt[:])
        P.tt("dve", ge, ge[:], tb, tb[:, 0:8], kv, kv[:], ALU.is_ge)
        P.tt("dve", d1, d1[:], mid, mid[:], lo, lo[:], ALU.subtract)
        P.tt("dve", d1, d1[:], d1, d1[:], ge, ge[:], ALU.mult)
        P.tt("dve", lo, lo[:], lo, lo[:], d1, d1[:], ALU.add)
        P.tt("dve", d1, d1[:], hi, hi[:], mid, mid[:], ALU.subtract)
        P.tt("dve", d1, d1[:], d1, d1[:], ge, ge[:], ALU.mult)
        P.tt("dve", hi, hi[:], mid, mid[:], d1, d1[:], ALU.add)
    gm = P.sbuf("gm", [128, 4, 130], F32)
    for e in range(4):
        P.ts("dve", gm, gm[:, e, 2:130], aff, aff[:, e, 2:130], lo[:, e:e + 1], None, ALU.is_ge, extra=[lo])
        P.ts("dve", gm, gm[:, e, 0:2], aff, aff[:, e, 0:2], lo[:, 4 + e:5 + e], None, ALU.is_ge, extra=[lo])
    P.tt("dve", gm, gm[:], gm, gm[:], aff, aff[:], ALU.mult)

    hxs = [P.sbuf("hx%d" % i, [128, 8, 1024], BF16) for i in range(2)]
    facc = P.sbuf("facc", [128, 8, 1024], F32)
    NWB = 4
    wgs = [P.sbuf("wgu%d" % i, [128, 8, 256], BF16) for i in range(NWB)]
    wus = [P.sbuf("wuu%d" % i, [128, 8, 256], BF16) for i in range(NWB)]
    wds = [P.sbuf("wdu%d" % i, [128, 2, 1024], BF16) for i in range(NWB)]
    hids = [P.sbuf("hid%d" % i, [128, 2, 1024], BF16) for i in range(2)]
    sgs = [P.sbuf("sg%d" % i, [128, 512], F32) for i in range(2)]
    ui = 0
    si = 0
    for c in range(17):
        tok0, n = chunkC(c)
        hx = hxs[c % 2]
        P.dma(hx, hx[:, :, :n], h2T, h2T[:, tok0:tok0 + n].rearrange("(k p) t -> p k t", p=128))
        P.memset("pool", facc, facc[:], 0.0)
        units = [(e, un) for e in range(4) for un in range(11)]
        pend = []
        for idx in range(len(units) + 1):
            if idx < len(units):
                e, un = units[idx]
                f0 = un * 256
                wgu, wuu, wdu, hid = wgs[ui % NWB], wus[ui % NWB], wds[ui % NWB], hids[ui % 2]
                ui += 1
                P.dma(wgu, wgu[:], wg, wg[e][:, f0:f0 + 256].rearrange("(k p) f -> p k f", p=128), q="pool")
                P.dma(wuu, wuu[:], wu, wu[e][:, f0:f0 + 256].rearrange("(k p) f -> p k f", p=128), q="pool")
                P.dma(wdu, wdu[:], wd, wd[e][f0:f0 + 256, :].rearrange("(c p) o -> p c o", p=128), q="pool")
                for fc in range(2):
                    for th in range(n // 512 if n >= 512 else 1):
                        w_ = min(512, n)
                        ts_ = slice(th * 512, th * 512 + w_)
                        G = P.bank()
                        for k in range(8):
                            P.mm(G, G[:, :w_], wgu, wgu[:, k, fc * 128:(fc + 1) * 128], hx, hx[:, k, ts_], start=(k == 0), stop=(k == 7))
                        U = P.bank()
                        for k in range(8):
                            P.mm(U, U[:, :w_], wuu, wuu[:, k, fc * 128:(fc + 1) * 128], hx, hx[:, k, ts_], start=(k == 0), stop=(k == 7))
                        sg = sgs[si % 2]
                        si += 1
                        P.act(sg, sg[:, :w_], G, G[:, :w_], AF.Silu)
                        P.tt("dve", hid, hid[:, fc, ts_], U, U[:, :w_], sg, sg[:, :w_], ALU.mult)
                pend.append((e, wdu, hid))
            j = idx - 1
            if j >= 0:
                e, wdu, hid = pend[j]
                for tl in range(n // 128):
                    gt = tok0 // 128 + tl
                    for oh in range(2):
                        O = P.bank()
                        for fc in range(2):
                            P.mm(O, O[:, :], hid, hid[:, fc, tl * 128:(tl + 1) * 128], wdu, wdu[:, fc, oh * 512:(oh + 1) * 512],
                                 start=(fc == 0), stop=(fc == 1))
                        fs = facc[:, tl, oh * 512:(oh + 1) * 512]
                        P.stt("dve", facc, fs, O, O[:, :], gm[:, e, gt:gt + 1], facc, fs, ALU.mult, ALU.add, extra=[gm])
        P.dma(fpo, fpo[tok0:tok0 + n, :].rearrange("(t p) o -> p t o", p=128), facc, facc[:, :n // 128, :])
    P.finish()
    return P.build()


XROWS = 2304
TRASH0 = 2176


def build_C():
    P = Prog("C")
    EI, EO = "ExternalInput", "ExternalOutput"
    h2tm = P.dram("h2tm", [NT, 1024], BF16, EI)
    affd = P.dram("affL", [128, 4, 130], F32, EI)
    trid = P.dram("tri", [128, 128], F32, EI)
    trashd = P.dram("trashc", [128, 1], F32, EI)
    identd = P.dram("ident", [128, 128], F32, EI)
    wg = P.dram("wg", [4, 1024, 2816], F32, EI)
    wu = P.dram("wu", [4, 1024, 2816], F32, EI)
    wd = P.dram("wd", [4, 2816, 1024], F32, EI)
    fpo = P.dram("fp", [NT, 1024], F32, EO)
    xsel = [P.dram("xsel%d" % e, [XROWS, 1024], BF16) for e in range(4)]
    osel = [P.dram("osel%d" % e, [XROWS, 1024], F32) for e in range(4)]
    P.start()
    P.banks(7)
    tp = P.psum("tp", [128, 1024], BF16)
    onesf = P.sbuf("onesf", [128, 128], F32)
    P.memset("dve", onesf, onesf[:], 1.0)
    onesb = P.sbuf("onesb", [128, 128], BF16)
    P.memset("dve", onesb, onesb[:], 1.0)
    trib = P.sbuf("trib", [128, 128], BF16)
    P.dma(trib, trib[:], trid, trid[:], q="pool")
    ident = P.sbuf("identb", [128, 128], BF16)
    P.dma(ident, ident[:], identd, identd[:], q="pool")
    trc = P.sbuf("trc", [128, 1], F32)
    P.dma(trc, trc[:], trashd, trashd[:])
    aff = P.sbuf("aff", [128, 4, 130], F32)
    P.dma(aff, aff[:], affd, affd[:])
    lo = P.sbuf("lo", [128, 8], F32)
    hi = P.sbuf("hi", [128, 8], F32)
    mid = P.sbuf("mid", [128, 8], F32)
    kv = P.sbuf("kv", [128, 8], F32)
    cnt = P.sbuf("cnt", [128, 8], F32)
    ge = P.sbuf("ge", [128, 8], F32)
    d1 = P.sbuf("d1", [128, 8], F32)
    cmp_ = P.sbuf("cmp", [128, 128], F32)
    P.memset("dve", lo, lo[:], 0.0)
    P.memset("dve", hi, hi[:], 1.0)
    P.memset("dve", kv, kv[:, 0:4], 2048.0)
    P.memset("dve", kv, kv[:, 4:8], 32.0)
    for it in range(30):
        P.tt("dve", mid, mid[:], lo, lo[:], hi, hi[:], ALU.add)
        P.ts("dve", mid, mid[:], mid, mid[:], 0.5, None, ALU.mult)
        for e in range(4):
            P.ts("dve", cmp_, cmp_[:, 0:128], aff, aff[:, e, 2:130], mid[:, e:e + 1], None, ALU.is_ge, extra=[mid])
            P.op("dve", lambda e_, e=e: e_.reduce_sum(cnt[:, e:e + 1], cmp_[:, 0:128], AX.X), reads=[cmp_], writes=[cnt])
            P.ts("dve", cmp_, cmp_[:, 0:2], aff, aff[:, e, 0:2], mid[:, 4 + e:5 + e], None, ALU.is_ge, extra=[mid])
            P.op("dve", lambda e_, e=e: e_.reduce_sum(cnt[:, 4 + e:5 + e], cmp_[:, 0:2], AX.X), reads=[cmp_], writes=[cnt])
        tb = P.bank()
        P.mm(tb, tb[:, 0:8], onesf, onesf[:], cnt, cnt[:])
        P.tt("dve", ge, ge[:], tb, tb[:, 0:8], kv, kv[:], ALU.is_ge)
        P.tt("dve", d1, d1[:], mid, mid[:], lo, lo[:], ALU.subtract)
        P.tt("dve", d1, d1[:], d1, d1[:], ge, ge[:], ALU.mult)
        P.tt("dve", lo, lo[:], lo, lo[:], d1, d1[:], ALU.add)
        P.tt("dve", d1, d1[:], hi, hi[:], mid, mid[:], ALU.subtract)
        P.tt("dve", d1, d1[:], d1, d1[:], ge, ge[:], ALU.mult)
        P.tt("dve", hi, hi[:], mid, mid[:], d1, d1[:], ALU.add)
    m = P.sbuf("m", [128, 4, 130], F32)
    for e in range(4):
        P.ts("dve", m, m[:, e, 2:130], aff, aff[:, e, 2:130], lo[:, e:e + 1], None, ALU.is_ge, extra=[lo])
        P.ts("dve", m, m[:, e, 0:2], aff, aff[:, e, 0:2], lo[:, 4 + e:5 + e], None, ALU.is_ge, extra=[lo])
    mb = P.sbuf("mb", [128, 4, 130], BF16)
    P.cp("dve", mb, mb[:], m, m[:])
    posf = P.sbuf("posf", [128, 4, 130], F32)
    posi = P.sbuf("posi", [128, 4, 130], I32)
    cT = P.sbuf("cT", [128, 128], BF16)
    offs = P.sbuf("offs", [128, 130], F32)
    ltm = P.sbuf("ltm", [128, 130], F32)
    for e in range(4):
        R = P.bank()
        P.mm(R, R[:, 0:130], trib, trib[:], mb, mb[:, e, :])
        CT = P.bank()
        P.mm(CT, CT[:, 0:128], mb, mb[:, e, 2:130], onesb, onesb[:])
        P.cp("act", cT, cT[:], CT, CT[:, 0:128])
        OF = P.bank()
        P.mm(OF, OF[:, 0:128], cT, cT[:], trib, trib[:])
        C0 = P.bank()
        P.mm(C0, C0[:, 0:2], onesb, onesb[:], mb, mb[:, e, 0:2])
        P.cp("act", offs, offs[:, 2:130], OF, OF[:, 0:128])
        P.memset("dve", offs, offs[:, 0:1], 2048.0)
        P.ts("dve", offs, offs[:, 1:2], C0, C0[:, 0:1], 2048.0, None, ALU.add)
        P.tt("dve", posf, posf[:, e, :], R, R[:, 0:130], offs, offs[:], ALU.add)
        P.ts("dve", ltm, ltm[:, 2:130], posf, posf[:, e, 2:130], 2047.5, None, ALU.is_lt)
        P.ts("dve", ltm, ltm[:, 0:2], posf, posf[:, e, 0:2], 2079.5, None, ALU.is_lt)
        P.tt("dve", m, m[:, e, :], m, m[:, e, :], ltm, ltm[:], ALU.mult)
        P.ts("dve", posf, posf[:, e, :], posf, posf[:, e, :], trc[:, 0:1], None, ALU.subtract, extra=[trc])
        P.tt("dve", posf, posf[:, e, :], posf, posf[:, e, :], m, m[:, e, :], ALU.mult)
        P.ts("dve", posf, posf[:, e, :], posf, posf[:, e, :], trc[:, 0:1], None, ALU.add, extra=[trc])
    P.cp("dve", posi, posi[:], posf, posf[:])
    gm = P.sbuf("gm", [128, 4, 130], F32)
    P.tt("dve", gm, gm[:], m, m[:], aff, aff[:], ALU.mult)

    htb = [P.sbuf("ht%d" % i, [128, 1024], BF16) for i in range(2)]
    for i in range(130):
        ht = htb[i % 2]
        P.dma(ht, ht[:], h2tm, h2tm[i * 128:(i + 1) * 128, :])
        for e in range(4):
            P.op("pool", lambda e_, e=e, i=i, ht=ht: e_.indirect_dma_start(
                out=xsel[e][:, :], out_offset=bass.IndirectOffsetOnAxis(ap=posi[:, e, i:i + 1], axis=0),
                in_=ht[:, :], in_offset=None),
                reads=[ht, posi], writes=[xsel[e]], dma_dst=xsel[e])
    zt = P.sbuf("zt", [128, 1024], F32)
    P.memset("pool", zt, zt[:], 0.0)
    for e in range(4):
        P.dma(osel[e], osel[e][TRASH0:TRASH0 + 128, :], zt, zt[:])

    NSL = 2080
    hx = P.sbuf("hx", [128, 8, NSL], BF16)
    oacc = P.sbuf("oacc", [128, 17, 1024], F32)
    NWB = 3
    wgs = [P.sbuf("wgu%d" % i, [128, 8, 256], BF16) for i in range(NWB)]
    wus = [P.sbuf("wuu%d" % i, [128, 8, 256], BF16) for i in range(NWB)]
    wds = [P.sbuf("wdu%d" % i, [128, 2, 1024], BF16) for i in range(NWB)]
    hids = [P.sbuf("hid%d" % i, [128, 2, NSL], BF16) for i in range(2)]
    sgs = [P.sbuf("sg%d" % i, [128, 512], F32) for i in range(2)]
    xts = [P.sbuf("xts%d" % i, [128, 1024], BF16) for i in range(2)]
    cchunks = [(j * 512, 512) for j in range(4)] + [(2048, 32)]
    ui = 0
    si = 0
    for e in range(4):
        for st in range(17):
            rows = 128 if st < 16 else 32
            xt = xts[st % 2]
            P.dma(xt, xt[:rows, :], xsel[e], xsel[e][st * 128:st * 128 + rows, :])
            for k in range(8):
                P.tr(tp, tp[:, k * 128:k * 128 + rows], xt, xt[:rows, k * 128:(k + 1) * 128], ident, ident[:rows, :rows])
            P.cp("act" if st % 2 else "dve", hx, hx[:, :, st * 128:st * 128 + rows],
                 tp, tp[:].rearrange("p (a b) -> p a b", a=8)[:, :, :rows])
        pend = []
        for idx in range(12):
            if idx < 11:
                f0 = idx * 256
                wgu, wuu, wdu, hid = wgs[ui % NWB], wus[ui % NWB], wds[ui % NWB], hids[ui % 2]
                ui += 1
                P.dma(wgu, wgu[:], wg, wg[e][:, f0:f0 + 256].rearrange("(k p) f -> p k f", p=128), q="pool")
                P.dma(wuu, wuu[:], wu, wu[e][:, f0:f0 + 256].rearrange("(k p) f -> p k f", p=128), q="pool")
                P.dma(wdu, wdu[:], wd, wd[e][f0:f0 + 256, :].rearrange("(c p) o -> p c o", p=128), q="pool")
                for fc in range(2):
                    for (c0, w_) in cchunks:
                        ts_ = slice(c0, c0 + w_)
                        G = P.bank()
                        for k in range(8):
                            P.mm(G, G[:, :w_], wgu, wgu[:, k, fc * 128:(fc + 1) * 128], hx, hx[:, k, ts_], start=(k == 0), stop=(k == 7))
                        U = P.bank()
                        for k in range(8):
                            P.mm(U, U[:, :w_], wuu, wuu[:, k, fc * 128:(fc + 1) * 128], hx, hx[:, k, ts_], start=(k == 0), stop=(k == 7))
                        sg = sgs[si % 2]
                        si += 1
                        P.act(sg, sg[:, :w_], G, G[:, :w_], AF.Silu)
                        P.tt("dve", hid, hid[:, fc, ts_], U, U[:, :w_], sg, sg[:, :w_], ALU.mult)
                pend.append((wdu, hid))
            j = idx - 1
            if j >= 0:
                wdu, hid = pend[j]
                for st in range(17):
                    rows = 128 if st < 16 else 32
                    for oh in range(2):
                        O = P.bank()
                        for fc in range(2):
                            P.mm(O, O[:rows, :], hid, hid[:, fc, st * 128:st * 128 + rows], wdu, wdu[:, fc, oh * 512:(oh + 1) * 512],
                                 start=(fc == 0), stop=(fc == 1))
                        os_ = oacc[:rows, st, oh * 512:(oh + 1) * 512]
                        if j == 0:
                            P.cp("act", oacc, os_, O, O[:rows, :])
                        else:
                            P.tt("dve", oacc, os_, O, O[:rows, :], oacc, os_, ALU.add)
        P.dma(osel[e], osel[e][0:2048, :].rearrange("(t p) o -> p t o", p=128), oacc, oacc[:, 0:16, :])
        P.dma(osel[e], osel[e][2048:2080, :], oacc, oacc[:32, 16, :])

    gts = [P.sbuf("gt%d" % i, [128, 4, 1024], F32) for i in range(1)]
    fos = [P.sbuf("fo0", [128, 1024], F32), zt]
    for i in range(130):
        gt, fo = gts[0], fos[i % 2]
        for e in range(4):
            P.op("pool", lambda e_, e=e, i=i, gt=gt: e_.indirect_dma_start(
                out=gt[:, e, :], out_offset=None, in_=osel[e][:, :],
                in_offset=bass.IndirectOffsetOnAxis(ap=posi[:, e, i:i + 1], axis=0)),
                reads=[osel[e], posi], writes=[gt], dma_dst=gt)
        P.ts("dve", fo, fo[:], gt, gt[:, 0, :], gm[:, 0, i:i + 1], None, ALU.mult, extra=[gm])
        for e in range(1, 4):
            P.stt("dve", fo, fo[:], gt, gt[:, e, :], gm[:, e, i:i + 1], fo, fo[:], ALU.mult, ALU.add, extra=[gm])
        P.dma(fpo, fpo[i * 128:(i + 1) * 128, :], fo, fo[:])
    P.finish()
    return P.build()


def build_D():
    P = Prog("D")
    EI, EO = "ExternalInput", "ExternalOutput"
    x1d = P.dram("x1", [NB, 1024], F32, EI)
    fps = P.dram("fps", [4, NB, 1024], F32, EI)
    rows = P.dram("rows", [128, 4, 1024], F32, EI)
    x2o = P.dram("x2", [NB, 1024], F32, EO)
    P.start()
    rw = P.sbuf("rw", [128, 4, 1024], F32)
    P.dma(rw, rw[:], rows, rows[:])
    epst = P.sbuf("epst", [128, 1], F32)
    P.memset("dve", epst, epst[:], LN_EPS)
    st6 = P.sbuf("st6", [128, 2, 6], F32)
    mv = P.sbuf("mv", [128, 2], F32)
    rstd = P.sbuf("rstd", [128, 1], F32)
    xts = [P.sbuf("xt%d" % i, [128, 1024], F32) for i in range(2)]
    pts = [P.sbuf("pp%d" % i, [128, 4, 1024], F32) for i in range(2)]
    u = P.sbuf("u", [128, 1024], F32)
    ots = [P.sbuf("ot%d" % i, [128, 1024], F32) for i in range(2)]
    for t in range(33):
        r0, tsz = (0, 64) if t == 0 else (64 + (t - 1) * 128, 128)
        m = 1 if t == 0 else 0
        xt, pp, ot = xts[t % 2], pts[t % 2], ots[t % 2]
        P.dma(xt, xt[:tsz, :], x1d, x1d[r0:r0 + tsz, :])
        for j in range(4):
            P.dma(pp, pp[:tsz, j, :], fps, fps[j, r0:r0 + tsz, :])
        P.tt("dve", pp, pp[:tsz, 0, :], pp, pp[:tsz, 0, :], pp, pp[:tsz, 1, :], ALU.add)
        P.tt("pool", pp, pp[:tsz, 2, :], pp, pp[:tsz, 2, :], pp, pp[:tsz, 3, :], ALU.add)
        P.tt("dve", pp, pp[:tsz, 0, :], pp, pp[:tsz, 0, :], pp, pp[:tsz, 2, :], ALU.add)
        P.tt("dve", pp, pp[:tsz, 0, :], pp, pp[:tsz, 0, :], rw, rw[:tsz, m, :], ALU.mult)
        P.stt("dve", u, u[:tsz, :], xt, xt[:tsz, :], ALPHA, pp, pp[:tsz, 0, :], ALU.mult, ALU.add)
        ln_tile(P, u, u[:tsz, :], tsz, st6, mv, rstd, epst, ot, ot[:tsz, :])
        P.tt("dve", ot, ot[:tsz, :], ot, ot[:tsz, :], rw, rw[:tsz, 2, :], ALU.mult)
        P.tt("pool", ot, ot[:tsz, :], ot, ot[:tsz, :], rw, rw[:tsz, 3, :], ALU.add)
        P.dma(x2o, x2o[r0:r0 + tsz, :], ot, ot[:tsz, :])
    P.finish()
    return P.build()


def _lam_init(l):
    import math
    return 0.8 - 0.6 * math.exp(-0.3 * l)


def kernel(**inp):
    inp = {k: np.asarray(v) for k, v in inp.items()}
    mods = run_L0(inp)
    x_cur = [np.asarray(inp["x"][b], np.float32) for b in range(2)]
    ctx_cur = [np.asarray(inp["ctx"][b], np.float32) for b in range(2)]
    for l in range(2):
        mod = mods[l]
        resA = run(build_A(_lam_init(l)), prep_A(inp, l, mod, x_cur, ctx_cur)).results
        yTs = []
        for b in range(2):
            parts = [np.asarray(resA[b * 4 + g][nm]) for nm in ("yaT", "ybT", "ycT") for g in range(4)]
            yTs.append(np.concatenate(parts, 0))
        del resA
        resB = run(build_B(), prep_B(inp, l, mod, x_cur, ctx_cur, yTs)).results
        x1p = [np.asarray(resB[c]["x1"]) for c in range(8)]
        h2Tf, afff = [], []
        for b in range(2):
            hp = [np.asarray(resB[b * 4 + s]["h2tm"]) for s in range(4)]
            ap = [np.asarray(resB[b * 4 + s]["aff"]) for s in range(4)]
            h2Tf.append(np.concatenate([p[:64] for p in hp] + [p[64:] for p in hp], 0))
            afff.append(np.concatenate([p[:64] for p in ap] + [p[64:] for p in ap], 0))
        del resB
        mapsC = []
        for core in range(8):
            b, j = core // 4, core % 4
            a4 = afff[b][:, 4 * j:4 * j + 4].reshape(130, 128, 4).transpose(1, 2, 0)
            mapsC.append({
                "h2tm": np.ascontiguousarray(h2Tf[b]), "affL": np.ascontiguousarray(a4),
                "tri": np.triu(np.ones((128, 128), np.float32), 1),
                "trashc": (TRASH0 + np.arange(128, dtype=np.float32)).reshape(128, 1),
                "ident": _consts()["ident"],
                "wg": np.ascontiguousarray(inp["w_exp_gate"][l][4 * j:4 * j + 4]),
                "wu": np.ascontiguousarray(inp["w_exp_up"][l][4 * j:4 * j + 4]),
                "wd": np.ascontiguousarray(inp["w_exp_down"][l][4 * j:4 * j + 4]),
            })
        resC = run(build_C(), mapsC).results
        fpc = [np.asarray(resC[c]["fp"]) for c in range(8)]
        del resC, mapsC
        rowsD = lambda b: np.ascontiguousarray(np.broadcast_to(
            np.stack([mod[5120:6144, b], mod[5120:6144, 2], inp["ln2_g"][l], inp["ln2_b"][l]], 0)[None], (128, 4, 1024)))
        mapsD = []
        for core in range(8):
            b, s = core // 4, core % 4
            fps = np.stack([np.concatenate([fpc[b * 4 + jj][64 * s:64 * (s + 1)],
                                            fpc[b * 4 + jj][256 + 4096 * s:256 + 4096 * (s + 1)]], 0) for jj in range(4)], 0)
            mapsD.append({"x1": x1p[core], "fps": np.ascontiguousarray(fps), "rows": rowsD(b)})
        resD = run(build_D(), mapsD).results
        for b in range(2):
            xp = [np.asarray(resD[b * 4 + s]["x2"]) for s in range(4)]
            ctx_cur[b] = np.concatenate([p[:64] for p in xp], 0)
            x_cur[b] = np.concatenate([p[64:] for p in xp], 0)
        del resD, mapsD, fpc
    return np.stack(x_cur, 0).astype(np.float32)
```

```python
import numpy as np
import concourse.bass as bass
import concourse.mybir as mybir
from concourse.bass_utils import run_bass_kernel_spmd

F32 = mybir.dt.float32
BF16 = mybir.dt.bfloat16
I32 = mybir.dt.int32
U32 = mybir.dt.uint32
ALU = mybir.AluOpType
AF = mybir.ActivationFunctionType
AX = mybir.AxisListType

COMPUTE = ("pe", "act", "dve", "pool")
SAME_ENGINE_SYNC = True
import os
SIGALL = bool(int(os.environ.get('K_SIGALL', '0')))
SIGENG = set(os.environ.get('K_SIGENG', 'act,dve,pool').split(','))


class T:
    def __init__(self, prog, handle, name):
        self.p = prog
        self.h = handle
        self.name = name
        self.lastw = None
        self.reads = []
        self.dsem = None
        self.dcnt = 0

    def __getitem__(self, idx):
        return self.h[idx]


class Prog:
    def __init__(self, name):
        self.name = name
        self.nc = bass.Bass("TRN2", target_bir_lowering=False)
        self.ops = {e: [] for e in ("pe", "act", "dve", "pool", "sp")}
        self.cnt = {e: 0 for e in COMPUTE}
        self.opidx = {e: {} for e in COMPUTE}
        self.seen = {e: {} for e in self.ops}
        self.ctxs = []
        self.tiles = []
        self.sems = {}
        self.outs = []
        self.nsem = 0

    def _enter(self, cm):
        v = cm.__enter__()
        self.ctxs.append(cm)
        return v

    def sem(self, name):
        self.nsem += 1
        return self._enter(self.nc.semaphore(name))

    def sbuf(self, name, shape, dt):
        t = T(self, self._enter(self.nc.sbuf_tensor(name, list(shape), dt)), name)
        self.tiles.append(t)
        return t

    def psum(self, name, shape, dt=F32):
        t = T(self, self._enter(self.nc.psum_tensor(name, list(shape), dt)), name)
        self.tiles.append(t)
        return t

    def dram(self, name, shape, dt, kind="Internal", addr_space="Local"):
        h = self.nc.dram_tensor(name, list(shape), dt, kind=kind, addr_space=addr_space)
        t = T(self, h.ap(), name)
        self.tiles.append(t)
        if kind == "ExternalOutput":
            self.outs.append(t)
        return t

    def start(self):
        for e in COMPUTE:
            self.sems[e] = self.sem("s_" + e)

    def _need(self, eng, hz, waits):
        if hz is None:
            return
        kind, key, val = hz
        if kind == "eng" and key == eng and (eng == "pe" or not SAME_ENGINE_SYNC):
            return
        k = (kind, key if kind == "eng" else id(key))
        if self.seen[eng].get(k, 0) >= val:
            return
        self.seen[eng][k] = val
        if kind == "eng":
            rec = self.ops[key][self.opidx[key][val]]
            rec[3] = True
            waits.append(("eng", key, rec))
        else:
            waits.append(("dma", key.dsem, val))

    def op(self, eng, fn, reads=(), writes=(), dma_dst=None, ordered=False):
        waits = []
        for t in reads:
            self._need(eng, t.lastw, waits)
        for t in writes:
            if dma_dst is not None and t is dma_dst and not ordered and t.lastw is not None \
                    and t.lastw[0] == "dma" and not t.reads:
                pass
            else:
                self._need(eng, t.lastw, waits)
            for r in t.reads:
                self._need(eng, r, waits)
        if dma_dst is not None:
            if dma_dst.dsem is None:
                dma_dst.dsem = self.sem("d_" + dma_dst.name)
            dma_dst.dcnt += 16
            hz = ("dma", dma_dst, dma_dst.dcnt)
            inc = (dma_dst.dsem, 16)
        else:
            self.cnt[eng] += 1
            hz = ("eng", eng, self.cnt[eng])
            inc = None
            self.opidx[eng][self.cnt[eng]] = len(self.ops[eng])
        for t in reads:
            t.reads.append(hz)
        for t in writes:
            t.lastw = hz
            t.reads = []
        self.ops[eng].append([waits, fn, inc, False, 0])

    def dma(self, out_t, out_ap, in_t, in_ap, q="sp", ordered=False, **kw):
        self.op(q, lambda e: e.dma_start(out=out_ap, in_=in_ap, **kw),
                reads=[in_t], writes=[out_t], dma_dst=out_t, ordered=ordered)

    def mm(self, out_t, out_ap, a_t, lhsT, b_t, rhs, start=True, stop=True, **kw):
        self.op("pe", lambda e: e.matmul(out_ap, lhsT, rhs, start=start, stop=stop, **kw),
                reads=[a_t, b_t], writes=[out_t])

    def tr(self, out_t, out_ap, in_t, in_ap, ident_t, ident_ap):
        self.op("pe", lambda e: e.transpose(out_ap, in_ap, ident_ap), reads=[in_t, ident_t], writes=[out_t])


    def act(self, out_t, out_ap, in_t, in_ap, func, extra=(), **kw):
        self.op("act", lambda e: e.activation(out_ap, in_ap, func, **kw), reads=[in_t, *extra], writes=[out_t])

    def ts(self, eng, out_t, out_ap, in_t, in_ap, s1, s2, op0, op1=None, extra=()):
        if op1 is None:
            self.op(eng, lambda e: e.tensor_scalar(out_ap, in_ap, s1, None, op0), reads=[in_t, *extra], writes=[out_t])
        else:
            self.op(eng, lambda e: e.tensor_scalar(out_ap, in_ap, s1, s2, op0, op1), reads=[in_t, *extra], writes=[out_t])

    def tt(self, eng, out_t, out_ap, a_t, a_ap, b_t, b_ap, op):
        self.op(eng, lambda e: e.tensor_tensor(out_ap, a_ap, b_ap, op), reads=[a_t, b_t], writes=[out_t])

    def stt(self, eng, out_t, out_ap, a_t, a_ap, scalar, b_t, b_ap, op0, op1, extra=()):
        self.op(eng, lambda e: e.scalar_tensor_tensor(out_ap, a_ap, scalar, b_ap, op0, op1),
                reads=[a_t, b_t, *extra], writes=[out_t])

    def cp(self, eng, out_t, out_ap, in_t, in_ap):
        if eng == "act":
            self.op("act", lambda e: e.copy(out_ap, in_ap), reads=[in_t], writes=[out_t])
        else:
            self.op(eng, lambda e: e.tensor_copy(out_ap, in_ap), reads=[in_t], writes=[out_t])

    def memset(self, eng, t, ap, val):
        self.op(eng, lambda e: e.memset(ap, val), reads=[], writes=[t])

    def banks(self, n=7):
        self.pb = [self.psum("pb%d" % i, [128, 512], F32) for i in range(n)]
        self.pbi = 0

    def bank(self):
        b = self.pb[self.pbi % len(self.pb)]
        self.pbi += 1
        return b

    def rsqrt(self, out_t, out_ap, in_t, in_ap, scale, eps_t, eps_ap):
        self.op("act", lambda e: e.activation(out_ap, in_ap, AF.Sqrt, bias=eps_ap, scale=scale),
                reads=[in_t, eps_t], writes=[out_t])
        self.op("dve", lambda e: e.reciprocal(out_ap, out_ap), reads=[out_t], writes=[out_t])

    def finish(self):
        waits = []
        for t in self.tiles:
            if t.lastw is not None and t.lastw[0] == "dma":
                self._need("sp", t.lastw, waits)
        self.ops["sp"].append([waits, None, None, False, 0])

    def build(self):
        nc = self.nc
        engs = {"pe": "tensor", "act": "scalar", "dve": "vector", "pool": "gpsimd", "sp": "sync"}
        for e in COMPUTE:
            c = 0
            for rec in self.ops[e]:
                if (SIGALL or e in SIGENG) and rec[2] is None and rec[1] is not None:
                    rec[3] = True
                if rec[2] is None and rec[1] is not None and rec[3]:
                    c += 1
                    rec[4] = c
        with nc.Block() as block:
            for e, attr in engs.items():
                ops = self.ops[e]

                def body(engobj, ops=ops, e=e):
                    for waits, fn, inc, sig, _c in ops:
                        for w in waits:
                            if w[0] == "eng":
                                engobj.wait_ge(self.sems[w[1]], w[2][4])
                            else:
                                engobj.wait_ge(w[1], w[2])
                        if fn is not None:
                            ins = fn(engobj)
                            if inc is not None:
                                ins.then_inc(inc[0], inc[1])
                            elif sig:
                                ins.then_inc(self.sems[e], 1)
                getattr(block, attr)(body)
        for cm in reversed(self.ctxs):
            cm.__exit__(None, None, None)
        self.ctxs = []
        return nc

    def n_ops(self):
        return {e: len(v) for e, v in self.ops.items()}


def run(prog_nc, in_maps, trace=False):
    import sys, time
    t0 = time.time()
    res = run_bass_kernel_spmd(prog_nc, in_maps, core_ids=list(range(len(in_maps))), trace=trace)
    print("[kernel] launch done in %.1fs" % (time.time() - t0), file=sys.stderr, flush=True)
    return res


def build_L0():
    P = Prog("L0")
    nc = P.nc
    w = P.dram("w", [1024, 1536], F32, kind="ExternalInput")
    bT = P.dram("bT", [128, 12], F32, kind="ExternalInput")
    cT = P.dram("cT", [128, 8, 3], F32, kind="ExternalInput")
    out = P.dram("modT", [128, 12, 3], F32, kind="ExternalOutput")
    P.start()
    wt = P.sbuf("wt", [128, 8, 1536], F32)
    bt = P.sbuf("bt", [128, 12], F32)
    ct = P.sbuf("ct", [128, 8, 3], F32)
    st = P.sbuf("st", [128, 8, 3], F32)
    ot = P.sbuf("ot", [128, 12, 3], F32)
    ps = P.psum("ps", [128, 512], F32)
    P.dma(ct, ct[:], cT, cT[:])
    P.dma(bt, bt[:], bT, bT[:])
    for k in range(8):
        P.dma(wt, wt[:, k, :], w, w[k * 128:(k + 1) * 128, :])
    P.op("act", lambda e: e.activation(st[:], ct[:], AF.Silu), reads=[ct], writes=[st])
    for j in range(12):
        for k in range(8):
            P.mm(ps, ps[:, j * 3:(j + 1) * 3], wt, wt[:, k, j * 128:(j + 1) * 128], st, st[:, k, :],
                 start=(k == 0), stop=(k == 7))
    for j in range(12):
        P.op("dve", lambda e, j=j: e.tensor_scalar(ot[:, j, :], ps[:, j * 3:(j + 1) * 3], bt[:, j:j + 1], None, ALU.add),
             reads=[ps, bt], writes=[ot])
    P.dma(out, out[:], ot, ot[:])
    P.finish()
    return P.build()


def run_L0(inp):
    cvec = np.stack([inp["c"][0], inp["c"][1], inp["c_ctx"]], 0)
    cT = np.ascontiguousarray(cvec.T.reshape(8, 128, 3).transpose(1, 0, 2))
    in_maps = []
    for core in range(8):
        l, q = core // 4, core % 4
        cols = slice(q * 1536, (q + 1) * 1536)
        in_maps.append({
            "w": np.ascontiguousarray(inp["w_ada"][l][:, cols]),
            "bT": np.ascontiguousarray(inp["b_ada"][l][cols].reshape(12, 128).T),
            "cT": cT,
        })
    res = run(build_L0(), in_maps)
    mods = []
    for l in range(2):
        parts = []
        for q in range(4):
            m = np.asarray(res.results[l * 4 + q]["modT"])
            parts.append(m.transpose(1, 0, 2).reshape(1536, 3))
        mods.append(np.concatenate(parts, 0))
    return mods


NT = 16640
LN_EPS = 1e-6
NCH = 33


def chunk_rng(c):
    return (0, 256) if c == 0 else (256 + (c - 1) * 512, 512)


def build_A(lam_init, dbg=False):
    P = Prog("A")
    EI = "ExternalInput"
    xin = P.dram("xin", [NT, 1024], F32, EI)
    msT = P.dram("msT", [128, 8, 4], F32, EI)
    wfeat = P.dram("wfeat", [1024, 1216], F32, EI)
    wtok = P.dram("wtok", [1024, 256], F32, EI)
    wuq = P.dram("wuq", [256, 2, 192], F32, EI)
    wukvk = P.dram("wukvk", [128, 2, 64], F32, EI)
    wukvv = P.dram("wukvv", [128, 128], F32, EI)
    gq = P.dram("gq", [128, 2], F32, EI)
    gkv = P.dram("gkv", [128, 1], F32, EI)
    cs32 = P.dram("cs32", [32, 2, NT], F32, EI)
    cs128 = P.dram("cs128", [128, 2, NT], F32, EI)
    nabias = P.dram("nabias", [2, 3, 768, 256], F32, EI)
    lamv = P.dram("lamv", [128, 4, 64], F32, EI)
    gsub = P.dram("gsub", [128, 1], F32, EI)
    identd = P.dram("ident", [128, 128], F32, EI)
    EO = "ExternalOutput"
    yaT = P.dram("yaT", [128, NT], BF16, EO)
    ybT = P.dram("ybT", [128, NT], BF16, EO)
    ycT = P.dram("ycT", [128, NT], BF16, EO)
    sk = EO if dbg else "Internal"
    qaT = P.dram("qaT", [128, NT], BF16, sk)
    kaT = P.dram("kaT", [128, NT], BF16, sk)
    va = P.dram("va", [NT, 128], BF16, sk)
    qbT = P.dram("qbT", [2, 96, NT], BF16, sk)
    kbT = P.dram("kbT", [2, 96, NT], BF16, sk)
    vb = P.dram("vb", [NT, 128], BF16, sk)
    dqT = P.dram("dqT", [128, NT], BF16, sk)
    dkT = P.dram("dkT", [128, NT], BF16, sk)
    dvv = P.dram("dvv", [NT, 128], BF16, sk)
    P.start()
    P.banks(7)
    tp = P.psum("tp", [128, 1024], BF16)

    Wf = P.sbuf("Wf", [128, 8, 1216], BF16)
    Wt = P.sbuf("Wt", [128, 8, 256], BF16)
    for k in range(8):
        P.dma(Wf, Wf[:, k, :], wfeat, wfeat[k * 128:(k + 1) * 128, :], q="pool")
        P.dma(Wt, Wt[:, k, :], wtok, wtok[k * 128:(k + 1) * 128, :], q="pool")
    ident = P.sbuf("identb", [128, 128], BF16)
    P.dma(ident, ident[:], identd, identd[:], q="pool")
    ones = P.sbuf("ones", [128, 128], BF16)
    P.memset("dve", ones, ones[:], 1.0)
    onesf = P.sbuf("onesf", [128, 128], F32)
    epst = P.sbuf("epst", [128, 1], F32)
    P.memset("dve", epst, epst[:], LN_EPS)
    P.memset("dve", onesf, onesf[:], 1.0)
    ms = P.sbuf("ms", [128, 8, 4], F32)
    P.dma(ms, ms[:], msT, msT[:])
    scp = P.sbuf("scp", [128, 8, 2], F32)
    for m in range(2):
        P.ts("dve", scp, scp[:, :, m], ms, ms[:, :, 2 * m + 1], 1.0, None, ALU.add)
    gqt = P.sbuf("gqt", [128, 2], F32)
    P.dma(gqt, gqt[:], gq, gq[:])
    gkvt = P.sbuf("gkvt", [128, 1], F32)
    P.dma(gkvt, gkvt[:], gkv, gkv[:])
    wuq_r = P.sbuf("wuq_r", [128, 2, 2, 192], F32)
    for k in range(2):
        P.dma(wuq_r, wuq_r[:, k], wuq, wuq[k * 128:(k + 1) * 128])
    Wuq = P.sbuf("Wuq", [128, 2, 2, 192], BF16)
    for k in range(2):
        P.ts("dve", Wuq, Wuq[:, k], wuq_r, wuq_r[:, k], gqt[:, k:k + 1], None, ALU.mult, extra=[gqt])
    wk_r = P.sbuf("wk_r", [128, 2, 64], F32)
    P.dma(wk_r, wk_r[:], wukvk, wukvk[:])
    Wkk = P.sbuf("Wkk", [128, 2, 64], BF16)
    P.ts("dve", Wkk, Wkk[:], wk_r, wk_r[:], gkvt[:, 0:1], None, ALU.mult, extra=[gkvt])
    wv_r = P.sbuf("wv_r", [128, 128], F32)
    P.dma(wv_r, wv_r[:], wukvv, wukvv[:])
    Wkv = P.sbuf("Wkv", [128, 128], BF16)
    P.ts("dve", Wkv, Wkv[:], wv_r, wv_r[:], gkvt[:, 0:1], None, ALU.mult, extra=[gkvt])
    lv = P.sbuf("lv", [128, 4, 64], F32)
    P.dma(lv, lv[:], lamv, lamv[:])
    lpr = P.sbuf("lpr", [128, 2, 64], F32)
    P.tt("dve", lpr, lpr[:, 0], lv, lv[:, 0], lv, lv[:, 1], ALU.mult)
    P.tt("dve", lpr, lpr[:, 1], lv, lv[:, 2], lv, lv[:, 3], ALU.mult)
    lsum = P.sbuf("lsum", [128, 2], F32)
    P.op("dve", lambda e: e.reduce_sum(lsum[:], lpr[:], AX.X), reads=[lpr], writes=[lsum])
    lexp = P.sbuf("lexp", [128, 2], F32)
    P.act(lexp, lexp[:], lsum, lsum[:], AF.Exp)
    neglam = P.sbuf("neglam", [128, 1], F32)
    P.stt("dve", neglam, neglam[:], lexp, lexp[:, 1:2], -float(lam_init), lexp, lexp[:, 0:1], ALU.add, ALU.subtract)
    gsr = P.sbuf("gsr", [128, 1], F32)
    P.dma(gsr, gsr[:], gsub, gsub[:])
    gss = P.sbuf("gss", [128, 1], F32)
    P.ts("dve", gss, gss[:], gsr, gsr[:], 1.0 - float(lam_init), None, ALU.mult)

    xb = [P.sbuf("xb%d" % i, [128, 1024], F32) for i in range(2)]
    xnb = [P.sbuf("xn%d" % i, [128, 1024], BF16) for i in range(2)]
    st6 = P.sbuf("st6", [128, 2, 6], F32)
    mv = P.sbuf("mv", [128, 2], F32)
    rstd = P.sbuf("rstd", [128, 1], F32)
    htmp = P.sbuf("htmp", [128, 8, 128], F32)
    hTb = [P.sbuf("hT%d" % i, [128, 8, 512], BF16) for i in range(2)]
    t32 = [P.sbuf("t32_%d" % i, [128, 2, 512], F32) for i in range(2)]
    t128 = [P.sbuf("t128_%d" % i, [128, 2, 512], F32) for i in range(2)]
    stg = [P.sbuf("stg%d" % i, [128, 512], BF16) for i in range(3)]
    stgi = [0]

    def nstg():
        s_ = stg[stgi[0] % len(stg)]
        stgi[0] += 1
        return s_
    cqT = P.sbuf("cqT", [128, 2, 512], BF16)
    cqsq = P.sbuf("cqsq", [128, 2, 512], BF16)
    ckvT = P.sbuf("ckvT", [128, 512], BF16)
    ckvsq = P.sbuf("ckvsq", [128, 512], BF16)
    rq = P.sbuf("rq", [128, 512], F32)
    rkv = P.sbuf("rkv", [128, 512], F32)
    rkvt = P.sbuf("rkvt", [128, 4], F32)
    ra = [P.sbuf("ra%d" % i, [128, 512], F32) for i in range(1)]
    rb = [P.sbuf("rb%d" % i, [128, 512], F32) for i in range(1)]
    vst = [P.sbuf("vst%d" % i, [128, 256], BF16) for i in range(2)]
    krT = P.sbuf("krT", [32, 512], BF16)

    def rope(n, pa, pb_, rows, tab, ti, out_t, out_ap, mul_t=None, mul_ap=None):
        a = ra[0]
        b = rb[0]
        P.tt("dve", a, a[rows, :n], pa, pa[rows, :n], tab, tab[rows, 0, :n], ALU.mult)
        P.tt("dve", b, b[rows, :n], pb_, pb_[rows, :n], tab, tab[rows, 1, :n], ALU.mult)
        if mul_t is None:
            P.tt("pool", out_t, out_ap, a, a[rows, :n], b, b[rows, :n], ALU.add)
        else:
            P.tt("pool", a, a[rows, :n], a, a[rows, :n], b, b[rows, :n], ALU.add)
            P.tt("pool", out_t, out_ap, a, a[rows, :n], mul_t, mul_ap, ALU.mult)

    FO = {"qa": 0, "ka": 128, "cq0": 256, "cq1": 384, "ckv": 512, "kr": 640, "krp": 672,
          "dq": 704, "dqp": 832, "dk": 960, "dkp": 1088}

    def proj(hT, name, M, n):
        pbk = P.bank()
        off = FO[name]
        for k in range(8):
            P.mm(pbk, pbk[:M, :n], Wf, Wf[:, k, off:off + M], hT, hT[:, k, :n], start=(k == 0), stop=(k == 7))
        return pbk

    def stage1(c):
        tok0, n = chunk_rng(c)
        m = 1 if c == 0 else 0
        nt = n // 128
        hT = hTb[c % 2]
        tb32 = t32[c % 2]
        tb128 = t128[c % 2]
        P.dma(tb32, tb32[0:32, :, :n], cs32, cs32[:, :, tok0:tok0 + n])
        P.dma(tb32, tb32[64:96, :, :n], cs32, cs32[:, :, tok0:tok0 + n])
        P.dma(tb128, tb128[:, :, :n], cs128, cs128[:, :, tok0:tok0 + n])
        for i in range(nt):
            xt = xb[i % 2]
            xn = xnb[i % 2]
            P.dma(xt, xt[:], xin, xin[tok0 + i * 128: tok0 + (i + 1) * 128, :])
            for a_ in range(2):
                P.op("dve", lambda e, xt=xt, a_=a_: e.bn_stats(st6[:, a_, :], xt[:, a_ * 512:(a_ + 1) * 512]),
                     reads=[xt], writes=[st6])
            P.op("dve", lambda e: e.bn_aggr(mv[:], st6[:].rearrange("p a b -> p (a b)")), reads=[st6], writes=[mv])
            P.rsqrt(rstd, rstd[:], mv, mv[:, 1:2], 1.0, epst, epst[:, 0:1])
            P.ts("dve", xn, xn[:], xt, xt[:], mv[:, 0:1], rstd[:, 0:1], ALU.subtract, ALU.mult, extra=[mv, rstd])
            for k in range(8):
                P.tr(tp, tp[:, k * 128:(k + 1) * 128], xn, xn[:, k * 128:(k + 1) * 128], ident, ident[:])
            P.tt("dve", htmp, htmp[:], tp, tp[:].rearrange("p (a b) -> p a b", a=8),
                 scp, scp[:, :, m:m + 1].to_broadcast([128, 8, 128]), ALU.mult)
            P.tt("pool", hT, hT[:, :, i * 128:(i + 1) * 128], htmp, htmp[:],
                 ms, ms[:, :, 2 * m:2 * m + 1].to_broadcast([128, 8, 128]), ALU.add)

    def stage2(c):
        tok0, n = chunk_rng(c)
        nt = n // 128
        hT = hTb[c % 2]
        tb32 = t32[c % 2]
        tb128 = t128[c % 2]
        for name, dst in (("qa", qaT), ("ka", kaT)):
            pbk = proj(hT, name, 128, n)
            s_ = nstg()
            P.cp("act", s_, s_[:, :n], pbk, pbk[:, :n])
            P.dma(dst, dst[:, tok0:tok0 + n], s_, s_[:, :n])
        for j, name in enumerate(("cq0", "cq1")):
            pbk = proj(hT, name, 128, n)
            P.cp("act", cqT, cqT[:, j, :n], pbk, pbk[:, :n])
            P.act(cqsq, cqsq[:, j, :n], pbk, pbk[:, :n], AF.Square)
        pbk = proj(hT, "ckv", 128, n)
        P.cp("act", ckvT, ckvT[:, :n], pbk, pbk[:, :n])
        P.act(ckvsq, ckvsq[:, :n], pbk, pbk[:, :n], AF.Square)
        pa = proj(hT, "kr", 32, n)
        pb_ = proj(hT, "krp", 32, n)
        rope(n, pa, pb_, slice(0, 32), tb32, 0, krT, krT[:, :n])
        for h in range(2):
            P.dma(kbT, kbT[h, 64:96, tok0:tok0 + n], krT, krT[:, :n])
        for (nm, nmp, dst, ti) in (("dq", "dqp", dqT, 0), ("dk", "dkp", dkT, 1)):
            pa = proj(hT, nm, 128, n)
            pb_ = proj(hT, nmp, 128, n)
            s_ = nstg()
            rope(n, pa, pb_, slice(0, 128), tb128, ti, s_, s_[:, :n])
            P.dma(dst, dst[:, tok0:tok0 + n], s_, s_[:, :n])
        for i in range(nt):
            pbk = P.bank()
            for k in range(8):
                P.mm(pbk, pbk[:, 0:256], hT, hT[:, k, i * 128:(i + 1) * 128], Wt, Wt[:, k, :], start=(k == 0), stop=(k == 7))
            v_ = vst[i % 2]
            P.cp("act", v_, v_[:], pbk, pbk[:, 0:256])
            r0 = tok0 + i * 128
            P.dma(va, va[r0:r0 + 128, :], v_, v_[:, 0:128])
            P.dma(dvv, dvv[r0:r0 + 128, :], v_, v_[:, 128:256])
        pbk = P.bank()
        P.mm(pbk, pbk[:, :n], ones, ones[:], cqsq, cqsq[:, 0, :n], start=True, stop=False)
        P.mm(pbk, pbk[:, :n], ones, ones[:], cqsq, cqsq[:, 1, :n], start=False, stop=True)
        P.rsqrt(rq, rq[:, :n], pbk, pbk[:, :n], 1.0 / 256, epst, epst[:, 0:1])
        pbk = P.bank()
        P.mm(pbk, pbk[:, :n], ones, ones[:], ckvsq, ckvsq[:, :n])
        P.rsqrt(rkv, rkv[:, :n], pbk, pbk[:, :n], 1.0 / 128, epst, epst[:, 0:1])
        pbk = P.bank()
        for i in range(nt):
            P.mm(pbk, pbk[:, i:i + 1], ckvsq, ckvsq[:, i * 128:(i + 1) * 128], ones, ones[:, 0:1])
        P.rsqrt(rkvt, rkvt[:, :nt], pbk, pbk[:, :nt], 1.0 / 128, epst, epst[:, 0:1])
        for h in range(2):
            qa_ = P.bank()
            for k in range(2):
                P.mm(qa_, qa_[:96, :n], Wuq, Wuq[:, k, h, 0:96], cqT, cqT[:, k, :n], start=(k == 0), stop=(k == 1))
            qb_ = P.bank()
            for k in range(2):
                P.mm(qb_, qb_[:96, :n], Wuq, Wuq[:, k, h, 96:192], cqT, cqT[:, k, :n], start=(k == 0), stop=(k == 1))
            s_ = nstg()
            P.tt("dve", s_, s_[0:64, :n], qa_, qa_[0:64, :n], rq, rq[0:64, :n], ALU.mult)
            rope(n, qa_, qb_, slice(64, 96), tb32, h, s_, s_[64:96, :n], rq, rq[64:96, :n])
            P.dma(qbT, qbT[h, :, tok0:tok0 + n], s_, s_[0:96, :n])
            kn = P.bank()
            P.mm(kn, kn[:64, :n], Wkk, Wkk[:, h, :], ckvT, ckvT[:, :n])
            s_ = nstg()
            P.tt("dve", s_, s_[0:64, :n], kn, kn[0:64, :n], rkv, rkv[0:64, :n], ALU.mult)
            P.dma(kbT, kbT[h, 0:64, tok0:tok0 + n], s_, s_[0:64, :n])
        for i in range(nt):
            pbk = P.bank()
            P.mm(pbk, pbk[:, 0:128], ckvT, ckvT[:, i * 128:(i + 1) * 128], Wkv, Wkv[:])
            v_ = vst[i % 2]
            P.ts("dve", v_, v_[:, 0:128], pbk, pbk[:, 0:128], rkvt[:, i:i + 1], None, ALU.mult, extra=[rkvt])
            r0 = tok0 + i * 128
            P.dma(vb, vb[r0:r0 + 128, :], v_, v_[:, 0:128])

    stage1(0)
    for c in range(NCH):
        if c + 1 < NCH:
            stage1(c + 1)
        stage2(c)
    if dbg == "A1":
        P.finish()
        return P.build()

    BK = P.sbuf("BK", [128, NT], BF16)
    BV = P.sbuf("BV", [128, 130, 130], BF16)
    LOOK = 2
    pts = [P.sbuf("pt%d" % i, [128, 512], BF16) for i in range(5)]
    pti = [0]

    def npt():
        t_ = pts[pti[0] % 5]
        pti[0] += 1
        return t_
    qcs = [P.sbuf("qc%d" % i, [128, 512], BF16) for i in range(2)]
    rec = P.sbuf("rec", [128, 512], F32)
    rec2 = P.sbuf("rec2", [128, 512], F32)
    bsb = P.sbuf("bsb", [128, 512], F32)
    ysb = [P.sbuf("ysb%d" % i, [128, 512], BF16) for i in range(2)]
    yf = P.sbuf("yf", [128, 512], F32)
    yf2 = P.sbuf("yf2", [128, 512], F32)
    tmpf = [P.sbuf("tmpf%d" % i, [128, 256], F32) for i in range(2)]
    nb = P.sbuf("nb", [128, 3, 6, 256], F32)
    acc_banks = [P.pb[0], P.pb[1]]
    bb_bank = P.pb[2]
    s_banks = [P.pb[3], P.pb[4], P.pb[5], P.pb[6]]
    cnt = {"acc": 0, "s": 0, "y": 0}

    def sbank():
        b = s_banks[cnt["s"] % 4]
        cnt["s"] += 1
        return b

    def finalize(acc, n, dst, drow, tok0):
        P.op("dve", lambda e: e.reciprocal(rec[64:65, :n], acc[64:65, :n]), reads=[acc], writes=[rec])
        P.mm(bb_bank, bb_bank[:64, :n], onesf, onesf[64:65, 0:64], rec, rec[64:65, :n])
        P.cp("act", bsb, bsb[:64, :n], bb_bank, bb_bank[:64, :n])
        y_ = ysb[cnt["y"] % 2]
        cnt["y"] += 1
        P.tt("dve", y_, y_[:64, :n], acc, acc[:64, :n], bsb, bsb[:64, :n], ALU.mult)
        P.dma(dst, dst[drow:drow + 64, tok0:tok0 + n], y_, y_[:64, :n])

    for q4 in range(4):
        P.dma(BK, BK[:, q4 * 4160:(q4 + 1) * 4160], kaT, kaT[:, q4 * 4160:(q4 + 1) * 4160])
    BV4 = BV[:].rearrange("p t (h d) -> p t h d", h=2)
    for h in range(2):
        P.dma(BV, BV4[:, :, h, 0:64], va, va[:, h * 64:(h + 1) * 64].rearrange("(t p) d -> p t d", p=128))
    P.memset("pool", BV, BV4[:, :, :, 64:65], 1.0)
    for h in range(2):
        P.dma(nb, nb[:], nabias, nabias[h].rearrange("v (j p) q -> p v j q", p=128))
        for blk in range(-1, 64):
            if blk < 0:
                qtok0, tiles = 0, [(0, 0, None), (128, 1, None)]
            else:
                r0 = blk * 4
                var = 0 if blk == 0 else (2 if blk == 63 else 1)
                se = 0 if blk == 0 else (244 if blk == 63 else r0 - 4)
                qtok0 = 256 + r0 * 64
                tiles = [(0, 0, None), (128, 1, None)]
                for j in range(6):
                    tiles.append((256 + se * 64 + j * 128, 2 + se // 2 + j, (var, j)))
            qc = qcs[cnt["acc"] % 2]
            P.dma(qc, qc[:, :256], qaT, qaT[:, qtok0:qtok0 + 256])
            acc = acc_banks[cnt["acc"] % 2]
            cnt["acc"] += 1
            pend = []
            for idx in range(len(tiles) + LOOK):
                if idx < len(tiles):
                    kcol, vt, bi = tiles[idx]
                    sb = sbank()
                    P.mm(sb, sb[:, :256], BK, BK[h * 64:(h + 1) * 64, kcol:kcol + 128], qc, qc[h * 64:(h + 1) * 64, :256])
                    pt = npt()
                    if bi is None:
                        P.act(pt, pt[:, :256], sb, sb[:, :256], AF.Exp, scale=0.125)
                    else:
                        tf = tmpf[idx % 2]
                        P.stt("dve", tf, tf[:], sb, sb[:, :256], 0.125, nb, nb[:, bi[0], bi[1], :], ALU.mult, ALU.add)
                        P.act(pt, pt[:, :256], tf, tf[:], AF.Exp)
                    pend.append(pt)
                j = idx - LOOK
                if j >= 0:
                    pt = pend[j]
                    P.mm(acc, acc[:65, :256], BV, BV4[:, tiles[j][1], h, :], pt, pt[:, :256], start=(j == 0), stop=(j == len(tiles) - 1))
            finalize(acc, 256, yaT, h * 64, qtok0)

    BV3 = BV[:]
    sc_b = 96.0 ** -0.5
    for h in range(2):
        for q4 in range(4):
            P.dma(BK, BK[0:96, q4 * 4160:(q4 + 1) * 4160], kbT, kbT[h, :, q4 * 4160:(q4 + 1) * 4160])
        P.dma(BV, BV3[:, :, 0:64], vb, vb[:, h * 64:(h + 1) * 64].rearrange("(t p) d -> p t d", p=128))
        P.memset("pool", BV, BV3[:, :, 64:65], 1.0)
        for c in range(NCH):
            tok0, n = chunk_rng(c)
            kts = [0, 1] if c == 0 else list(range(130))
            qc = qcs[cnt["acc"] % 2]
            P.dma(qc, qc[0:96, :n], qbT, qbT[h, :, tok0:tok0 + n])
            acc = acc_banks[cnt["acc"] % 2]
            cnt["acc"] += 1
            pend = []
            for idx in range(len(kts) + LOOK):
                if idx < len(kts):
                    kt = kts[idx]
                    sb = sbank()
                    P.mm(sb, sb[:, :n], BK, BK[0:96, kt * 128:(kt + 1) * 128], qc, qc[0:96, :n])
                    pt = npt()
                    P.act(pt, pt[:, :n], sb, sb[:, :n], AF.Exp, scale=sc_b)
                    pend.append(pt)
                j = idx - LOOK
                if j >= 0:
                    pt = pend[j]
                    P.mm(acc, acc[:65, :n], BV, BV3[:, kts[j], 0:65], pt, pt[:, :n], start=(j == 0), stop=(j == len(kts) - 1))
            finalize(acc, n, ybT, h * 64, tok0)

    for q4 in range(4):
        P.dma(BK, BK[:, q4 * 4160:(q4 + 1) * 4160], dkT, dkT[:, q4 * 4160:(q4 + 1) * 4160])
    P.dma(BV, BV3[:, :, 0:128], dvv, dvv[:, :].rearrange("(t p) d -> p t d", p=128))
    O1, O2 = P.pb[0], P.pb[1]
    s3 = [P.pb[2], P.pb[3], P.pb[4], P.pb[5]]
    S2 = P.pb[6]
    sacc = [P.sbuf("sacc%d" % i, [128, 512], F32) for i in range(1)]
    for c in range(NCH):
        tok0, n = chunk_rng(c)
        kts = [0, 1] if c == 0 else list(range(130))
        qc = qcs[c % 2]
        P.dma(qc, qc[:, :n], dqT, dqT[:, tok0:tok0 + n])
        pend = []
        for idx in range(len(kts) + 1):
            if idx < len(kts):
                kt = kts[idx]
                pp = []
                for half in range(2):
                    sb = s3[cnt["s"] % 4]
                    cnt["s"] += 1
                    rows = slice(half * 64, (half + 1) * 64)
                    P.mm(sb, sb[:, :n], BK, BK[rows, kt * 128:(kt + 1) * 128], qc, qc[rows, :n])
                    pt = npt()
                    P.act(pt, pt[:, :n], sb, sb[:, :n], AF.Exp, scale=0.125)
                    pp.append(pt)
                pend.append(pp)
            j = idx - 1
            if j >= 0:
                st_, sp_ = (j == 0), (j == len(kts) - 1)
                for half, O in enumerate((O1, O2)):
                    pt = pend[j][half]
                    P.mm(O, O[:, :n], BV, BV3[:, kts[j], 0:128], pt, pt[:, :n], start=st_, stop=sp_)
                    if half == 0:
                        sa = sacc[0]
                        if j == 0:
                            P.cp("dve", sa, sa[:, :n], pt, pt[:, :n])
                        else:
                            P.tt("dve", sa, sa[:, :n], sa, sa[:, :n], pt, pt[:, :n], ALU.add)
                    else:
                        P.mm(S2, S2[:, :n], ones, ones[:], pt, pt[:, :n], start=st_, stop=sp_)
        sb = s3[cnt["s"] % 4]
        cnt["s"] += 1
        P.mm(sb, sb[:, :n], onesf, onesf[:, :], sacc[0], sacc[0][:, :n])
        P.op("dve", lambda e, n=n, sb=sb: e.reciprocal(rec[:, :n], sb[:, :n]), reads=[sb], writes=[rec])
        P.op("dve", lambda e, n=n: e.reciprocal(rec2[:, :n], S2[:, :n]), reads=[S2], writes=[rec2])
        P.tt("dve", yf, yf[:, :n], O1, O1[:, :n], rec, rec[:, :n], ALU.mult)
        P.tt("dve", yf2, yf2[:, :n], O2, O2[:, :n], rec2, rec2[:, :n], ALU.mult)
        P.stt("dve", yf, yf[:, :n], yf2, yf2[:, :n], neglam[:, 0:1], yf, yf[:, :n], ALU.mult, ALU.add, extra=[neglam])
        pt = npt()
        P.act(pt, pt[:, :n], yf, yf[:, :n], AF.Square)
        sb = s3[cnt["s"] % 4]
        cnt["s"] += 1
        P.mm(sb, sb[:, :n], ones, ones[:], pt, pt[:, :n])
        P.rsqrt(bsb, bsb[:, :n], sb, sb[:, :n], 1.0 / 128, epst, epst[:, 0:1])
        P.tt("dve", yf, yf[:, :n], yf, yf[:, :n], bsb, bsb[:, :n], ALU.mult)
        y_ = ysb[c % 2]
        P.ts("pool", y_, y_[:, :n], yf, yf[:, :n], gss[:, 0:1], None, ALU.mult, extra=[gss])
        P.dma(ycT, ycT[:, tok0:tok0 + n], y_, y_[:, :n])
    P.finish()
    return P.build()


def _pm(vec):
    return np.ascontiguousarray(np.asarray(vec).reshape(-1, 128).T)


def _partner_perm(D):
    m = D // 2
    half = m // 2
    perm = np.zeros(D, np.int64)
    sign = np.zeros(D, np.float32)
    for d in range(D):
        seg, i = d // m, d % m
        if i < half:
            perm[d] = seg * m + i + half
            sign[d] = -1.0
        else:
            perm[d] = seg * m + i - half
            sign[d] = 1.0
    return perm, sign


def _rope_table(D):
    m = D // 2
    half = m // 2
    inv = (np.float32(10000.0) ** (-np.arange(0, m, 2, dtype=np.float32) / np.float32(m))).astype(np.float32)
    t = np.arange(16384)
    row = (t // 64).astype(np.float32)
    col = (t % 64).astype(np.float32)
    _, sign = _partner_perm(D)
    tab = np.zeros((D, 2, NT), np.float32)
    tab[:, 0, :256] = 1.0
    for d in range(D):
        seg, i = d // m, d % m
        fi = i % half
        pos = row if seg == 0 else col
        ang = (pos * inv[fi]).astype(np.float32)
        tab[d, 0, 256:] = np.cos(ang)
        tab[d, 1, 256:] = sign[d] * np.sin(ang)
    return tab


def _na_bias(rpb2):
    out = np.full((2, 3, 768, 256), -30000.0, np.float32)
    for var, (r0, se) in enumerate(((0, 0), (8, 4), (252, 244))):
        kr = se + np.arange(768) // 64
        kc = np.arange(768) % 64
        r = r0 + np.arange(256) // 64
        c = np.arange(256) % 64
        rs = np.clip(r - 4, 0, 248)
        cs = np.clip(c - 8, 0, 48)
        okr = (kr[:, None] >= rs[None, :]) & (kr[:, None] < rs[None, :] + 8)
        okc = (kc[:, None] >= cs[None, :]) & (kc[:, None] < cs[None, :] + 16)
        ok = okr & okc
        ro = np.clip(kr[:, None] - r[None, :] + 7, 0, 14)
        co = np.clip(kc[:, None] - c[None, :] + 15, 0, 30)
        for h in range(2):
            g = rpb2[h][ro, co]
            out[h, var] = np.where(ok, g, np.float32(-30000.0))
    return out


_CONST = {}


def _consts():
    if not _CONST:
        _CONST["cs32"] = _rope_table(32)
        t64 = _rope_table(64)
        _CONST["cs128"] = np.ascontiguousarray(np.concatenate([t64, t64], 0))
        _CONST["ident"] = np.eye(128, dtype=np.float32)
    return _CONST


def prep_A(inp, l, mod, x_cur, ctx_cur):
    C = _consts()
    w_in = inp["w_in"][l]
    p32, _ = _partner_perm(32)
    p64, _ = _partner_perm(64)
    p128 = np.concatenate([p64, 64 + p64])
    maps = []
    for core in range(8):
        b, g = core // 4, core % 4
        xin = np.ascontiguousarray(np.concatenate([ctx_cur[b], x_cur[b]], 0))
        msT = np.stack([_pm(mod[0:1024, b]), _pm(mod[1024:2048, b]), _pm(mod[0:1024, 2]), _pm(mod[1024:2048, 2])], -1)
        kr = w_in[:, 1920:1952]
        dq = w_in[:, 1952 + 128 * g: 1952 + 128 * (g + 1)]
        dk = w_in[:, 2464 + 128 * g: 2464 + 128 * (g + 1)]
        wfeat = np.concatenate([
            w_in[:, 128 * g:128 * (g + 1)], w_in[:, 512 + 128 * g:512 + 128 * (g + 1)],
            w_in[:, 1536:1664], w_in[:, 1664:1792], w_in[:, 1792:1920],
            kr, kr[:, p32], dq, dq[:, p128], dk, dk[:, p128]], 1)
        wtok = np.concatenate([w_in[:, 1024 + 128 * g:1024 + 128 * (g + 1)], w_in[:, 2976 + 128 * g:2976 + 128 * (g + 1)]], 1)
        wuq = np.zeros((256, 2, 192), np.float32)
        wukvk = np.zeros((128, 2, 64), np.float32)
        wukvv = np.zeros((128, 128), np.float32)
        for hh in range(2):
            H = 2 * g + hh
            uq = inp["mla_w_uq"][l][:, H * 96:(H + 1) * 96]
            rope_c = uq[:, 64:96]
            wuq[:, hh, 0:96] = uq
            wuq[:, hh, 160:192] = rope_c[:, p32]
            ukv = inp["mla_w_ukv"][l][:, H * 128:(H + 1) * 128]
            wukvk[:, hh, :] = ukv[:, 0:64]
            wukvv[:, hh * 64:(hh + 1) * 64] = ukv[:, 64:128]
        lamv = np.stack([inp["diff_lq1"][l], inp["diff_lk1"][l], inp["diff_lq2"][l], inp["diff_lk2"][l]], 0)
        maps.append({
            "xin": xin, "msT": np.ascontiguousarray(msT), "wfeat": np.ascontiguousarray(wfeat),
            "wtok": np.ascontiguousarray(wtok), "wuq": wuq, "wukvk": wukvk, "wukvv": wukvv,
            "gq": _pm(inp["mla_g_q"][l]), "gkv": _pm(inp["mla_g_kv"][l]),
            "cs32": C["cs32"], "cs128": C["cs128"],
            "nabias": _na_bias(inp["na_rpb"][l][2 * g:2 * g + 2]),
            "lamv": np.ascontiguousarray(np.broadcast_to(lamv[None], (128, 4, 64))),
            "gsub": _pm(inp["diff_g_sub"][l]), "ident": C["ident"],
        })
    return maps


NB = 4160
ALPHA = 4.0 ** 0.25


def chunkB(c):
    return (0, 64) if c == 0 else (64 + (c - 1) * 512, 512)


def ln_tile(P, src_t, src_ap, np_, st6, mv, rstd, epst, out_t, out_ap):
    for a_ in range(2):
        P.op("dve", lambda e, a_=a_: e.bn_stats(st6[:np_, a_, :], src_ap[:, a_ * 512:(a_ + 1) * 512]),
             reads=[src_t], writes=[st6])
    P.op("dve", lambda e: e.bn_aggr(mv[:np_, :], st6[:np_].rearrange("p a b -> p (a b)")), reads=[st6], writes=[mv])
    P.rsqrt(rstd, rstd[:np_, :], mv, mv[:np_, 1:2], 1.0, epst, epst[:np_, 0:1])
    P.ts("dve", out_t, out_ap, src_t, src_ap, mv[:np_, 0:1], rstd[:np_, 0:1], ALU.subtract, ALU.mult, extra=[mv, rstd])


def build_B():
    P = Prog("B")
    EI, EO = "ExternalInput", "ExternalOutput"
    xin = P.dram("xin", [NB, 1024], F32, EI)
    yT = P.dram("yT", [1536, NB], BF16, EI)
    msT = P.dram("msT", [128, 8, 4], F32, EI)
    wgate = P.dram("wgate", [1024, 3072], F32, EI)
    bgT = P.dram("bgT", [128, 24], F32, EI)
    wbr = P.dram("wbr", [1536, 1024], F32, EI)
    wout = P.dram("wout", [1024, 1024], F32, EI)
    wrt = P.dram("wrt", [128, 8, 16], F32, EI)
    rows = P.dram("rows", [128, 8, 1024], F32, EI)
    identd = P.dram("ident", [128, 128], F32, EI)
    x1o = P.dram("x1", [NB, 1024], F32, EO)
    h2tmo = P.dram("h2tm", [NB, 1024], BF16, EO)
    affo = P.dram("aff", [NB, 16], F32, EO)
    P.start()
    P.banks(7)
    tp = P.psum("tp", [128, 1024], BF16)
    Wg = P.sbuf("Wg", [128, 8, 3072], BF16)
    Wb = P.sbuf("Wb", [128, 12, 1024], BF16)
    Wo = P.sbuf("Wo", [128, 8, 1024], BF16)
    for k in range(8):
        P.dma(Wg, Wg[:, k, :], wgate, wgate[k * 128:(k + 1) * 128, :], q="pool")
        P.dma(Wo, Wo[:, k, :], wout, wout[k * 128:(k + 1) * 128, :], q="pool")
    for k in range(12):
        P.dma(Wb, Wb[:, k, :], wbr, wbr[k * 128:(k + 1) * 128, :], q="pool")
    ident = P.sbuf("identb", [128, 128], BF16)
    P.dma(ident, ident[:], identd, identd[:], q="pool")
    bg = P.sbuf("bg", [128, 24], F32)
    P.dma(bg, bg[:], bgT, bgT[:])
    ms = P.sbuf("ms", [128, 8, 4], F32)
    P.dma(ms, ms[:], msT, msT[:])
    scp = P.sbuf("scp", [128, 8, 2], F32)
    for m in range(2):
        P.ts("dve", scp, scp[:, :, m], ms, ms[:, :, 2 * m + 1], 1.0, None, ALU.add)
    rw = P.sbuf("rw", [128, 8, 1024], F32)
    P.dma(rw, rw[:], rows, rows[:])
    for j in (4, 6):
        P.ts("dve", rw, rw[:, j, :], rw, rw[:, j, :], 1.0, None, ALU.add)
    wr_f = P.sbuf("wr_f", [128, 8, 16], F32)
    P.dma(wr_f, wr_f[:], wrt, wrt[:])
    wr_hi = P.sbuf("wr_hi", [128, 8, 16], BF16)
    wr_lo = P.sbuf("wr_lo", [128, 8, 16], BF16)
    P.cp("dve", wr_hi, wr_hi[:], wr_f, wr_f[:])
    P.tt("dve", wr_lo, wr_lo[:], wr_f, wr_f[:], wr_hi, wr_hi[:], ALU.subtract)
    epst = P.sbuf("epst", [128, 1], F32)
    P.memset("dve", epst, epst[:], LN_EPS)

    xc = P.sbuf("xc", [128, 4, 1024], F32)
    xnb = [P.sbuf("xn%d" % i, [128, 1024], BF16) for i in range(2)]
    st6 = P.sbuf("st6", [128, 2, 6], F32)
    mv = P.sbuf("mv", [128, 2], F32)
    rstd = P.sbuf("rstd", [128, 1], F32)
    htmp = P.sbuf("htmp", [128, 8, 128], F32)
    hT = P.sbuf("hT", [128, 8, 512], BF16)
    yt = P.sbuf("yt", [128, 12, 512], BF16)
    sig = [P.sbuf("sig%d" % i, [128, 512], F32) for i in range(2)]
    mf = P.sbuf("mf", [128, 512], F32)
    mt2 = P.sbuf("mt2", [128, 512], F32)
    mT = P.sbuf("mT", [128, 8, 512], BF16)
    u = P.sbuf("u", [128, 1024], F32)
    ut = P.sbuf("ut", [128, 1024], F32)
    x1 = [P.sbuf("x1_%d" % i, [128, 1024], F32) for i in range(1)]
    h2 = P.sbuf("h2", [128, 1024], F32)
    h2hi = P.sbuf("h2hi", [128, 1024], BF16)
    h2lo = P.sbuf("h2lo", [128, 1024], BF16)
    h2Ts = [P.sbuf("h2Ts%d" % i, [128, 8, 128], BF16) for i in range(1)]
    h2Tl = P.sbuf("h2Tl", [128, 8, 128], BF16)
    lg = P.sbuf("lg", [128, 16], F32)
    mx = P.sbuf("mx", [128, 1], F32)
    sm = P.sbuf("sm", [128, 1], F32)
    affs = [P.sbuf("affs%d" % i, [128, 16], F32) for i in range(2)]

    for c in range(9):
        tok0, n = chunkB(c)
        m = 1 if c == 0 else 0
        tsz = 64 if c == 0 else 128
        nt = n // tsz
        P.dma(yt, yt[:, :, :n], yT, yT[:, tok0:tok0 + n].rearrange("(j p) t -> p j t", p=128))
        for i in range(nt):
            xn = xnb[i % 2]
            P.dma(xc, xc[:tsz, i, :], xin, xin[tok0 + i * tsz: tok0 + (i + 1) * tsz, :])
            ln_tile(P, xc, xc[:tsz, i, :], tsz, st6, mv, rstd, epst, xn, xn[:tsz, :])
            for k in range(8):
                P.tr(tp, tp[:, k * 128:k * 128 + tsz], xn, xn[:tsz, k * 128:(k + 1) * 128], ident, ident[:tsz, :tsz])
            P.tt("dve", htmp, htmp[:, :, :tsz], tp, tp[:].rearrange("p (a b) -> p a b", a=8)[:, :, :tsz],
                 scp, scp[:, :, m:m + 1].to_broadcast([128, 8, tsz]), ALU.mult)
            P.tt("pool", hT, hT[:, :, i * tsz:(i + 1) * tsz], htmp, htmp[:, :, :tsz],
                 ms, ms[:, :, 2 * m:2 * m + 1].to_broadcast([128, 8, tsz]), ALU.add)
        for oc in range(8):
            for br in range(3):
                G = P.bank()
                for k in range(8):
                    P.mm(G, G[:, :n], Wg, Wg[:, k, br * 1024 + oc * 128: br * 1024 + (oc + 1) * 128], hT, hT[:, k, :n],
                         start=(k == 0), stop=(k == 7))
                sg = sig[br % 2]
                P.act(sg, sg[:, :n], G, G[:, :n], AF.Sigmoid, extra=[bg], bias=bg[:, br * 8 + oc: br * 8 + oc + 1])
                Pb = P.bank()
                for k in range(4):
                    P.mm(Pb, Pb[:, :n], Wb, Wb[:, br * 4 + k, oc * 128:(oc + 1) * 128], yt, yt[:, br * 4 + k, :n],
                         start=(k == 0), stop=(k == 3))
                if br == 0:
                    P.tt("dve", mf, mf[:, :n], Pb, Pb[:, :n], sg, sg[:, :n], ALU.mult)
                else:
                    P.tt("dve", mt2, mt2[:, :n], Pb, Pb[:, :n], sg, sg[:, :n], ALU.mult)
                    P.tt("pool", mf, mf[:, :n], mf, mf[:, :n], mt2, mt2[:, :n], ALU.add)
            P.cp("pool", mT, mT[:, oc, :n], mf, mf[:, :n])
        for i in range(nt):
            for half in range(2):
                O = P.bank()
                for k in range(8):
                    P.mm(O, O[:tsz, :], mT, mT[:, k, i * tsz:(i + 1) * tsz], Wo, Wo[:, k, half * 512:(half + 1) * 512],
                         start=(k == 0), stop=(k == 7))
                hs = slice(half * 512, (half + 1) * 512)
                P.tt("dve", ut, ut[:tsz, hs], O, O[:tsz, :], rw, rw[:tsz, m, hs], ALU.mult)
            P.stt("dve", u, u[:tsz, :], xc, xc[:tsz, i, :], ALPHA, ut, ut[:tsz, :], ALU.mult, ALU.add)
            x1t = x1[0]
            ln_tile(P, u, u[:tsz, :], tsz, st6, mv, rstd, epst, x1t, x1t[:tsz, :])
            P.tt("dve", x1t, x1t[:tsz, :], x1t, x1t[:tsz, :], rw, rw[:tsz, 2, :], ALU.mult)
            P.tt("pool", x1t, x1t[:tsz, :], x1t, x1t[:tsz, :], rw, rw[:tsz, 3, :], ALU.add)
            r0 = tok0 + i * tsz
            P.dma(x1o, x1o[r0:r0 + tsz, :], x1t, x1t[:tsz, :])
            ln_tile(P, x1t, x1t[:tsz, :], tsz, st6, mv, rstd, epst, h2, h2[:tsz, :])
            P.tt("dve", h2, h2[:tsz, :], h2, h2[:tsz, :], rw, rw[:tsz, 4 + 2 * m, :], ALU.mult)
            P.tt("pool", h2, h2[:tsz, :], h2, h2[:tsz, :], rw, rw[:tsz, 5 + 2 * m, :], ALU.add)
            P.cp("act", h2hi, h2hi[:tsz, :], h2, h2[:tsz, :])
            P.tt("dve", h2lo, h2lo[:tsz, :], h2, h2[:tsz, :], h2hi, h2hi[:tsz, :], ALU.subtract)
            hs_ = h2Ts[0]
            for k in range(8):
                P.tr(tp, tp[:, k * 128:k * 128 + tsz], h2hi, h2hi[:tsz, k * 128:(k + 1) * 128], ident, ident[:tsz, :tsz])
            P.cp("act", hs_, hs_[:, :, :tsz], tp, tp[:].rearrange("p (a b) -> p a b", a=8)[:, :, :tsz])
            P.dma(h2tmo, h2tmo[r0:r0 + tsz, :], h2hi, h2hi[:tsz, :])
            for k in range(8):
                P.tr(tp, tp[:, k * 128:k * 128 + tsz], h2lo, h2lo[:tsz, k * 128:(k + 1) * 128], ident, ident[:tsz, :tsz])
            P.cp("dve", h2Tl, h2Tl[:, :, :tsz], tp, tp[:].rearrange("p (a b) -> p a b", a=8)[:, :, :tsz])
            L = P.bank()
            for k in range(8):
                P.mm(L, L[:tsz, 0:16], hs_, hs_[:, k, :tsz], wr_hi, wr_hi[:, k, :], start=(k == 0), stop=False)
                P.mm(L, L[:tsz, 0:16], hs_, hs_[:, k, :tsz], wr_lo, wr_lo[:, k, :], start=False, stop=False)
                P.mm(L, L[:tsz, 0:16], h2Tl, h2Tl[:, k, :tsz], wr_hi, wr_hi[:, k, :], start=False, stop=(k == 7))
            P.op("dve", lambda e, L=L, tsz=tsz: e.reduce_max(mx[:tsz, :], L[:tsz, 0:16], AX.X), reads=[L], writes=[mx])
            P.ts("dve", lg, lg[:tsz, :], L, L[:tsz, 0:16], mx[:tsz, 0:1], None, ALU.subtract, extra=[mx])
            P.act(lg, lg[:tsz, :], lg, lg[:tsz, :], AF.Exp)
            P.op("dve", lambda e, tsz=tsz: e.reduce_sum(sm[:tsz, :], lg[:tsz, :], AX.X), reads=[lg], writes=[sm])
            P.op("dve", lambda e, tsz=tsz: e.reciprocal(sm[:tsz, :], sm[:tsz, :]), reads=[sm], writes=[sm])
            af_ = affs[i % 2]
            P.ts("dve", af_, af_[:tsz, :], lg, lg[:tsz, :], sm[:tsz, 0:1], None, ALU.mult, extra=[sm])
            P.dma(affo, affo[r0:r0 + tsz, :], af_, af_[:tsz, :])
    P.finish()
    return P.build()


def prep_B(inp, l, mod, x_cur, ctx_cur, yTs):
    C = _consts()
    wbr = np.ascontiguousarray(np.concatenate([inp["w_br_a"][l], inp["w_br_b"][l], inp["w_br_c"][l]], 0))
    maps = []
    for core in range(8):
        b, s = core // 4, core % 4
        xin = np.concatenate([ctx_cur[b][64 * s:64 * (s + 1)], x_cur[b][4096 * s:4096 * (s + 1)]], 0)
        yT = np.concatenate([yTs[b][:, 64 * s:64 * (s + 1)], yTs[b][:, 256 + 4096 * s:256 + 4096 * (s + 1)]], 1)
        msT = np.stack([_pm(mod[0:1024, b]), _pm(mod[1024:2048, b]), _pm(mod[0:1024, 2]), _pm(mod[1024:2048, 2])], -1)
        rows = np.stack([mod[2048:3072, b], mod[2048:3072, 2], inp["ln1_g"][l], inp["ln1_b"][l],
                         mod[4096:5120, b], mod[3072:4096, b], mod[4096:5120, 2], mod[3072:4096, 2]], 0)
        maps.append({
            "xin": np.ascontiguousarray(xin), "yT": np.ascontiguousarray(yT), "msT": np.ascontiguousarray(msT),
            "wgate": inp["w_gate"][l], "bgT": _pm(inp["b_gate"][l]), "wbr": wbr, "wout": inp["w_out"][l],
            "wrt": np.ascontiguousarray(inp["w_router"][l].reshape(8, 128, 16).transpose(1, 0, 2)),
            "rows": np.ascontiguousarray(np.broadcast_to(rows[None], (128, 8, 1024))), "ident": C["ident"],
        })
    return maps


def chunkC(c):
    return (0, 256) if c == 0 else (256 + (c - 1) * 1024, 1024)


def build_C_dense():
    P = Prog("C")
    EI, EO = "ExternalInput", "ExternalOutput"
    h2T = P.dram("h2T", [1024, NT], BF16, EI)
    affd = P.dram("affL", [128, 4, 130], F32, EI)
    wg = P.dram("wg", [4, 1024, 2816], F32, EI)
    wu = P.dram("wu", [4, 1024, 2816], F32, EI)
    wd = P.dram("wd", [4, 2816, 1024], F32, EI)
    fpo = P.dram("fp", [NT, 1024], F32, EO)
    P.start()
    P.banks(7)
    onesf = P.sbuf("onesf", [128, 128], F32)
    P.memset("dve", onesf, onesf[:], 1.0)
    aff = P.sbuf("aff", [128, 4, 130], F32)
    P.dma(aff, aff[:], affd, affd[:])
    lo = P.sbuf("lo", [128, 8], F32)
    hi = P.sbuf("hi", [128, 8], F32)
    mid = P.sbuf("mid", [128, 8], F32)
    kv = P.sbuf("kv", [128, 8], F32)
    cnt = P.sbuf("cnt", [128, 8], F32)
    ge = P.sbuf("ge", [128, 8], F32)
    d1 = P.sbuf("d1", [128, 8], F32)
    cmp_ = P.sbuf("cmp", [128, 128], F32)
    P.memset("dve", lo, lo[:], 0.0)
    P.memset("dve", hi, hi[:], 1.0)
    P.memset("dve", kv, kv[:, 0:4], 2048.0)
    P.memset("dve", kv, kv[:, 4:8], 32.0)
    for it in range(30):
        P.tt("dve", mid, mid[:], lo, lo[:], hi, hi[:], ALU.add)
        P.ts("dve", mid, mid[:], mid, mid[:], 0.5, None, ALU.mult)
        for e in range(4):
            P.ts("dve", cmp_, cmp_[:, 0:128], aff, aff[:, e, 2:130], mid[:, e:e + 1], None, ALU.is_ge, extra=[mid])
            P.op("dve", lambda e_, e=e: e_.reduce_sum(cnt[:, e:e + 1], cmp_[:, 0:128], AX.X), reads=[cmp_], writes=[cnt])
            P.ts("dve", cmp_, cmp_[:, 0:2], aff, aff[:, e, 0:2], mid[:, 4 + e:5 + e], None, ALU.is_ge, extra=[mid])
            P.op("dve", lambda e_, e=e: e_.reduce_sum(cnt[:, 4 + e:5 + e], cmp_[:, 0:2], AX.X), reads=[cmp_], writes=[cnt])
        tb = P.bank()
        P.mm(tb, tb[:, 0:8], onesf, onesf[:], cnt, cnt[:])
        P.tt("dve", ge, ge[:], tb, tb[:, 0:8], kv, kv[:], ALU.is_ge)
        P.tt("dve", d1, d1[:], mid, mid[:], lo, lo[:], ALU.subtract)
        P.tt("dve", d1, d1[:], d1, d1[:], ge, ge[:], ALU.mult)
        P.tt("dve", lo, lo[:], lo, lo[:], d1, d1[:], ALU.add)
        P.tt("dve", d1, d1[:], hi, hi[:], mid, mid[:], ALU.subtract)
        P.tt("dve", d1, d1[:], d1, d1[:], ge, ge[:], ALU.mult)
        P.tt("dve", hi, hi[:], mid, mid[:], d1, d1[:], ALU.add)
    gm = P.sbuf("gm", [128, 4, 130], F32)
    for e in range(4):
        P.ts("dve", gm, gm[:, e, 2:130], aff, aff[:, e, 2:130], lo[:, e:e + 1], None, ALU.is_ge, extra=[lo])
        P.ts("dve", gm, gm[:, e, 0:2], aff, aff[:, e, 0:2], lo[:, 4 + e:5 + e], None, ALU.is_ge, extra=[lo])
    P.tt("dve", gm, gm[:], gm, gm[:], aff, aff[:], ALU.mult)

    hxs = [P.sbuf("hx%d" % i, [128, 8, 1024], BF16) for i in range(2)]
    facc = P.sbuf("facc", [128, 8, 1024], F32)
    NWB = 4
    wgs = [P.sbuf("wgu%d" % i, [128, 8, 256], BF16) for i in range(NWB)]
    wus = [P.sbuf("wuu%d" % i, [128, 8, 256], BF16) for i in range(NWB)]
    wds = [P.sbuf("wdu%d" % i, [128, 2, 1024], BF16) for i in range(NWB)]
    hids = [P.sbuf("hid%d" % i, [128, 2, 1024], BF16) for i in range(2)]
    sgs = [P.sbuf("sg%d" % i, [128, 512], F32) for i in range(2)]
    ui = 0
    si = 0
    for c in range(17):
        tok0, n = chunkC(c)
        hx = hxs[c % 2]
        P.dma(hx, hx[:, :, :n], h2T, h2T[:, tok0:tok0 + n].rearrange("(k p) t -> p k t", p=128))
        P.memset("pool", facc, facc[:], 0.0)
        units = [(e, un) for e in range(4) for un in range(11)]
        pend = []
        for idx in range(len(units) + 1):
            if idx < len(units):
                e, un = units[idx]
                f0 = un * 256
                wgu, wuu, wdu, hid = wgs[ui % NWB], wus[ui % NWB], wds[ui % NWB], hids[ui % 2]
                ui += 1
                P.dma(wgu, wgu[:], wg, wg[e][:, f0:f0 + 256].rearrange("(k p) f -> p k f", p=128), q="pool")
                P.dma(wuu, wuu[:], wu, wu[e][:, f0:f0 + 256].rearrange("(k p) f -> p k f", p=128), q="pool")
                P.dma(wdu, wdu[:], wd, wd[e][f0:f0 + 256, :].rearrange("(c p) o -> p c o", p=128), q="pool")
                for fc in range(2):
                    for th in range(n // 512 if n >= 512 else 1):
                        w_ = min(512, n)
                        ts_ = slice(th * 512, th * 512 + w_)
                        G = P.bank()
                        for k in range(8):
                            P.mm(G, G[:, :w_], wgu, wgu[:, k, fc * 128:(fc + 1) * 128], hx, hx[:, k, ts_], start=(k == 0), stop=(k == 7))
                        U = P.bank()
                        for k in range(8):
                            P.mm(U, U[:, :w_], wuu, wuu[:, k, fc * 128:(fc + 1) * 128], hx, hx[:, k, ts_], start=(k == 0), stop=(k == 7))
                        sg = sgs[si % 2]
                        si += 1
                        P.act(sg, sg[:, :w_], G, G[:, :w_], AF.Silu)
                        P.tt("dve", hid, hid[:, fc, ts_], U, U[:, :w_], sg, sg[:, :w_], ALU.mult)
                pend.append((e, wdu, hid))
            j = idx - 1
            if j >= 0:
                e, wdu, hid = pend[j]
                for tl in range(n // 128):
                    gt = tok0 // 128 + tl
                    for oh in range(2):
                        O = P.bank()
                        for fc in range(2):
                            P.mm(O, O[:, :], hid, hid[:, fc, tl * 128:(tl + 1) * 128], wdu, wdu[:, fc, oh * 512:(oh + 1) * 512],
                                 start=(fc == 0), stop=(fc == 1))
                        fs = facc[:, tl, oh * 512:(oh + 1) * 512]
                        P.stt("dve", facc, fs, O, O[:, :], gm[:, e, gt:gt + 1], facc, fs, ALU.mult, ALU.add, extra=[gm])
        P.dma(fpo, fpo[tok0:tok0 + n, :].rearrange("(t p) o -> p t o", p=128), facc, facc[:, :n // 128, :])
    P.finish()
    return P.build()


XROWS = 2304
TRASH0 = 2176


def build_C():
    P = Prog("C")
    EI, EO = "ExternalInput", "ExternalOutput"
    h2tm = P.dram("h2tm", [NT, 1024], BF16, EI)
    affd = P.dram("affL", [128, 4, 130], F32, EI)
    trid = P.dram("tri", [128, 128], F32, EI)
    trashd = P.dram("trashc", [128, 1], F32, EI)
    identd = P.dram("ident", [128, 128], F32, EI)
    wg = P.dram("wg", [4, 1024, 2816], F32, EI)
    wu = P.dram("wu", [4, 1024, 2816], F32, EI)
    wd = P.dram("wd", [4, 2816, 1024], F32, EI)
    fpo = P.dram("fp", [NT, 1024], F32, EO)
    xsel = [P.dram("xsel%d" % e, [XROWS, 1024], BF16) for e in range(4)]
    osel = [P.dram("osel%d" % e, [XROWS, 1024], F32) for e in range(4)]
    P.start()
    P.banks(7)
    tp = P.psum("tp", [128, 1024], BF16)
    onesf = P.sbuf("onesf", [128, 128], F32)
    P.memset("dve", onesf, onesf[:], 1.0)
    onesb = P.sbuf("onesb", [128, 128], BF16)
    P.memset("dve", onesb, onesb[:], 1.0)
    trib = P.sbuf("trib", [128, 128], BF16)
    P.dma(trib, trib[:], trid, trid[:], q="pool")
    ident = P.sbuf("identb", [128, 128], BF16)
    P.dma(ident, ident[:], identd, identd[:], q="pool")
    trc = P.sbuf("trc", [128, 1], F32)
    P.dma(trc, trc[:], trashd, trashd[:])
    aff = P.sbuf("aff", [128, 4, 130], F32)
    P.dma(aff, aff[:], affd, affd[:])
    lo = P.sbuf("lo", [128, 8], F32)
    hi = P.sbuf("hi", [128, 8], F32)
    mid = P.sbuf("mid", [128, 8], F32)
    kv = P.sbuf("kv", [128, 8], F32)
    cnt = P.sbuf("cnt", [128, 8], F32)
    ge = P.sbuf("ge", [128, 8], F32)
    d1 = P.sbuf("d1", [128, 8], F32)
    cmp_ = P.sbuf("cmp", [128, 128], F32)
    P.memset("dve", lo, lo[:], 0.0)
    P.memset("dve", hi, hi[:], 1.0)
    P.memset("dve", kv, kv[:, 0:4], 2048.0)
    P.memset("dve", kv, kv[:, 4:8], 32.0)
    for it in range(30):
        P.tt("dve", mid, mid[:], lo, lo[:], hi, hi[:], ALU.add)
        P.ts("dve", mid, mid[:], mid, mid[:], 0.5, None, ALU.mult)
        for e in range(4):
            P.ts("dve", cmp_, cmp_[:, 0:128], aff, aff[:, e, 2:130], mid[:, e:e + 1], None, ALU.is_ge, extra=[mid])
            P.op("dve", lambda e_, e=e: e_.reduce_sum(cnt[:, e:e + 1], cmp_[:, 0:128], AX.X), reads=[cmp_], writes=[cnt])
            P.ts("dve", cmp_, cmp_[:, 0:2], aff, aff[:, e, 0:2], mid[:, 4 + e:5 + e], None, ALU.is_ge, extra=[mid])
            P.op("dve", lambda e_, e=e: e_.reduce_sum(cnt[:, 4 + e:5 + e], cmp_[:, 0:2], AX.X), reads=[cmp_], writes=[cnt])
        tb = P.bank()
        P.mm(tb, tb[:, 0:8], onesf, onesf[:], cnt, cnt[:])
        P.tt("dve", ge, ge[:], tb, tb[:, 0:8], kv, kv[:], ALU.is_ge)
        P.tt("dve", d1, d1[:], mid, mid[:], lo, lo[:], ALU.subtract)
        P.tt("dve", d1, d1[:], d1, d1[:], ge, ge[:], ALU.mult)
        P.tt("dve", lo, lo[:], lo, lo[:], d1, d1[:], ALU.add)
        P.tt("dve", d1, d1[:], hi, hi[:], mid, mid[:], ALU.subtract)
        P.tt("dve", d1, d1[:], d1, d1[:], ge, ge[:], ALU.mult)
        P.tt("dve", hi, hi[:], mid, mid[:], d1, d1[:], ALU.add)
    m = P.sbuf("m", [128, 4, 130], F32)
    for e in range(4):
        P.ts("dve", m, m[:, e, 2:130], aff, aff[:, e, 2:130], lo[:, e:e + 1], None, ALU.is_ge, extra=[lo])
        P.ts("dve", m, m[:, e, 0:2], aff, aff[:, e, 0:2], lo[:, 4 + e:5 + e], None, ALU.is_ge, extra=[lo])
    mb = P.sbuf("mb", [128, 4, 130], BF16)
    P.cp("dve", mb, mb[:], m, m[:])
    posf = P.sbuf("posf", [128, 4, 130], F32)
    posi = P.sbuf("posi", [128, 4, 130], I32)
    cT = P.sbuf("cT", [128, 128], BF16)
    offs = P.sbuf("offs", [128, 130], F32)
    ltm = P.sbuf("ltm", [128, 130], F32)
    for e in range(4):
        R = P.bank()
        P.mm(R, R[:, 0:130], trib, trib[:], mb, mb[:, e, :])
        CT = P.bank()
        P.mm(CT, CT[:, 0:128], mb, mb[:, e, 2:130], onesb, onesb[:])
        P.cp("act", cT, cT[:], CT, CT[:, 0:128])
        OF = P.bank()
        P.mm(OF, OF[:, 0:128], cT, cT[:], trib, trib[:])
        C0 = P.bank()
        P.mm(C0, C0[:, 0:2], onesb, onesb[:], mb, mb[:, e, 0:2])
        P.cp("act", offs, offs[:, 2:130], OF, OF[:, 0:128])
        P.memset("dve", offs, offs[:, 0:1], 2048.0)
        P.ts("dve", offs, offs[:, 1:2], C0, C0[:, 0:1], 2048.0, None, ALU.add)
        P.tt("dve", posf, posf[:, e, :], R, R[:, 0:130], offs, offs[:], ALU.add)
        P.ts("dve", ltm, ltm[:, 2:130], posf, posf[:, e, 2:130], 2047.5, None, ALU.is_lt)
        P.ts("dve", ltm, ltm[:, 0:2], posf, posf[:, e, 0:2], 2079.5, None, ALU.is_lt)
        P.tt("dve", m, m[:, e, :], m, m[:, e, :], ltm, ltm[:], ALU.mult)
        P.ts("dve", posf, posf[:, e, :], posf, posf[:, e, :], trc[:, 0:1], None, ALU.subtract, extra=[trc])
        P.tt("dve", posf, posf[:, e, :], posf, posf[:, e, :], m, m[:, e, :], ALU.mult)
        P.ts("dve", posf, posf[:, e, :], posf, posf[:, e, :], trc[:, 0:1], None, ALU.add, extra=[trc])
    P.cp("dve", posi, posi[:], posf, posf[:])
    gm = P.sbuf("gm", [128, 4, 130], F32)
    P.tt("dve", gm, gm[:], m, m[:], aff, aff[:], ALU.mult)

    htb = [P.sbuf("ht%d" % i, [128, 1024], BF16) for i in range(2)]
    for i in range(130):
        ht = htb[i % 2]
        P.dma(ht, ht[:], h2tm, h2tm[i * 128:(i + 1) * 128, :])
        for e in range(4):
            P.op("pool", lambda e_, e=e, i=i, ht=ht: e_.indirect_dma_start(
                out=xsel[e][:, :], out_offset=bass.IndirectOffsetOnAxis(ap=posi[:, e, i:i + 1], axis=0),
                in_=ht[:, :], in_offset=None),
                reads=[ht, posi], writes=[xsel[e]], dma_dst=xsel[e])
    zt = P.sbuf("zt", [128, 1024], F32)
    P.memset("pool", zt, zt[:], 0.0)
    for e in range(4):
        P.dma(osel[e], osel[e][TRASH0:TRASH0 + 128, :], zt, zt[:])

    NSL = 2080
    hx = P.sbuf("hx", [128, 8, NSL], BF16)
    oacc = P.sbuf("oacc", [128, 17, 1024], F32)
    NWB = 3
    wgs = [P.sbuf("wgu%d" % i, [128, 8, 256], BF16) for i in range(NWB)]
    wus = [P.sbuf("wuu%d" % i, [128, 8, 256], BF16) for i in range(NWB)]
    wds = [P.sbuf("wdu%d" % i, [128, 2, 1024], BF16) for i in range(NWB)]
    hids = [P.sbuf("hid%d" % i, [128, 2, NSL], BF16) for i in range(2)]
    sgs = [P.sbuf("sg%d" % i, [128, 512], F32) for i in range(2)]
    xts = [P.sbuf("xts%d" % i, [128, 1024], BF16) for i in range(2)]
    cchunks = [(j * 512, 512) for j in range(4)] + [(2048, 32)]
    ui = 0
    si = 0
    for e in range(4):
        for st in range(17):
            rows = 128 if st < 16 else 32
            xt = xts[st % 2]
            P.dma(xt, xt[:rows, :], xsel[e], xsel[e][st * 128:st * 128 + rows, :])
            for k in range(8):
                P.tr(tp, tp[:, k * 128:k * 128 + rows], xt, xt[:rows, k * 128:(k + 1) * 128], ident, ident[:rows, :rows])
            P.cp("act" if st % 2 else "dve", hx, hx[:, :, st * 128:st * 128 + rows],
                 tp, tp[:].rearrange("p (a b) -> p a b", a=8)[:, :, :rows])
        pend = []
        for idx in range(12):
            if idx < 11:
                f0 = idx * 256
                wgu, wuu, wdu, hid = wgs[ui % NWB], wus[ui % NWB], wds[ui % NWB], hids[ui % 2]
                ui += 1
                P.dma(wgu, wgu[:], wg, wg[e][:, f0:f0 + 256].rearrange("(k p) f -> p k f", p=128), q="pool")
                P.dma(wuu, wuu[:], wu, wu[e][:, f0:f0 + 256].rearrange("(k p) f -> p k f", p=128), q="pool")
                P.dma(wdu, wdu[:], wd, wd[e][f0:f0 + 256, :].rearrange("(c p) o -> p c o", p=128), q="pool")
                for fc in range(2):
                    for (c0, w_) in cchunks:
                        ts_ = slice(c0, c0 + w_)
                        G = P.bank()
                        for k in range(8):
                            P.mm(G, G[:, :w_], wgu, wgu[:, k, fc * 128:(fc + 1) * 128], hx, hx[:, k, ts_], start=(k == 0), stop=(k == 7))
                        U = P.bank()
                        for k in range(8):
                            P.mm(U, U[:, :w_], wuu, wuu[:, k, fc * 128:(fc + 1) * 128], hx, hx[:, k, ts_], start=(k == 0), stop=(k == 7))
                        sg = sgs[si % 2]
                        si += 1
                        P.act(sg, sg[:, :w_], G, G[:, :w_], AF.Silu)
                        P.tt("dve", hid, hid[:, fc, ts_], U, U[:, :w_], sg, sg[:, :w_], ALU.mult)
                pend.append((wdu, hid))
            j = idx - 1
            if j >= 0:
                wdu, hid = pend[j]
                for st in range(17):
                    rows = 128 if st < 16 else 32
                    for oh in range(2):
                        O = P.bank()
                        for fc in range(2):
                            P.mm(O, O[:rows, :], hid, hid[:, fc, st * 128:st * 128 + rows], wdu, wdu[:, fc, oh * 512:(oh + 1) * 512],
                                 start=(fc == 0), stop=(fc == 1))
                        os_ = oacc[:rows, st, oh * 512:(oh + 1) * 512]
                        if j == 0:
                            P.cp("act", oacc, os_, O, O[:rows, :])
                        else:
                            P.tt("dve", oacc, os_, O, O[:rows, :], oacc, os_, ALU.add)
        P.dma(osel[e], osel[e][0:2048, :].rearrange("(t p) o -> p t o", p=128), oacc, oacc[:, 0:16, :])
        P.dma(osel[e], osel[e][2048:2080, :], oacc, oacc[:32, 16, :])

    gts = [P.sbuf("gt%d" % i, [128, 4, 1024], F32) for i in range(1)]
    fos = [P.sbuf("fo0", [128, 1024], F32), zt]
    for i in range(130):
        gt, fo = gts[0], fos[i % 2]
        for e in range(4):
            P.op("pool", lambda e_, e=e, i=i, gt=gt: e_.indirect_dma_start(
                out=gt[:, e, :], out_offset=None, in_=osel[e][:, :],
                in_offset=bass.IndirectOffsetOnAxis(ap=posi[:, e, i:i + 1], axis=0)),
                reads=[osel[e], posi], writes=[gt], dma_dst=gt)
        P.ts("dve", fo, fo[:], gt, gt[:, 0, :], gm[:, 0, i:i + 1], None, ALU.mult, extra=[gm])
        for e in range(1, 4):
            P.stt("dve", fo, fo[:], gt, gt[:, e, :], gm[:, e, i:i + 1], fo, fo[:], ALU.mult, ALU.add, extra=[gm])
        P.dma(fpo, fpo[i * 128:(i + 1) * 128, :], fo, fo[:])
    P.finish()
    return P.build()


def build_D():
    P = Prog("D")
    EI, EO = "ExternalInput", "ExternalOutput"
    x1d = P.dram("x1", [NB, 1024], F32, EI)
    fps = P.dram("fps", [4, NB, 1024], F32, EI)
    rows = P.dram("rows", [128, 4, 1024], F32, EI)
    x2o = P.dram("x2", [NB, 1024], F32, EO)
    P.start()
    rw = P.sbuf("rw", [128, 4, 1024], F32)
    P.dma(rw, rw[:], rows, rows[:])
    epst = P.sbuf("epst", [128, 1], F32)
    P.memset("dve", epst, epst[:], LN_EPS)
    st6 = P.sbuf("st6", [128, 2, 6], F32)
    mv = P.sbuf("mv", [128, 2], F32)
    rstd = P.sbuf("rstd", [128, 1], F32)
    xts = [P.sbuf("xt%d" % i, [128, 1024], F32) for i in range(2)]
    pts = [P.sbuf("pp%d" % i, [128, 4, 1024], F32) for i in range(2)]
    u = P.sbuf("u", [128, 1024], F32)
    ots = [P.sbuf("ot%d" % i, [128, 1024], F32) for i in range(2)]
    for t in range(33):
        r0, tsz = (0, 64) if t == 0 else (64 + (t - 1) * 128, 128)
        m = 1 if t == 0 else 0
        xt, pp, ot = xts[t % 2], pts[t % 2], ots[t % 2]
        P.dma(xt, xt[:tsz, :], x1d, x1d[r0:r0 + tsz, :])
        for j in range(4):
            P.dma(pp, pp[:tsz, j, :], fps, fps[j, r0:r0 + tsz, :])
        P.tt("dve", pp, pp[:tsz, 0, :], pp, pp[:tsz, 0, :], pp, pp[:tsz, 1, :], ALU.add)
        P.tt("pool", pp, pp[:tsz, 2, :], pp, pp[:tsz, 2, :], pp, pp[:tsz, 3, :], ALU.add)
        P.tt("dve", pp, pp[:tsz, 0, :], pp, pp[:tsz, 0, :], pp, pp[:tsz, 2, :], ALU.add)
        P.tt("dve", pp, pp[:tsz, 0, :], pp, pp[:tsz, 0, :], rw, rw[:tsz, m, :], ALU.mult)
        P.stt("dve", u, u[:tsz, :], xt, xt[:tsz, :], ALPHA, pp, pp[:tsz, 0, :], ALU.mult, ALU.add)
        ln_tile(P, u, u[:tsz, :], tsz, st6, mv, rstd, epst, ot, ot[:tsz, :])
        P.tt("dve", ot, ot[:tsz, :], ot, ot[:tsz, :], rw, rw[:tsz, 2, :], ALU.mult)
        P.tt("pool", ot, ot[:tsz, :], ot, ot[:tsz, :], rw, rw[:tsz, 3, :], ALU.add)
        P.dma(x2o, x2o[r0:r0 + tsz, :], ot, ot[:tsz, :])
    P.finish()
    return P.build()


def _lam_init(l):
    import math
    return 0.8 - 0.6 * math.exp(-0.3 * l)


def kernel(**inp):
    inp = {k: np.asarray(v) for k, v in inp.items()}
    mods = run_L0(inp)
    x_cur = [np.asarray(inp["x"][b], np.float32) for b in range(2)]
    ctx_cur = [np.asarray(inp["ctx"][b], np.float32) for b in range(2)]
    for l in range(2):
        mod = mods[l]
        resA = run(build_A(_lam_init(l)), prep_A(inp, l, mod, x_cur, ctx_cur)).results
        yTs = []
        for b in range(2):
            parts = [np.asarray(resA[b * 4 + g][nm]) for nm in ("yaT", "ybT", "ycT") for g in range(4)]
            yTs.append(np.concatenate(parts, 0))
        del resA
        resB = run(build_B(), prep_B(inp, l, mod, x_cur, ctx_cur, yTs)).results
        x1p = [np.asarray(resB[c]["x1"]) for c in range(8)]
        h2Tf, afff = [], []
        for b in range(2):
            hp = [np.asarray(resB[b * 4 + s]["h2tm"]) for s in range(4)]
            ap = [np.asarray(resB[b * 4 + s]["aff"]) for s in range(4)]
            h2Tf.append(np.concatenate([p[:64] for p in hp] + [p[64:] for p in hp], 0))
            afff.append(np.concatenate([p[:64] for p in ap] + [p[64:] for p in ap], 0))
        del resB
        mapsC = []
        for core in range(8):
            b, j = core // 4, core % 4
            a4 = afff[b][:, 4 * j:4 * j + 4].reshape(130, 128, 4).transpose(1, 2, 0)
            mapsC.append({
                "h2tm": np.ascontiguousarray(h2Tf[b]), "affL": np.ascontiguousarray(a4),
                "tri": np.triu(np.ones((128, 128), np.float32), 1),
                "trashc": (TRASH0 + np.arange(128, dtype=np.float32)).reshape(128, 1),
                "ident": _consts()["ident"],
                "wg": np.ascontiguousarray(inp["w_exp_gate"][l][4 * j:4 * j + 4]),
                "wu": np.ascontiguousarray(inp["w_exp_up"][l][4 * j:4 * j + 4]),
                "wd": np.ascontiguousarray(inp["w_exp_down"][l][4 * j:4 * j + 4]),
            })
        resC = run(build_C(), mapsC).results
        fpc = [np.asarray(resC[c]["fp"]) for c in range(8)]
        del resC, mapsC
        rowsD = lambda b: np.ascontiguousarray(np.broadcast_to(
            np.stack([mod[5120:6144, b], mod[5120:6144, 2], inp["ln2_g"][l], inp["ln2_b"][l]], 0)[None], (128, 4, 1024)))
        mapsD = []
        for core in range(8):
            b, s = core // 4, core % 4
            fps = np.stack([np.concatenate([fpc[b * 4 + jj][64 * s:64 * (s + 1)],
                                            fpc[b * 4 + jj][256 + 4096 * s:256 + 4096 * (s + 1)]], 0) for jj in range(4)], 0)
            mapsD.append({"x1": x1p[core], "fps": np.ascontiguousarray(fps), "rows": rowsD(b)})
        resD = run(build_D(), mapsD).results
        for b in range(2):
            xp = [np.asarray(resD[b * 4 + s]["x2"]) for s in range(4)]
            ctx_cur[b] = np.concatenate([p[:64] for p in xp], 0)
            x_cur[b] = np.concatenate([p[64:] for p in xp], 0)
        del resD, mapsD, fpc
    return np.stack(x_cur, 0).astype(np.float32)
```

```python
import numpy as np
import concourse.bass as bass
import concourse.mybir as mybir
from concourse.bass_utils import run_bass_kernel_spmd

F32 = mybir.dt.float32
BF16 = mybir.dt.bfloat16
I32 = mybir.dt.int32
U32 = mybir.dt.uint32
ALU = mybir.AluOpType
AF = mybir.ActivationFunctionType
AX = mybir.AxisListType

COMPUTE = ("pe", "act", "dve", "pool")
SAME_ENGINE_SYNC = True
import os
SIGALL = bool(int(os.environ.get('K_SIGALL', '0')))
SIGENG = set(os.environ.get('K_SIGENG', 'act,dve,pool').split(','))


class T:
    def __init__(self, prog, handle, name):
        self.p = prog
        self.h = handle
        self.name = name
        self.lastw = None
        self.reads = []
        self.dsem = None
        self.dcnt = 0

    def __getitem__(self, idx):
        return self.h[idx]


class Prog:
    def __init__(self, name):
        self.name = name
        self.nc = bass.Bass("TRN2", target_bir_lowering=False)
        self.ops = {e: [] for e in ("pe", "act", "dve", "pool", "sp")}
        self.cnt = {e: 0 for e in COMPUTE}
        self.opidx = {e: {} for e in COMPUTE}
        self.seen = {e: {} for e in self.ops}
        self.ctxs = []
        self.tiles = []
        self.sems = {}
        self.outs = []
        self.nsem = 0

    def _enter(self, cm):
        v = cm.__enter__()
        self.ctxs.append(cm)
        return v

    def sem(self, name):
        self.nsem += 1
        return self._enter(self.nc.semaphore(name))

    def sbuf(self, name, shape, dt):
        t = T(self, self._enter(self.nc.sbuf_tensor(name, list(shape), dt)), name)
        self.tiles.append(t)
        return t

    def psum(self, name, shape, dt=F32):
        t = T(self, self._enter(self.nc.psum_tensor(name, list(shape), dt)), name)
        self.tiles.append(t)
        return t

    def dram(self, name, shape, dt, kind="Internal", addr_space="Local"):
        h = self.nc.dram_tensor(name, list(shape), dt, kind=kind, addr_space=addr_space)
        t = T(self, h.ap(), name)
        self.tiles.append(t)
        if kind == "ExternalOutput":
            self.outs.append(t)
        return t

    def start(self):
        for e in COMPUTE:
            self.sems[e] = self.sem("s_" + e)

    def _need(self, eng, hz, waits):
        if hz is None:
            return
        kind, key, val = hz
        if kind == "eng" and key == eng and (eng == "pe" or not SAME_ENGINE_SYNC):
            return
        k = (kind, key if kind == "eng" else id(key))
        if self.seen[eng].get(k, 0) >= val:
            return
        self.seen[eng][k] = val
        if kind == "eng":
            rec = self.ops[key][self.opidx[key][val]]
            rec[3] = True
            waits.append(("eng", key, rec))
        else:
            waits.append(("dma", key.dsem, val))

    def op(self, eng, fn, reads=(), writes=(), dma_dst=None, ordered=False):
        waits = []
        for t in reads:
            self._need(eng, t.lastw, waits)
        for t in writes:
            if dma_dst is not None and t is dma_dst and not ordered and t.lastw is not None \
                    and t.lastw[0] == "dma" and not t.reads:
                pass
            else:
                self._need(eng, t.lastw, waits)
            for r in t.reads:
                self._need(eng, r, waits)
        if dma_dst is not None:
            if dma_dst.dsem is None:
                dma_dst.dsem = self.sem("d_" + dma_dst.name)
            dma_dst.dcnt += 16
            hz = ("dma", dma_dst, dma_dst.dcnt)
            inc = (dma_dst.dsem, 16)
        else:
            self.cnt[eng] += 1
            hz = ("eng", eng, self.cnt[eng])
            inc = None
            self.opidx[eng][self.cnt[eng]] = len(self.ops[eng])
        for t in reads:
            t.reads.append(hz)
        for t in writes:
            t.lastw = hz
            t.reads = []
        self.ops[eng].append([waits, fn, inc, False, 0])

    def dma(self, out_t, out_ap, in_t, in_ap, q="sp", ordered=False, **kw):
        self.op(q, lambda e: e.dma_start(out=out_ap, in_=in_ap, **kw),
                reads=[in_t], writes=[out_t], dma_dst=out_t, ordered=ordered)

    def mm(self, out_t, out_ap, a_t, lhsT, b_t, rhs, start=True, stop=True, **kw):
        self.op("pe", lambda e: e.matmul(out_ap, lhsT, rhs, start=start, stop=stop, **kw),
                reads=[a_t, b_t], writes=[out_t])

    def tr(self, out_t, out_ap, in_t, in_ap, ident_t, ident_ap):
        self.op("pe", lambda e: e.transpose(out_ap, in_ap, ident_ap), reads=[in_t, ident_t], writes=[out_t])


    def act(self, out_t, out_ap, in_t, in_ap, func, extra=(), **kw):
        self.op("act", lambda e: e.activation(out_ap, in_ap, func, **kw), reads=[in_t, *extra], writes=[out_t])

    def ts(self, eng, out_t, out_ap, in_t, in_ap, s1, s2, op0, op1=None, extra=()):
        if op1 is None:
            self.op(eng, lambda e: e.tensor_scalar(out_ap, in_ap, s1, None, op0), reads=[in_t, *extra], writes=[out_t])
        else:
            self.op(eng, lambda e: e.tensor_scalar(out_ap, in_ap, s1, s2, op0, op1), reads=[in_t, *extra], writes=[out_t])

    def tt(self, eng, out_t, out_ap, a_t, a_ap, b_t, b_ap, op):
        self.op(eng, lambda e: e.tensor_tensor(out_ap, a_ap, b_ap, op), reads=[a_t, b_t], writes=[out_t])

    def stt(self, eng, out_t, out_ap, a_t, a_ap, scalar, b_t, b_ap, op0, op1, extra=()):
        self.op(eng, lambda e: e.scalar_tensor_tensor(out_ap, a_ap, scalar, b_ap, op0, op1),
                reads=[a_t, b_t, *extra], writes=[out_t])

    def cp(self, eng, out_t, out_ap, in_t, in_ap):
        if eng == "act":
            self.op("act", lambda e: e.copy(out_ap, in_ap), reads=[in_t], writes=[out_t])
        else:
            self.op(eng, lambda e: e.tensor_copy(out_ap, in_ap), reads=[in_t], writes=[out_t])

    def memset(self, eng, t, ap, val):
        self.op(eng, lambda e: e.memset(ap, val), reads=[], writes=[t])

    def banks(self, n=7):
        self.pb = [self.psum("pb%d" % i, [128, 512], F32) for i in range(n)]
        self.pbi = 0

    def bank(self):
        b = self.pb[self.pbi % len(self.pb)]
        self.pbi += 1
        return b

    def rsqrt(self, out_t, out_ap, in_t, in_ap, scale, eps_t, eps_ap):
        self.op("act", lambda e: e.activation(out_ap, in_ap, AF.Sqrt, bias=eps_ap, scale=scale),
                reads=[in_t, eps_t], writes=[out_t])
        self.op("dve", lambda e: e.reciprocal(out_ap, out_ap), reads=[out_t], writes=[out_t])

    def finish(self):
        waits = []
        for t in self.tiles:
            if t.lastw is not None and t.lastw[0] == "dma":
                self._need("sp", t.lastw, waits)
        self.ops["sp"].append([waits, None, None, False, 0])

    def build(self):
        nc = self.nc
        engs = {"pe": "tensor", "act": "scalar", "dve": "vector", "pool": "gpsimd", "sp": "sync"}
        for e in COMPUTE:
            c = 0
            for rec in self.ops[e]:
                if (SIGALL or e in SIGENG) and rec[2] is None and rec[1] is not None:
                    rec[3] = True
                if rec[2] is None and rec[1] is not None and rec[3]:
                    c += 1
                    rec[4] = c
        with nc.Block() as block:
            for e, attr in engs.items():
                ops = self.ops[e]

                def body(engobj, ops=ops, e=e):
                    for waits, fn, inc, sig, _c in ops:
                        for w in waits:
                            if w[0] == "eng":
                                engobj.wait_ge(self.sems[w[1]], w[2][4])
                            else:
                                engobj.wait_ge(w[1], w[2])
                        if fn is not None:
                            ins = fn(engobj)
                            if inc is not None:
                                ins.then_inc(inc[0], inc[1])
                            elif sig:
                                ins.then_inc(self.sems[e], 1)
                getattr(block, attr)(body)
        for cm in reversed(self.ctxs):
            cm.__exit__(None, None, None)
        self.ctxs = []
        return nc

    def n_ops(self):
        return {e: len(v) for e, v in self.ops.items()}


def run(prog_nc, in_maps, trace=False):
    import sys, time
    t0 = time.time()
    res = run_bass_kernel_spmd(prog_nc, in_maps, core_ids=list(range(len(in_maps))), trace=trace)
    print("[kernel] launch done in %.1fs" % (time.time() - t0), file=sys.stderr, flush=True)
    return res


def build_L0():
    P = Prog("L0")
    nc = P.nc
    w = P.dram("w", [1024, 1536], F32, kind="ExternalInput")
    bT = P.dram("bT", [128, 12], F32, kind="ExternalInput")
    cT = P.dram("cT", [128, 8, 3], F32, kind="ExternalInput")
    out = P.dram("modT", [128, 12, 3], F32, kind="ExternalOutput")
    P.start()
    wt = P.sbuf("wt", [128, 8, 1536], F32)
    bt = P.sbuf("bt", [128, 12], F32)
    ct = P.sbuf("ct", [128, 8, 3], F32)
    st = P.sbuf("st", [128, 8, 3], F32)
    ot = P.sbuf("ot", [128, 12, 3], F32)
    ps = P.psum("ps", [128, 512], F32)
    P.dma(ct, ct[:], cT, cT[:])
    P.dma(bt, bt[:], bT, bT[:])
    for k in range(8):
        P.dma(wt, wt[:, k, :], w, w[k * 128:(k + 1) * 128, :])
    P.op("act", lambda e: e.activation(st[:], ct[:], AF.Silu), reads=[ct], writes=[st])
    for j in range(12):
        for k in range(8):
            P.mm(ps, ps[:, j * 3:(j + 1) * 3], wt, wt[:, k, j * 128:(j + 1) * 128], st, st[:, k, :],
                 start=(k == 0), stop=(k == 7))
    for j in range(12):
        P.op("dve", lambda e, j=j: e.tensor_scalar(ot[:, j, :], ps[:, j * 3:(j + 1) * 3], bt[:, j:j + 1], None, ALU.add),
             reads=[ps, bt], writes=[ot])
    P.dma(out, out[:], ot, ot[:])
    P.finish()
    return P.build()


def run_L0(inp):
    cvec = np.stack([inp["c"][0], inp["c"][1], inp["c_ctx"]], 0)
    cT = np.ascontiguousarray(cvec.T.reshape(8, 128, 3).transpose(1, 0, 2))
    in_maps = []
    for core in range(8):
        l, q = core // 4, core % 4
        cols = slice(q * 1536, (q + 1) * 1536)
        in_maps.append({
            "w": np.ascontiguousarray(inp["w_ada"][l][:, cols]),
            "bT": np.ascontiguousarray(inp["b_ada"][l][cols].reshape(12, 128).T),
            "cT": cT,
        })
    res = run(build_L0(), in_maps)
    mods = []
    for l in range(2):
        parts = []
        for q in range(4):
            m = np.asarray(res.results[l * 4 + q]["modT"])
            parts.append(m.transpose(1, 0, 2).reshape(1536, 3))
        mods.append(np.concatenate(parts, 0))
    return mods


NT = 16640
LN_EPS = 1e-6
NCH = 33


def chunk_rng(c):
    return (0, 256) if c == 0 else (256 + (c - 1) * 512, 512)


def build_A(lam_init, dbg=False):
    P = Prog("A")
    EI = "ExternalInput"
    xin = P.dram("xin", [NT, 1024], F32, EI)
    msT = P.dram("msT", [128, 8, 4], F32, EI)
    wfeat = P.dram("wfeat", [1024, 1216], F32, EI)
    wtok = P.dram("wtok", [1024, 256], F32, EI)
    wuq = P.dram("wuq", [256, 2, 192], F32, EI)
    wukvk = P.dram("wukvk", [128, 2, 64], F32, EI)
    wukvv = P.dram("wukvv", [128, 128], F32, EI)
    gq = P.dram("gq", [128, 2], F32, EI)
    gkv = P.dram("gkv", [128, 1], F32, EI)
    cs32 = P.dram("cs32", [32, 2, NT], F32, EI)
    cs128 = P.dram("cs128", [128, 2, NT], F32, EI)
    nabias = P.dram("nabias", [2, 3, 768, 256], F32, EI)
    lamv = P.dram("lamv", [128, 4, 64], F32, EI)
    gsub = P.dram("gsub", [128, 1], F32, EI)
    identd = P.dram("ident", [128, 128], F32, EI)
    EO = "ExternalOutput"
    yaT = P.dram("yaT", [128, NT], BF16, EO)
    ybT = P.dram("ybT", [128, NT], BF16, EO)
    ycT = P.dram("ycT", [128, NT], BF16, EO)
    sk = EO if dbg else "Internal"
    qaT = P.dram("qaT", [128, NT], BF16, sk)
    kaT = P.dram("kaT", [128, NT], BF16, sk)
    va = P.dram("va", [NT, 128], BF16, sk)
    qbT = P.dram("qbT", [2, 96, NT], BF16, sk)
    kbT = P.dram("kbT", [2, 96, NT], BF16, sk)
    vb = P.dram("vb", [NT, 128], BF16, sk)
    dqT = P.dram("dqT", [128, NT], BF16, sk)
    dkT = P.dram("dkT", [128, NT], BF16, sk)
    dvv = P.dram("dvv", [NT, 128], BF16, sk)
    P.start()
    P.banks(7)
    tp = P.psum("tp", [128, 1024], BF16)

    Wf = P.sbuf("Wf", [128, 8, 1216], BF16)
    Wt = P.sbuf("Wt", [128, 8, 256], BF16)
    for k in range(8):
        P.dma(Wf, Wf[:, k, :], wfeat, wfeat[k * 128:(k + 1) * 128, :], q="pool")
        P.dma(Wt, Wt[:, k, :], wtok, wtok[k * 128:(k + 1) * 128, :], q="pool")
    ident = P.sbuf("identb", [128, 128], BF16)
    P.dma(ident, ident[:], identd, identd[:], q="pool")
    ones = P.sbuf("ones", [128, 128], BF16)
    P.memset("dve", ones, ones[:], 1.0)
    onesf = P.sbuf("onesf", [128, 128], F32)
    epst = P.sbuf("epst", [128, 1], F32)
    P.memset("dve", epst, epst[:], LN_EPS)
    P.memset("dve", onesf, onesf[:], 1.0)
    ms = P.sbuf("ms", [128, 8, 4], F32)
    P.dma(ms, ms[:], msT, msT[:])
    scp = P.sbuf("scp", [128, 8, 2], F32)
    for m in range(2):
        P.ts("dve", scp, scp[:, :, m], ms, ms[:, :, 2 * m + 1], 1.0, None, ALU.add)
    gqt = P.sbuf("gqt", [128, 2], F32)
    P.dma(gqt, gqt[:], gq, gq[:])
    gkvt = P.sbuf("gkvt", [128, 1], F32)
    P.dma(gkvt, gkvt[:], gkv, gkv[:])
    wuq_r = P.sbuf("wuq_r", [128, 2, 2, 192], F32)
    for k in range(2):
        P.dma(wuq_r, wuq_r[:, k], wuq, wuq[k * 128:(k + 1) * 128])
    Wuq = P.sbuf("Wuq", [128, 2, 2, 192], BF16)
    for k in range(2):
        P.ts("dve", Wuq, Wuq[:, k], wuq_r, wuq_r[:, k], gqt[:, k:k + 1], None, ALU.mult, extra=[gqt])
    wk_r = P.sbuf("wk_r", [128, 2, 64], F32)
    P.dma(wk_r, wk_r[:], wukvk, wukvk[:])
    Wkk = P.sbuf("Wkk", [128, 2, 64], BF16)
    P.ts("dve", Wkk, Wkk[:], wk_r, wk_r[:], gkvt[:, 0:1], None, ALU.mult, extra=[gkvt])
    wv_r = P.sbuf("wv_r", [128, 128], F32)
    P.dma(wv_r, wv_r[:], wukvv, wukvv[:])
    Wkv = P.sbuf("Wkv", [128, 128], BF16)
    P.ts("dve", Wkv, Wkv[:], wv_r, wv_r[:], gkvt[:, 0:1], None, ALU.mult, extra=[gkvt])
    lv = P.sbuf("lv", [128, 4, 64], F32)
    P.dma(lv, lv[:], lamv, lamv[:])
    lpr = P.sbuf("lpr", [128, 2, 64], F32)
    P.tt("dve", lpr, lpr[:, 0], lv, lv[:, 0], lv, lv[:, 1], ALU.mult)
    P.tt("dve", lpr, lpr[:, 1], lv, lv[:, 2], lv, lv[:, 3], ALU.mult)
    lsum = P.sbuf("lsum", [128, 2], F32)
    P.op("dve", lambda e: e.reduce_sum(lsum[:], lpr[:], AX.X), reads=[lpr], writes=[lsum])
    lexp = P.sbuf("lexp", [128, 2], F32)
    P.act(lexp, lexp[:], lsum, lsum[:], AF.Exp)
    neglam = P.sbuf("neglam", [128, 1], F32)
    P.stt("dve", neglam, neglam[:], lexp, lexp[:, 1:2], -float(lam_init), lexp, lexp[:, 0:1], ALU.add, ALU.subtract)
    gsr = P.sbuf("gsr", [128, 1], F32)
    P.dma(gsr, gsr[:], gsub, gsub[:])
    gss = P.sbuf("gss", [128, 1], F32)
    P.ts("dve", gss, gss[:], gsr, gsr[:], 1.0 - float(lam_init), None, ALU.mult)

    xb = [P.sbuf("xb%d" % i, [128, 1024], F32) for i in range(2)]
    xnb = [P.sbuf("xn%d" % i, [128, 1024], BF16) for i in range(2)]
    st6 = P.sbuf("st6", [128, 2, 6], F32)
    mv = P.sbuf("mv", [128, 2], F32)
    rstd = P.sbuf("rstd", [128, 1], F32)
    htmp = P.sbuf("htmp", [128, 8, 128], F32)
    hTb = [P.sbuf("hT%d" % i, [128, 8, 512], BF16) for i in range(2)]
    t32 = [P.sbuf("t32_%d" % i, [128, 2, 512], F32) for i in range(2)]
    t128 = [P.sbuf("t128_%d" % i, [128, 2, 512], F32) for i in range(2)]
    stg = [P.sbuf("stg%d" % i, [128, 512], BF16) for i in range(3)]
    stgi = [0]

    def nstg():
        s_ = stg[stgi[0] % len(stg)]
        stgi[0] += 1
        return s_
    cqT = P.sbuf("cqT", [128, 2, 512], BF16)
    cqsq = P.sbuf("cqsq", [128, 2, 512], BF16)
    ckvT = P.sbuf("ckvT", [128, 512], BF16)
    ckvsq = P.sbuf("ckvsq", [128, 512], BF16)
    rq = P.sbuf("rq", [128, 512], F32)
    rkv = P.sbuf("rkv", [128, 512], F32)
    rkvt = P.sbuf("rkvt", [128, 4], F32)
    ra = [P.sbuf("ra%d" % i, [128, 512], F32) for i in range(1)]
    rb = [P.sbuf("rb%d" % i, [128, 512], F32) for i in range(1)]
    vst = [P.sbuf("vst%d" % i, [128, 256], BF16) for i in range(2)]
    krT = P.sbuf("krT", [32, 512], BF16)

    def rope(n, pa, pb_, rows, tab, ti, out_t, out_ap, mul_t=None, mul_ap=None):
        a = ra[0]
        b = rb[0]
        P.tt("dve", a, a[rows, :n], pa, pa[rows, :n], tab, tab[rows, 0, :n], ALU.mult)
        P.tt("dve", b, b[rows, :n], pb_, pb_[rows, :n], tab, tab[rows, 1, :n], ALU.mult)
        if mul_t is None:
            P.tt("pool", out_t, out_ap, a, a[rows, :n], b, b[rows, :n], ALU.add)
        else:
            P.tt("pool", a, a[rows, :n], a, a[rows, :n], b, b[rows, :n], ALU.add)
            P.tt("pool", out_t, out_ap, a, a[rows, :n], mul_t, mul_ap, ALU.mult)

    FO = {"qa": 0, "ka": 128, "cq0": 256, "cq1": 384, "ckv": 512, "kr": 640, "krp": 672,
          "dq": 704, "dqp": 832, "dk": 960, "dkp": 1088}

    def proj(hT, name, M, n):
        pbk = P.bank()
        off = FO[name]
        for k in range(8):
            P.mm(pbk, pbk[:M, :n], Wf, Wf[:, k, off:off + M], hT, hT[:, k, :n], start=(k == 0), stop=(k == 7))
        return pbk

    def stage1(c):
        tok0, n = chunk_rng(c)
        m = 1 if c == 0 else 0
        nt = n // 128
        hT = hTb[c % 2]
        tb32 = t32[c % 2]
        tb128 = t128[c % 2]
        P.dma(tb32, tb32[0:32, :, :n], cs32, cs32[:, :, tok0:tok0 + n])
        P.dma(tb32, tb32[64:96, :, :n], cs32, cs32[:, :, tok0:tok0 + n])
        P.dma(tb128, tb128[:, :, :n], cs128, cs128[:, :, tok0:tok0 + n])
        for i in range(nt):
            xt = xb[i % 2]
            xn = xnb[i % 2]
            P.dma(xt, xt[:], xin, xin[tok0 + i * 128: tok0 + (i + 1) * 128, :])
            for a_ in range(2):
                P.op("dve", lambda e, xt=xt, a_=a_: e.bn_stats(st6[:, a_, :], xt[:, a_ * 512:(a_ + 1) * 512]),
                     reads=[xt], writes=[st6])
            P.op("dve", lambda e: e.bn_aggr(mv[:], st6[:].rearrange("p a b -> p (a b)")), reads=[st6], writes=[mv])
            P.rsqrt(rstd, rstd[:], mv, mv[:, 1:2], 1.0, epst, epst[:, 0:1])
            P.ts("dve", xn, xn[:], xt, xt[:], mv[:, 0:1], rstd[:, 0:1], ALU.subtract, ALU.mult, extra=[mv, rstd])
            for k in range(8):
                P.tr(tp, tp[:, k * 128:(k + 1) * 128], xn, xn[:, k * 128:(k + 1) * 128], ident, ident[:])
            P.tt("dve", htmp, htmp[:], tp, tp[:].rearrange("p (a b) -> p a b", a=8),
                 scp, scp[:, :, m:m + 1].to_broadcast([128, 8, 128]), ALU.mult)
            P.tt("pool", hT, hT[:, :, i * 128:(i + 1) * 128], htmp, htmp[:],
                 ms, ms[:, :, 2 * m:2 * m + 1].to_broadcast([128, 8, 128]), ALU.add)

    def stage2(c):
        tok0, n = chunk_rng(c)
        nt = n // 128
        hT = hTb[c % 2]
        tb32 = t32[c % 2]
        tb128 = t128[c % 2]
        for name, dst in (("qa", qaT), ("ka", kaT)):
            pbk = proj(hT, name, 128, n)
            s_ = nstg()
            P.cp("act", s_, s_[:, :n], pbk, pbk[:, :n])
            P.dma(dst, dst[:, tok0:tok0 + n], s_, s_[:, :n])
        for j, name in enumerate(("cq0", "cq1")):
            pbk = proj(hT, name, 128, n)
            P.cp("act", cqT, cqT[:, j, :n], pbk, pbk[:, :n])
            P.act(cqsq, cqsq[:, j, :n], pbk, pbk[:, :n], AF.Square)
        pbk = proj(hT, "ckv", 128, n)
        P.cp("act", ckvT, ckvT[:, :n], pbk, pbk[:, :n])
        P.act(ckvsq, ckvsq[:, :n], pbk, pbk[:, :n], AF.Square)
        pa = proj(hT, "kr", 32, n)
        pb_ = proj(hT, "krp", 32, n)
        rope(n, pa, pb_, slice(0, 32), tb32, 0, krT, krT[:, :n])
        for h in range(2):
            P.dma(kbT, kbT[h, 64:96, tok0:tok0 + n], krT, krT[:, :n])
        for (nm, nmp, dst, ti) in (("dq", "dqp", dqT, 0), ("dk", "dkp", dkT, 1)):
            pa = proj(hT, nm, 128, n)
            pb_ = proj(hT, nmp, 128, n)
            s_ = nstg()
            rope(n, pa, pb_, slice(0, 128), tb128, ti, s_, s_[:, :n])
            P.dma(dst, dst[:, tok0:tok0 + n], s_, s_[:, :n])
        for i in range(nt):
            pbk = P.bank()
            for k in range(8):
                P.mm(pbk, pbk[:, 0:256], hT, hT[:, k, i * 128:(i + 1) * 128], Wt, Wt[:, k, :], start=(k == 0), stop=(k == 7))
            v_ = vst[i % 2]
            P.cp("act", v_, v_[:], pbk, pbk[:, 0:256])
            r0 = tok0 + i * 128
            P.dma(va, va[r0:r0 + 128, :], v_, v_[:, 0:128])
            P.dma(dvv, dvv[r0:r0 + 128, :], v_, v_[:, 128:256])
        pbk = P.bank()
        P.mm(pbk, pbk[:, :n], ones, ones[:], cqsq, cqsq[:, 0, :n], start=True, stop=False)
        P.mm(pbk, pbk[:, :n], ones, ones[:], cqsq, cqsq[:, 1, :n], start=False, stop=True)
        P.rsqrt(rq, rq[:, :n], pbk, pbk[:, :n], 1.0 / 256, epst, epst[:, 0:1])
        pbk = P.bank()
        P.mm(pbk, pbk[:, :n], ones, ones[:], ckvsq, ckvsq[:, :n])
        P.rsqrt(rkv, rkv[:, :n], pbk, pbk[:, :n], 1.0 / 128, epst, epst[:, 0:1])
        pbk = P.bank()
        for i in range(nt):
            P.mm(pbk, pbk[:, i:i + 1], ckvsq, ckvsq[:, i * 128:(i + 1) * 128], ones, ones[:, 0:1])
        P.rsqrt(rkvt, rkvt[:, :nt], pbk, pbk[:, :nt], 1.0 / 128, epst, epst[:, 0:1])
        for h in range(2):
            qa_ = P.bank()
            for k in range(2):
                P.mm(qa_, qa_[:96, :n], Wuq, Wuq[:, k, h, 0:96], cqT, cqT[:, k, :n], start=(k == 0), stop=(k == 1))
            qb_ = P.bank()
            for k in range(2):
                P.mm(qb_, qb_[:96, :n], Wuq, Wuq[:, k, h, 96:192], cqT, cqT[:, k, :n], start=(k == 0), stop=(k == 1))
            s_ = nstg()
            P.tt("dve", s_, s_[0:64, :n], qa_, qa_[0:64, :n], rq, rq[0:64, :n], ALU.mult)
            rope(n, qa_, qb_, slice(64, 96), tb32, h, s_, s_[64:96, :n], rq, rq[64:96, :n])
            P.dma(qbT, qbT[h, :, tok0:tok0 + n], s_, s_[0:96, :n])
            kn = P.bank()
            P.mm(kn, kn[:64, :n], Wkk, Wkk[:, h, :], ckvT, ckvT[:, :n])
            s_ = nstg()
            P.tt("dve", s_, s_[0:64, :n], kn, kn[0:64, :n], rkv, rkv[0:64, :n], ALU.mult)
            P.dma(kbT, kbT[h, 0:64, tok0:tok0 + n], s_, s_[0:64, :n])
        for i in range(nt):
            pbk = P.bank()
            P.mm(pbk, pbk[:, 0:128], ckvT, ckvT[:, i * 128:(i + 1) * 128], Wkv, Wkv[:])
            v_ = vst[i % 2]
            P.ts("dve", v_, v_[:, 0:128], pbk, pbk[:, 0:128], rkvt[:, i:i + 1], None, ALU.mult, extra=[rkvt])
            r0 = tok0 + i * 128
            P.dma(vb, vb[r0:r0 + 128, :], v_, v_[:, 0:128])

    stage1(0)
    for c in range(NCH):
        if c + 1 < NCH:
            stage1(c + 1)
        stage2(c)
    if dbg == "A1":
        P.finish()
        return P.build()

    BK = P.sbuf("BK", [128, NT], BF16)
    BV = P.sbuf("BV", [128, 130, 130], BF16)
    LOOK = 2
    pts = [P.sbuf("pt%d" % i, [128, 512], BF16) for i in range(5)]
    pti = [0]

    def npt():
        t_ = pts[pti[0] % 5]
        pti[0] += 1
        return t_
    qcs = [P.sbuf("qc%d" % i, [128, 512], BF16) for i in range(2)]
    rec = P.sbuf("rec", [128, 512], F32)
    rec2 = P.sbuf("rec2", [128, 512], F32)
    bsb = P.sbuf("bsb", [128, 512], F32)
    ysb = [P.sbuf("ysb%d" % i, [128, 512], BF16) for i in range(2)]
    yf = P.sbuf("yf", [128, 512], F32)
    yf2 = P.sbuf("yf2", [128, 512], F32)
    tmpf = [P.sbuf("tmpf%d" % i, [128, 256], F32) for i in range(2)]
    nb = P.sbuf("nb", [128, 3, 6, 256], F32)
    acc_banks = [P.pb[0], P.pb[1]]
    bb_bank = P.pb[2]
    s_banks = [P.pb[3], P.pb[4], P.pb[5], P.pb[6]]
    cnt = {"acc": 0, "s": 0, "y": 0}

    def sbank():
        b = s_banks[cnt["s"] % 4]
        cnt["s"] += 1
        return b

    def finalize(acc, n, dst, drow, tok0):
        P.op("dve", lambda e: e.reciprocal(rec[64:65, :n], acc[64:65, :n]), reads=[acc], writes=[rec])
        P.mm(bb_bank, bb_bank[:64, :n], onesf, onesf[64:65, 0:64], rec, rec[64:65, :n])
        P.cp("act", bsb, bsb[:64, :n], bb_bank, bb_bank[:64, :n])
        y_ = ysb[cnt["y"] % 2]
        cnt["y"] += 1
        P.tt("dve", y_, y_[:64, :n], acc, acc[:64, :n], bsb, bsb[:64, :n], ALU.mult)
        P.dma(dst, dst[drow:drow + 64, tok0:tok0 + n], y_, y_[:64, :n])

    for q4 in range(4):
        P.dma(BK, BK[:, q4 * 4160:(q4 + 1) * 4160], kaT, kaT[:, q4 * 4160:(q4 + 1) * 4160])
    BV4 = BV[:].rearrange("p t (h d) -> p t h d", h=2)
    for h in range(2):
        P.dma(BV, BV4[:, :, h, 0:64], va, va[:, h * 64:(h + 1) * 64].rearrange("(t p) d -> p t d", p=128))
    P.memset("pool", BV, BV4[:, :, :, 64:65], 1.0)
    for h in range(2):
        P.dma(nb, nb[:], nabias, nabias[h].rearrange("v (j p) q -> p v j q", p=128))
        for blk in range(-1, 64):
            if blk < 0:
                qtok0, tiles = 0, [(0, 0, None), (128, 1, None)]
            else:
                r0 = blk * 4
                var = 0 if blk == 0 else (2 if blk == 63 else 1)
                se = 0 if blk == 0 else (244 if blk == 63 else r0 - 4)
                qtok0 = 256 + r0 * 64
                tiles = [(0, 0, None), (128, 1, None)]
                for j in range(6):
                    tiles.append((256 + se * 64 + j * 128, 2 + se // 2 + j, (var, j)))
            qc = qcs[cnt["acc"] % 2]
            P.dma(qc, qc[:, :256], qaT, qaT[:, qtok0:qtok0 + 256])
            acc = acc_banks[cnt["acc"] % 2]
            cnt["acc"] += 1
            pend = []
            for idx in range(len(tiles) + LOOK):
                if idx < len(tiles):
                    kcol, vt, bi = tiles[idx]
                    sb = sbank()
                    P.mm(sb, sb[:, :256], BK, BK[h * 64:(h + 1) * 64, kcol:kcol + 128], qc, qc[h * 64:(h + 1) * 64, :256])
                    pt = npt()
                    if bi is None:
                        P.act(pt, pt[:, :256], sb, sb[:, :256], AF.Exp, scale=0.125)
                    else:
                        tf = tmpf[idx % 2]
                        P.stt("dve", tf, tf[:], sb, sb[:, :256], 0.125, nb, nb[:, bi[0], bi[1], :], ALU.mult, ALU.add)
                        P.act(pt, pt[:, :256], tf, tf[:], AF.Exp)
                    pend.append(pt)
                j = idx - LOOK
                if j >= 0:
                    pt = pend[j]
                    P.mm(acc, acc[:65, :256], BV, BV4[:, tiles[j][1], h, :], pt, pt[:, :256], start=(j == 0), stop=(j == len(tiles) - 1))
            finalize(acc, 256, yaT, h * 64, qtok0)

    BV3 = BV[:]
    sc_b = 96.0 ** -0.5
    for h in range(2):
        for q4 in range(4):
            P.dma(BK, BK[0:96, q4 * 4160:(q4 + 1) * 4160], kbT, kbT[h, :, q4 * 4160:(q4 + 1) * 4160])
        P.dma(BV, BV3[:, :, 0:64], vb, vb[:, h * 64:(h + 1) * 64].rearrange("(t p) d -> p t d", p=128))
        P.memset("pool", BV, BV3[:, :, 64:65], 1.0)
        for c in range(NCH):
            tok0, n = chunk_rng(c)
            kts = [0, 1] if c == 0 else list(range(130))
            qc = qcs[cnt["acc"] % 2]
            P.dma(qc, qc[0:96, :n], qbT, qbT[h, :, tok0:tok0 + n])
            acc = acc_banks[cnt["acc"] % 2]
            cnt["acc"] += 1
            pend = []
            for idx in range(len(kts) + LOOK):
                if idx < len(kts):
                    kt = kts[idx]
                    sb = sbank()
                    P.mm(sb, sb[:, :n], BK, BK[0:96, kt * 128:(kt + 1) * 128], qc, qc[0:96, :n])
                    pt = npt()
                    P.act(pt, pt[:, :n], sb, sb[:, :n], AF.Exp, scale=sc_b)
                    pend.append(pt)
                j = idx - LOOK
                if j >= 0:
                    pt = pend[j]
                    P.mm(acc, acc[:65, :n], BV, BV3[:, kts[j], 0:65], pt, pt[:, :n], start=(j == 0), stop=(j == len(kts) - 1))
            finalize(acc, n, ybT, h * 64, tok0)

    for q4 in range(4):
        P.dma(BK, BK[:, q4 * 4160:(q4 + 1) * 4160], dkT, dkT[:, q4 * 4160:(q4 + 1) * 4160])
    P.dma(BV, BV3[:, :, 0:128], dvv, dvv[:, :].rearrange("(t p) d -> p t d", p=128))
    O1, O2 = P.pb[0], P.pb[1]
    s3 = [P.pb[2], P.pb[3], P.pb[4], P.pb[5]]
    S2 = P.pb[6]
    sacc = [P.sbuf("sacc%d" % i, [128, 512], F32) for i in range(1)]
    for c in range(NCH):
        tok0, n = chunk_rng(c)
        kts = [0, 1] if c == 0 else list(range(130))
        qc = qcs[c % 2]
        P.dma(qc, qc[:, :n], dqT, dqT[:, tok0:tok0 + n])
        pend = []
        for idx in range(len(kts) + 1):
            if idx < len(kts):
                kt = kts[idx]
                pp = []
                for half in range(2):
                    sb = s3[cnt["s"] % 4]
                    cnt["s"] += 1
                    rows = slice(half * 64, (half + 1) * 64)
                    P.mm(sb, sb[:, :n], BK, BK[rows, kt * 128:(kt + 1) * 128], qc, qc[rows, :n])
                    pt = npt()
                    P.act(pt, pt[:, :n], sb, sb[:, :n], AF.Exp, scale=0.125)
                    pp.append(pt)
                pend.append(pp)
            j = idx - 1
            if j >= 0:
                st_, sp_ = (j == 0), (j == len(kts) - 1)
                for half, O in enumerate((O1, O2)):
                    pt = pend[j][half]
                    P.mm(O, O[:, :n], BV, BV3[:, kts[j], 0:128], pt, pt[:, :n], start=st_, stop=sp_)
                    if half == 0:
                        sa = sacc[0]
                        if j == 0:
                            P.cp("dve", sa, sa[:, :n], pt, pt[:, :n])
                        else:
                            P.tt("dve", sa, sa[:, :n], sa, sa[:, :n], pt, pt[:, :n], ALU.add)
                    else:
                        P.mm(S2, S2[:, :n], ones, ones[:], pt, pt[:, :n], start=st_, stop=sp_)
        sb = s3[cnt["s"] % 4]
        cnt["s"] += 1
        P.mm(sb, sb[:, :n], onesf, onesf[:, :], sacc[0], sacc[0][:, :n])
        P.op("dve", lambda e, n=n, sb=sb: e.reciprocal(rec[:, :n], sb[:, :n]), reads=[sb], writes=[rec])
        P.op("dve", lambda e, n=n: e.reciprocal(rec2[:, :n], S2[:, :n]), reads=[S2], writes=[rec2])
        P.tt("dve", yf, yf[:, :n], O1, O1[:, :n], rec, rec[:, :n], ALU.mult)
        P.tt("dve", yf2, yf2[:, :n], O2, O2[:, :n], rec2, rec2[:, :n], ALU.mult)
        P.stt("dve", yf, yf[:, :n], yf2, yf2[:, :n], neglam[:, 0:1], yf, yf[:, :n], ALU.mult, ALU.add, extra=[neglam])
        pt = npt()
        P.act(pt, pt[:, :n], yf, yf[:, :n], AF.Square)
        sb = s3[cnt["s"] % 4]
        cnt["s"] += 1
        P.mm(sb, sb[:, :n], ones, ones[:], pt, pt[:, :n])
        P.rsqrt(bsb, bsb[:, :n], sb, sb[:, :n], 1.0 / 128, epst, epst[:, 0:1])
        P.tt("dve", yf, yf[:, :n], yf, yf[:, :n], bsb, bsb[:, :n], ALU.mult)
        y_ = ysb[c % 2]
        P.ts("pool", y_, y_[:, :n], yf, yf[:, :n], gss[:, 0:1], None, ALU.mult, extra=[gss])
        P.dma(ycT, ycT[:, tok0:tok0 + n], y_, y_[:, :n])
    P.finish()
    return P.build()


def _pm(vec):
    return np.ascontiguousarray(np.asarray(vec).reshape(-1, 128).T)


def _partner_perm(D):
    m = D // 2
    half = m // 2
    perm = np.zeros(D, np.int64)
    sign = np.zeros(D, np.float32)
    for d in range(D):
        seg, i = d // m, d % m
        if i < half:
            perm[d] = seg * m + i + half
            sign[d] = -1.0
        else:
            perm[d] = seg * m + i - half
            sign[d] = 1.0
    return perm, sign


def _rope_table(D):
    m = D // 2
    half = m // 2
    inv = (np.float32(10000.0) ** (-np.arange(0, m, 2, dtype=np.float32) / np.float32(m))).astype(np.float32)
    t = np.arange(16384)
    row = (t // 64).astype(np.float32)
    col = (t % 64).astype(np.float32)
    _, sign = _partner_perm(D)
    tab = np.zeros((D, 2, NT), np.float32)
    tab[:, 0, :256] = 1.0
    for d in range(D):
        seg, i = d // m, d % m
        fi = i % half
        pos = row if seg == 0 else col
        ang = (pos * inv[fi]).astype(np.float32)
        tab[d, 0, 256:] = np.cos(ang)
        tab[d, 1, 256:] = sign[d] * np.sin(ang)
    return tab


def _na_bias(rpb2):
    out = np.full((2, 3, 768, 256), -30000.0, np.float32)
    for var, (r0, se) in enumerate(((0, 0), (8, 4), (252, 244))):
        kr = se + np.arange(768) // 64
        kc = np.arange(768) % 64
        r = r0 + np.arange(256) // 64
        c = np.arange(256) % 64
        rs = np.clip(r - 4, 0, 248)
        cs = np.clip(c - 8, 0, 48)
        okr = (kr[:, None] >= rs[None, :]) & (kr[:, None] < rs[None, :] + 8)
        okc = (kc[:, None] >= cs[None, :]) & (kc[:, None] < cs[None, :] + 16)
        ok = okr & okc
        ro = np.clip(kr[:, None] - r[None, :] + 7, 0, 14)
        co = np.clip(kc[:, None] - c[None, :] + 15, 0, 30)
        for h in range(2):
            g = rpb2[h][ro, co]
            out[h, var] = np.where(ok, g, np.float32(-30000.0))
    return out


_CONST = {}


def _consts():
    if not _CONST:
        _CONST["cs32"] = _rope_table(32)
        t64 = _rope_table(64)
        _CONST["cs128"] = np.ascontiguousarray(np.concatenate([t64, t64], 0))
        _CONST["ident"] = np.eye(128, dtype=np.float32)
    return _CONST


def prep_A(inp, l, mod, x_cur, ctx_cur):
    C = _consts()
    w_in = inp["w_in"][l]
    p32, _ = _partner_perm(32)
    p64, _ = _partner_perm(64)
    p128 = np.concatenate([p64, 64 + p64])
    maps = []
    for core in range(8):
        b, g = core // 4, core % 4
        xin = np.ascontiguousarray(np.concatenate([ctx_cur[b], x_cur[b]], 0))
        msT = np.stack([_pm(mod[0:1024, b]), _pm(mod[1024:2048, b]), _pm(mod[0:1024, 2]), _pm(mod[1024:2048, 2])], -1)
        kr = w_in[:, 1920:1952]
        dq = w_in[:, 1952 + 128 * g: 1952 + 128 * (g + 1)]
        dk = w_in[:, 2464 + 128 * g: 2464 + 128 * (g + 1)]
        wfeat = np.concatenate([
            w_in[:, 128 * g:128 * (g + 1)], w_in[:, 512 + 128 * g:512 + 128 * (g + 1)],
            w_in[:, 1536:1664], w_in[:, 1664:1792], w_in[:, 1792:1920],
            kr, kr[:, p32], dq, dq[:, p128], dk, dk[:, p128]], 1)
        wtok = np.concatenate([w_in[:, 1024 + 128 * g:1024 + 128 * (g + 1)], w_in[:, 2976 + 128 * g:2976 + 128 * (g + 1)]], 1)
        wuq = np.zeros((256, 2, 192), np.float32)
        wukvk = np.zeros((128, 2, 64), np.float32)
        wukvv = np.zeros((128, 128), np.float32)
        for hh in range(2):
            H = 2 * g + hh
            uq = inp["mla_w_uq"][l][:, H * 96:(H + 1) * 96]
            rope_c = uq[:, 64:96]
            wuq[:, hh, 0:96] = uq
            wuq[:, hh, 160:192] = rope_c[:, p32]
            ukv = inp["mla_w_ukv"][l][:, H * 128:(H + 1) * 128]
            wukvk[:, hh, :] = ukv[:, 0:64]
            wukvv[:, hh * 64:(hh + 1) * 64] = ukv[:, 64:128]
        lamv = np.stack([inp["diff_lq1"][l], inp["diff_lk1"][l], inp["diff_lq2"][l], inp["diff_lk2"][l]], 0)
        maps.append({
            "xin": xin, "msT": np.ascontiguousarray(msT), "wfeat": np.ascontiguousarray(wfeat),
            "wtok": np.ascontiguousarray(wtok), "wuq": wuq, "wukvk": wukvk, "wukvv": wukvv,
            "gq": _pm(inp["mla_g_q"][l]), "gkv": _pm(inp["mla_g_kv"][l]),
            "cs32": C["cs32"], "cs128": C["cs128"],
            "nabias": _na_bias(inp["na_rpb"][l][2 * g:2 * g + 2]),
            "lamv": np.ascontiguousarray(np.broadcast_to(lamv[None], (128, 4, 64))),
            "gsub": _pm(inp["diff_g_sub"][l]), "ident": C["ident"],
        })
    return maps


NB = 4160
ALPHA = 4.0 ** 0.25


def chunkB(c):
    return (0, 64) if c == 0 else (64 + (c - 1) * 512, 512)


def ln_tile(P, src_t, src_ap, np_, st6, mv, rstd, epst, out_t, out_ap):
    for a_ in range(2):
        P.op("dve", lambda e, a_=a_: e.bn_stats(st6[:np_, a_, :], src_ap[:, a_ * 512:(a_ + 1) * 512]),
             reads=[src_t], writes=[st6])
    P.op("dve", lambda e: e.bn_aggr(mv[:np_, :], st6[:np_].rearrange("p a b -> p (a b)")), reads=[st6], writes=[mv])
    P.rsqrt(rstd, rstd[:np_, :], mv, mv[:np_, 1:2], 1.0, epst, epst[:np_, 0:1])
    P.ts("dve", out_t, out_ap, src_t, src_ap, mv[:np_, 0:1], rstd[:np_, 0:1], ALU.subtract, ALU.mult, extra=[mv, rstd])


def build_B():
    P = Prog("B")
    EI, EO = "ExternalInput", "ExternalOutput"
    xin = P.dram("xin", [NB, 1024], F32, EI)
    yT = P.dram("yT", [1536, NB], BF16, EI)
    msT = P.dram("msT", [128, 8, 4], F32, EI)
    wgate = P.dram("wgate", [1024, 3072], F32, EI)
    bgT = P.dram("bgT", [128, 24], F32, EI)
    wbr = P.dram("wbr", [1536, 1024], F32, EI)
    wout = P.dram("wout", [1024, 1024], F32, EI)
    wrt = P.dram("wrt", [128, 8, 16], F32, EI)
    rows = P.dram("rows", [128, 8, 1024], F32, EI)
    identd = P.dram("ident", [128, 128], F32, EI)
    x1o = P.dram("x1", [NB, 1024], F32, EO)
    h2tmo = P.dram("h2tm", [NB, 1024], BF16, EO)
    affo = P.dram("aff", [NB, 16], F32, EO)
    P.start()
    P.banks(7)
    tp = P.psum("tp", [128, 1024], BF16)
    Wg = P.sbuf("Wg", [128, 8, 3072], BF16)
    Wb = P.sbuf("Wb", [128, 12, 1024], BF16)
    Wo = P.sbuf("Wo", [128, 8, 1024], BF16)
    for k in range(8):
        P.dma(Wg, Wg[:, k, :], wgate, wgate[k * 128:(k + 1) * 128, :], q="pool")
        P.dma(Wo, Wo[:, k, :], wout, wout[k * 128:(k + 1) * 128, :], q="pool")
    for k in range(12):
        P.dma(Wb, Wb[:, k, :], wbr, wbr[k * 128:(k + 1) * 128, :], q="pool")
    ident = P.sbuf("identb", [128, 128], BF16)
    P.dma(ident, ident[:], identd, identd[:], q="pool")
    bg = P.sbuf("bg", [128, 24], F32)
    P.dma(bg, bg[:], bgT, bgT[:])
    ms = P.sbuf("ms", [128, 8, 4], F32)
    P.dma(ms, ms[:], msT, msT[:])
    scp = P.sbuf("scp", [128, 8, 2], F32)
    for m in range(2):
        P.ts("dve", scp, scp[:, :, m], ms, ms[:, :, 2 * m + 1], 1.0, None, ALU.add)
    rw = P.sbuf("rw", [128, 8, 1024], F32)
    P.dma(rw, rw[:], rows, rows[:])
    for j in (4, 6):
        P.ts("dve", rw, rw[:, j, :], rw, rw[:, j, :], 1.0, None, ALU.add)
    wr_f = P.sbuf("wr_f", [128, 8, 16], F32)
    P.dma(wr_f, wr_f[:], wrt, wrt[:])
    wr_hi = P.sbuf("wr_hi", [128, 8, 16], BF16)
    wr_lo = P.sbuf("wr_lo", [128, 8, 16], BF16)
    P.cp("dve", wr_hi, wr_hi[:], wr_f, wr_f[:])
    P.tt("dve", wr_lo, wr_lo[:], wr_f, wr_f[:], wr_hi, wr_hi[:], ALU.subtract)
    epst = P.sbuf("epst", [128, 1], F32)
    P.memset("dve", epst, epst[:], LN_EPS)

    xc = P.sbuf("xc", [128, 4, 1024], F32)
    xnb = [P.sbuf("xn%d" % i, [128, 1024], BF16) for i in range(2)]
    st6 = P.sbuf("st6", [128, 2, 6], F32)
    mv = P.sbuf("mv", [128, 2], F32)
    rstd = P.sbuf("rstd", [128, 1], F32)
    htmp = P.sbuf("htmp", [128, 8, 128], F32)
    hT = P.sbuf("hT", [128, 8, 512], BF16)
    yt = P.sbuf("yt", [128, 12, 512], BF16)
    sig = [P.sbuf("sig%d" % i, [128, 512], F32) for i in range(2)]
    mf = P.sbuf("mf", [128, 512], F32)
    mt2 = P.sbuf("mt2", [128, 512], F32)
    mT = P.sbuf("mT", [128, 8, 512], BF16)
    u = P.sbuf("u", [128, 1024], F32)
    ut = P.sbuf("ut", [128, 1024], F32)
    x1 = [P.sbuf("x1_%d" % i, [128, 1024], F32) for i in range(1)]
    h2 = P.sbuf("h2", [128, 1024], F32)
    h2hi = P.sbuf("h2hi", [128, 1024], BF16)
    h2lo = P.sbuf("h2lo", [128, 1024], BF16)
    h2Ts = [P.sbuf("h2Ts%d" % i, [128, 8, 128], BF16) for i in range(1)]
    h2Tl = P.sbuf("h2Tl", [128, 8, 128], BF16)
    lg = P.sbuf("lg", [128, 16], F32)
    mx = P.sbuf("mx", [128, 1], F32)
    sm = P.sbuf("sm", [128, 1], F32)
    affs = [P.sbuf("affs%d" % i, [128, 16], F32) for i in range(2)]

    for c in range(9):
        tok0, n = chunkB(c)
        m = 1 if c == 0 else 0
        tsz = 64 if c == 0 else 128
        nt = n // tsz
        P.dma(yt, yt[:, :, :n], yT, yT[:, tok0:tok0 + n].rearrange("(j p) t -> p j t", p=128))
        for i in range(nt):
            xn = xnb[i % 2]
            P.dma(xc, xc[:tsz, i, :], xin, xin[tok0 + i * tsz: tok0 + (i + 1) * tsz, :])
            ln_tile(P, xc, xc[:tsz, i, :], tsz, st6, mv, rstd, epst, xn, xn[:tsz, :])
            for k in range(8):
                P.tr(tp, tp[:, k * 128:k * 128 + tsz], xn, xn[:tsz, k * 128:(k + 1) * 128], ident, ident[:tsz, :tsz])
            P.tt("dve", htmp, htmp[:, :, :tsz], tp, tp[:].rearrange("p (a b) -> p a b", a=8)[:, :, :tsz],
                 scp, scp[:, :, m:m + 1].to_broadcast([128, 8, tsz]), ALU.mult)
            P.tt("pool", hT, hT[:, :, i * tsz:(i + 1) * tsz], htmp, htmp[:, :, :tsz],
                 ms, ms[:, :, 2 * m:2 * m + 1].to_broadcast([128, 8, tsz]), ALU.add)
        for oc in range(8):
            for br in range(3):
                G = P.bank()
                for k in range(8):
                    P.mm(G, G[:, :n], Wg, Wg[:, k, br * 1024 + oc * 128: br * 1024 + (oc + 1) * 128], hT, hT[:, k, :n],
                         start=(k == 0), stop=(k == 7))
                sg = sig[br % 2]
                P.act(sg, sg[:, :n], G, G[:, :n], AF.Sigmoid, extra=[bg], bias=bg[:, br * 8 + oc: br * 8 + oc + 1])
                Pb = P.bank()
                for k in range(4):
                    P.mm(Pb, Pb[:, :n], Wb, Wb[:, br * 4 + k, oc * 128:(oc + 1) * 128], yt, yt[:, br * 4 + k, :n],
                         start=(k == 0), stop=(k == 3))
                if br == 0:
                    P.tt("dve", mf, mf[:, :n], Pb, Pb[:, :n], sg, sg[:, :n], ALU.mult)
                else:
                    P.tt("dve", mt2, mt2[:, :n], Pb, Pb[:, :n], sg, sg[:, :n], ALU.mult)
                    P.tt("pool", mf, mf[:, :n], mf, mf[:, :n], mt2, mt2[:, :n], ALU.add)
            P.cp("pool", mT, mT[:, oc, :n], mf, mf[:, :n])
        for i in range(nt):
            for half in range(2):
                O = P.bank()
                for k in range(8):
                    P.mm(O, O[:tsz, :], mT, mT[:, k, i * tsz:(i + 1) * tsz], Wo, Wo[:, k, half * 512:(half + 1) * 512],
                         start=(k == 0), stop=(k == 7))
                hs = slice(half * 512, (half + 1) * 512)
                P.tt("dve", ut, ut[:tsz, hs], O, O[:tsz, :], rw, rw[:tsz, m, hs], ALU.mult)
            P.stt("dve", u, u[:tsz, :], xc, xc[:tsz, i, :], ALPHA, ut, ut[:tsz, :], ALU.mult, ALU.add)
            x1t = x1[0]
            ln_tile(P, u, u[:tsz, :], tsz, st6, mv, rstd, epst, x1t, x1t[:tsz, :])
            P.tt("dve", x1t, x1t[:tsz, :], x1t, x1t[:tsz, :], rw, rw[:tsz, 2, :], ALU.mult)
            P.tt("pool", x1t, x1t[:tsz, :], x1t, x1t[:tsz, :], rw, rw[:tsz, 3, :], ALU.add)
            r0 = tok0 + i * tsz
            P.dma(x1o, x1o[r0:r0 + tsz, :], x1t, x1t[:tsz, :])
            ln_tile(P, x1t, x1t[:tsz, :], tsz, st6, mv, rstd, epst, h2, h2[:tsz, :])
            P.tt("dve", h2, h2[:tsz, :], h2, h2[:tsz, :], rw, rw[:tsz, 4 + 2 * m, :], ALU.mult)
            P.tt("pool", h2, h2[:tsz, :], h2, h2[:tsz, :], rw, rw[:tsz, 5 + 2 * m, :], ALU.add)
            P.cp("act", h2hi, h2hi[:tsz, :], h2, h2[:tsz, :])
            P.tt("dve", h2lo, h2lo[:tsz, :], h2, h2[:tsz, :], h2hi, h2hi[:tsz, :], ALU.subtract)
            hs_ = h2Ts[0]
            for k in range(8):
                P.tr(tp, tp[:, k * 128:k * 128 + tsz], h2hi, h2hi[:tsz, k * 128:(k + 1) * 128], ident, ident[:tsz, :tsz])
            P.cp("act", hs_, hs_[:, :, :tsz], tp, tp[:].rearrange("p (a b) -> p a b", a=8)[:, :, :tsz])
            P.dma(h2tmo, h2tmo[r0:r0 + tsz, :], h2hi, h2hi[:tsz, :])
            for k in range(8):
                P.tr(tp, tp[:, k * 128:k * 128 + tsz], h2lo, h2lo[:tsz, k * 128:(k + 1) * 128], ident, ident[:tsz, :tsz])
            P.cp("dve", h2Tl, h2Tl[:, :, :tsz], tp, tp[:].rearrange("p (a b) -> p a b", a=8)[:, :, :tsz])
            L = P.bank()
            for k in range(8):
                P.mm(L, L[:tsz, 0:16], hs_, hs_[:, k, :tsz], wr_hi, wr_hi[:, k, :], start=(k == 0), stop=False)
                P.mm(L, L[:tsz, 0:16], hs_, hs_[:, k, :tsz], wr_lo, wr_lo[:, k, :], start=False, stop=False)
                P.mm(L, L[:tsz, 0:16], h2Tl, h2Tl[:, k, :tsz], wr_hi, wr_hi[:, k, :], start=False, stop=(k == 7))
            P.op("dve", lambda e, L=L, tsz=tsz: e.reduce_max(mx[:tsz, :], L[:tsz, 0:16], AX.X), reads=[L], writes=[mx])
            P.ts("dve", lg, lg[:tsz, :], L, L[:tsz, 0:16], mx[:tsz, 0:1], None, ALU.subtract, extra=[mx])
            P.act(lg, lg[:tsz, :], lg, lg[:tsz, :], AF.Exp)
            P.op("dve", lambda e, tsz=tsz: e.reduce_sum(sm[:tsz, :], lg[:tsz, :], AX.X), reads=[lg], writes=[sm])
            P.op("dve", lambda e, tsz=tsz: e.reciprocal(sm[:tsz, :], sm[:tsz, :]), reads=[sm], writes=[sm])
            af_ = affs[i % 2]
            P.ts("dve", af_, af_[:tsz, :], lg, lg[:tsz, :], sm[:tsz, 0:1], None, ALU.mult, extra=[sm])
            P.dma(affo, affo[r0:r0 + tsz, :], af_, af_[:tsz, :])
    P.finish()
    return P.build()


def prep_B(inp, l, mod, x_cur, ctx_cur, yTs):
    C = _consts()
    wbr = np.ascontiguousarray(np.concatenate([inp["w_br_a"][l], inp["w_br_b"][l], inp["w_br_c"][l]], 0))
    maps = []
    for core in range(8):
        b, s = core // 4, core % 4
        xin = np.concatenate([ctx_cur[b][64 * s:64 * (s + 1)], x_cur[b][4096 * s:4096 * (s + 1)]], 0)
        yT = np.concatenate([yTs[b][:, 64 * s:64 * (s + 1)], yTs[b][:, 256 + 4096 * s:256 + 4096 * (s + 1)]], 1)
        msT = np.stack([_pm(mod[0:1024, b]), _pm(mod[1024:2048, b]), _pm(mod[0:1024, 2]), _pm(mod[1024:2048, 2])], -1)
        rows = np.stack([mod[2048:3072, b], mod[2048:3072, 2], inp["ln1_g"][l], inp["ln1_b"][l],
                         mod[4096:5120, b], mod[3072:4096, b], mod[4096:5120, 2], mod[3072:4096, 2]], 0)
        maps.append({
            "xin": np.ascontiguousarray(xin), "yT": np.ascontiguousarray(yT), "msT": np.ascontiguousarray(msT),
            "wgate": inp["w_gate"][l], "bgT": _pm(inp["b_gate"][l]), "wbr": wbr, "wout": inp["w_out"][l],
            "wrt": np.ascontiguousarray(inp["w_router"][l].reshape(8, 128, 16).transpose(1, 0, 2)),
            "rows": np.ascontiguousarray(np.broadcast_to(rows[None], (128, 8, 1024))), "ident": C["ident"],
        })
    return maps


def chunkC(c):
    return (0, 256) if c == 0 else (256 + (c - 1) * 1024, 1024)


def build_C_dense():
    P = Prog("C")
    EI, EO = "ExternalInput", "ExternalOutput"
    h2T = P.dram("h2T", [1024, NT], BF16, EI)
    affd = P.dram("affL", [128, 4, 130], F32, EI)
    wg = P.dram("wg", [4, 1024, 2816], F32, EI)
    wu = P.dram("wu", [4, 1024, 2816], F32, EI)
    wd = P.dram("wd", [4, 2816, 1024], F32, EI)
    fpo = P.dram("fp", [NT, 1024], F32, EO)
    P.start()
    P.banks(7)
    onesf = P.sbuf("onesf", [128, 128], F32)
    P.memset("dve", onesf, onesf[:], 1.0)
    aff = P.sbuf("aff", [128, 4, 130], F32)
    P.dma(aff, aff[:], affd, affd[:])
    lo = P.sbuf("lo", [128, 8], F32)
    hi = P.sbuf("hi", [128, 8], F32)
    mid = P.sbuf("mid", [128, 8], F32)
    kv = P.sbuf("kv", [128, 8], F32)
    cnt = P.sbuf("cnt", [128, 8], F32)
    ge = P.sbuf("ge", [128, 8], F32)
    d1 = P.sbuf("d1", [128, 8], F32)
    cmp_ = P.sbuf("cmp", [128, 128], F32)
    P.memset("dve", lo, lo[:], 0.0)
    P.memset("dve", hi, hi[:], 1.0)
    P.memset("dve", kv, kv[:, 0:4], 2048.0)
    P.memset("dve", kv, kv[:, 4:8], 32.0)
    for it in range(30):
        P.tt("dve", mid, mid[:], lo, lo[:], hi, hi[:], ALU.add)
        P.ts("dve", mid, mid[:], mid, mid[:], 0.5, None, ALU.mult)
        for e in range(4):
            P.ts("dve", cmp_, cmp_[:, 0:128], aff, aff[:, e, 2:130], mid[:, e:e + 1], None, ALU.is_ge, extra=[mid])
            P.op("dve", lambda e_, e=e: e_.reduce_sum(cnt[:, e:e + 1], cmp_[:, 0:128], AX.X), reads=[cmp_], writes=[cnt])
            P.ts("dve", cmp_, cmp_[:, 0:2], aff, aff[:, e, 0:2], mid[:, 4 + e:5 + e], None, ALU.is_ge, extra=[mid])
            P.op("dve", lambda e_, e=e: e_.reduce_sum(cnt[:, 4 + e:5 + e], cmp_[:, 0:2], AX.X), reads=[cmp_], writes=[cnt])
        tb = P.bank()
        P.mm(tb, tb[:, 0:8], onesf, onesf[:], cnt, cnt[:])
        P.tt("dve", ge, ge[:], tb, tb[:, 0:8], kv, kv[:], ALU.is_ge)
        P.tt("dve", d1, d1[:], mid, mid[:], lo, lo[:], ALU.subtract)
        P.tt("dve", d1, d1[:], d1, d1[:], ge, ge[:], ALU.mult)
        P.tt("dve", lo, lo[:], lo, lo[:], d1, d1[:], ALU.add)
        P.tt("dve", d1, d1[:], hi, hi[:], mid, mid[:], ALU.subtract)
        P.tt("dve", d1, d1[:], d1, d1[:], ge, ge[:], ALU.mult)
        P.tt("dve", hi, hi[:], mid, mid[:], d1, d1[:], ALU.add)
    gm = P.sbuf("gm", [128, 4, 130], F32)
    for e in range(4):
        P.ts("dve", gm, gm[:, e, 2:130], aff, aff[:, e, 2:130], lo[:, e:e + 1], None, ALU.is_ge, extra=[lo])
        P.ts("dve", gm, gm[:, e, 0:2], aff, aff[:, e, 0:2], lo[:, 4 + e:5 + e], None, ALU.is_ge, extra=[lo])
    P.tt("dve", gm, gm[:], gm, gm[:], aff, aff[:], ALU.mult)

    hxs = [P.sbuf("hx%d" % i, [128, 8, 1024], BF16) for i in range(2)]
    facc = P.sbuf("facc", [128, 8, 1024], F32)
    NWB = 4
    wgs = [P.sbuf("wgu%d" % i, [128, 8, 256], BF16) for i in range(NWB)]
    wus = [P.sbuf("wuu%d" % i, [128, 8, 256], BF16) for i in range(NWB)]
    wds = [P.sbuf("wdu%d" % i, [128, 2, 1024], BF16) for i in range(NWB)]
    hids = [P.sbuf("hid%d" % i, [128, 2, 1024], BF16) for i in range(2)]
    sgs = [P.sbuf("sg%d" % i, [128, 512], F32) for i in range(2)]
    ui = 0
    si = 0
    for c in range(17):
        tok0, n = chunkC(c)
        hx = hxs[c % 2]
        P.dma(hx, hx[:, :, :n], h2T, h2T[:, tok0:tok0 + n].rearrange("(k p) t -> p k t", p=128))
        P.memset("pool", facc, facc[:], 0.0)
        units = [(e, un) for e in range(4) for un in range(11)]
        pend = []
        for idx in range(len(units) + 1):
            if idx < len(units):
                e, un = units[idx]
                f0 = un * 256
                wgu, wuu, wdu, hid = wgs[ui % NWB], wus[ui % NWB], wds[ui % NWB], hids[ui % 2]
                ui += 1
                P.dma(wgu, wgu[:], wg, wg[e][:, f0:f0 + 256].rearrange("(k p) f -> p k f", p=128), q="pool")
                P.dma(wuu, wuu[:], wu, wu[e][:, f0:f0 + 256].rearrange("(k p) f -> p k f", p=128), q="pool")
                P.dma(wdu, wdu[:], wd, wd[e][f0:f0 + 256, :].rearrange("(c p) o -> p c o", p=128), q="pool")
                for fc in range(2):
                    for th in range(n // 512 if n >= 512 else 1):
                        w_ = min(512, n)
                        ts_ = slice(th * 512, th * 512 + w_)
                        G = P.bank()
                        for k in range(8):
                            P.mm(G, G[:, :w_], wgu, wgu[:, k, fc * 128:(fc + 1) * 128], hx, hx[:, k, ts_], start=(k == 0), stop=(k == 7))
                        U = P.bank()
                        for k in range(8):
                            P.mm(U, U[:, :w_], wuu, wuu[:, k, fc * 128:(fc + 1) * 128], hx, hx[:, k, ts_], start=(k == 0), stop=(k == 7))
                        sg = sgs[si % 2]
                        si += 1
                        P.act(sg, sg[:, :w_], G, G[:, :w_], AF.Silu)
                        P.tt("dve", hid, hid[:, fc, ts_], U, U[:, :w_], sg, sg[:, :w_], ALU.mult)
                pend.append((e, wdu, hid))
            j = idx - 1
            if j >= 0:
                e, wdu, hid = pend[j]
                for tl in range(n // 128):
                    gt = tok0 // 128 + tl
                    for oh in range(2):
                        O = P.bank()
                        for fc in range(2):
                            P.mm(O, O[:, :], hid, hid[:, fc, tl * 128:(tl + 1) * 128], wdu, wdu[:, fc, oh * 512:(oh + 1) * 512],
                                 start=(fc == 0), stop=(fc == 1))
                        fs = facc[:, tl, oh * 512:(oh + 1) * 512]
                        P.stt("dve", facc, fs, O, O[:, :], gm[:, e, gt:gt + 1], facc, fs, ALU.mult, ALU.add, extra=[gm])
        P.dma(fpo, fpo[tok0:tok0 + n, :].rearrange("(t p) o -> p t o", p=128), facc, facc[:, :n // 128, :])
    P.finish()
    return P.build()


XROWS = 2304
TRASH0 = 2176


def build_C():
    P = Prog("C")
    EI, EO = "ExternalInput", "ExternalOutput"
    h2tm = P.dram("h2tm", [NT, 1024], BF16, EI)
    affd = P.dram("affL", [128, 4, 130], F32, EI)
    trid = P.dram("tri", [128, 128], F32, EI)
    trashd = P.dram("trashc", [128, 1], F32, EI)
    identd = P.dram("ident", [128, 128], F32, EI)
    wg = P.dram("wg", [4, 1024, 2816], F32, EI)
    wu = P.dram("wu", [4, 1024, 2816], F32, EI)
    wd = P.dram("wd", [4, 2816, 1024], F32, EI)
    fpo = P.dram("fp", [NT, 1024], F32, EO)
    xsel = [P.dram("xsel%d" % e, [XROWS, 1024], BF16) for e in range(4)]
    osel = [P.dram("osel%d" % e, [XROWS, 1024], F32) for e in range(4)]
    P.start()
    P.banks(7)
    tp = P.psum("tp", [128, 1024], BF16)
    onesf = P.sbuf("onesf", [128, 128], F32)
    P.memset("dve", onesf, onesf[:], 1.0)
    onesb = P.sbuf("onesb", [128, 128], BF16)
    P.memset("dve", onesb, onesb[:], 1.0)
    trib = P.sbuf("trib", [128, 128], BF16)
    P.dma(trib, trib[:], trid, trid[:], q="pool")
    ident = P.sbuf("identb", [128, 128], BF16)
    P.dma(ident, ident[:], identd, identd[:], q="pool")
    trc = P.sbuf("trc", [128, 1], F32)
    P.dma(trc, trc[:], trashd, trashd[:])
    aff = P.sbuf("aff", [128, 4, 130], F32)
    P.dma(aff, aff[:], affd, affd[:])
    lo = P.sbuf("lo", [128, 8], F32)
    hi = P.sbuf("hi", [128, 8], F32)
    mid = P.sbuf("mid", [128, 8], F32)
    kv = P.sbuf("kv", [128, 8], F32)
    cnt = P.sbuf("cnt", [128, 8], F32)
    ge = P.sbuf("ge", [128, 8], F32)
    d1 = P.sbuf("d1", [128, 8], F32)
    cmp_ = P.sbuf("cmp", [128, 128], F32)
    P.memset("dve", lo, lo[:], 0.0)
    P.memset("dve", hi, hi[:], 1.0)
    P.memset("dve", kv, kv[:, 0:4], 2048.0)
    P.memset("dve", kv, kv[:, 4:8], 32.0)
    for it in range(30):
        P.tt("dve", mid, mid[:], lo, lo[:], hi, hi[:], ALU.add)
        P.ts("dve", mid, mid[:], mid, mid[:], 0.5, None, ALU.mult)
        for e in range(4):
            P.ts("dve", cmp_, cmp_[:, 0:128], aff, aff[:, e, 2:130], mid[:, e:e + 1], None, ALU.is_ge, extra=[mid])
            P.op("dve", lambda e_, e=e: e_.reduce_sum(cnt[:, e:e + 1], cmp_[:, 0:128], AX.X), reads=[cmp_], writes=[cnt])
            P.ts("dve", cmp_, cmp_[:, 0:2], aff, aff[:, e, 0:2], mid[:, 4 + e:5 + e], None, ALU.is_ge, extra=[mid])
            P.op("dve", lambda e_, e=e: e_.reduce_sum(cnt[:, 4 + e:5 + e], cmp_[:, 0:2], AX.X), reads=[cmp_], writes=[cnt])
        tb = P.bank()
        P.mm(tb, tb[:, 0:8], onesf, onesf[:], cnt, cnt[:])
        P.tt("dve", ge, ge[:], tb, tb[:, 0:8], kv, kv[:], ALU.is_ge)
        P.tt("dve", d1, d1[:], mid, mid[:], lo, lo[:], ALU.subtract)
        P.tt("dve", d1, d1[:], d1, d1[:], ge, ge[:], ALU.mult)
        P.tt("dve", lo, lo[:], lo, lo[:], d1, d1[:], ALU.add)
        P.tt("dve", d1, d1[:], hi, hi[:], mid, mid[:], ALU.subtract)
        P.tt("dve", d1, d1[:], d1, d1[:], ge, ge[:], ALU.mult)
        P.tt("dve", hi, hi[:], mid, mid[:], d1, d1[:], ALU.add)
    m = P.sbuf("m", [128, 4, 130], F32)
    for e in range(4):
        P.ts("dve", m, m[:, e, 2:130], aff, aff[:, e, 2:130], lo[:, e:e + 1], None, ALU.is_ge, extra=[lo])
        P.ts("dve", m, m[:, e, 0:2], aff, aff[:, e, 0:2], lo[:, 4 + e:5 + e], None, ALU.is_ge, extra=[lo])
    mb = P.sbuf("mb", [128, 4, 130], BF16)
    P.cp("dve", mb, mb[:], m, m[:])
    posf = P.sbuf("posf", [128, 4, 130], F32)
    posi = P.sbuf("posi", [128, 4, 130], I32)
    cT = P.sbuf("cT", [128, 128], BF16)
    offs = P.sbuf("offs", [128, 130], F32)
    ltm = P.sbuf("ltm", [128, 130], F32)
    for e in range(4):
        R = P.bank()
        P.mm(R, R[:, 0:130], trib, trib[:], mb, mb[:, e, :])
        CT = P.bank()
        P.mm(CT, CT[:, 0:128], mb, mb[:, e, 2:130], onesb, onesb[:])
        P.cp("act", cT, cT[:], CT, CT[:, 0:128])
        OF = P.bank()
        P.mm(OF, OF[:, 0:128], cT, cT[:], trib, trib[:])
        C0 = P.bank()
        P.mm(C0, C0[:, 0:2], onesb, onesb[:], mb, mb[:, e, 0:2])
        P.cp("act", offs, offs[:, 2:130], OF, OF[:, 0:128])
        P.memset("dve", offs, offs[:, 0:1], 2048.0)
        P.ts("dve", offs, offs[:, 1:2], C0, C0[:, 0:1], 2048.0, None, ALU.add)
        P.tt("dve", posf, posf[:, e, :], R, R[:, 0:130], offs, offs[:], ALU.add)
        P.ts("dve", ltm, ltm[:, 2:130], posf, posf[:, e, 2:130], 2047.5, None, ALU.is_lt)
        P.ts("dve", ltm, ltm[:, 0:2], posf, posf[:, e, 0:2], 2079.5, None, ALU.is_lt)
        P.tt("dve", m, m[:, e, :], m, m[:, e, :], ltm, ltm[:], ALU.mult)
        P.ts("dve", posf, posf[:, e, :], posf, posf[:, e, :], trc[:, 0:1], None, ALU.subtract, extra=[trc])
        P.tt("dve", posf, posf[:, e, :], posf, posf[:, e, :], m, m[:, e, :], ALU.mult)
        P.ts("dve", posf, posf[:, e, :], posf, posf[:, e, :], trc[:, 0:1], None, ALU.add, extra=[trc])
    P.cp("dve", posi, posi[:], posf, posf[:])
    gm = P.sbuf("gm", [128, 4, 130], F32)
    P.tt("dve", gm, gm[:], m, m[:], aff, aff[:], ALU.mult)

    htb = [P.sbuf("ht%d" % i, [128, 1024], BF16) for i in range(2)]
    for i in range(130):
        ht = htb[i % 2]
        P.dma(ht, ht[:], h2tm, h2tm[i * 128:(i + 1) * 128, :])
        for e in range(4):
            P.op("pool", lambda e_, e=e, i=i, ht=ht: e_.indirect_dma_start(
                out=xsel[e][:, :], out_offset=bass.IndirectOffsetOnAxis(ap=posi[:, e, i:i + 1], axis=0),
                in_=ht[:, :], in_offset=None),
                reads=[ht, posi], writes=[xsel[e]], dma_dst=xsel[e])
    zt = P.sbuf("zt", [128, 1024], F32)
    P.memset("pool", zt, zt[:], 0.0)
    for e in range(4):
        P.dma(osel[e], osel[e][TRASH0:TRASH0 + 128, :], zt, zt[:])

    NSL = 2080
    hx = P.sbuf("hx", [128, 8, NSL], BF16)
    oacc = P.sbuf("oacc", [128, 17, 1024], F32)
    NWB = 3
    wgs = [P.sbuf("wgu%d" % i, [128, 8, 256], BF16) for i in range(NWB)]
    wus = [P.sbuf("wuu%d" % i, [128, 8, 256], BF16) for i in range(NWB)]
    wds = [P.sbuf("wdu%d" % i, [128, 2, 1024], BF16) for i in range(NWB)]
    hids = [P.sbuf("hid%d" % i, [128, 2, NSL], BF16) for i in range(2)]
    sgs = [P.sbuf("sg%d" % i, [128, 512], F32) for i in range(2)]
    xts = [P.sbuf("xts%d" % i, [128, 1024], BF16) for i in range(2)]
    cchunks = [(j * 512, 512) for j in range(4)] + [(2048, 32)]
    ui = 0
    si = 0
    for e in range(4):
        for st in range(17):
            rows = 128 if st < 16 else 32
            xt = xts[st % 2]
            P.dma(xt, xt[:rows, :], xsel[e], xsel[e][st * 128:st * 128 + rows, :])
            for k in range(8):
                P.tr(tp, tp[:, k * 128:k * 128 + rows], xt, xt[:rows, k * 128:(k + 1) * 128], ident, ident[:rows, :rows])
            P.cp("act" if st % 2 else "dve", hx, hx[:, :, st * 128:st * 128 + rows],
                 tp, tp[:].rearrange("p (a b) -> p a b", a=8)[:, :, :rows])
        pend = []
        for idx in range(12):
            if idx < 11:
                f0 = idx * 256
                wgu, wuu, wdu, hid = wgs[ui % NWB], wus[ui % NWB], wds[ui % NWB], hids[ui % 2]
                ui += 1
                P.dma(wgu, wgu[:], wg, wg[e][:, f0:f0 + 256].rearrange("(k p) f -> p k f", p=128), q="pool")
                P.dma(wuu, wuu[:], wu, wu[e][:, f0:f0 + 256].rearrange("(k p) f -> p k f", p=128), q="pool")
                P.dma(wdu, wdu[:], wd, wd[e][f0:f0 + 256, :].rearrange("(c p) o -> p c o", p=128), q="pool")
                for fc in range(2):
                    for (c0, w_) in cchunks:
                        ts_ = slice(c0, c0 + w_)
                        G = P.bank()
                        for k in range(8):
                            P.mm(G, G[:, :w_], wgu, wgu[:, k, fc * 128:(fc + 1) * 128], hx, hx[:, k, ts_], start=(k == 0), stop=(k == 7))
                        U = P.bank()
                        for k in range(8):
                            P.mm(U, U[:, :w_], wuu, wuu[:, k, fc * 128:(fc + 1) * 128], hx, hx[:, k, ts_], start=(k == 0), stop=(k == 7))
                        sg = sgs[si % 2]
                        si += 1
                        P.act(sg, sg[:, :w_], G, G[:, :w_], AF.Silu)
                        P.tt("dve", hid, hid[:, fc, ts_], U, U[:, :w_], sg, sg[:, :w_], ALU.mult)
                pend.append((wdu, hid))
            j = idx - 1
            if j >= 0:
                wdu, hid = pend[j]
                for st in range(17):
                    rows = 128 if st < 16 else 32
                    for oh in range(2):
                        O = P.bank()
                        for fc in range(2):
                            P.mm(O, O[:rows, :], hid, hid[:, fc, st * 128:st * 128 + rows], wdu, wdu[:, fc, oh * 512:(oh + 1) * 512],
                                 start=(fc == 0), stop=(fc == 1))
                        os_ = oacc[:rows, st, oh * 512:(oh + 1) * 512]
                        if j == 0:
                            P.cp("act", oacc, os_, O, O[:rows, :])
                        else:
                            P.tt("dve", oacc, os_, O, O[:rows, :], oacc, os_, ALU.add)
        P.dma(osel[e], osel[e][0:2048, :].rearrange("(t p) o -> p t o", p=128), oacc, oacc[:, 0:16, :])
        P.dma(osel[e], osel[e][2048:2080, :], oacc, oacc[:32, 16, :])

    gts = []
    for i in range(2):
        gv = T(P, oacc.h[:, 4 * i:4 * i + 4, :], "gtv%d" % i)
        P.tiles.append(gv)
        gts.append(gv)
    P.op("dve", lambda e: e.memset(oacc[:, 0, 0:1], 0.0), reads=[], writes=[oacc, gts[0], gts[1]])
    fos = [P.sbuf("fo0", [128, 1024], F32), zt]
    for i in range(130):
        gt, fo = gts[i % 2], fos[i % 2]
        for e in range(4):
            P.op("pool", lambda e_, e=e, i=i, gt=gt: e_.indirect_dma_start(
                out=gt[:, e, :], out_offset=None, in_=osel[e][:, :],
                in_offset=bass.IndirectOffsetOnAxis(ap=posi[:, e, i:i + 1], axis=0)),
                reads=[osel[e], posi], writes=[gt], dma_dst=gt)
        P.ts("dve", fo, fo[:], gt, gt[:, 0, :], gm[:, 0, i:i + 1], None, ALU.mult, extra=[gm])
        for e in range(1, 4):
            P.stt("dve", fo, fo[:], gt, gt[:, e, :], gm[:, e, i:i + 1], fo, fo[:], ALU.mult, ALU.add, extra=[gm])
        P.dma(fpo, fpo[i * 128:(i + 1) * 128, :], fo, fo[:])
    P.finish()
    return P.build()


def build_D():
    P = Prog("D")
    EI, EO = "ExternalInput", "ExternalOutput"
    x1d = P.dram("x1", [NB, 1024], F32, EI)
    fps = P.dram("fps", [4, NB, 1024], F32, EI)
    rows = P.dram("rows", [128, 4, 1024], F32, EI)
    x2o = P.dram("x2", [NB, 1024], F32, EO)
    P.start()
    rw = P.sbuf("rw", [128, 4, 1024], F32)
    P.dma(rw, rw[:], rows, rows[:])
    epst = P.sbuf("epst", [128, 1], F32)
    P.memset("dve", epst, epst[:], LN_EPS)
    st6 = P.sbuf("st6", [128, 2, 6], F32)
    mv = P.sbuf("mv", [128, 2], F32)
    rstd = P.sbuf("rstd", [128, 1], F32)
    xts = [P.sbuf("xt%d" % i, [128, 1024], F32) for i in range(2)]
    pts = [P.sbuf("pp%d" % i, [128, 4, 1024], F32) for i in range(2)]
    u = P.sbuf("u", [128, 1024], F32)
    ots = [P.sbuf("ot%d" % i, [128, 1024], F32) for i in range(2)]
    for t in range(33):
        r0, tsz = (0, 64) if t == 0 else (64 + (t - 1) * 128, 128)
        m = 1 if t == 0 else 0
        xt, pp, ot = xts[t % 2], pts[t % 2], ots[t % 2]
        P.dma(xt, xt[:tsz, :], x1d, x1d[r0:r0 + tsz, :])
        for j in range(4):
            P.dma(pp, pp[:tsz, j, :], fps, fps[j, r0:r0 + tsz, :])
        P.tt("dve", pp, pp[:tsz, 0, :], pp, pp[:tsz, 0, :], pp, pp[:tsz, 1, :], ALU.add)
        P.tt("pool", pp, pp[:tsz, 2, :], pp, pp[:tsz, 2, :], pp, pp[:tsz, 3, :], ALU.add)
        P.tt("dve", pp, pp[:tsz, 0, :], pp, pp[:tsz, 0, :], pp, pp[:tsz, 2, :], ALU.add)
        P.tt("dve", pp, pp[:tsz, 0, :], pp, pp[:tsz, 0, :], rw, rw[:tsz, m, :], ALU.mult)
        P.stt("dve", u, u[:tsz, :], xt, xt[:tsz, :], ALPHA, pp, pp[:tsz, 0, :], ALU.mult, ALU.add)
        ln_tile(P, u, u[:tsz, :], tsz, st6, mv, rstd, epst, ot, ot[:tsz, :])
        P.tt("dve", ot, ot[:tsz, :], ot, ot[:tsz, :], rw, rw[:tsz, 2, :], ALU.mult)
        P.tt("pool", ot, ot[:tsz, :], ot, ot[:tsz, :], rw, rw[:tsz, 3, :], ALU.add)
        P.dma(x2o, x2o[r0:r0 + tsz, :], ot, ot[:tsz, :])
    P.finish()
    return P.build()


def _lam_init(l):
    import math
    return 0.8 - 0.6 * math.exp(-0.3 * l)


def kernel(**inp):
    inp = {k: np.asarray(v) for k, v in inp.items()}
    mods = run_L0(inp)
    x_cur = [np.asarray(inp["x"][b], np.float32) for b in range(2)]
    ctx_cur = [np.asarray(inp["ctx"][b], np.float32) for b in range(2)]
    for l in range(2):
        mod = mods[l]
        resA = run(build_A(_lam_init(l)), prep_A(inp, l, mod, x_cur, ctx_cur)).results
        yTs = []
        for b in range(2):
            parts = [np.asarray(resA[b * 4 + g][nm]) for nm in ("yaT", "ybT", "ycT") for g in range(4)]
            yTs.append(np.concatenate(parts, 0))
        del resA
        resB = run(build_B(), prep_B(inp, l, mod, x_cur, ctx_cur, yTs)).results
        x1p = [np.asarray(resB[c]["x1"]) for c in range(8)]
        h2Tf, afff = [], []
        for b in range(2):
            hp = [np.asarray(resB[b * 4 + s]["h2tm"]) for s in range(4)]
            ap = [np.asarray(resB[b * 4 + s]["aff"]) for s in range(4)]
            h2Tf.append(np.concatenate([p[:64] for p in hp] + [p[64:] for p in hp], 0))
            afff.append(np.concatenate([p[:64] for p in ap] + [p[64:] for p in ap], 0))
        del resB
        mapsC = []
        for core in range(8):
            b, j = core // 4, core % 4
            a4 = afff[b][:, 4 * j:4 * j + 4].reshape(130, 128, 4).transpose(1, 2, 0)
            mapsC.append({
                "h2tm": np.ascontiguousarray(h2Tf[b]), "affL": np.ascontiguousarray(a4),
                "tri": np.triu(np.ones((128, 128), np.float32), 1),
                "trashc": (TRASH0 + np.arange(128, dtype=np.float32)).reshape(128, 1),
                "ident": _consts()["ident"],
                "wg": np.ascontiguousarray(inp["w_exp_gate"][l][4 * j:4 * j + 4]),
                "wu": np.ascontiguousarray(inp["w_exp_up"][l][4 * j:4 * j + 4]),
                "wd": np.ascontiguousarray(inp["w_exp_down"][l][4 * j:4 * j + 4]),
            })
        resC = run(build_C(), mapsC).results
        fpc = [np.asarray(resC[c]["fp"]) for c in range(8)]
        del resC, mapsC
        rowsD = lambda b: np.ascontiguousarray(np.broadcast_to(
            np.stack([mod[5120:6144, b], mod[5120:6144, 2], inp["ln2_g"][l], inp["ln2_b"][l]], 0)[None], (128, 4, 1024)))
        mapsD = []
        for core in range(8):
            b, s = core // 4, core % 4
            fps = np.stack([np.concatenate([fpc[b * 4 + jj][64 * s:64 * (s + 1)],
                                            fpc[b * 4 + jj][256 + 4096 * s:256 + 4096 * (s + 1)]], 0) for jj in range(4)], 0)
            mapsD.append({"x1": x1p[core], "fps": np.ascontiguousarray(fps), "rows": rowsD(b)})
        resD = run(build_D(), mapsD).results
        for b in range(2):
            xp = [np.asarray(resD[b * 4 + s]["x2"]) for s in range(4)]
            ctx_cur[b] = np.concatenate([p[:64] for p in xp], 0)
            x_cur[b] = np.concatenate([p[64:] for p in xp], 0)
        del resD, mapsD, fpc
    return np.stack(x_cur, 0).astype(np.float32)
```
